# Optimizing a Trainium2 kernel written in Bass

```python
import jax, jax.numpy as jnp
from jax import lax
import numpy as np

D_MODEL = 1024
BATCH = 8
SEQ = 4096
DEPTH = 1
DEC_BATCH = 128
DEC_SEQ = 8
PAST_LEN = 16384
PAGE_SIZE = 128

N_HEADS = 8
N_KV_HEADS = 2
HEAD_DIM = 64
Q_GROUP = N_HEADS // N_KV_HEADS
WINDOW = 128
ATT_WIDTH = N_HEADS * HEAD_DIM
KV_WIDTH = N_KV_HEADS * HEAD_DIM
CHUNK = 128
SGU_WIDTH = D_MODEL // 2
SGU_GROUPS = 4
SGU_GROUP_DIM = SGU_WIDTH // SGU_GROUPS
N_BRANCH = 2
SPLITS = (ATT_WIDTH,
          ATT_WIDTH + KV_WIDTH,
          ATT_WIDTH + 2 * KV_WIDTH,
          ATT_WIDTH + 2 * KV_WIDTH + SGU_WIDTH,
          ATT_WIDTH + 2 * KV_WIDTH + 2 * SGU_WIDTH,
          ATT_WIDTH + 2 * KV_WIDTH + 2 * SGU_WIDTH + D_MODEL)
IN_WIDTH = ATT_WIDTH + 2 * KV_WIDTH + 2 * SGU_WIDTH + N_BRANCH * D_MODEL
PEER_HEADS = 8
PEER_N_KEYS = 128
PEER_N_EXPERTS = PEER_N_KEYS * PEER_N_KEYS
PEER_TOPK = 16
PEER_QDIM = 256
PEER_HALF = PEER_QDIM // 2
PEER_BLOCK = 256
PLE_DIM = 256
EPS = 1e-6
NEG_INF = -1e30

kernel_name = "hybrid_swa_sgu_peer_decoder_step"


def _rms_norm(x, g):
    x32 = x.astype(jnp.float32)
    y = x32 * lax.rsqrt(jnp.mean(x32 * x32, axis=-1, keepdims=True) + EPS)
    return (y * g.astype(jnp.float32)).astype(x.dtype)


def _layer_norm(x, g, b):
    x32 = x.astype(jnp.float32)
    mu = jnp.mean(x32, axis=-1, keepdims=True)
    xc = x32 - mu
    y = xc * lax.rsqrt(jnp.mean(xc * xc, axis=-1, keepdims=True) + EPS)
    return (y * g.astype(jnp.float32) + b.astype(jnp.float32)).astype(x.dtype)


def _alibi_slopes():
    h = jnp.arange(1, N_HEADS + 1, dtype=jnp.float32)
    return jnp.exp2(-8.0 * h / N_HEADS)


def _sink_attend(q, k, v, dist, valid, sinks):
    scores = jnp.einsum('...qkgd,...skd->...kgqs', q, k).astype(jnp.float32) * (HEAD_DIM ** -0.5)
    slopes = _alibi_slopes().reshape(N_KV_HEADS, Q_GROUP, 1, 1)
    scores = scores - slopes * dist.astype(jnp.float32)
    scores = jnp.where(valid, scores, NEG_INF)
    sink = jnp.broadcast_to(sinks.astype(jnp.float32).reshape(N_KV_HEADS, Q_GROUP, 1, 1),
                            scores.shape[:-1] + (1,))
    probs = jax.nn.softmax(jnp.concatenate([scores, sink], axis=-1), axis=-1)[..., :-1]
    return jnp.einsum('...kgqs,...skd->...qkgd', probs.astype(v.dtype), v)


def _window_attention_prompt(q, k, v, sinks):
    B, S = q.shape[:2]
    nb = S // WINDOW
    qb = q.reshape(B, nb, WINDOW, N_KV_HEADS, Q_GROUP, HEAD_DIM)
    kb = k.reshape(B, nb, WINDOW, N_KV_HEADS, HEAD_DIM)
    vb = v.reshape(B, nb, WINDOW, N_KV_HEADS, HEAD_DIM)
    pad = ((0, 0), (1, 0), (0, 0), (0, 0), (0, 0))
    kk = jnp.concatenate([jnp.pad(kb[:, :-1], pad), kb], axis=2)
    vv = jnp.concatenate([jnp.pad(vb[:, :-1], pad), vb], axis=2)
    i = jnp.arange(WINDOW)[:, None]
    r = jnp.arange(2 * WINDOW)[None, :]
    dist = i - r + WINDOW
    has_prev = (jnp.arange(nb) > 0)[:, None, None]
    valid = (dist >= 0) & (dist < WINDOW) & (has_prev | (r >= WINDOW))
    out = _sink_attend(qb, kk, vv, dist, valid[:, None, None], sinks)
    return out.reshape(B, S, ATT_WIDTH)


def _window_attention_sample(q, k_new, v_new, k_past, v_past, sinks):
    Bd, L = q.shape[:2]
    W = k_past.shape[1]
    kk = jnp.concatenate([k_past.astype(k_new.dtype), k_new], axis=1)
    vv = jnp.concatenate([v_past.astype(v_new.dtype), v_new], axis=1)
    key_off = jnp.concatenate([jnp.arange(W) - W, jnp.arange(L)])
    dist = jnp.arange(L)[:, None] - key_off[None, :]
    valid = (dist >= 0) & (dist < WINDOW)
    out = _sink_attend(q, kk, vv, dist, valid, sinks)
    return out.reshape(Bd, L, ATT_WIDTH)


def _sgu_prompt(v, w_s, b_s):
    B, S = v.shape[:2]
    nc = S // CHUNK
    vb = v.reshape(B, nc, CHUNK, SGU_GROUPS, SGU_GROUP_DIM)
    wm = w_s * jnp.tril(jnp.ones((CHUNK, CHUNK), w_s.dtype))
    s = jnp.einsum('gts,bnsgd->bntgd', wm, vb) + b_s.T[:, :, None]
    return s.reshape(B, S, SGU_WIDTH)


def _sgu_sample(v, w_s, b_s):
    Bd, L = v.shape[:2]
    vb = v.reshape(Bd, L, SGU_GROUPS, SGU_GROUP_DIM)
    wm = w_s[:, :L, :L] * jnp.tril(jnp.ones((L, L), w_s.dtype))
    s = jnp.einsum('gts,bsgd->btgd', wm, vb) + b_s[:, :L].T[:, :, None]
    return s.reshape(Bd, L, SGU_WIDTH)


def _peer(xn, w_q, sub_keys, expert_u, expert_v):
    shape = xn.shape
    t = xn.reshape(-1, D_MODEL)
    n = t.shape[0]
    blk = min(PEER_BLOCK, n)
    nb = -(-n // blk)
    t = jnp.pad(t, ((0, nb * blk - n), (0, 0)))

    def one_block(xb):
        q = (xb @ w_q).reshape(blk, PEER_HEADS, 2, PEER_HALF)
        sc = jnp.einsum('thcd,hcnd->thcn', q, sub_keys).astype(jnp.float32)
        s_top, i_top = lax.top_k(sc, PEER_TOPK)
        cand = (s_top[:, :, 0, :, None] + s_top[:, :, 1, None, :]).reshape(blk, PEER_HEADS, -1)
        cand_idx = (i_top[:, :, 0, :, None] * PEER_N_KEYS + i_top[:, :, 1, None, :]).reshape(blk, PEER_HEADS, -1)
        best, pos = lax.top_k(cand, PEER_TOPK)
        idx = jnp.take_along_axis(cand_idx, pos, axis=-1)
        gate = jax.nn.softmax(best, axis=-1)
        u = expert_u[idx]
        act = jax.nn.gelu(jnp.einsum('thkd,td->thk', u, xb).astype(jnp.float32))
        w = (gate * act).astype(xb.dtype)
        return jnp.einsum('thk,thkd->td', w, expert_v[idx])

    out = lax.map(one_block, t.reshape(nb, blk, D_MODEL))
    return out.reshape(-1, D_MODEL)[:n].reshape(shape)


def _layer(x, p_i, lp, past_k, past_v):
    prompt = past_k is None
    B, S = x.shape[:2]
    xn = _rms_norm(x, lp['attn_norm_g'])
    z = xn @ lp['w_in']
    q, k, v, su, sv, g_a, g_b = jnp.split(z, SPLITS, axis=-1)
    q = _rms_norm(q.reshape(B, S, N_HEADS, HEAD_DIM), lp['q_norm_g'])
    q = q.reshape(B, S, N_KV_HEADS, Q_GROUP, HEAD_DIM)
    k = _rms_norm(k.reshape(B, S, N_KV_HEADS, HEAD_DIM), lp['k_norm_g'])
    v = v.reshape(B, S, N_KV_HEADS, HEAD_DIM)
    if prompt:
        a = _window_attention_prompt(q, k, v, lp['attn_sinks'])
    else:
        a = _window_attention_sample(q, k, v, past_k, past_v, lp['attn_sinks'])
    u = jax.nn.gelu(su)
    vn = _layer_norm(jax.nn.gelu(sv), lp['sgu_norm_g'], lp['sgu_norm_b'])
    if prompt:
        s = _sgu_prompt(vn, lp['sgu_w'], lp['sgu_b'])
    else:
        s = _sgu_sample(vn, lp['sgu_w'], lp['sgu_b'])
    m = u * s
    h = jax.nn.sigmoid(g_a) * (a @ lp['w_branch_a']) + jax.nn.sigmoid(g_b) * (m @ lp['w_branch_b'])
    x1 = x + h @ lp['w_out']
    x2 = x1 + _peer(_rms_norm(x1, lp['ffn_norm_g']), lp['peer_w_q'], lp['peer_sub_keys'],
                    lp['peer_u'], lp['peer_v'])
    gate = jax.nn.sigmoid(_rms_norm(x2, lp['ple_norm_g']) @ lp['w_ple_gate'])
    x3 = x2 + gate * (p_i @ lp['w_ple'])
    if prompt:
        wp = min(WINDOW, S)
        return x3, k[:, S - wp:], v[:, S - wp:], vn[:, S - CHUNK:]
    return x3, k, v, vn


def setup_inputs(seed: int = 0) -> dict:
    key = jax.random.key(seed)
    ks = jax.random.split(key, 32)
    f32 = jnp.float32

    def nrm(k, shape, scale):
        return jax.random.normal(k, shape, f32) * scale

    w_buf = min(WINDOW, PAST_LEN)
    return {
        'x_prompt': nrm(ks[0], (BATCH, SEQ, D_MODEL), 1.0),
        'x_sample': nrm(ks[1], (DEC_BATCH, DEC_SEQ, D_MODEL), 1.0),
        'cache_k': nrm(ks[2], (DEPTH, DEC_BATCH, w_buf, N_KV_HEADS, HEAD_DIM), 1.0),
        'cache_v': nrm(ks[3], (DEPTH, DEC_BATCH, w_buf, N_KV_HEADS, HEAD_DIM), 1.0),
        'p_prompt': nrm(ks[4], (DEPTH, BATCH, SEQ, PLE_DIM), 1.0),
        'p_sample': nrm(ks[5], (DEPTH, DEC_BATCH, DEC_SEQ, PLE_DIM), 1.0),
        'attn_norm_g': 1.0 + nrm(ks[6], (DEPTH, D_MODEL), 0.02),
        'w_in': nrm(ks[7], (DEPTH, D_MODEL, IN_WIDTH), D_MODEL ** -0.5),
        'q_norm_g': 1.0 + nrm(ks[8], (DEPTH, HEAD_DIM), 0.02),
        'k_norm_g': 1.0 + nrm(ks[9], (DEPTH, HEAD_DIM), 0.02),
        'attn_sinks': nrm(ks[10], (DEPTH, N_HEADS), 0.5),
        'sgu_norm_g': 1.0 + nrm(ks[11], (DEPTH, SGU_WIDTH), 0.02),
        'sgu_norm_b': nrm(ks[12], (DEPTH, SGU_WIDTH), 0.02),
        'sgu_w': nrm(ks[13], (DEPTH, SGU_GROUPS, CHUNK, CHUNK), CHUNK ** -0.5),
        'sgu_b': 1.0 + nrm(ks[14], (DEPTH, SGU_GROUPS, CHUNK), 0.02),
        'w_branch_a': nrm(ks[15], (DEPTH, ATT_WIDTH, D_MODEL), ATT_WIDTH ** -0.5),
        'w_branch_b': nrm(ks[16], (DEPTH, SGU_WIDTH, D_MODEL), SGU_WIDTH ** -0.5),
        'w_out': nrm(ks[17], (DEPTH, D_MODEL, D_MODEL), D_MODEL ** -0.5),
        'ffn_norm_g': 1.0 + nrm(ks[18], (DEPTH, D_MODEL), 0.02),
        'peer_w_q': nrm(ks[19], (DEPTH, D_MODEL, PEER_HEADS * PEER_QDIM), D_MODEL ** -0.5),
        'peer_sub_keys': nrm(ks[20], (DEPTH, PEER_HEADS, 2, PEER_N_KEYS, PEER_HALF), PEER_HALF ** -0.5),
        'peer_u': nrm(ks[21], (DEPTH, PEER_N_EXPERTS, D_MODEL), D_MODEL ** -0.5),
        'peer_v': nrm(ks[22], (DEPTH, PEER_N_EXPERTS, D_MODEL), (PEER_HEADS * PEER_TOPK) ** -0.5),
        'ple_norm_g': 1.0 + nrm(ks[23], (DEPTH, D_MODEL), 0.02),
        'w_ple': nrm(ks[24], (DEPTH, PLE_DIM, D_MODEL), PLE_DIM ** -0.5),
        'w_ple_gate': nrm(ks[25], (DEPTH, D_MODEL, D_MODEL), D_MODEL ** -0.5),
    }


def reference(x_prompt, x_sample, cache_k, cache_v, p_prompt, p_sample,
              attn_norm_g, w_in, q_norm_g, k_norm_g, attn_sinks,
              sgu_norm_g, sgu_norm_b, sgu_w, sgu_b,
              w_branch_a, w_branch_b, w_out,
              ffn_norm_g, peer_w_q, peer_sub_keys, peer_u, peer_v,
              ple_norm_g, w_ple, w_ple_gate):
    hp, hs = x_prompt, x_sample
    nkp, nvp, nks, nvs, nsp, nss = [], [], [], [], [], []
    for i in range(DEPTH):
        lp = dict(attn_norm_g=attn_norm_g[i], w_in=w_in[i], q_norm_g=q_norm_g[i],
                  k_norm_g=k_norm_g[i], attn_sinks=attn_sinks[i],
                  sgu_norm_g=sgu_norm_g[i], sgu_norm_b=sgu_norm_b[i], sgu_w=sgu_w[i], sgu_b=sgu_b[i],
                  w_branch_a=w_branch_a[i], w_branch_b=w_branch_b[i], w_out=w_out[i],
                  ffn_norm_g=ffn_norm_g[i], peer_w_q=peer_w_q[i], peer_sub_keys=peer_sub_keys[i],
                  peer_u=peer_u[i], peer_v=peer_v[i],
                  ple_norm_g=ple_norm_g[i], w_ple=w_ple[i], w_ple_gate=w_ple_gate[i])
        hp, kp, vp, sp = _layer(hp, p_prompt[i], lp, None, None)
        hs, ks_, vs_, ss = _layer(hs, p_sample[i], lp, cache_k[i], cache_v[i])
        nkp.append(kp); nvp.append(vp); nsp.append(sp)
        nks.append(ks_); nvs.append(vs_); nss.append(ss)
    new_k_prompt = jnp.stack(nkp)
    new_v_prompt = jnp.stack(nvp)
    new_k_sample = jnp.stack(nks)
    new_v_sample = jnp.stack(nvs)
    new_sgu_v_prompt = jnp.stack(nsp)
    new_sgu_v_sample = jnp.stack(nss)
    return (hp, hs, new_k_prompt, new_v_prompt, new_k_sample, new_v_sample, new_sgu_v_prompt, new_sgu_v_sample)
```

```python
import numpy as np
from contextlib import ExitStack
import concourse.bass as bass
import concourse.mybir as mybir
from concourse.bass_utils import run_bass_kernel_spmd

F32 = mybir.dt.float32
BF16 = mybir.dt.bfloat16
I32 = mybir.dt.int32
U32 = mybir.dt.uint32
ALU = mybir.AluOpType
AF = mybir.ActivationFunctionType
AX = mybir.AxisListType

NCORES = 8
D = 1024
SEQ = 4096
NPT = SEQ // 128
NT = NPT + 1
NTOK = NT * 128
IN_W = 3840
EPS = 1e-6
NEG = -30000.0
GELU = AF.Gelu_apprx_tanh


class St:
    __slots__ = ("w", "r", "dsem")

    def __init__(self):
        self.w = None
        self.r = []
        self.dsem = None


class T:
    def __init__(self, name, h=None, st=None):
        self.name = name
        self.h = h
        self.st = st if st is not None else St()

    def __getitem__(self, k):
        return self.h[k]


class MK:
    def __init__(self, nc, es):
        self.nc = nc
        self.es_sem = es
        self.es = es
        self.engs = {'pe': nc.tensor, 'act': nc.scalar, 'dve': nc.vector, 'pool': nc.gpsimd, 'sp': nc.sync}
        self.sems = {}
        self.cnt = {}
        self.waited = {k: {} for k in self.engs}
        for k in ('pe', 'act', 'dve', 'pool'):
            self.sems[k] = es.enter_context(nc.semaphore("c_" + k))
            self.cnt[k] = 0
        self.nd = 0
        self.n_ins = 0
        self.rec = None
        self.rate = 4.0
        self.slot = 0
        self.inter = None
        self._in_inter = False
        self._inter_cnt = 0
        self.prev_eng = 'dve'
        self.prev_slot = -100

    def sb(self, name, shape, dtype):
        h = self.es.enter_context(self.nc.sbuf_tensor(name, list(shape), dtype))
        return T(name, h)

    def ps(self, name, shape, dtype=F32):
        h = self.es.enter_context(self.nc.psum_tensor(name, list(shape), dtype))
        return T(name, h)

    def _dsem(self, t):
        st = t.st
        if st.dsem is None:
            key = "d%d" % self.nd
            self.nd += 1
            self.sems[key] = self.es_sem.enter_context(self.nc.semaphore(key))
            self.cnt[key] = 0
            st.dsem = key
        return st.dsem

    def _waits(self, eng, reads, writes, skip_self=False):
        need = {}

        def add(tok):
            if tok is None:
                return
            k, v = tok
            if need.get(k, 0) < v:
                need[k] = v
        for t in reads:
            add(t.st.w)
        for t in writes:
            add(t.st.w)
            for tok in t.st.r:
                add(tok)
        e = self.engs[eng]
        for k, v in need.items():
            if skip_self and k == eng:
                continue
            if self.waited[eng].get(k, 0) >= v:
                continue
            e.wait_ge(self.sems[k], v)
            self.n_ins += 1
            self.waited[eng][k] = v

    def _commit(self, tok, reads, writes):
        for t in reads:
            r = t.st.r
            r.append(tok)
            if len(r) > 48:
                best = {}
                for k, v in r:
                    if best.get(k, 0) < v:
                        best[k] = v
                t.st.r = list(best.items())
        for t in writes:
            t.st.w = tok
            t.st.r = []

    def op(self, eng, fn, reads=(), writes=(), skip_self=False):
        if self.rec is not None:
            self.rec.append(('op', (eng, fn), dict(reads=list(reads), writes=list(writes), skip_self=skip_self), self.rate))
            return None
        self._waits(eng, reads, writes, skip_self)
        ins = fn(self.engs[eng])
        self.cnt[eng] += 1
        ins.then_inc(self.sems[eng], 1)
        self.n_ins += 1
        self._commit((eng, self.cnt[eng]), reads, writes)
        if self.inter is not None and not self._in_inter:
            q, every = self.inter
            self._inter_cnt += 1
            if q and self._inter_cnt % every == 0:
                self._in_inter = True
                kind, a, kw, _rate = q.pop(0)
                getattr(self, kind)(*a, **kw)
                self._in_inter = False
        return ins

    def dma(self, q, out, in_, reads=(), writes=(), semt=None, **kw):
        if self.rec is not None:
            self.rec.append(('dma', (q, out, in_), dict(reads=list(reads), writes=list(writes), semt=semt, **kw), self.rate))
            return None
        self._waits(q, reads, writes)
        key = self._dsem(semt)
        ins = self.engs[q].dma_start(out=out, in_=in_, **kw)
        self.cnt[key] += 16
        ins.then_inc(self.sems[key], 16)
        self.n_ins += 1
        self._commit((key, self.cnt[key]), reads, writes)
        return ins

    def gather(self, out_t, out_ap, table_ap, idx_t, idx_ap):
        self._waits('pool', [idx_t], [out_t])
        key = self._dsem(out_t)
        ins = self.nc.gpsimd.indirect_dma_start(
            out=out_ap, out_offset=None, in_=table_ap,
            in_offset=bass.IndirectOffsetOnAxis(ap=idx_ap, axis=0))
        self.cnt[key] += 16
        ins.then_inc(self.sems[key], 16)
        self.n_ins += 1
        self._commit((key, self.cnt[key]), [idx_t], [out_t])
        return ins

    def settle(self, tiles):
        for t in tiles:
            if t.st.w is not None and t.st.w[0] in self.cnt and t.st.w[0].startswith("d"):
                t.st.w = (t.st.w[0], self.cnt[t.st.w[0]])

    def play(self, q, budget, lag=0):
        self.slot += 1
        while q and budget > 1e-9:
            kind, a, kw, rate = q[0]
            if budget + 1e-9 < 1.0 / rate and budget < 0.999:
                break
            eng = a[0]
            if lag and eng == 'dve' and self.prev_eng != 'dve' and self.slot - self.prev_slot < lag:
                break
            q.pop(0)
            getattr(self, kind)(*a, **kw)
            budget -= 1.0 / rate
            self.prev_eng = eng
            self.prev_slot = self.slot

    def barrier(self):
        for eng in self.engs:
            e = self.engs[eng]
            for k, v in self.cnt.items():
                if v == 0 or self.waited[eng].get(k, 0) >= v:
                    continue
                e.wait_ge(self.sems[k], v)
                self.n_ins += 1
                self.waited[eng][k] = v


def _consts():
    slopes = np.exp2(-np.arange(1, 9, dtype=np.float64)).astype(np.float32)
    s = np.arange(128)[:, None]
    t = np.arange(128)[None, :]
    c = {}
    c["ident"] = np.eye(128, dtype=np.float32)
    bp = np.full((128, 2, 8, 128), NEG, np.float32)
    d0 = (128 + t - s).astype(np.float32)
    d1 = (t - s).astype(np.float32)
    for h in range(8):
        bp[:, 0, h, :] = np.where(s > t, -slopes[h] * d0, NEG)
        bp[:, 1, h, :] = np.where(s <= t, -slopes[h] * d1, NEG)
    c["bias_p"] = bp.reshape(128, 2048)
    bs_, ts_ = s // 8, s % 8
    bq_, tq_ = t // 8, t % 8
    bsn = np.full((128, 8, 128), NEG, np.float32)
    ok = (bs_ == bq_) & (ts_ <= tq_)
    for h in range(8):
        bsn[:, h, :] = np.where(ok, -slopes[h] * (tq_ - ts_).astype(np.float32), NEG)
    c["bias_sn"] = bsn.reshape(128, 1024)
    r = np.arange(128)[:, None]
    tq = np.arange(8)[None, :]
    bsp = np.full((128, 8, 8), NEG, np.float32)
    for h in range(8):
        bsp[:, h, :] = np.where(r > tq, -slopes[h] * (tq + 128 - r).astype(np.float32), NEG)
    c["bias_sp"] = bsp.reshape(128, 64)
    c["maskT_p"] = (s <= t).astype(np.float32)
    c["maskT_s"] = ok.astype(np.float32)
    c["iota16"] = np.broadcast_to(np.arange(16, dtype=np.float32)[None, :], (128, 16)).copy()
    return c


CONST_SHAPES = {"ident": (128, 128), "bias_p": (128, 2048), "bias_sn": (128, 1024), "bias_sp": (128, 64),
                "maskT_p": (128, 128), "maskT_s": (128, 128), "iota16": (128, 16)}
SMALL_SHAPES = {"g_attn": (128, 1024), "g_ffn": (128, 1024), "g_ple": (128, 1024), "gq": (128, 64), "gk": (128, 64),
                "sinks": (128, 8), "ln_g": (128, 512), "ln_b": (128, 512),
                "bcol_p": (128, 4), "bcol_s": (128, 4), "wsT_p": (128, 512), "wsT_s": (128, 512)}


CFG = {}


class _StopTile(Exception):
    pass


def chk(n):
    if CFG.get('stage', 99) < n:
        raise _StopTile()


def build():
    nc = bass.Bass("TRN2", target_bir_lowering=False)
    cfg = CFG
    tiles = cfg.get('tiles', list(range(NT)))

    def din(name, shape, dt=F32):
        return nc.dram_tensor(name, list(shape), dt, kind="ExternalInput").ap()

    def dout(name, shape, dt=F32):
        return nc.dram_tensor(name, list(shape), dt, kind="ExternalOutput").ap()

    xin = din("xin", [NTOK, D])
    pin = din("pin", [NTOK, 256])
    ck = din("ck", [128, 16, 128])
    cv = din("cv", [128, 16, 128])
    w_in = din("w_in", [D, IN_W])
    w_a = din("w_a", [512, D])
    w_b = din("w_b", [512, D])
    w_out = din("w_out", [D, D])
    w_q = din("w_q", [D, 2048])
    keysT = din("keysT", [128, 2048])
    peer_u = din("peer_u", [16384, D])
    peer_v = din("peer_v", [16384, D])
    w_ple = din("w_ple", [256, D])
    w_pg = din("w_pg", [D, D])
    cin = {k: din("c_" + k, v) for k, v in CONST_SHAPES.items()}
    sin = {k: din("s_" + k, v) for k, v in SMALL_SHAPES.items()}

    y = dout("y", [NTOK, D])
    o_kp = dout("o_kp", [128, 128])
    o_vp = dout("o_vp", [128, 128])
    o_ks = dout("o_ks", [128, 128])
    o_vs = dout("o_vs", [128, 128])
    o_sp = dout("o_sp", [128, 512])
    o_ss = dout("o_ss", [128, 512])
    x1s = nc.dram_tensor("x1s", [NTOK, D], F32, kind="Internal").ap()
    comb = nc.dram_tensor("comb", [16384, 2 * D], BF16, kind="Internal").ap()
    dbg_x1 = dout("dbg_x1", [NTOK, D]) if cfg.get('dbg') else None
    dbg_am = dout("dbg_am", [NTOK, D]) if cfg.get('dbg') else None

    with ExitStack() as es0:
        mk = MK(nc, es0)
        x1_tok = [T("x1s%d" % i) for i in range(NT)]
        out_tiles = []

        ident = mk.sb("ident", [128, 128], BF16)
        cgrp = T("cgrp")
        mk.dma('pool', ident[:], cin["ident"][:, :], writes=[ident], semt=ident)
        pbanks = [mk.ps("pb%d" % i, [128, 512], F32) for i in range(8)]

        def pe_T(bank, col0, src_t, src_ap, n=128, extra_reads=()):
            pv = bank[:].bitcast(BF16)
            return lambda e: e.transpose(out=pv[0:n, col0:col0 + 128], in_=src_ap, identity=ident[:])

        epst = mk.sb("epst", [128, 1], F32)
        mk.op('pool', lambda e: e.memset(epst[:], EPS), writes=[epst])

        def rsqrt(tile, dst, src, scale):
            mk.op('act', lambda e: e.activation(out=dst, in_=src, func=AF.Sqrt, bias=epst[:, 0:1], scale=scale),
                  reads=[tile, epst], writes=[tile])
            mk.op('dve', lambda e: e.reciprocal(out=dst, in_=dst), reads=[tile], writes=[tile])

        with ExitStack() as es1:
            mk.es = es1
            warena = mk.sb("warena1", [128, 8 * IN_W + 4 * D + 4 * D + 8 * D], BF16)
            o = 0
            win_v = warena[:, o:o + 8 * IN_W].rearrange("p (c n) -> p c n", c=8); o += 8 * IN_W
            wa_v = warena[:, o:o + 4 * D].rearrange("p (c n) -> p c n", c=4); o += 4 * D
            wb_v = warena[:, o:o + 4 * D].rearrange("p (c n) -> p c n", c=4); o += 4 * D
            wo_v = warena[:, o:o + 8 * D].rearrange("p (c n) -> p c n", c=8); o += 8 * D
            ZG = [(0, 512), (512, 256), (768, 512), (1280, 512), (1792, 512), (2304, 512), (2816, 512), (3328, 512)]
            wg = [T("wg%d" % i) for i in range(8)]
            w_in_r = w_in.rearrange("(c p) n -> p c n", p=128)
            for gi, (c0, n) in enumerate(ZG):
                for c in range(8):
                    mk.dma('pool', win_v[:, c, c0:c0 + n], w_in_r[:, c, c0:c0 + n], semt=wg[gi])
                wg[gi].st.w = (wg[gi].st.dsem, mk.cnt[wg[gi].st.dsem])
            w_a_r = w_a.rearrange("(c p) n -> p c n", p=128)
            w_b_r = w_b.rearrange("(c p) n -> p c n", p=128)
            w_o_r = w_out.rearrange("(c p) n -> p c n", p=128)
            wab = T("wab")
            wot = T("wot")
            for c in range(4):
                mk.dma('pool', wa_v[:, c, :], w_a_r[:, c, :], semt=wab)
                mk.dma('pool', wb_v[:, c, :], w_b_r[:, c, :], semt=wab)
            wab.st.w = (wab.st.dsem, mk.cnt[wab.st.dsem])
            for c in range(8):
                mk.dma('pool', wo_v[:, c, :], w_o_r[:, c, :], semt=wot)
            wot.st.w = (wot.st.dsem, mk.cnt[wot.st.dsem])
            comb_t = T("comb")
            for c in range(16):
                rs = slice(c * 1024, (c + 1) * 1024)
                mk.dma('pool', comb[rs, 0:D], peer_u[rs, :], semt=comb_t)
                mk.dma('pool', comb[rs, D:2 * D], peer_v[rs, :], semt=comb_t)
            comb_t.st.w = (comb_t.st.dsem, mk.cnt[comb_t.st.dsem])

            def cload(name, src, shape, dt=F32, q='sp'):
                t = mk.sb(name, shape, dt)
                mk.dma(q, t[:], src[:, :], writes=[t], semt=cgrp)
                return t
            g_attn = cload("g_attn", sin["g_attn"], [128, 1024])
            gq = cload("gq", sin["gq"], [128, 64])
            gk = cload("gk", sin["gk"], [128, 64])
            sinks = cload("sinks", sin["sinks"], [128, 8])
            ln_g = cload("ln_g", sin["ln_g"], [128, 512])
            ln_b = cload("ln_b", sin["ln_b"], [128, 512])
            bcol_p = cload("bcol_p", sin["bcol_p"], [128, 4])
            bcol_s = cload("bcol_s", sin["bcol_s"], [128, 4])
            bias_p = cload("bias_p", cin["bias_p"], [128, 2048])
            bias_sn = cload("bias_sn", cin["bias_sn"], [128, 1024])
            bias_sp = cload("bias_sp", cin["bias_sp"], [128, 64])
            maskT_p = cload("maskT_p", cin["maskT_p"], [128, 128])
            maskT_s = cload("maskT_s", cin["maskT_s"], [128, 128])
            t1 = mk.sb("t1", [128, D], F32)
            ws32_p = T("ws32_p", t1[:, 0:512], st=t1.st)
            ws32_s = T("ws32_s", t1[:, 512:1024], st=t1.st)
            mk.dma('sp', ws32_p[:], sin["wsT_p"][:, :], writes=[ws32_p], semt=cgrp)
            mk.dma('sp', ws32_s[:], sin["wsT_s"][:, :], writes=[ws32_s], semt=cgrp)
            consts1 = [ident, g_attn, gq, gk, sinks, ln_g, ln_b, bcol_p, bcol_s, bias_p, bias_sn, bias_sp,
                       maskT_p, maskT_s, ws32_p, ws32_s]
            mk.settle(consts1)

            sinkexp = mk.sb("sinkexp", [128, 8], F32)
            mk.op('act', lambda e: e.activation(out=sinkexp[:], in_=sinks[:], func=AF.Exp), reads=[sinks], writes=[sinkexp])
            gq8 = mk.sb("gq8", [128, 64], F32)
            mk.op('dve', lambda e: e.tensor_scalar(out=gq8[:], in0=gq[:], scalar1=0.125, scalar2=None, op0=ALU.mult),
                  reads=[gq], writes=[gq8])
            wsT_p = mk.sb("wsT_p", [128, 512], BF16)
            wsT_s = mk.sb("wsT_s", [128, 512], BF16)
            mk.op('dve', lambda e: e.tensor_tensor(out=wsT_p[:].rearrange("s (g t) -> s g t", g=4),
                                                   in0=ws32_p[:].rearrange("s (g t) -> s g t", g=4),
                                                   in1=maskT_p[:].unsqueeze(1).to_broadcast([128, 4, 128]), op=ALU.mult),
                  reads=[ws32_p, maskT_p], writes=[wsT_p])
            mk.op('dve', lambda e: e.tensor_tensor(out=wsT_s[:].rearrange("s (g t) -> s g t", g=4),
                                                   in0=ws32_s[:].rearrange("s (g t) -> s g t", g=4),
                                                   in1=maskT_s[:].unsqueeze(1).to_broadcast([128, 4, 128]), op=ALU.mult),
                  reads=[ws32_s, maskT_s], writes=[wsT_s])

            xt = [mk.sb("xt%d" % i, [128, D], F32) for i in range(3)]
            x1t = [mk.sb("x1t%d" % i, [128, D], F32) for i in range(2)]
            xn = mk.sb("xn", [128, D], BF16)
            xnT = mk.sb("xnT", [128, D], BF16)
            st1 = mk.sb("st1", [128, 16], F32)
            q32 = mk.sb("q32", [128, 512], F32)
            qn = mk.sb("qn", [128, 512], BF16)
            qT = mk.sb("qT", [128, 512], BF16)
            k32 = mk.sb("k32", [128, 128], F32)
            kn32 = mk.sb("kn32", [128, 128], F32)
            knb = mk.sb("knb", [128, 128], BF16)
            kT = [mk.sb("kT%d" % i, [128, 128], BF16) for i in range(2)]
            v32 = mk.sb("v32", [128, 128], F32)
            vaug = [mk.sb("vaug%d" % i, [128, 2 * 65], BF16) for i in range(2)]
            sc32 = mk.sb("sc32", [128, 1024], F32)
            pT = mk.sb("pT", [128, 2048], BF16)
            att_ = [mk.sb("att%d" % i, [128, 512], BF16) for i in range(2)]
            u32 = mk.sb("u32", [128, 512], F32)
            gv32 = mk.sb("gv32", [128, 512], F32)
            vn32 = mk.sb("vn32", [128, 512], F32)
            vnb = mk.sb("vnb", [128, 512], BF16)
            mb_ = [mk.sb("mb%d" % i, [128, 512], BF16) for i in range(2)]
            amT = mk.sb("amT", [128, 1024], BF16)
            siga_ = [mk.sb("siga%d" % i, [128, D], BF16) for i in range(2)]
            sigb_ = [mk.sb("sigb%d" % i, [128, D], BF16) for i in range(2)]
            hb = mk.sb("hb", [128, D], BF16)
            hT = mk.sb("hT", [128, D], BF16)
            stage = mk.sb("stage", [128, 2048], F32)
            qsq = T("qsq", stage[:, 1024:1536], st=stage.st)
            ksq = T("ksq", stage[:, 1536:1664], st=stage.st)
            ckb = T("ckb", pT[:], st=pT.st)
            kTp = mk.sb("kTp", [128, 2048], BF16)
            vpaug = mk.sb("vpaug", [128, 16 * 130], BF16)
            qTs = mk.sb("qTs", [128, 512], BF16)
            Eb = [mk.sb("Eb%d" % i, [128, 1024], BF16) for i in range(2)]
            print("phase1 sbuf remaining", nc.sbuf_bytes_remaining, flush=True)

            for i in range(2):
                mk.op('pool', lambda e, i=i: e.memset(vaug[i][:], 1.0), writes=[vaug[i]])
            mk.op('dve', lambda e: e.memset(vpaug[:], 1.0), writes=[vpaug])

            B = pbanks

            def load_x(ti):
                mk.dma('sp', xt[ti % 3][:], xin[ti * 128:(ti + 1) * 128, :], writes=[xt[ti % 3]], semt=xt[ti % 3])

            load_x(tiles[0])

            def tr8(e, src, bank):
                last = None
                for c in range(8):
                    last = pe_T(bank, c * 128, src, src[:, c * 128:(c + 1) * 128])(e)
                return last

            def emit_F1(tj):
                x = xt[tj % 3]
                mk.op('act', lambda e: e.activation(out=xn[:], in_=x[:], func=AF.Square, accum_out=st1[:, 0:1]),
                      reads=[x], writes=[xn, st1])
                rsqrt(st1, st1[:, 2:3], st1[:, 0:1], 1.0 / D)
                mk.op('dve', lambda e: e.scalar_tensor_tensor(out=xn[:], in0=x[:], scalar=st1[:, 2:3], in1=g_attn[:],
                                                              op0=ALU.mult, op1=ALU.mult), reads=[x, st1, g_attn], writes=[xn])
                mk.op('pe', lambda e: tr8(e, src=xn, bank=B[0]), reads=[xn, ident], writes=[B[0]], skip_self=True)
                mk.op('act', lambda e: e.copy(out=xnT[:], in_=B[0][:].bitcast(BF16)), reads=[B[0]], writes=[xnT])


            def tile_body(tpos, ti, prevE):
                try:
                    sample = (ti == NPT)
                    x = xt[ti % 3]
                    att = att_[ti % 2]
                    mb = mb_[ti % 2]
                    siga = siga_[ti % 2]
                    sigb = sigb_[ti % 2]
                    if tpos + 1 < len(tiles):
                        load_x(tiles[tpos + 1])
                    cur, prv = ti % 2, (ti + 1) % 2
                    if tpos == 0:
                        emit_F1(ti)
                    chk(1)
                    xnT3 = xnT[:].rearrange("p (c t) -> p c t", c=8)

                    def zgroup(bank, c0, n):
                        def f(e):
                            last = None
                            for c in range(8):
                                last = e.matmul(bank[:, 0:n], lhsT=xnT3[:, c, :], rhs=win_v[:, c, c0:c0 + n],
                                                start=(c == 0), stop=(c == 7))
                            return last
                        mk.op('pe', f, reads=[xnT, wg[[z[0] for z in ZG].index(c0)]], writes=[bank], skip_self=True)

                    zgroup(B[1], 0, 512)
                    mk.op('act', lambda e: e.copy(out=q32[:], in_=B[1][:]), reads=[B[1]], writes=[q32])
                    mk.op('act', lambda e: e.activation(out=qsq[:], in_=B[1][:], func=AF.Square), reads=[B[1]], writes=[qsq])
                    zgroup(B[2], 512, 256)
                    mk.op('act', lambda e: e.copy(out=k32[:], in_=B[2][:, 0:128]), reads=[B[2]], writes=[k32])
                    mk.op('act', lambda e: e.activation(out=ksq[:], in_=B[2][:, 0:128], func=AF.Square), reads=[B[2]], writes=[ksq])
                    mk.op('act', lambda e: e.copy(out=v32[:], in_=B[2][:, 128:256]), reads=[B[2]], writes=[v32])
                    zgroup(B[3], 768, 512)
                    mk.op('act', lambda e: e.activation(out=u32[:], in_=B[3][:], func=GELU), reads=[B[3]], writes=[u32])
                    zgroup(B[1], 1280, 512)
                    mk.op('act', lambda e: e.activation(out=gv32[:], in_=B[1][:], func=GELU, accum_out=st1[:, 4:5]),
                          reads=[B[1]], writes=[gv32, st1])
                    zgroup(B[2], 1792, 512)
                    mk.op('act', lambda e: e.activation(out=siga[:, 0:512], in_=B[2][:], func=AF.Sigmoid), reads=[B[2]], writes=[siga])
                    zgroup(B[3], 2304, 512)
                    mk.op('act', lambda e: e.activation(out=siga[:, 512:1024], in_=B[3][:], func=AF.Sigmoid), reads=[B[3]], writes=[siga])
                    zgroup(B[1], 2816, 512)
                    mk.op('act', lambda e: e.activation(out=sigb[:, 0:512], in_=B[1][:], func=AF.Sigmoid), reads=[B[1]], writes=[sigb])
                    zgroup(B[2], 3328, 512)
                    mk.op('act', lambda e: e.activation(out=sigb[:, 512:1024], in_=B[2][:], func=AF.Sigmoid), reads=[B[2]], writes=[sigb])

                    chk(2)
                    mk.inter = (prevE, 2)
                    mk.op('dve', lambda e: e.tensor_reduce(out=st1[:, 8:16], in_=qsq[:].rearrange("t (h d) -> t h d", h=8),
                                                           axis=AX.X, op=ALU.add), reads=[qsq], writes=[st1])
                    rsqrt(st1, st1[:, 8:16], st1[:, 8:16], 1.0 / 64)
                    mk.op('dve', lambda e: e.tensor_tensor(out=q32[:].rearrange("t (h d) -> t h d", h=8),
                                                           in0=q32[:].rearrange("t (h d) -> t h d", h=8),
                                                           in1=st1[:, 8:16].unsqueeze(2).to_broadcast([128, 8, 64]), op=ALU.mult),
                          reads=[q32, st1], writes=[q32])
                    mk.op('dve', lambda e: e.tensor_tensor(out=qn[:].rearrange("t (p w d) -> t w p d", p=4, w=2),
                                                           in0=q32[:].rearrange("t (w p d) -> t w p d", w=2, p=4),
                                                           in1=gq8[:].unsqueeze(1).unsqueeze(1).to_broadcast([128, 2, 4, 64]),
                                                           op=ALU.mult), reads=[q32, gq8], writes=[qn])
                    mk.op('dve', lambda e: e.tensor_reduce(out=st1[:, 5:7], in_=ksq[:].rearrange("t (h d) -> t h d", h=2),
                                                           axis=AX.X, op=ALU.add), reads=[ksq], writes=[st1])
                    rsqrt(st1, st1[:, 5:7], st1[:, 5:7], 1.0 / 64)
                    mk.op('dve', lambda e: e.tensor_tensor(out=k32[:].rearrange("t (h d) -> t h d", h=2),
                                                           in0=k32[:].rearrange("t (h d) -> t h d", h=2),
                                                           in1=st1[:, 5:7].unsqueeze(2).to_broadcast([128, 2, 64]), op=ALU.mult),
                          reads=[k32, st1], writes=[k32])
                    mk.op('dve', lambda e: e.tensor_tensor(out=kn32[:].rearrange("t (h d) -> t h d", h=2),
                                                           in0=k32[:].rearrange("t (h d) -> t h d", h=2),
                                                           in1=gk[:].unsqueeze(1).to_broadcast([128, 2, 64]), op=ALU.mult),
                          reads=[k32, gk], writes=[kn32])
                    mk.op('act', lambda e: e.copy(out=knb[:], in_=kn32[:]), reads=[kn32], writes=[knb])
                    mk.op('act', lambda e: e.copy(out=vaug[cur][:].rearrange("t (g e) -> t g e", g=2)[:, :, 0:64],
                                                  in_=v32[:].rearrange("t (g d) -> t g d", g=2)), reads=[v32], writes=[vaug[cur]])
                    chk(5)
                    mk.op('act', lambda e: e.activation(out=sc32[:, 0:512], in_=gv32[:], func=AF.Square, accum_out=st1[:, 3:4]),
                          reads=[gv32], writes=[sc32, st1])
                    mk.op('dve', lambda e: e.tensor_scalar(out=st1[:, 4:5], in0=st1[:, 4:5], scalar1=1.0 / 512, scalar2=None,
                                                           op0=ALU.mult), reads=[st1], writes=[st1])
                    mk.op('dve', lambda e: e.tensor_tensor(out=st1[:, 7:8], in0=st1[:, 4:5], in1=st1[:, 4:5], op=ALU.mult),
                          reads=[st1], writes=[st1])
                    mk.op('dve', lambda e: e.scalar_tensor_tensor(out=st1[:, 3:4], in0=st1[:, 3:4], scalar=1.0 / 512, in1=st1[:, 7:8],
                                                                  op0=ALU.mult, op1=ALU.subtract), reads=[st1], writes=[st1])
                    rsqrt(st1, st1[:, 3:4], st1[:, 3:4], 1.0)
                    mk.op('dve', lambda e: e.tensor_scalar(out=vn32[:], in0=gv32[:], scalar1=st1[:, 4:5], scalar2=st1[:, 3:4],
                                                           op0=ALU.subtract, op1=ALU.mult), reads=[gv32, st1], writes=[vn32])
                    mk.op('dve', lambda e: e.tensor_tensor(out=vn32[:], in0=vn32[:], in1=ln_g[:], op=ALU.mult),
                          reads=[vn32, ln_g], writes=[vn32])
                    mk.op('dve', lambda e: e.tensor_tensor(out=vn32[:], in0=vn32[:], in1=ln_b[:], op=ALU.add),
                          reads=[vn32, ln_b], writes=[vn32])
                    mk.op('act', lambda e: e.copy(out=vnb[:], in_=vn32[:]), reads=[vn32], writes=[vnb])
                    wsT = wsT_s if sample else wsT_p
                    bcol = bcol_s if sample else bcol_p

                    def fsgu(e, wsT=wsT):
                        last = None
                        for g in range(4):
                            last = e.matmul(B[3][:, g * 128:(g + 1) * 128], lhsT=wsT[:, g * 128:(g + 1) * 128],
                                            rhs=vnb[:, g * 128:(g + 1) * 128], start=True, stop=True)
                        return last
                    mk.op('pe', fsgu, reads=[wsT, vnb], writes=[B[3]], skip_self=True)
                    for g in range(4):
                        mk.op('dve', lambda e, g=g, bcol=bcol: e.scalar_tensor_tensor(
                            out=mb[:, g * 128:(g + 1) * 128], in0=B[3][:, g * 128:(g + 1) * 128], scalar=bcol[:, g:g + 1],
                            in1=u32[:, g * 128:(g + 1) * 128], op0=ALU.add, op1=ALU.mult), reads=[B[3], bcol, u32], writes=[mb])

                    chk(3)
                    def trq(e):
                        last = None
                        for p in range(4):
                            last = pe_T(B[0], p * 128, qn, qn[:, p * 128:(p + 1) * 128])(e)
                        last = pe_T(B[0], 512, knb, knb[:])(e)
                        return last
                    mk.op('pe', trq, reads=[qn, knb, ident], writes=[B[0]], skip_self=True)
                    b0v = B[0][:].bitcast(BF16)
                    mk.op('act', lambda e: e.copy(out=qT[:], in_=b0v[:, 0:512]), reads=[B[0]], writes=[qT])
                    mk.op('act', lambda e: e.copy(out=kT[cur][:], in_=b0v[:, 512:640]), reads=[B[0]], writes=[kT[cur]])
                    qT3 = qT[:].rearrange("p (q t) -> p q t", q=4)

                    chk(4)
                    pT4 = pT[:].rearrange("s (b h t) -> s b h t", b=2, h=8)
                    blks = [1] if (ti == 0 or sample) else [0, 1]
                    for blk in blks:
                        kt = kT[cur] if blk == 1 else kT[prv]
                        banks = (B[4], B[5])

                        def fsc(e, kt=kt, banks=banks):
                            last = None
                            for h in range(8):
                                w, p = h // 4, h % 4
                                bank = banks[h // 4]
                                last = e.matmul(bank[:, (h % 4) * 128:(h % 4 + 1) * 128], lhsT=kt[w * 64:(w + 1) * 64, :],
                                                rhs=qT3[w * 64:(w + 1) * 64, p, :], start=True, stop=True)
                            return last
                        mk.op('pe', fsc, reads=[kt, qT], writes=list(banks), skip_self=True)
                        bsrc = bias_sn if sample else bias_p
                        boff = 0 if sample else blk * 1024
                        for hb_ in range(2):
                            mk.op('dve', lambda e, hb_=hb_, banks=banks, bsrc=bsrc, boff=boff: e.tensor_tensor(
                                out=sc32[:, hb_ * 512:(hb_ + 1) * 512], in0=banks[hb_][:],
                                in1=bsrc[:, boff + hb_ * 512: boff + (hb_ + 1) * 512], op=ALU.add),
                                reads=[banks[hb_], bsrc], writes=[sc32])
                        mk.op('act', lambda e, blk=blk: e.activation(out=pT[:, blk * 1024:(blk + 1) * 1024], in_=sc32[:], func=AF.Exp),
                              reads=[sc32], writes=[pT])

                    def pv_out(h):
                        return B[1 + h // 4][:, (h % 4) * 128:(h % 4) * 128 + 65]

                    def fpv(e):
                        last = None
                        for h in range(8):
                            w = h // 4
                            for bi, blk in enumerate(blks):
                                va = vaug[cur] if blk == 1 else vaug[prv]
                                st_ = (bi == 0) if not sample else (h % 4 == 0)
                                last = e.matmul(pv_out(h), lhsT=pT4[:, blk, h, :], rhs=va[:, w * 65:(w + 1) * 65],
                                                start=st_, stop=(bi == len(blks) - 1 and not sample), skip_group_check=sample)
                        return last
                    mk.op('pe', fpv, reads=[pT, vaug[0], vaug[1]], writes=[B[1], B[2]], skip_self=True)

                    if sample:
                        chk(4.1)
                        mk.inter = None
                        mk.play(prevE, 10 ** 9)
                        mk.dma('sp', stage[:].rearrange("s (b c) -> s b c", b=16), ck[:, :, :], writes=[stage], semt=stage)
                        mk.op('act', lambda e: e.copy(out=ckb[:], in_=stage[:]), reads=[stage], writes=[ckb])
                        for half in range(2):
                            bank = B[4 + half * 2], B[5 + half * 2]

                            def ftr(e, half=half, bank=bank):
                                last = None
                                for bb in range(8):
                                    b_ = half * 8 + bb
                                    last = pe_T(bank[bb // 4], (bb % 4) * 128, ckb, ckb[:, b_ * 128:(b_ + 1) * 128])(e)
                                return last
                            mk.op('pe', ftr, reads=[ckb, ident], writes=list(bank), skip_self=True)
                            for q_ in range(2):
                                mk.op('act', lambda e, half=half, q_=q_, bank=bank: e.copy(
                                    out=kTp[:, (half * 8 + q_ * 4) * 128:(half * 8 + q_ * 4 + 4) * 128],
                                    in_=bank[q_][:].bitcast(BF16)[:, 0:512]), reads=[bank[q_]], writes=[kTp])
                        chk(4.2)
                        if not cfg.get('skip_vdma'):
                            mk.dma('sp', stage[:].rearrange("s (b c) -> s b c", b=16), cv[:, :, :], writes=[stage], semt=stage)
                        if cfg.get('skip_vcopy'):
                            raise _StopTile()
                        for g_ in range(2):
                            mk.op('dve', lambda e, g_=g_: e.tensor_copy(
                                out=vpaug[:].rearrange("s (b g e) -> s b g e", b=16, g=2)[:, :, g_, 0:64],
                                in_=stage[:].rearrange("s (b g d) -> s b g d", b=16, g=2)[:, :, g_, :]),
                                reads=[stage], writes=[vpaug])
                        chk(4.3)
                        kTp3 = kTp[:].rearrange("p (b s) -> p b s", b=16)
                        mk.op('act', lambda e: e.copy(out=qTs[:].rearrange("p (b q t) -> p b q t", b=16, q=4),
                                                      in_=qT[:].rearrange("p (q b t) -> p b q t", q=4, b=16)),
                              reads=[qT], writes=[qTs])
                        chk(4.31)

                        def fps(e):
                            last = None
                            for w in range(2):
                                bank = B[4 + w]
                                for b_ in range(16):
                                    c0 = b_ * 32
                                    last = e.matmul(bank[:, c0:c0 + 32], lhsT=kTp3[w * 64:(w + 1) * 64, b_, :],
                                                    rhs=qTs[w * 64:(w + 1) * 64, b_ * 32:(b_ + 1) * 32], start=True, stop=True)
                            return last
                        mk.op('pe', fps, reads=[kTp, qTs], writes=[B[4], B[5]], skip_self=True)
                        chk(4.32)
                        for half in range(2):
                            mk.op('dve', lambda e, half=half: e.tensor_tensor(
                                out=sc32[:, half * 512:(half + 1) * 512].rearrange("s (b c) -> s b c", b=16),
                                in0=B[4 + half][:].rearrange("s (b c) -> s b c", b=16),
                                in1=bias_sp[:, half * 32:(half + 1) * 32].unsqueeze(1).to_broadcast([128, 16, 32]), op=ALU.add),
                                reads=[B[4 + half], bias_sp], writes=[sc32])
                        chk(4.4)
                        sc5 = sc32[:].rearrange("s (w b p t) -> s b w p t", w=2, b=16, p=4)
                        vp4 = vpaug[:].rearrange("s (b g e) -> s b g e", b=16, g=2)
                        for b_ in range(16):
                            E = Eb[b_ % 2]
                            E3 = E[:].rearrange("s (h t) -> s h t", h=8)
                            mk.op('pool', lambda e, E=E: e.memset(E[:], 0.0), writes=[E])
                            mk.op('act', lambda e, E=E, b_=b_: e.activation(
                                out=E[:].rearrange("s (w p t) -> s w p t", w=2, p=4)[:, :, :, b_ * 8:(b_ + 1) * 8],
                                in_=sc5[:, b_, :, :, :], func=AF.Exp), reads=[sc32], writes=[E])

                            def fpp(e, E3=E3, b_=b_):
                                last = None
                                for h in range(8):
                                    last = e.matmul(pv_out(h), lhsT=E3[:, h, :], rhs=vp4[:, b_, h // 4, :],
                                                    start=False, stop=(b_ == 15), skip_group_check=True)
                                return last
                            mk.op('pe', fpp, reads=[E, vpaug], writes=[B[1], B[2]], skip_self=True)

                    chk(4.5)
                    for half in range(2):
                        bank = B[1 + half]
                        b3 = bank[:].rearrange("t (h e) -> t h e", h=4)
                        mk.op('dve', lambda e, b3=b3, half=half: e.tensor_tensor(
                            out=st1[:, 8 + half * 4: 12 + half * 4].unsqueeze(2), in0=b3[:, :, 64:65],
                            in1=sinkexp[:, half * 4:(half + 1) * 4].unsqueeze(2), op=ALU.add),
                            reads=[bank, sinkexp], writes=[st1])
                    mk.op('dve', lambda e: e.reciprocal(out=st1[:, 8:16], in_=st1[:, 8:16]), reads=[st1], writes=[st1])
                    for half in range(2):
                        bank = B[1 + half]
                        b3 = bank[:].rearrange("t (h e) -> t h e", h=4)
                        mk.op('dve', lambda e, b3=b3, half=half: e.tensor_tensor(
                            out=att[:, half * 256:(half + 1) * 256].rearrange("t (h d) -> t h d", h=4), in0=b3[:, :, 0:64],
                            in1=st1[:, 8 + half * 4: 12 + half * 4].unsqueeze(2).to_broadcast([128, 4, 64]), op=ALU.mult),
                            reads=[bank, st1], writes=[att])

                    if tpos + 1 < len(tiles):
                        emit_F1(tiles[tpos + 1])
                    chk(6)
                    if ti == NPT - 1 or sample:
                        ok_, ov_, os_ = (o_ks, o_vs, o_ss) if sample else (o_kp, o_vp, o_sp)
                        mk.dma('sp', ok_[:, :], kn32[:], reads=[kn32], semt=kn32)
                        mk.dma('sp', ov_[:, :], v32[:], reads=[v32], semt=v32)
                        mk.dma('sp', os_[:, :], vn32[:], reads=[vn32], semt=vn32)
                        out_tiles.extend([kn32, v32, vn32])

                    chk(7)
                    mk.inter = None
                    mk.play(prevE, 10 ** 9)
                    if dbg_am is not None:
                        mk.op('dve', lambda e: e.tensor_copy(out=t1[:, 0:512], in_=att[:]), reads=[att], writes=[t1])
                        mk.op('dve', lambda e: e.tensor_copy(out=t1[:, 512:1024], in_=mb[:]), reads=[mb], writes=[t1])
                        mk.dma('sp', dbg_am[ti * 128:(ti + 1) * 128, :], t1[:], reads=[t1], semt=t1)
                    mk.rec = []
                    mk.rate = 1.0
                    def tram(e):
                        last = None
                        for c in range(4):
                            last = pe_T(B[0], c * 128, att, att[:, c * 128:(c + 1) * 128])(e)
                        for c in range(4):
                            last = pe_T(B[0], 512 + c * 128, mb, mb[:, c * 128:(c + 1) * 128])(e)
                        return last
                    mk.op('pe', tram, reads=[att, mb, ident], writes=[B[0]], skip_self=True)
                    mk.op('act', lambda e: e.copy(out=amT[:], in_=B[0][:].bitcast(BF16)), reads=[B[0]], writes=[amT])
                    amT3 = amT[:].rearrange("p (c t) -> p c t", c=8)

                    def fmerge_ab(e, w_v, off):
                        last = None
                        for n in range(2):
                            for c in range(4):
                                last = e.matmul(B[6 + n][:], lhsT=amT3[:, off + c, :], rhs=w_v[:, c, n * 512:(n + 1) * 512],
                                                start=(c == 0), stop=(c == 3))
                        return last
                    mk.op('pe', lambda e: fmerge_ab(e, wa_v, 0), reads=[amT, wab], writes=[B[6], B[7]], skip_self=True)
                    for n in range(2):
                        sl = slice(n * 512, (n + 1) * 512)
                        mk.op('dve', lambda e, n=n, sl=sl: e.tensor_tensor(out=t1[:, sl], in0=B[6 + n][:], in1=siga[:, sl], op=ALU.mult),
                              reads=[B[6 + n], siga], writes=[t1])
                    mk.op('pe', lambda e: fmerge_ab(e, wb_v, 4), reads=[amT, wab], writes=[B[6], B[7]], skip_self=True)
                    for n in range(2):
                        sl = slice(n * 512, (n + 1) * 512)
                        mk.op('dve', lambda e, n=n, sl=sl: e.tensor_tensor(out=stage[:, sl], in0=B[6 + n][:], in1=sigb[:, sl], op=ALU.mult),
                              reads=[B[6 + n], sigb], writes=[stage])
                    mk.op('dve', lambda e: e.tensor_tensor(out=hb[:], in0=stage[:, 0:1024], in1=t1[:], op=ALU.add), reads=[stage, t1], writes=[hb])
                    mk.op('pe', lambda e: tr8(e, src=hb, bank=B[0]), reads=[hb, ident], writes=[B[0]], skip_self=True)
                    mk.op('act', lambda e: e.copy(out=hT[:], in_=B[0][:].bitcast(BF16)), reads=[B[0]], writes=[hT])
                    hT3 = hT[:].rearrange("p (c t) -> p c t", c=8)

                    def fwo(e):
                        last = None
                        for n in range(2):
                            for c in range(8):
                                last = e.matmul(B[6 + n][:], lhsT=hT3[:, c, :], rhs=wo_v[:, c, n * 512:(n + 1) * 512],
                                                start=(c == 0), stop=(c == 7))
                        return last
                    mk.op('pe', fwo, reads=[hT, wot], writes=[B[6], B[7]], skip_self=True)
                    x1 = x1t[ti % 2]
                    for n in range(2):
                        sl = slice(n * 512, (n + 1) * 512)
                        mk.op('dve', lambda e, n=n, sl=sl, x1=x1: e.tensor_tensor(out=x1[:, sl], in0=B[6 + n][:], in1=x[:, sl], op=ALU.add),
                              reads=[B[6 + n], x], writes=[x1])
                    mk.dma('sp', x1s[ti * 128:(ti + 1) * 128, :], x1[:], reads=[x1], writes=[x1_tok[ti]], semt=x1)
                    if dbg_x1 is not None:
                        mk.dma('sp', dbg_x1[ti * 128:(ti + 1) * 128, :], x1[:], reads=[x1], semt=x1)
                    pend = mk.rec
                    mk.rec = None
                    return pend
                except _StopTile:
                    mk.rec = None
                    mk.inter = None
                    return []

            prevE = []
            for tpos, ti in enumerate(tiles):
                prevE = tile_body(tpos, ti, prevE)
            mk.play(prevE, 10 ** 9)

            mk.barrier()
        with ExitStack() as es2:
            mk.es = es2
            warena2 = mk.sb("warena2", [128, 8 * 2048 + 2048 + 8 * D + 2 * D], BF16)
            o = 0
            wq_v = warena2[:, o:o + 8 * 2048].rearrange("p (c n) -> p c n", c=8); o += 8 * 2048
            keys_v = warena2[:, o:o + 2048].rearrange("p (g n) -> p g n", g=16); o += 2048
            wpg_v = warena2[:, o:o + 8 * D].rearrange("p (c n) -> p c n", c=8); o += 8 * D
            wpl_v = warena2[:, o:o + 2 * D].rearrange("p (c n) -> p c n", c=2); o += 2 * D
            wsem2 = T("wsem2")
            w_q_r = w_q.rearrange("(c p) n -> p c n", p=128)
            w_pg_r = w_pg.rearrange("(c p) n -> p c n", p=128)
            w_pl_r = w_ple.rearrange("(c p) n -> p c n", p=128)
            for c in range(8):
                mk.dma('pool', wq_v[:, c, :], w_q_r[:, c, :], semt=wsem2)
                mk.dma('pool', wpg_v[:, c, :], w_pg_r[:, c, :], semt=wsem2)
            for c in range(2):
                mk.dma('pool', wpl_v[:, c, :], w_pl_r[:, c, :], semt=wsem2)
            mk.dma('pool', keys_v, keysT.rearrange("p (g n) -> p g n", g=16), semt=wsem2)
            warena2.st.w = (wsem2.st.dsem, mk.cnt[wsem2.st.dsem])
            cgrp2 = T("cgrp2")

            def cload2(name, src, shape, dt=F32, q='sp'):
                t = mk.sb(name, shape, dt)
                mk.dma(q, t[:], src[:, :], writes=[t], semt=cgrp2)
                return t
            g_ffn = cload2("g_ffn", sin["g_ffn"], [128, 1024])
            g_ple = cload2("g_ple", sin["g_ple"], [128, 1024])
            iota16 = cload2("iota16", cin["iota16"], [128, 16])
            identf = cload2("identf", cin["ident"], [128, 128])
            mk.settle([g_ffn, g_ple, iota16, identf])

            xt2 = [mk.sb("x2t%d" % i, [128, D], F32) for i in range(2)]
            pt2 = [mk.sb("p2t%d" % i, [128, 256], F32) for i in range(2)]
            xn2_ = [mk.sb("xn2_%d" % i, [128, D], F32) for i in range(2)]
            xn2b = mk.sb("xn2b", [128, D], BF16)
            xn2T = mk.sb("xn2T", [128, D], BF16)
            st2 = mk.sb("st2", [128, 16], F32)
            st2A = mk.sb("st2A", [128, 16], F32)
            qpT = mk.sb("qpT", [128, 2048], BF16)
            scp = mk.sb("scp", [128, 2048], F32)
            work = mk.sb("work", [128, 2048], F32)
            stop_ = mk.sb("stop", [128, 256], F32)
            itopu = mk.sb("itopu", [128, 256], U32)
            itopf = mk.sb("itopf", [128, 256], F32)
            cand = mk.sb("cand", [128, 2048], F32)
            best = mk.sb("best", [128, 128], F32)
            posu = mk.sb("posu", [128, 128], U32)
            posi = mk.sb("posi", [128, 256], I32)
            posf = mk.sb("posf", [128, 256], F32)
            oh = mk.sb("oh", [128, 2048], F32)
            oh2 = mk.sb("oh2", [128, 2048], F32)
            sel = mk.sb("sel", [128, 256], F32)
            junkA = T("junkA", oh[:, 0:1024], st=oh.st)
            idxf = mk.sb("idxf", [128, 128], F32)
            idxi_ = [mk.sb("idxi%d" % i, [128, 128], I32) for i in range(2)]
            gate_ = [mk.sb("gate%d" % i, [128, 128], F32) for i in range(2)]
            dots = mk.sb("dots", [128, 128], F32)
            actv = mk.sb("actv", [128, 128], F32)
            wgt = mk.sb("wgt", [128, 128], F32)
            NGB = 12
            gbuf = [mk.sb("gbuf%d" % i, [128, 2 * D], BF16) for i in range(NGB)]
            junk = mk.sb("junk", [128, D], F32)
            pleg = mk.sb("pleg", [128, D], F32)
            NSM = 12
            dot_r = [mk.sb("dot_r%d" % i, [128, 1], F32) for i in range(NSM)]
            wv_r = [mk.sb("wv_r%d" % i, [128, 2], F32) for i in range(NSM)]
            dg_r = [mk.sb("dg_r%d" % i, [128, 128], BF16) for i in range(8)]
            xn3 = mk.sb("xn3", [128, D], BF16)
            xn3T = mk.sb("xn3T", [128, D], BF16)
            pb16 = mk.sb("pb16", [128, 256], BF16)
            ppT = mk.sb("ppT", [128, 256], BF16)
            print("phase2 sbuf remaining", nc.sbuf_bytes_remaining, flush=True)
            B = pbanks

            def load2(ti):
                mk.dma('sp', xt2[ti % 2][:], x1s[ti * 128:(ti + 1) * 128, :], reads=[x1_tok[ti]], writes=[xt2[ti % 2]],
                       semt=xt2[ti % 2])
                mk.dma('sp', pt2[ti % 2][:], pin[ti * 128:(ti + 1) * 128, :], writes=[pt2[ti % 2]], semt=pt2[ti % 2])

            def rms(xsrc, gt, out32, outb, st2, junk):
                mk.op('act', lambda e: e.activation(out=junk[:], in_=xsrc[:], func=AF.Square, accum_out=st2[:, 0:1]),
                      reads=[xsrc], writes=[junk, st2])
                rsqrt(st2, st2[:, 2:3], st2[:, 0:1], 1.0 / D)
                if out32 is not None:
                    mk.op('dve', lambda e: e.scalar_tensor_tensor(out=out32[:], in0=xsrc[:], scalar=st2[:, 2:3], in1=gt[:],
                                                                  op0=ALU.mult, op1=ALU.mult), reads=[xsrc, st2, gt], writes=[out32])
                    mk.op('act', lambda e: e.copy(out=outb[:], in_=out32[:]), reads=[out32], writes=[outb])
                else:
                    mk.op('dve', lambda e: e.scalar_tensor_tensor(out=outb[:], in0=xsrc[:], scalar=st2[:, 2:3], in1=gt[:],
                                                                  op0=ALU.mult, op1=ALU.mult), reads=[xsrc, st2, gt], writes=[outb])

            def tr8b(src, dst, n=8):
                def f(e):
                    last = None
                    for c in range(n):
                        last = pe_T(B[0], c * 128, src, src[:, c * 128:(c + 1) * 128])(e)
                    return last
                mk.op('pe', f, reads=[src, ident], writes=[B[0]], skip_self=True)
                mk.op('act', lambda e: e.copy(out=dst[:, 0:n * 128], in_=B[0][:].bitcast(BF16)[:, 0:n * 128]),
                      reads=[B[0]], writes=[dst])

            tiles2 = tiles if cfg.get('phase2', True) else []

            def stageA(ti):
                par = ti % 2
                x = xt2[par]
                xn2 = xn2_[par]
                idxi = idxi_[par]
                gate = gate_[par]
                mk.rate = 1.0
                load2(ti)
                rms(x, g_ffn, xn2, xn2b, st2A, junkA)
                tr8b(xn2b, xn2T)
                xT3 = xn2T[:].rearrange("p (c t) -> p c t", c=8)
                for bq in range(4):
                    bank = B[1 + bq]

                    def fq(e, bq=bq, bank=bank):
                        last = None
                        for gi in range(4):
                            g_ = bq * 4 + gi
                            for c in range(8):
                                last = e.matmul(bank[:, gi * 128:(gi + 1) * 128], lhsT=wq_v[:, c, g_ * 128:(g_ + 1) * 128],
                                                rhs=xT3[:, c, :], start=(c == 0), stop=(c == 7))
                        return last
                    mk.op('pe', fq, reads=[xn2T, warena2], writes=[bank], skip_self=True)
                    mk.op('act', lambda e, bq=bq, bank=bank: e.copy(out=qpT[:, bq * 512:(bq + 1) * 512], in_=bank[:]),
                          reads=[bank], writes=[qpT])
                for bq in range(4):
                    bank = B[1 + bq]

                    def fs(e, bq=bq, bank=bank):
                        last = None
                        for gi in range(4):
                            g_ = bq * 4 + gi
                            last = e.matmul(bank[:, gi * 128:(gi + 1) * 128], lhsT=qpT[:, g_ * 128:(g_ + 1) * 128],
                                            rhs=keys_v[:, g_, :], start=True, stop=True)
                        return last
                    mk.op('pe', fs, reads=[qpT, warena2], writes=[bank], skip_self=True)
                    mk.op('act', lambda e, bq=bq, bank=bank: e.copy(out=scp[:, bq * 512:(bq + 1) * 512], in_=bank[:]),
                          reads=[bank], writes=[scp])
                mk.rate = 4.0
                def sg_(g_):
                    return scp[:, g_ * 128:(g_ + 1) * 128]

                def wk_(g_):
                    return work[:, g_ * 128:(g_ + 1) * 128]
                for g_ in range(16):
                    mk.op('dve', lambda e, g_=g_: e.max(out=stop_[:, g_ * 16:g_ * 16 + 8], in_=sg_(g_)),
                          reads=[scp], writes=[stop_], skip_self=True)
                for g_ in range(16):
                    mk.op('dve', lambda e, g_=g_: e.match_replace(out=wk_(g_), in_to_replace=stop_[:, g_ * 16:g_ * 16 + 8],
                                                                  in_values=sg_(g_), imm_value=-1e30),
                          reads=[scp, stop_], writes=[work], skip_self=True)
                for g_ in range(16):
                    mk.op('dve', lambda e, g_=g_: e.max(out=stop_[:, g_ * 16 + 8:g_ * 16 + 16], in_=wk_(g_)),
                          reads=[work], writes=[stop_], skip_self=True)
                for g_ in range(16):
                    mk.op('dve', lambda e, g_=g_: e.max_index(out=itopu[:, g_ * 16:g_ * 16 + 8], in_max=stop_[:, g_ * 16:g_ * 16 + 8],
                                                              in_values=sg_(g_)), reads=[scp, stop_], writes=[itopu], skip_self=True)
                for g_ in range(16):
                    mk.op('dve', lambda e, g_=g_: e.max_index(out=itopu[:, g_ * 16 + 8:g_ * 16 + 16],
                                                              in_max=stop_[:, g_ * 16 + 8:g_ * 16 + 16], in_values=sg_(g_)),
                          reads=[scp, stop_], writes=[itopu], skip_self=True)
                mk.op('dve', lambda e: e.tensor_copy(out=itopf[:], in_=itopu[:]), reads=[itopu], writes=[itopf])
                st4 = stop_[:].rearrange("t (h c k) -> t h c k", h=8, c=2)
                mk.op('dve', lambda e: e.tensor_tensor(out=cand[:].rearrange("t (h i j) -> t h i j", h=8, i=16),
                                                       in0=st4[:, :, 0, :].unsqueeze(3).to_broadcast([128, 8, 16, 16]),
                                                       in1=st4[:, :, 1, :].unsqueeze(2).to_broadcast([128, 8, 16, 16]), op=ALU.add),
                      reads=[stop_], writes=[cand])
                def cg_(h):
                    return cand[:, h * 256:(h + 1) * 256]

                def wk2_(h):
                    return work[:, h * 256:(h + 1) * 256]
                for h in range(8):
                    mk.op('dve', lambda e, h=h: e.max(out=best[:, h * 16:h * 16 + 8], in_=cg_(h)),
                          reads=[cand], writes=[best], skip_self=(h > 0))
                for h in range(8):
                    mk.op('dve', lambda e, h=h: e.match_replace(out=wk2_(h), in_to_replace=best[:, h * 16:h * 16 + 8], in_values=cg_(h),
                                                                imm_value=-1e30), reads=[cand, best], writes=[work], skip_self=True)
                for h in range(8):
                    mk.op('dve', lambda e, h=h: e.max(out=best[:, h * 16 + 8:h * 16 + 16], in_=wk2_(h)),
                          reads=[work], writes=[best], skip_self=True)
                for h in range(8):
                    mk.op('dve', lambda e, h=h: e.max_index(out=posu[:, h * 16:h * 16 + 8], in_max=best[:, h * 16:h * 16 + 8],
                                                            in_values=cg_(h)), reads=[cand, best], writes=[posu], skip_self=True)
                for h in range(8):
                    mk.op('dve', lambda e, h=h: e.max_index(out=posu[:, h * 16 + 8:h * 16 + 16], in_max=best[:, h * 16 + 8:h * 16 + 16],
                                                            in_values=cg_(h)), reads=[cand, best], writes=[posu], skip_self=True)
                mk.op('dve', lambda e: e.tensor_single_scalar(out=posi[:, 0:128], in_=posu[:].bitcast(I32), scalar=4,
                                                              op=ALU.logical_shift_right), reads=[posu], writes=[posi])
                mk.op('dve', lambda e: e.tensor_single_scalar(out=posi[:, 128:256], in_=posu[:].bitcast(I32), scalar=15,
                                                              op=ALU.bitwise_and), reads=[posu], writes=[posi])
                mk.op('dve', lambda e: e.tensor_copy(out=posf[:], in_=posi[:]), reads=[posi], writes=[posf])
                it4 = itopf[:].rearrange("t (h c k) -> t h c k", h=8, c=2)
                ohs = (oh, oh2)
                oh4s = [o_[:].rearrange("t (h k i) -> t h k i", h=8, k=16) for o_ in ohs]
                for c in range(2):
                    pf = posf[:, c * 128:(c + 1) * 128].rearrange("t (h k) -> t h k", h=8)
                    mk.op('dve', lambda e, pf=pf, c=c: e.tensor_tensor(
                        out=oh4s[c], in0=iota16[:].unsqueeze(1).unsqueeze(1).to_broadcast([128, 8, 16, 16]),
                        in1=pf.unsqueeze(3).to_broadcast([128, 8, 16, 16]), op=ALU.is_equal), reads=[iota16, posf], writes=[ohs[c]])
                for c in range(2):
                    mk.op('dve', lambda e, c=c: e.tensor_tensor(
                        out=oh4s[c], in0=oh4s[c], in1=it4[:, :, c, :].unsqueeze(2).to_broadcast([128, 8, 16, 16]), op=ALU.mult),
                        reads=[ohs[c], itopf], writes=[ohs[c]], skip_self=True)
                for c in range(2):
                    mk.op('dve', lambda e, c=c: e.tensor_reduce(out=sel[:, c * 128:(c + 1) * 128],
                                                                in_=ohs[c][:].rearrange("t (s i) -> t s i", i=16), axis=AX.X, op=ALU.add),
                          reads=[ohs[c]], writes=[sel], skip_self=True)
                mk.op('dve', lambda e: e.scalar_tensor_tensor(out=idxf[:], in0=sel[:, 0:128], scalar=128.0, in1=sel[:, 128:256],
                                                              op0=ALU.mult, op1=ALU.add), reads=[sel], writes=[idxf])
                mk.op('dve', lambda e: e.tensor_copy(out=idxi[:], in_=idxf[:]), reads=[idxf], writes=[idxi])
                b3 = best[:].rearrange("t (h k) -> t h k", h=8)
                mk.op('dve', lambda e: e.tensor_tensor(out=gate[:].rearrange("t (h k) -> t h k", h=8), in0=b3,
                                                       in1=b3[:, :, 0:1].to_broadcast([128, 8, 16]), op=ALU.subtract),
                      reads=[best], writes=[gate])
                mk.op('act', lambda e: e.activation(out=gate[:], in_=gate[:], func=AF.Exp), reads=[gate], writes=[gate])
                mk.op('dve', lambda e: e.tensor_reduce(out=st2A[:, 8:16], in_=gate[:].rearrange("t (h k) -> t h k", h=8),
                                                       axis=AX.X, op=ALU.add), reads=[gate], writes=[st2A])
                mk.op('dve', lambda e: e.reciprocal(out=st2A[:, 8:16], in_=st2A[:, 8:16]), reads=[st2A], writes=[st2A])
                mk.op('dve', lambda e: e.tensor_tensor(out=gate[:].rearrange("t (h k) -> t h k", h=8),
                                                       in0=gate[:].rearrange("t (h k) -> t h k", h=8),
                                                       in1=st2A[:, 8:16].unsqueeze(2).to_broadcast([128, 8, 16]), op=ALU.mult),
                      reads=[gate, st2A], writes=[gate])

            def stageB(ti, nxt_q):
                par = ti % 2
                x = xt2[par]
                pt = pt2[par]
                xn2 = xn2_[par]
                idxi = idxi_[par]
                gate = gate_[par]
                mk._waits('pool', [comb_t], [])
                for s_ in range(128):
                    gb = gbuf[s_ % NGB]
                    dt_ = dot_r[s_ % NSM]
                    wv = wv_r[s_ % NSM]
                    dg = dg_r[s_ % 8]
                    mk.gather(gb, gb[:], comb[:, :], idxi, idxi[:, s_:s_ + 1])
                    mk.op('dve', lambda e, gb=gb, dt_=dt_: e.scalar_tensor_tensor(
                        out=junk[:], in0=gb[:, 0:D], scalar=1.0, in1=xn2[:], op0=ALU.mult, op1=ALU.mult,
                        accum_out=dt_[:, 0:1]), reads=[gb, xn2], writes=[junk, dt_], skip_self=True)
                    mk.op('act', lambda e, dt_=dt_, wv=wv: e.activation(out=wv[:, 0:1], in_=dt_[:, 0:1], func=GELU),
                          reads=[dt_], writes=[wv])
                    mk.op('act', lambda e, wv=wv, s_=s_: e.mul(out=wv[:, 1:2], in_=wv[:, 0:1], mul=gate[:, s_:s_ + 1]),
                          reads=[wv, gate], writes=[wv])
                    mk.op('act', lambda e, wv=wv, dg=dg: e.activation(out=dg[:], in_=identf[:], func=AF.Copy, scale=wv[:, 1:2]),
                          reads=[wv, identf], writes=[dg])

                    def fv(e, s_=s_, gb=gb, dg=dg):
                        last = None
                        for n in range(2):
                            last = e.matmul(B[5 + n][:], lhsT=dg[:], rhs=gb[:, D + n * 512:D + (n + 1) * 512],
                                            start=(s_ == 0), stop=(s_ == 127))
                        return last
                    mk.op('pe', fv, reads=[gb, dg], writes=[B[5], B[6]], skip_self=True)
                    mk.play(nxt_q, 1.0, lag=5)
                if len(nxt_q) and ti == 1:
                    print('replay leftover at drain:', len(nxt_q), flush=True)
                mk.play(nxt_q, 10 ** 9)
                for n in range(2):
                    sl = slice(n * 512, (n + 1) * 512)
                    mk.op('dve', lambda e, n=n, sl=sl: e.tensor_tensor(out=x[:, sl], in0=B[5 + n][:], in1=x[:, sl], op=ALU.add),
                          reads=[B[5 + n], x], writes=[x])
                mk.rec = []
                mk.rate = 1.0
                rms(x, g_ple, None, xn3, st2, pleg)
                tr8b(xn3, xn3T)
                x3T = xn3T[:].rearrange("p (c t) -> p c t", c=8)

                def fpg(e):
                    last = None
                    for n in range(2):
                        for c in range(8):
                            last = e.matmul(B[1 + n][:], lhsT=x3T[:, c, :], rhs=wpg_v[:, c, n * 512:(n + 1) * 512],
                                            start=(c == 0), stop=(c == 7))
                    return last
                mk.op('pe', fpg, reads=[xn3T, warena2], writes=[B[1], B[2]], skip_self=True)
                for n in range(2):
                    mk.op('act', lambda e, n=n: e.activation(out=pleg[:, n * 512:(n + 1) * 512], in_=B[1 + n][:], func=AF.Sigmoid),
                          reads=[B[1 + n]], writes=[pleg])
                mk.op('act', lambda e: e.copy(out=pb16[:], in_=pt[:]), reads=[pt], writes=[pb16])
                tr8b(pb16, ppT, n=2)
                pT3 = ppT[:].rearrange("p (c t) -> p c t", c=2)

                def fpl(e):
                    last = None
                    for n in range(2):
                        for c in range(2):
                            last = e.matmul(B[3 + n][:], lhsT=pT3[:, c, :], rhs=wpl_v[:, c, n * 512:(n + 1) * 512],
                                            start=(c == 0), stop=(c == 1))
                    return last
                mk.op('pe', fpl, reads=[ppT, warena2], writes=[B[3], B[4]], skip_self=True)
                for n in range(2):
                    sl = slice(n * 512, (n + 1) * 512)
                    mk.op('dve', lambda e, n=n, sl=sl: e.tensor_tensor(out=pleg[:, sl], in0=B[3 + n][:], in1=pleg[:, sl], op=ALU.mult),
                          reads=[B[3 + n], pleg], writes=[pleg])
                mk.op('dve', lambda e: e.tensor_tensor(out=x[:], in0=x[:], in1=pleg[:], op=ALU.add), reads=[x, pleg], writes=[x])
                mk.dma('sp', y[ti * 128:(ti + 1) * 128, :], x[:], reads=[x], semt=x)
                tail = mk.rec
                mk.rec = None
                return tail

            if tiles2:
                mk.rec = []
                stageA(tiles2[0])
                q0 = mk.rec
                mk.rec = None
                mk.play(q0, 10 ** 9)
            pending_tail = []
            for tpos, ti in enumerate(tiles2):
                nxt_q = pending_tail
                if tpos + 1 < len(tiles2):
                    mk.rec = []
                    stageA(tiles2[tpos + 1])
                    nxt_q = nxt_q + mk.rec
                    mk.rec = None
                if tpos == 1:
                    print('replay queue: ops', len(nxt_q), 'slot budget needed', sum(1.0 / r[3] for r in nxt_q), flush=True)
                pending_tail = stageB(ti, nxt_q)
            mk.play(pending_tail, 10 ** 9)
            out_tiles_all = out_tiles + xt2
            mk._waits('sp', [], xt2)
            mk.barrier()
        print("instructions emitted:", mk.n_ins, flush=True)
    return nc


_NC_CACHE = {}


def _get_nc():
    if "nc" not in _NC_CACHE:
        _NC_CACHE["nc"] = build()
    return _NC_CACHE["nc"]


def kernel(x_prompt, x_sample, cache_k, cache_v, p_prompt, p_sample,
           attn_norm_g, w_in, q_norm_g, k_norm_g, attn_sinks,
           sgu_norm_g, sgu_norm_b, sgu_w, sgu_b,
           w_branch_a, w_branch_b, w_out,
           ffn_norm_g, peer_w_q, peer_sub_keys, peer_u, peer_v,
           ple_norm_g, w_ple, w_ple_gate):
    f = lambda a: np.ascontiguousarray(np.asarray(a, dtype=np.float32))
    x_prompt, x_sample, cache_k, cache_v = f(x_prompt), f(x_sample), f(cache_k), f(cache_v)
    p_prompt, p_sample = f(p_prompt), f(p_sample)
    nc = _get_nc()
    consts = _consts()
    rep = lambda v, n=128: np.ascontiguousarray(np.broadcast_to(f(v).reshape(1, -1), (n, f(v).size)))
    sw = f(sgu_w)[0]
    sb = f(sgu_b)[0]
    small = {
        "g_attn": rep(attn_norm_g[0]), "g_ffn": rep(ffn_norm_g[0]), "g_ple": rep(ple_norm_g[0]),
        "gq": rep(q_norm_g[0]), "gk": rep(k_norm_g[0]), "sinks": rep(attn_sinks[0]),
        "ln_g": rep(sgu_norm_g[0]), "ln_b": rep(sgu_norm_b[0]),
        "bcol_p": np.ascontiguousarray(sb.T),
        "bcol_s": np.ascontiguousarray(np.tile(sb[:, :8].T, (16, 1))),
        "wsT_p": np.ascontiguousarray(sw.transpose(2, 0, 1)).reshape(128, 512),
        "wsT_s": np.ascontiguousarray(np.tile(sw[:, :8, :8].transpose(2, 0, 1), (16, 1, 16))).reshape(128, 512),
    }
    shared = {
        "w_in": f(w_in)[0], "w_a": f(w_branch_a)[0], "w_b": f(w_branch_b)[0], "w_out": f(w_out)[0],
        "w_q": f(peer_w_q)[0],
        "keysT": np.ascontiguousarray(f(peer_sub_keys)[0].reshape(16, 128, 128).transpose(2, 0, 1)).reshape(128, 2048),
        "peer_u": f(peer_u)[0], "peer_v": f(peer_v)[0], "w_ple": f(w_ple)[0], "w_pg": f(w_ple_gate)[0],
    }
    for k, v in consts.items():
        shared["c_" + k] = v
    for k, v in small.items():
        shared["s_" + k] = v
    in_maps = []
    for c in range(NCORES):
        m = dict(shared)
        xs = x_sample[c * 16:(c + 1) * 16].reshape(128, D)
        m["xin"] = np.ascontiguousarray(np.concatenate([x_prompt[c], xs], axis=0))
        ps = p_sample[0, c * 16:(c + 1) * 16].reshape(128, 256)
        m["pin"] = np.ascontiguousarray(np.concatenate([p_prompt[0, c], ps], axis=0))
        m["ck"] = np.ascontiguousarray(cache_k[0, c * 16:(c + 1) * 16].reshape(16, 128, 128).transpose(1, 0, 2))
        m["cv"] = np.ascontiguousarray(cache_v[0, c * 16:(c + 1) * 16].reshape(16, 128, 128).transpose(1, 0, 2))
        in_maps.append(m)
    res = run_bass_kernel_spmd(nc, in_maps, core_ids=list(range(NCORES)))
    R = res.results
    y_p = np.stack([R[c]["y"][:SEQ] for c in range(NCORES)])
    y_s = np.concatenate([R[c]["y"][SEQ:].reshape(16, 8, D) for c in range(NCORES)], axis=0)
    nkp = np.stack([R[c]["o_kp"].reshape(128, 2, 64) for c in range(NCORES)])[None]
    nvp = np.stack([R[c]["o_vp"].reshape(128, 2, 64) for c in range(NCORES)])[None]
    nks = np.concatenate([R[c]["o_ks"].reshape(16, 8, 2, 64) for c in range(NCORES)], axis=0)[None]
    nvs = np.concatenate([R[c]["o_vs"].reshape(16, 8, 2, 64) for c in range(NCORES)], axis=0)[None]
    nsp = np.stack([R[c]["o_sp"] for c in range(NCORES)])[None]
    nss = np.concatenate([R[c]["o_ss"].reshape(16, 8, 512) for c in range(NCORES)], axis=0)[None]
    return (y_p.astype(np.float32), y_s.astype(np.float32), nkp.astype(np.float32), nvp.astype(np.float32),
            nks.astype(np.float32), nvs.astype(np.float32), nsp.astype(np.float32), nss.astype(np.float32))
```

```python
import numpy as np
from contextlib import ExitStack
import concourse.bass as bass
import concourse.mybir as mybir
from concourse.bass_utils import run_bass_kernel_spmd

F32 = mybir.dt.float32
BF16 = mybir.dt.bfloat16
I32 = mybir.dt.int32
U32 = mybir.dt.uint32
ALU = mybir.AluOpType
AF = mybir.ActivationFunctionType
AX = mybir.AxisListType

NCORES = 8
D = 1024
SEQ = 4096
NPT = SEQ // 128
NT = NPT + 1
NTOK = NT * 128
IN_W = 3840
EPS = 1e-6
NEG = -30000.0
GELU = AF.Gelu_apprx_tanh


class St:
    __slots__ = ("w", "r", "dsem")

    def __init__(self):
        self.w = None
        self.r = []
        self.dsem = None


class T:
    def __init__(self, name, h=None, st=None):
        self.name = name
        self.h = h
        self.st = st if st is not None else St()

    def __getitem__(self, k):
        return self.h[k]


class MK:
    def __init__(self, nc, es):
        self.nc = nc
        self.es_sem = es
        self.es = es
        self.engs = {'pe': nc.tensor, 'act': nc.scalar, 'dve': nc.vector, 'pool': nc.gpsimd, 'sp': nc.sync}
        self.sems = {}
        self.cnt = {}
        self.waited = {k: {} for k in self.engs}
        for k in ('pe', 'act', 'dve', 'pool'):
            self.sems[k] = es.enter_context(nc.semaphore("c_" + k))
            self.cnt[k] = 0
        self.nd = 0
        self.n_ins = 0
        self.rec = None
        self.rate = 4.0
        self.slot = 0
        self.inter = None
        self._in_inter = False
        self._inter_cnt = 0
        self.prev_eng = 'dve'
        self.prev_slot = -100

    def sb(self, name, shape, dtype):
        h = self.es.enter_context(self.nc.sbuf_tensor(name, list(shape), dtype))
        return T(name, h)

    def ps(self, name, shape, dtype=F32):
        h = self.es.enter_context(self.nc.psum_tensor(name, list(shape), dtype))
        return T(name, h)

    def _dsem(self, t):
        st = t.st
        if st.dsem is None:
            key = "d%d" % self.nd
            self.nd += 1
            self.sems[key] = self.es_sem.enter_context(self.nc.semaphore(key))
            self.cnt[key] = 0
            st.dsem = key
        return st.dsem

    def _waits(self, eng, reads, writes, skip_self=False):
        need = {}

        def add(tok):
            if tok is None:
                return
            k, v = tok
            if need.get(k, 0) < v:
                need[k] = v
        for t in reads:
            add(t.st.w)
        for t in writes:
            add(t.st.w)
            for tok in t.st.r:
                add(tok)
        e = self.engs[eng]
        for k, v in need.items():
            if skip_self and k == eng:
                continue
            if self.waited[eng].get(k, 0) >= v:
                continue
            e.wait_ge(self.sems[k], v)
            self.n_ins += 1
            self.waited[eng][k] = v

    def _commit(self, tok, reads, writes):
        for t in reads:
            r = t.st.r
            r.append(tok)
            if len(r) > 48:
                best = {}
                for k, v in r:
                    if best.get(k, 0) < v:
                        best[k] = v
                t.st.r = list(best.items())
        for t in writes:
            t.st.w = tok
            t.st.r = []

    def op(self, eng, fn, reads=(), writes=(), skip_self=False):
        if self.rec is not None:
            self.rec.append(('op', (eng, fn), dict(reads=list(reads), writes=list(writes), skip_self=skip_self), self.rate))
            return None
        self._waits(eng, reads, writes, skip_self)
        ins = fn(self.engs[eng])
        self.cnt[eng] += 1
        ins.then_inc(self.sems[eng], 1)
        self.n_ins += 1
        self._commit((eng, self.cnt[eng]), reads, writes)
        if self.inter is not None and not self._in_inter:
            q, every = self.inter
            self._inter_cnt += 1
            if q and self._inter_cnt % every == 0:
                self._in_inter = True
                kind, a, kw, _rate = q.pop(0)
                getattr(self, kind)(*a, **kw)
                self._in_inter = False
        return ins

    def dma(self, q, out, in_, reads=(), writes=(), semt=None, **kw):
        if self.rec is not None:
            self.rec.append(('dma', (q, out, in_), dict(reads=list(reads), writes=list(writes), semt=semt, **kw), self.rate))
            return None
        self._waits(q, reads, writes)
        key = self._dsem(semt)
        ins = self.engs[q].dma_start(out=out, in_=in_, **kw)
        self.cnt[key] += 16
        ins.then_inc(self.sems[key], 16)
        self.n_ins += 1
        self._commit((key, self.cnt[key]), reads, writes)
        return ins

    def gather(self, out_t, out_ap, table_ap, idx_t, idx_ap):
        self._waits('pool', [idx_t], [out_t])
        key = self._dsem(out_t)
        ins = self.nc.gpsimd.indirect_dma_start(
            out=out_ap, out_offset=None, in_=table_ap,
            in_offset=bass.IndirectOffsetOnAxis(ap=idx_ap, axis=0))
        self.cnt[key] += 16
        ins.then_inc(self.sems[key], 16)
        self.n_ins += 1
        self._commit((key, self.cnt[key]), [idx_t], [out_t])
        return ins

    def settle(self, tiles):
        for t in tiles:
            if t.st.w is not None and t.st.w[0] in self.cnt and t.st.w[0].startswith("d"):
                t.st.w = (t.st.w[0], self.cnt[t.st.w[0]])

    def play(self, q, budget, lag=0):
        self.slot += 1
        while q and budget > 1e-9:
            kind, a, kw, rate = q[0]
            if budget + 1e-9 < 1.0 / rate and budget < 0.999:
                break
            eng = a[0]
            if lag and eng == 'dve' and self.prev_eng != 'dve' and self.slot - self.prev_slot < lag:
                break
            q.pop(0)
            getattr(self, kind)(*a, **kw)
            budget -= 1.0 / rate
            self.prev_eng = eng
            self.prev_slot = self.slot

    def barrier(self):
        for eng in self.engs:
            e = self.engs[eng]
            for k, v in self.cnt.items():
                if v == 0 or self.waited[eng].get(k, 0) >= v:
                    continue
                e.wait_ge(self.sems[k], v)
                self.n_ins += 1
                self.waited[eng][k] = v


def _consts():
    slopes = np.exp2(-np.arange(1, 9, dtype=np.float64)).astype(np.float32)
    s = np.arange(128)[:, None]
    t = np.arange(128)[None, :]
    c = {}
    c["ident"] = np.eye(128, dtype=np.float32)
    bp = np.full((128, 2, 8, 128), NEG, np.float32)
    d0 = (128 + t - s).astype(np.float32)
    d1 = (t - s).astype(np.float32)
    for h in range(8):
        bp[:, 0, h, :] = np.where(s > t, -slopes[h] * d0, NEG)
        bp[:, 1, h, :] = np.where(s <= t, -slopes[h] * d1, NEG)
    c["bias_p"] = bp.reshape(128, 2048)
    bs_, ts_ = s // 8, s % 8
    bq_, tq_ = t // 8, t % 8
    bsn = np.full((128, 8, 128), NEG, np.float32)
    ok = (bs_ == bq_) & (ts_ <= tq_)
    for h in range(8):
        bsn[:, h, :] = np.where(ok, -slopes[h] * (tq_ - ts_).astype(np.float32), NEG)
    c["bias_sn"] = bsn.reshape(128, 1024)
    r = np.arange(128)[:, None]
    tq = np.arange(8)[None, :]
    bsp = np.full((128, 8, 8), NEG, np.float32)
    for h in range(8):
        bsp[:, h, :] = np.where(r > tq, -slopes[h] * (tq + 128 - r).astype(np.float32), NEG)
    c["bias_sp"] = bsp.reshape(128, 64)
    c["maskT_p"] = (s <= t).astype(np.float32)
    c["maskT_s"] = ok.astype(np.float32)
    c["iota16"] = np.broadcast_to(np.arange(16, dtype=np.float32)[None, :], (128, 16)).copy()
    return c


CONST_SHAPES = {"ident": (128, 128), "bias_p": (128, 2048), "bias_sn": (128, 1024), "bias_sp": (128, 64),
                "maskT_p": (128, 128), "maskT_s": (128, 128), "iota16": (128, 16)}
SMALL_SHAPES = {"g_attn": (128, 1024), "g_ffn": (128, 1024), "g_ple": (128, 1024), "gq": (128, 64), "gk": (128, 64),
                "sinks": (128, 8), "ln_g": (128, 512), "ln_b": (128, 512),
                "bcol_p": (128, 4), "bcol_s": (128, 4), "wsT_p": (128, 512), "wsT_s": (128, 512)}


CFG = {}


class _StopTile(Exception):
    pass


def chk(n):
    if CFG.get('stage', 99) < n:
        raise _StopTile()


def build():
    nc = bass.Bass("TRN2", target_bir_lowering=False)
    cfg = CFG
    tiles = cfg.get('tiles', list(range(NT)))

    def din(name, shape, dt=F32):
        return nc.dram_tensor(name, list(shape), dt, kind="ExternalInput").ap()

    def dout(name, shape, dt=F32):
        return nc.dram_tensor(name, list(shape), dt, kind="ExternalOutput").ap()

    xin = din("xin", [NTOK, D])
    pin = din("pin", [NTOK, 256])
    ck = din("ck", [128, 16, 128])
    cv = din("cv", [128, 16, 128])
    w_in = din("w_in", [D, IN_W])
    w_a = din("w_a", [512, D])
    w_b = din("w_b", [512, D])
    w_out = din("w_out", [D, D])
    w_q = din("w_q", [D, 2048])
    keysT = din("keysT", [128, 2048])
    peer_u = din("peer_u", [16384, D])
    peer_v = din("peer_v", [16384, D])
    w_ple = din("w_ple", [256, D])
    w_pg = din("w_pg", [D, D])
    cin = {k: din("c_" + k, v) for k, v in CONST_SHAPES.items()}
    sin = {k: din("s_" + k, v) for k, v in SMALL_SHAPES.items()}

    y = dout("y", [NTOK, D])
    o_kp = dout("o_kp", [128, 128])
    o_vp = dout("o_vp", [128, 128])
    o_ks = dout("o_ks", [128, 128])
    o_vs = dout("o_vs", [128, 128])
    o_sp = dout("o_sp", [128, 512])
    o_ss = dout("o_ss", [128, 512])
    x1s = nc.dram_tensor("x1s", [NTOK, D], F32, kind="Internal").ap()
    comb = nc.dram_tensor("comb", [16384, 2 * D], BF16, kind="Internal").ap()
    dbg_x1 = dout("dbg_x1", [NTOK, D]) if cfg.get('dbg') else None
    dbg_am = dout("dbg_am", [NTOK, D]) if cfg.get('dbg') else None

    with ExitStack() as es0:
        mk = MK(nc, es0)
        x1_tok = [T("x1s%d" % i) for i in range(NT)]
        out_tiles = []

        ident = mk.sb("ident", [128, 128], BF16)
        cgrp = T("cgrp")
        mk.dma('pool', ident[:], cin["ident"][:, :], writes=[ident], semt=ident)
        pbanks = [mk.ps("pb%d" % i, [128, 512], F32) for i in range(8)]

        def pe_T(bank, col0, src_t, src_ap, n=128, extra_reads=()):
            pv = bank[:].bitcast(BF16)
            return lambda e: e.transpose(out=pv[0:n, col0:col0 + 128], in_=src_ap, identity=ident[:])

        epst = mk.sb("epst", [128, 1], F32)
        mk.op('pool', lambda e: e.memset(epst[:], EPS), writes=[epst])

        def rsqrt(tile, dst, src, scale):
            mk.op('act', lambda e: e.activation(out=dst, in_=src, func=AF.Sqrt, bias=epst[:, 0:1], scale=scale),
                  reads=[tile, epst], writes=[tile])
            mk.op('dve', lambda e: e.reciprocal(out=dst, in_=dst), reads=[tile], writes=[tile])

        with ExitStack() as es1:
            mk.es = es1
            warena = mk.sb("warena1", [128, 8 * IN_W + 4 * D + 4 * D + 8 * D], BF16)
            o = 0
            win_v = warena[:, o:o + 8 * IN_W].rearrange("p (c n) -> p c n", c=8); o += 8 * IN_W
            wa_v = warena[:, o:o + 4 * D].rearrange("p (c n) -> p c n", c=4); o += 4 * D
            wb_v = warena[:, o:o + 4 * D].rearrange("p (c n) -> p c n", c=4); o += 4 * D
            wo_v = warena[:, o:o + 8 * D].rearrange("p (c n) -> p c n", c=8); o += 8 * D
            ZG = [(0, 512), (512, 256), (768, 512), (1280, 512), (1792, 512), (2304, 512), (2816, 512), (3328, 512)]
            wg = [T("wg%d" % i) for i in range(8)]
            w_in_r = w_in.rearrange("(c p) n -> p c n", p=128)
            for gi, (c0, n) in enumerate(ZG):
                for c in range(8):
                    mk.dma('pool', win_v[:, c, c0:c0 + n], w_in_r[:, c, c0:c0 + n], semt=wg[gi])
                wg[gi].st.w = (wg[gi].st.dsem, mk.cnt[wg[gi].st.dsem])
            w_a_r = w_a.rearrange("(c p) n -> p c n", p=128)
            w_b_r = w_b.rearrange("(c p) n -> p c n", p=128)
            w_o_r = w_out.rearrange("(c p) n -> p c n", p=128)
            wab = T("wab")
            wot = T("wot")
            for c in range(4):
                mk.dma('pool', wa_v[:, c, :], w_a_r[:, c, :], semt=wab)
                mk.dma('pool', wb_v[:, c, :], w_b_r[:, c, :], semt=wab)
            wab.st.w = (wab.st.dsem, mk.cnt[wab.st.dsem])
            for c in range(8):
                mk.dma('pool', wo_v[:, c, :], w_o_r[:, c, :], semt=wot)
            wot.st.w = (wot.st.dsem, mk.cnt[wot.st.dsem])
            comb_t = T("comb")
            for c in range(16):
                rs = slice(c * 1024, (c + 1) * 1024)
                mk.dma('pool', comb[rs, 0:D], peer_u[rs, :], semt=comb_t)
                mk.dma('pool', comb[rs, D:2 * D], peer_v[rs, :], semt=comb_t)
            comb_t.st.w = (comb_t.st.dsem, mk.cnt[comb_t.st.dsem])

            def cload(name, src, shape, dt=F32, q='sp'):
                t = mk.sb(name, shape, dt)
                mk.dma(q, t[:], src[:, :], writes=[t], semt=cgrp)
                return t
            g_attn = cload("g_attn", sin["g_attn"], [128, 1024])
            gq = cload("gq", sin["gq"], [128, 64])
            gk = cload("gk", sin["gk"], [128, 64])
            sinks = cload("sinks", sin["sinks"], [128, 8])
            ln_g = cload("ln_g", sin["ln_g"], [128, 512])
            ln_b = cload("ln_b", sin["ln_b"], [128, 512])
            bcol_p = cload("bcol_p", sin["bcol_p"], [128, 4])
            bcol_s = cload("bcol_s", sin["bcol_s"], [128, 4])
            bias_p = cload("bias_p", cin["bias_p"], [128, 2048])
            bias_sn = cload("bias_sn", cin["bias_sn"], [128, 1024])
            bias_sp = cload("bias_sp", cin["bias_sp"], [128, 64])
            maskT_p = cload("maskT_p", cin["maskT_p"], [128, 128])
            maskT_s = cload("maskT_s", cin["maskT_s"], [128, 128])
            t1 = mk.sb("t1", [128, D], F32)
            ws32_p = T("ws32_p", t1[:, 0:512], st=t1.st)
            ws32_s = T("ws32_s", t1[:, 512:1024], st=t1.st)
            mk.dma('sp', ws32_p[:], sin["wsT_p"][:, :], writes=[ws32_p], semt=cgrp)
            mk.dma('sp', ws32_s[:], sin["wsT_s"][:, :], writes=[ws32_s], semt=cgrp)
            consts1 = [ident, g_attn, gq, gk, sinks, ln_g, ln_b, bcol_p, bcol_s, bias_p, bias_sn, bias_sp,
                       maskT_p, maskT_s, ws32_p, ws32_s]
            mk.settle(consts1)

            sinkexp = mk.sb("sinkexp", [128, 8], F32)
            mk.op('act', lambda e: e.activation(out=sinkexp[:], in_=sinks[:], func=AF.Exp), reads=[sinks], writes=[sinkexp])
            gq8 = mk.sb("gq8", [128, 64], F32)
            mk.op('dve', lambda e: e.tensor_scalar(out=gq8[:], in0=gq[:], scalar1=0.125, scalar2=None, op0=ALU.mult),
                  reads=[gq], writes=[gq8])
            wsT_p = mk.sb("wsT_p", [128, 512], BF16)
            wsT_s = mk.sb("wsT_s", [128, 512], BF16)
            mk.op('dve', lambda e: e.tensor_tensor(out=wsT_p[:].rearrange("s (g t) -> s g t", g=4),
                                                   in0=ws32_p[:].rearrange("s (g t) -> s g t", g=4),
                                                   in1=maskT_p[:].unsqueeze(1).to_broadcast([128, 4, 128]), op=ALU.mult),
                  reads=[ws32_p, maskT_p], writes=[wsT_p])
            mk.op('dve', lambda e: e.tensor_tensor(out=wsT_s[:].rearrange("s (g t) -> s g t", g=4),
                                                   in0=ws32_s[:].rearrange("s (g t) -> s g t", g=4),
                                                   in1=maskT_s[:].unsqueeze(1).to_broadcast([128, 4, 128]), op=ALU.mult),
                  reads=[ws32_s, maskT_s], writes=[wsT_s])

            xt = [mk.sb("xt%d" % i, [128, D], F32) for i in range(3)]
            x1t = [mk.sb("x1t%d" % i, [128, D], F32) for i in range(2)]
            xn = mk.sb("xn", [128, D], BF16)
            xnT = mk.sb("xnT", [128, D], BF16)
            st1 = mk.sb("st1", [128, 16], F32)
            q32 = mk.sb("q32", [128, 512], F32)
            qn = mk.sb("qn", [128, 512], BF16)
            qT = mk.sb("qT", [128, 512], BF16)
            k32 = mk.sb("k32", [128, 128], F32)
            kn32 = mk.sb("kn32", [128, 128], F32)
            knb = mk.sb("knb", [128, 128], BF16)
            kT = [mk.sb("kT%d" % i, [128, 128], BF16) for i in range(2)]
            v32 = mk.sb("v32", [128, 128], F32)
            vaug = [mk.sb("vaug%d" % i, [128, 2 * 65], BF16) for i in range(2)]
            sc32 = mk.sb("sc32", [128, 1024], F32)
            pT = mk.sb("pT", [128, 2048], BF16)
            att_ = [mk.sb("att%d" % i, [128, 512], BF16) for i in range(2)]
            u32 = mk.sb("u32", [128, 512], F32)
            gv32 = mk.sb("gv32", [128, 512], F32)
            vn32 = mk.sb("vn32", [128, 512], F32)
            vnb = mk.sb("vnb", [128, 512], BF16)
            mb_ = [mk.sb("mb%d" % i, [128, 512], BF16) for i in range(2)]
            amT = mk.sb("amT", [128, 1024], BF16)
            siga_ = [mk.sb("siga%d" % i, [128, D], BF16) for i in range(2)]
            sigb_ = [mk.sb("sigb%d" % i, [128, D], BF16) for i in range(2)]
            hb = mk.sb("hb", [128, D], BF16)
            hT = mk.sb("hT", [128, D], BF16)
            stage = mk.sb("stage", [128, 2048], F32)
            qsq = T("qsq", stage[:, 1024:1536], st=stage.st)
            ksq = T("ksq", stage[:, 1536:1664], st=stage.st)
            ckb = T("ckb", pT[:], st=pT.st)
            kTp = mk.sb("kTp", [128, 2048], BF16)
            vpaug = mk.sb("vpaug", [128, 16 * 130], BF16)
            qTs = mk.sb("qTs", [128, 512], BF16)
            Eb = [mk.sb("Eb%d" % i, [128, 1024], BF16) for i in range(2)]
            print("phase1 sbuf remaining", nc.sbuf_bytes_remaining, flush=True)

            for i in range(2):
                mk.op('dve', lambda e, i=i: e.memset(vaug[i][:], 1.0), writes=[vaug[i]])
            mk.op('dve', lambda e: e.memset(vpaug[:], 1.0), writes=[vpaug])

            B = pbanks

            def load_x(ti):
                mk.dma('sp', xt[ti % 3][:], xin[ti * 128:(ti + 1) * 128, :], writes=[xt[ti % 3]], semt=xt[ti % 3])

            load_x(tiles[0])

            def tr8(e, src, bank):
                last = None
                for c in range(8):
                    last = pe_T(bank, c * 128, src, src[:, c * 128:(c + 1) * 128])(e)
                return last

            def emit_F1(tj):
                x = xt[tj % 3]
                mk.op('act', lambda e: e.activation(out=xn[:], in_=x[:], func=AF.Square, accum_out=st1[:, 0:1]),
                      reads=[x], writes=[xn, st1])
                rsqrt(st1, st1[:, 2:3], st1[:, 0:1], 1.0 / D)
                mk.op('dve', lambda e: e.scalar_tensor_tensor(out=xn[:], in0=x[:], scalar=st1[:, 2:3], in1=g_attn[:],
                                                              op0=ALU.mult, op1=ALU.mult), reads=[x, st1, g_attn], writes=[xn])
                mk.op('pe', lambda e: tr8(e, src=xn, bank=B[0]), reads=[xn, ident], writes=[B[0]], skip_self=True)
                mk.op('act', lambda e: e.copy(out=xnT[:], in_=B[0][:].bitcast(BF16)), reads=[B[0]], writes=[xnT])


            def tile_body(tpos, ti, prevE):
                try:
                    sample = (ti == NPT)
                    x = xt[ti % 3]
                    att = att_[ti % 2]
                    mb = mb_[ti % 2]
                    siga = siga_[ti % 2]
                    sigb = sigb_[ti % 2]
                    if tpos + 1 < len(tiles):
                        load_x(tiles[tpos + 1])
                    cur, prv = ti % 2, (ti + 1) % 2
                    if tpos == 0:
                        emit_F1(ti)
                    chk(1)
                    xnT3 = xnT[:].rearrange("p (c t) -> p c t", c=8)

                    def zgroup(bank, c0, n):
                        def f(e):
                            last = None
                            for c in range(8):
                                last = e.matmul(bank[:, 0:n], lhsT=xnT3[:, c, :], rhs=win_v[:, c, c0:c0 + n],
                                                start=(c == 0), stop=(c == 7))
                            return last
                        mk.op('pe', f, reads=[xnT, wg[[z[0] for z in ZG].index(c0)]], writes=[bank], skip_self=True)

                    zgroup(B[1], 0, 512)
                    mk.op('act', lambda e: e.copy(out=q32[:], in_=B[1][:]), reads=[B[1]], writes=[q32])
                    mk.op('act', lambda e: e.activation(out=qsq[:], in_=B[1][:], func=AF.Square), reads=[B[1]], writes=[qsq])
                    zgroup(B[2], 512, 256)
                    mk.op('act', lambda e: e.copy(out=k32[:], in_=B[2][:, 0:128]), reads=[B[2]], writes=[k32])
                    mk.op('act', lambda e: e.activation(out=ksq[:], in_=B[2][:, 0:128], func=AF.Square), reads=[B[2]], writes=[ksq])
                    mk.op('act', lambda e: e.copy(out=v32[:], in_=B[2][:, 128:256]), reads=[B[2]], writes=[v32])
                    zgroup(B[3], 768, 512)
                    mk.op('act', lambda e: e.activation(out=u32[:], in_=B[3][:], func=GELU), reads=[B[3]], writes=[u32])
                    zgroup(B[1], 1280, 512)
                    mk.op('act', lambda e: e.activation(out=gv32[:], in_=B[1][:], func=GELU, accum_out=st1[:, 4:5]),
                          reads=[B[1]], writes=[gv32, st1])
                    zgroup(B[2], 1792, 512)
                    mk.op('act', lambda e: e.activation(out=siga[:, 0:512], in_=B[2][:], func=AF.Sigmoid), reads=[B[2]], writes=[siga])
                    zgroup(B[3], 2304, 512)
                    mk.op('act', lambda e: e.activation(out=siga[:, 512:1024], in_=B[3][:], func=AF.Sigmoid), reads=[B[3]], writes=[siga])
                    zgroup(B[1], 2816, 512)
                    mk.op('act', lambda e: e.activation(out=sigb[:, 0:512], in_=B[1][:], func=AF.Sigmoid), reads=[B[1]], writes=[sigb])
                    zgroup(B[2], 3328, 512)
                    mk.op('act', lambda e: e.activation(out=sigb[:, 512:1024], in_=B[2][:], func=AF.Sigmoid), reads=[B[2]], writes=[sigb])

                    chk(2)
                    mk.inter = (prevE, 2)
                    mk.op('dve', lambda e: e.tensor_reduce(out=st1[:, 8:16], in_=qsq[:].rearrange("t (h d) -> t h d", h=8),
                                                           axis=AX.X, op=ALU.add), reads=[qsq], writes=[st1])
                    rsqrt(st1, st1[:, 8:16], st1[:, 8:16], 1.0 / 64)
                    mk.op('dve', lambda e: e.tensor_tensor(out=q32[:].rearrange("t (h d) -> t h d", h=8),
                                                           in0=q32[:].rearrange("t (h d) -> t h d", h=8),
                                                           in1=st1[:, 8:16].unsqueeze(2).to_broadcast([128, 8, 64]), op=ALU.mult),
                          reads=[q32, st1], writes=[q32])
                    mk.op('dve', lambda e: e.tensor_tensor(out=qn[:].rearrange("t (p w d) -> t w p d", p=4, w=2),
                                                           in0=q32[:].rearrange("t (w p d) -> t w p d", w=2, p=4),
                                                           in1=gq8[:].unsqueeze(1).unsqueeze(1).to_broadcast([128, 2, 4, 64]),
                                                           op=ALU.mult), reads=[q32, gq8], writes=[qn])
                    mk.op('dve', lambda e: e.tensor_reduce(out=st1[:, 5:7], in_=ksq[:].rearrange("t (h d) -> t h d", h=2),
                                                           axis=AX.X, op=ALU.add), reads=[ksq], writes=[st1])
                    rsqrt(st1, st1[:, 5:7], st1[:, 5:7], 1.0 / 64)
                    mk.op('dve', lambda e: e.tensor_tensor(out=k32[:].rearrange("t (h d) -> t h d", h=2),
                                                           in0=k32[:].rearrange("t (h d) -> t h d", h=2),
                                                           in1=st1[:, 5:7].unsqueeze(2).to_broadcast([128, 2, 64]), op=ALU.mult),
                          reads=[k32, st1], writes=[k32])
                    mk.op('dve', lambda e: e.tensor_tensor(out=kn32[:].rearrange("t (h d) -> t h d", h=2),
                                                           in0=k32[:].rearrange("t (h d) -> t h d", h=2),
                                                           in1=gk[:].unsqueeze(1).to_broadcast([128, 2, 64]), op=ALU.mult),
                          reads=[k32, gk], writes=[kn32])
                    mk.op('act', lambda e: e.copy(out=knb[:], in_=kn32[:]), reads=[kn32], writes=[knb])
                    mk.op('act', lambda e: e.copy(out=vaug[cur][:].rearrange("t (g e) -> t g e", g=2)[:, :, 0:64],
                                                  in_=v32[:].rearrange("t (g d) -> t g d", g=2)), reads=[v32], writes=[vaug[cur]])
                    chk(5)
                    mk.op('act', lambda e: e.activation(out=sc32[:, 0:512], in_=gv32[:], func=AF.Square, accum_out=st1[:, 3:4]),
                          reads=[gv32], writes=[sc32, st1])
                    mk.op('dve', lambda e: e.tensor_scalar(out=st1[:, 4:5], in0=st1[:, 4:5], scalar1=1.0 / 512, scalar2=None,
                                                           op0=ALU.mult), reads=[st1], writes=[st1])
                    mk.op('dve', lambda e: e.tensor_tensor(out=st1[:, 7:8], in0=st1[:, 4:5], in1=st1[:, 4:5], op=ALU.mult),
                          reads=[st1], writes=[st1])
                    mk.op('dve', lambda e: e.scalar_tensor_tensor(out=st1[:, 3:4], in0=st1[:, 3:4], scalar=1.0 / 512, in1=st1[:, 7:8],
                                                                  op0=ALU.mult, op1=ALU.subtract), reads=[st1], writes=[st1])
                    rsqrt(st1, st1[:, 3:4], st1[:, 3:4], 1.0)
                    mk.op('dve', lambda e: e.tensor_scalar(out=vn32[:], in0=gv32[:], scalar1=st1[:, 4:5], scalar2=st1[:, 3:4],
                                                           op0=ALU.subtract, op1=ALU.mult), reads=[gv32, st1], writes=[vn32])
                    mk.op('dve', lambda e: e.tensor_tensor(out=vn32[:], in0=vn32[:], in1=ln_g[:], op=ALU.mult),
                          reads=[vn32, ln_g], writes=[vn32])
                    mk.op('dve', lambda e: e.tensor_tensor(out=vn32[:], in0=vn32[:], in1=ln_b[:], op=ALU.add),
                          reads=[vn32, ln_b], writes=[vn32])
                    mk.op('act', lambda e: e.copy(out=vnb[:], in_=vn32[:]), reads=[vn32], writes=[vnb])
                    wsT = wsT_s if sample else wsT_p
                    bcol = bcol_s if sample else bcol_p

                    def fsgu(e, wsT=wsT):
                        last = None
                        for g in range(4):
                            last = e.matmul(B[3][:, g * 128:(g + 1) * 128], lhsT=wsT[:, g * 128:(g + 1) * 128],
                                            rhs=vnb[:, g * 128:(g + 1) * 128], start=True, stop=True)
                        return last
                    mk.op('pe', fsgu, reads=[wsT, vnb], writes=[B[3]], skip_self=True)
                    for g in range(4):
                        mk.op('dve', lambda e, g=g, bcol=bcol: e.scalar_tensor_tensor(
                            out=mb[:, g * 128:(g + 1) * 128], in0=B[3][:, g * 128:(g + 1) * 128], scalar=bcol[:, g:g + 1],
                            in1=u32[:, g * 128:(g + 1) * 128], op0=ALU.add, op1=ALU.mult), reads=[B[3], bcol, u32], writes=[mb])

                    chk(3)
                    def trq(e):
                        last = None
                        for p in range(4):
                            last = pe_T(B[0], p * 128, qn, qn[:, p * 128:(p + 1) * 128])(e)
                        last = pe_T(B[0], 512, knb, knb[:])(e)
                        return last
                    mk.op('pe', trq, reads=[qn, knb, ident], writes=[B[0]], skip_self=True)
                    b0v = B[0][:].bitcast(BF16)
                    mk.op('act', lambda e: e.copy(out=qT[:], in_=b0v[:, 0:512]), reads=[B[0]], writes=[qT])
                    mk.op('act', lambda e: e.copy(out=kT[cur][:], in_=b0v[:, 512:640]), reads=[B[0]], writes=[kT[cur]])
                    qT3 = qT[:].rearrange("p (q t) -> p q t", q=4)

                    chk(4)
                    pT4 = pT[:].rearrange("s (b h t) -> s b h t", b=2, h=8)
                    blks = [1] if (ti == 0 or sample) else [0, 1]
                    for blk in blks:
                        kt = kT[cur] if blk == 1 else kT[prv]
                        banks = (B[4], B[5])

                        def fsc(e, kt=kt, banks=banks):
                            last = None
                            for h in range(8):
                                w, p = h // 4, h % 4
                                bank = banks[h // 4]
                                last = e.matmul(bank[:, (h % 4) * 128:(h % 4 + 1) * 128], lhsT=kt[w * 64:(w + 1) * 64, :],
                                                rhs=qT3[w * 64:(w + 1) * 64, p, :], start=True, stop=True)
                            return last
                        mk.op('pe', fsc, reads=[kt, qT], writes=list(banks), skip_self=True)
                        bsrc = bias_sn if sample else bias_p
                        boff = 0 if sample else blk * 1024
                        for hb_ in range(2):
                            mk.op('dve', lambda e, hb_=hb_, banks=banks, bsrc=bsrc, boff=boff: e.tensor_tensor(
                                out=sc32[:, hb_ * 512:(hb_ + 1) * 512], in0=banks[hb_][:],
                                in1=bsrc[:, boff + hb_ * 512: boff + (hb_ + 1) * 512], op=ALU.add),
                                reads=[banks[hb_], bsrc], writes=[sc32])
                        mk.op('act', lambda e, blk=blk: e.activation(out=pT[:, blk * 1024:(blk + 1) * 1024], in_=sc32[:], func=AF.Exp),
                              reads=[sc32], writes=[pT])

                    def pv_out(h):
                        return B[1 + h // 4][:, (h % 4) * 128:(h % 4) * 128 + 65]

                    def fpv(e):
                        last = None
                        for h in range(8):
                            w = h // 4
                            for bi, blk in enumerate(blks):
                                va = vaug[cur] if blk == 1 else vaug[prv]
                                st_ = (bi == 0) if not sample else (h % 4 == 0)
                                last = e.matmul(pv_out(h), lhsT=pT4[:, blk, h, :], rhs=va[:, w * 65:(w + 1) * 65],
                                                start=st_, stop=(bi == len(blks) - 1 and not sample), skip_group_check=sample)
                        return last
                    mk.op('pe', fpv, reads=[pT, vaug[0], vaug[1]], writes=[B[1], B[2]], skip_self=True)

                    if sample:
                        chk(4.1)
                        mk.inter = None
                        mk.play(prevE, 10 ** 9)
                        mk.dma('sp', stage[:].rearrange("s (b c) -> s b c", b=16), ck[:, :, :], writes=[stage], semt=stage)
                        mk.op('act', lambda e: e.copy(out=ckb[:], in_=stage[:]), reads=[stage], writes=[ckb])
                        for half in range(2):
                            bank = B[4 + half * 2], B[5 + half * 2]

                            def ftr(e, half=half, bank=bank):
                                last = None
                                for bb in range(8):
                                    b_ = half * 8 + bb
                                    last = pe_T(bank[bb // 4], (bb % 4) * 128, ckb, ckb[:, b_ * 128:(b_ + 1) * 128])(e)
                                return last
                            mk.op('pe', ftr, reads=[ckb, ident], writes=list(bank), skip_self=True)
                            for q_ in range(2):
                                mk.op('act', lambda e, half=half, q_=q_, bank=bank: e.copy(
                                    out=kTp[:, (half * 8 + q_ * 4) * 128:(half * 8 + q_ * 4 + 4) * 128],
                                    in_=bank[q_][:].bitcast(BF16)[:, 0:512]), reads=[bank[q_]], writes=[kTp])
                        chk(4.2)
                        if not cfg.get('skip_vdma'):
                            mk.dma('sp', stage[:].rearrange("s (b c) -> s b c", b=16), cv[:, :, :], writes=[stage], semt=stage)
                        if cfg.get('skip_vcopy'):
                            raise _StopTile()
                        for g_ in range(2):
                            mk.op('dve', lambda e, g_=g_: e.tensor_copy(
                                out=vpaug[:].rearrange("s (b g e) -> s b g e", b=16, g=2)[:, :, g_, 0:64],
                                in_=stage[:].rearrange("s (b g d) -> s b g d", b=16, g=2)[:, :, g_, :]),
                                reads=[stage], writes=[vpaug])
                        chk(4.3)
                        kTp3 = kTp[:].rearrange("p (b s) -> p b s", b=16)
                        mk.op('act', lambda e: e.copy(out=qTs[:].rearrange("p (b q t) -> p b q t", b=16, q=4),
                                                      in_=qT[:].rearrange("p (q b t) -> p b q t", q=4, b=16)),
                              reads=[qT], writes=[qTs])
                        chk(4.31)

                        def fps(e):
                            last = None
                            for w in range(2):
                                bank = B[4 + w]
                                for b_ in range(16):
                                    c0 = b_ * 32
                                    last = e.matmul(bank[:, c0:c0 + 32], lhsT=kTp3[w * 64:(w + 1) * 64, b_, :],
                                                    rhs=qTs[w * 64:(w + 1) * 64, b_ * 32:(b_ + 1) * 32], start=True, stop=True)
                            return last
                        mk.op('pe', fps, reads=[kTp, qTs], writes=[B[4], B[5]], skip_self=True)
                        chk(4.32)
                        for half in range(2):
                            mk.op('dve', lambda e, half=half: e.tensor_tensor(
                                out=sc32[:, half * 512:(half + 1) * 512].rearrange("s (b c) -> s b c", b=16),
                                in0=B[4 + half][:].rearrange("s (b c) -> s b c", b=16),
                                in1=bias_sp[:, half * 32:(half + 1) * 32].unsqueeze(1).to_broadcast([128, 16, 32]), op=ALU.add),
                                reads=[B[4 + half], bias_sp], writes=[sc32])
                        chk(4.4)
                        sc5 = sc32[:].rearrange("s (w b p t) -> s b w p t", w=2, b=16, p=4)
                        vp4 = vpaug[:].rearrange("s (b g e) -> s b g e", b=16, g=2)
                        for b_ in range(16):
                            E = Eb[b_ % 2]
                            E3 = E[:].rearrange("s (h t) -> s h t", h=8)
                            mk.op('pool', lambda e, E=E: e.memset(E[:], 0.0), writes=[E])
                            mk.op('act', lambda e, E=E, b_=b_: e.activation(
                                out=E[:].rearrange("s (w p t) -> s w p t", w=2, p=4)[:, :, :, b_ * 8:(b_ + 1) * 8],
                                in_=sc5[:, b_, :, :, :], func=AF.Exp), reads=[sc32], writes=[E])

                            def fpp(e, E3=E3, b_=b_):
                                last = None
                                for h in range(8):
                                    last = e.matmul(pv_out(h), lhsT=E3[:, h, :], rhs=vp4[:, b_, h // 4, :],
                                                    start=False, stop=(b_ == 15), skip_group_check=True)
                                return last
                            mk.op('pe', fpp, reads=[E, vpaug], writes=[B[1], B[2]], skip_self=True)

                    chk(4.5)
                    for half in range(2):
                        bank = B[1 + half]
                        b3 = bank[:].rearrange("t (h e) -> t h e", h=4)
                        mk.op('dve', lambda e, b3=b3, half=half: e.tensor_tensor(
                            out=st1[:, 8 + half * 4: 12 + half * 4].unsqueeze(2), in0=b3[:, :, 64:65],
                            in1=sinkexp[:, half * 4:(half + 1) * 4].unsqueeze(2), op=ALU.add),
                            reads=[bank, sinkexp], writes=[st1])
                    mk.op('dve', lambda e: e.reciprocal(out=st1[:, 8:16], in_=st1[:, 8:16]), reads=[st1], writes=[st1])
                    for half in range(2):
                        bank = B[1 + half]
                        b3 = bank[:].rearrange("t (h e) -> t h e", h=4)
                        mk.op('dve', lambda e, b3=b3, half=half: e.tensor_tensor(
                            out=att[:, half * 256:(half + 1) * 256].rearrange("t (h d) -> t h d", h=4), in0=b3[:, :, 0:64],
                            in1=st1[:, 8 + half * 4: 12 + half * 4].unsqueeze(2).to_broadcast([128, 4, 64]), op=ALU.mult),
                            reads=[bank, st1], writes=[att])

                    if tpos + 1 < len(tiles):
                        emit_F1(tiles[tpos + 1])
                    chk(6)
                    if ti == NPT - 1 or sample:
                        ok_, ov_, os_ = (o_ks, o_vs, o_ss) if sample else (o_kp, o_vp, o_sp)
                        mk.dma('sp', ok_[:, :], kn32[:], reads=[kn32], semt=kn32)
                        mk.dma('sp', ov_[:, :], v32[:], reads=[v32], semt=v32)
                        mk.dma('sp', os_[:, :], vn32[:], reads=[vn32], semt=vn32)
                        out_tiles.extend([kn32, v32, vn32])

                    chk(7)
                    mk.inter = None
                    mk.play(prevE, 10 ** 9)
                    if dbg_am is not None:
                        mk.op('dve', lambda e: e.tensor_copy(out=t1[:, 0:512], in_=att[:]), reads=[att], writes=[t1])
                        mk.op('dve', lambda e: e.tensor_copy(out=t1[:, 512:1024], in_=mb[:]), reads=[mb], writes=[t1])
                        mk.dma('sp', dbg_am[ti * 128:(ti + 1) * 128, :], t1[:], reads=[t1], semt=t1)
                    mk.rec = []
                    mk.rate = 1.0
                    def tram(e):
                        last = None
                        for c in range(4):
                            last = pe_T(B[0], c * 128, att, att[:, c * 128:(c + 1) * 128])(e)
                        for c in range(4):
                            last = pe_T(B[0], 512 + c * 128, mb, mb[:, c * 128:(c + 1) * 128])(e)
                        return last
                    mk.op('pe', tram, reads=[att, mb, ident], writes=[B[0]], skip_self=True)
                    mk.op('act', lambda e: e.copy(out=amT[:], in_=B[0][:].bitcast(BF16)), reads=[B[0]], writes=[amT])
                    amT3 = amT[:].rearrange("p (c t) -> p c t", c=8)

                    def fmerge_ab(e, w_v, off):
                        last = None
                        for n in range(2):
                            for c in range(4):
                                last = e.matmul(B[6 + n][:], lhsT=amT3[:, off + c, :], rhs=w_v[:, c, n * 512:(n + 1) * 512],
                                                start=(c == 0), stop=(c == 3))
                        return last
                    mk.op('pe', lambda e: fmerge_ab(e, wa_v, 0), reads=[amT, wab], writes=[B[6], B[7]], skip_self=True)
                    for n in range(2):
                        sl = slice(n * 512, (n + 1) * 512)
                        mk.op('dve', lambda e, n=n, sl=sl: e.tensor_tensor(out=t1[:, sl], in0=B[6 + n][:], in1=siga[:, sl], op=ALU.mult),
                              reads=[B[6 + n], siga], writes=[t1])
                    mk.op('pe', lambda e: fmerge_ab(e, wb_v, 4), reads=[amT, wab], writes=[B[6], B[7]], skip_self=True)
                    for n in range(2):
                        sl = slice(n * 512, (n + 1) * 512)
                        mk.op('dve', lambda e, n=n, sl=sl: e.tensor_tensor(out=stage[:, sl], in0=B[6 + n][:], in1=sigb[:, sl], op=ALU.mult),
                              reads=[B[6 + n], sigb], writes=[stage])
                    mk.op('dve', lambda e: e.tensor_tensor(out=hb[:], in0=stage[:, 0:1024], in1=t1[:], op=ALU.add), reads=[stage, t1], writes=[hb])
                    mk.op('pe', lambda e: tr8(e, src=hb, bank=B[0]), reads=[hb, ident], writes=[B[0]], skip_self=True)
                    mk.op('act', lambda e: e.copy(out=hT[:], in_=B[0][:].bitcast(BF16)), reads=[B[0]], writes=[hT])
                    hT3 = hT[:].rearrange("p (c t) -> p c t", c=8)

                    def fwo(e):
                        last = None
                        for n in range(2):
                            for c in range(8):
                                last = e.matmul(B[6 + n][:], lhsT=hT3[:, c, :], rhs=wo_v[:, c, n * 512:(n + 1) * 512],
                                                start=(c == 0), stop=(c == 7))
                        return last
                    mk.op('pe', fwo, reads=[hT, wot], writes=[B[6], B[7]], skip_self=True)
                    x1 = x1t[ti % 2]
                    for n in range(2):
                        sl = slice(n * 512, (n + 1) * 512)
                        mk.op('dve', lambda e, n=n, sl=sl, x1=x1: e.tensor_tensor(out=x1[:, sl], in0=B[6 + n][:], in1=x[:, sl], op=ALU.add),
                              reads=[B[6 + n], x], writes=[x1])
                    mk.dma('sp', x1s[ti * 128:(ti + 1) * 128, :], x1[:], reads=[x1], writes=[x1_tok[ti]], semt=x1)
                    if dbg_x1 is not None:
                        mk.dma('sp', dbg_x1[ti * 128:(ti + 1) * 128, :], x1[:], reads=[x1], semt=x1)
                    pend = mk.rec
                    mk.rec = None
                    return pend
                except _StopTile:
                    mk.rec = None
                    mk.inter = None
                    return []

            prevE = []
            for tpos, ti in enumerate(tiles):
                prevE = tile_body(tpos, ti, prevE)
            mk.play(prevE, 10 ** 9)

            mk.barrier()
        with ExitStack() as es2:
            mk.es = es2
            warena2 = mk.sb("warena2", [128, 8 * 2048 + 2048 + 8 * D + 2 * D], BF16)
            o = 0
            wq_v = warena2[:, o:o + 8 * 2048].rearrange("p (c n) -> p c n", c=8); o += 8 * 2048
            keys_v = warena2[:, o:o + 2048].rearrange("p (g n) -> p g n", g=16); o += 2048
            wpg_v = warena2[:, o:o + 8 * D].rearrange("p (c n) -> p c n", c=8); o += 8 * D
            wpl_v = warena2[:, o:o + 2 * D].rearrange("p (c n) -> p c n", c=2); o += 2 * D
            wsem2 = T("wsem2")
            w_q_r = w_q.rearrange("(c p) n -> p c n", p=128)
            w_pg_r = w_pg.rearrange("(c p) n -> p c n", p=128)
            w_pl_r = w_ple.rearrange("(c p) n -> p c n", p=128)
            for c in range(8):
                mk.dma('pool', wq_v[:, c, :], w_q_r[:, c, :], semt=wsem2)
                mk.dma('pool', wpg_v[:, c, :], w_pg_r[:, c, :], semt=wsem2)
            for c in range(2):
                mk.dma('pool', wpl_v[:, c, :], w_pl_r[:, c, :], semt=wsem2)
            mk.dma('pool', keys_v, keysT.rearrange("p (g n) -> p g n", g=16), semt=wsem2)
            warena2.st.w = (wsem2.st.dsem, mk.cnt[wsem2.st.dsem])
            cgrp2 = T("cgrp2")

            def cload2(name, src, shape, dt=F32, q='sp'):
                t = mk.sb(name, shape, dt)
                mk.dma(q, t[:], src[:, :], writes=[t], semt=cgrp2)
                return t
            g_ffn = cload2("g_ffn", sin["g_ffn"], [128, 1024])
            g_ple = cload2("g_ple", sin["g_ple"], [128, 1024])
            iota16 = cload2("iota16", cin["iota16"], [128, 16])
            identf = cload2("identf", cin["ident"], [128, 128])
            mk.settle([g_ffn, g_ple, iota16, identf])

            xt2 = [mk.sb("x2t%d" % i, [128, D], F32) for i in range(2)]
            pt2 = [mk.sb("p2t%d" % i, [128, 256], F32) for i in range(2)]
            xn2_ = [mk.sb("xn2_%d" % i, [128, D], F32) for i in range(2)]
            xn2b = mk.sb("xn2b", [128, D], BF16)
            xn2T = mk.sb("xn2T", [128, D], BF16)
            st2 = mk.sb("st2", [128, 16], F32)
            st2A = mk.sb("st2A", [128, 16], F32)
            qpT = mk.sb("qpT", [128, 2048], BF16)
            scp = mk.sb("scp", [128, 2048], F32)
            work = mk.sb("work", [128, 2048], F32)
            stop_ = mk.sb("stop", [128, 256], F32)
            itopu = mk.sb("itopu", [128, 256], U32)
            itopf = mk.sb("itopf", [128, 256], F32)
            cand = mk.sb("cand", [128, 2048], F32)
            best = mk.sb("best", [128, 128], F32)
            posu = mk.sb("posu", [128, 128], U32)
            posi = mk.sb("posi", [128, 256], I32)
            posf = mk.sb("posf", [128, 256], F32)
            oh = mk.sb("oh", [128, 2048], F32)
            oh2 = mk.sb("oh2", [128, 2048], F32)
            sel = mk.sb("sel", [128, 256], F32)
            junkA = T("junkA", oh[:, 0:1024], st=oh.st)
            idxf = mk.sb("idxf", [128, 128], F32)
            idxi_ = [mk.sb("idxi%d" % i, [128, 128], I32) for i in range(2)]
            gate_ = [mk.sb("gate%d" % i, [128, 128], F32) for i in range(2)]
            dots = mk.sb("dots", [128, 128], F32)
            actv = mk.sb("actv", [128, 128], F32)
            wgt = mk.sb("wgt", [128, 128], F32)
            NGB = 12
            gbuf = [mk.sb("gbuf%d" % i, [128, 2 * D], BF16) for i in range(NGB)]
            junk = mk.sb("junk", [128, D], F32)
            pleg = mk.sb("pleg", [128, D], F32)
            NSM = 12
            dot_r = [mk.sb("dot_r%d" % i, [128, 1], F32) for i in range(NSM)]
            wv_r = [mk.sb("wv_r%d" % i, [128, 2], F32) for i in range(NSM)]
            dg_r = [mk.sb("dg_r%d" % i, [128, 128], BF16) for i in range(8)]
            xn3 = mk.sb("xn3", [128, D], BF16)
            xn3T = mk.sb("xn3T", [128, D], BF16)
            pb16 = mk.sb("pb16", [128, 256], BF16)
            ppT = mk.sb("ppT", [128, 256], BF16)
            print("phase2 sbuf remaining", nc.sbuf_bytes_remaining, flush=True)
            B = pbanks

            def load2(ti):
                mk.dma('sp', xt2[ti % 2][:], x1s[ti * 128:(ti + 1) * 128, :], reads=[x1_tok[ti]], writes=[xt2[ti % 2]],
                       semt=xt2[ti % 2])
                mk.dma('sp', pt2[ti % 2][:], pin[ti * 128:(ti + 1) * 128, :], writes=[pt2[ti % 2]], semt=pt2[ti % 2])

            def rms(xsrc, gt, out32, outb, st2, junk):
                mk.op('act', lambda e: e.activation(out=junk[:], in_=xsrc[:], func=AF.Square, accum_out=st2[:, 0:1]),
                      reads=[xsrc], writes=[junk, st2])
                rsqrt(st2, st2[:, 2:3], st2[:, 0:1], 1.0 / D)
                if out32 is not None:
                    mk.op('dve', lambda e: e.scalar_tensor_tensor(out=out32[:], in0=xsrc[:], scalar=st2[:, 2:3], in1=gt[:],
                                                                  op0=ALU.mult, op1=ALU.mult), reads=[xsrc, st2, gt], writes=[out32])
                    mk.op('act', lambda e: e.copy(out=outb[:], in_=out32[:]), reads=[out32], writes=[outb])
                else:
                    mk.op('dve', lambda e: e.scalar_tensor_tensor(out=outb[:], in0=xsrc[:], scalar=st2[:, 2:3], in1=gt[:],
                                                                  op0=ALU.mult, op1=ALU.mult), reads=[xsrc, st2, gt], writes=[outb])

            def tr8b(src, dst, n=8):
                def f(e):
                    last = None
                    for c in range(n):
                        last = pe_T(B[0], c * 128, src, src[:, c * 128:(c + 1) * 128])(e)
                    return last
                mk.op('pe', f, reads=[src, ident], writes=[B[0]], skip_self=True)
                mk.op('act', lambda e: e.copy(out=dst[:, 0:n * 128], in_=B[0][:].bitcast(BF16)[:, 0:n * 128]),
                      reads=[B[0]], writes=[dst])

            tiles2 = tiles if cfg.get('phase2', True) else []

            def stageA(ti):
                par = ti % 2
                x = xt2[par]
                xn2 = xn2_[par]
                idxi = idxi_[par]
                gate = gate_[par]
                mk.rate = 1.0
                load2(ti)
                rms(x, g_ffn, xn2, xn2b, st2A, junkA)
                tr8b(xn2b, xn2T)
                xT3 = xn2T[:].rearrange("p (c t) -> p c t", c=8)
                for bq in range(4):
                    bank = B[1 + bq]

                    def fq(e, bq=bq, bank=bank):
                        last = None
                        for gi in range(4):
                            g_ = bq * 4 + gi
                            for c in range(8):
                                last = e.matmul(bank[:, gi * 128:(gi + 1) * 128], lhsT=wq_v[:, c, g_ * 128:(g_ + 1) * 128],
                                                rhs=xT3[:, c, :], start=(c == 0), stop=(c == 7))
                        return last
                    mk.op('pe', fq, reads=[xn2T, warena2], writes=[bank], skip_self=True)
                    mk.op('act', lambda e, bq=bq, bank=bank: e.copy(out=qpT[:, bq * 512:(bq + 1) * 512], in_=bank[:]),
                          reads=[bank], writes=[qpT])
                for bq in range(4):
                    bank = B[1 + bq]

                    def fs(e, bq=bq, bank=bank):
                        last = None
                        for gi in range(4):
                            g_ = bq * 4 + gi
                            last = e.matmul(bank[:, gi * 128:(gi + 1) * 128], lhsT=qpT[:, g_ * 128:(g_ + 1) * 128],
                                            rhs=keys_v[:, g_, :], start=True, stop=True)
                        return last
                    mk.op('pe', fs, reads=[qpT, warena2], writes=[bank], skip_self=True)
                    mk.op('act', lambda e, bq=bq, bank=bank: e.copy(out=scp[:, bq * 512:(bq + 1) * 512], in_=bank[:]),
                          reads=[bank], writes=[scp])
                mk.rate = 4.0
                def sg_(g_):
                    return scp[:, g_ * 128:(g_ + 1) * 128]

                def wk_(g_):
                    return work[:, g_ * 128:(g_ + 1) * 128]
                for g_ in range(16):
                    mk.op('dve', lambda e, g_=g_: e.max(out=stop_[:, g_ * 16:g_ * 16 + 8], in_=sg_(g_)),
                          reads=[scp], writes=[stop_], skip_self=True)
                for g_ in range(16):
                    mk.op('dve', lambda e, g_=g_: e.match_replace(out=wk_(g_), in_to_replace=stop_[:, g_ * 16:g_ * 16 + 8],
                                                                  in_values=sg_(g_), imm_value=-1e30),
                          reads=[scp, stop_], writes=[work], skip_self=True)
                for g_ in range(16):
                    mk.op('dve', lambda e, g_=g_: e.max(out=stop_[:, g_ * 16 + 8:g_ * 16 + 16], in_=wk_(g_)),
                          reads=[work], writes=[stop_], skip_self=True)
                for g_ in range(16):
                    mk.op('dve', lambda e, g_=g_: e.max_index(out=itopu[:, g_ * 16:g_ * 16 + 8], in_max=stop_[:, g_ * 16:g_ * 16 + 8],
                                                              in_values=sg_(g_)), reads=[scp, stop_], writes=[itopu], skip_self=True)
                for g_ in range(16):
                    mk.op('dve', lambda e, g_=g_: e.max_index(out=itopu[:, g_ * 16 + 8:g_ * 16 + 16],
                                                              in_max=stop_[:, g_ * 16 + 8:g_ * 16 + 16], in_values=sg_(g_)),
                          reads=[scp, stop_], writes=[itopu], skip_self=True)
                mk.op('dve', lambda e: e.tensor_copy(out=itopf[:], in_=itopu[:]), reads=[itopu], writes=[itopf])
                st4 = stop_[:].rearrange("t (h c k) -> t h c k", h=8, c=2)
                mk.op('dve', lambda e: e.tensor_tensor(out=cand[:].rearrange("t (h i j) -> t h i j", h=8, i=16),
                                                       in0=st4[:, :, 0, :].unsqueeze(3).to_broadcast([128, 8, 16, 16]),
                                                       in1=st4[:, :, 1, :].unsqueeze(2).to_broadcast([128, 8, 16, 16]), op=ALU.add),
                      reads=[stop_], writes=[cand])
                def cg_(h):
                    return cand[:, h * 256:(h + 1) * 256]

                def wk2_(h):
                    return work[:, h * 256:(h + 1) * 256]
                for h in range(8):
                    mk.op('dve', lambda e, h=h: e.max(out=best[:, h * 16:h * 16 + 8], in_=cg_(h)),
                          reads=[cand], writes=[best], skip_self=(h > 0))
                for h in range(8):
                    mk.op('dve', lambda e, h=h: e.match_replace(out=wk2_(h), in_to_replace=best[:, h * 16:h * 16 + 8], in_values=cg_(h),
                                                                imm_value=-1e30), reads=[cand, best], writes=[work], skip_self=True)
                for h in range(8):
                    mk.op('dve', lambda e, h=h: e.max(out=best[:, h * 16 + 8:h * 16 + 16], in_=wk2_(h)),
                          reads=[work], writes=[best], skip_self=True)
                for h in range(8):
                    mk.op('dve', lambda e, h=h: e.max_index(out=posu[:, h * 16:h * 16 + 8], in_max=best[:, h * 16:h * 16 + 8],
                                                            in_values=cg_(h)), reads=[cand, best], writes=[posu], skip_self=True)
                for h in range(8):
                    mk.op('dve', lambda e, h=h: e.max_index(out=posu[:, h * 16 + 8:h * 16 + 16], in_max=best[:, h * 16 + 8:h * 16 + 16],
                                                            in_values=cg_(h)), reads=[cand, best], writes=[posu], skip_self=True)
                mk.op('dve', lambda e: e.tensor_single_scalar(out=posi[:, 0:128], in_=posu[:].bitcast(I32), scalar=4,
                                                              op=ALU.logical_shift_right), reads=[posu], writes=[posi])
                mk.op('dve', lambda e: e.tensor_single_scalar(out=posi[:, 128:256], in_=posu[:].bitcast(I32), scalar=15,
                                                              op=ALU.bitwise_and), reads=[posu], writes=[posi])
                mk.op('dve', lambda e: e.tensor_copy(out=posf[:], in_=posi[:]), reads=[posi], writes=[posf])
                it4 = itopf[:].rearrange("t (h c k) -> t h c k", h=8, c=2)
                ohs = (oh, oh2)
                oh4s = [o_[:].rearrange("t (h k i) -> t h k i", h=8, k=16) for o_ in ohs]
                for c in range(2):
                    pf = posf[:, c * 128:(c + 1) * 128].rearrange("t (h k) -> t h k", h=8)
                    mk.op('dve', lambda e, pf=pf, c=c: e.tensor_tensor(
                        out=oh4s[c], in0=iota16[:].unsqueeze(1).unsqueeze(1).to_broadcast([128, 8, 16, 16]),
                        in1=pf.unsqueeze(3).to_broadcast([128, 8, 16, 16]), op=ALU.is_equal), reads=[iota16, posf], writes=[ohs[c]])
                for c in range(2):
                    mk.op('dve', lambda e, c=c: e.tensor_tensor(
                        out=oh4s[c], in0=oh4s[c], in1=it4[:, :, c, :].unsqueeze(2).to_broadcast([128, 8, 16, 16]), op=ALU.mult),
                        reads=[ohs[c], itopf], writes=[ohs[c]], skip_self=True)
                for c in range(2):
                    mk.op('dve', lambda e, c=c: e.tensor_reduce(out=sel[:, c * 128:(c + 1) * 128],
                                                                in_=ohs[c][:].rearrange("t (s i) -> t s i", i=16), axis=AX.X, op=ALU.add),
                          reads=[ohs[c]], writes=[sel], skip_self=True)
                mk.op('dve', lambda e: e.scalar_tensor_tensor(out=idxf[:], in0=sel[:, 0:128], scalar=128.0, in1=sel[:, 128:256],
                                                              op0=ALU.mult, op1=ALU.add), reads=[sel], writes=[idxf])
                mk.op('dve', lambda e: e.tensor_copy(out=idxi[:], in_=idxf[:]), reads=[idxf], writes=[idxi])
                b3 = best[:].rearrange("t (h k) -> t h k", h=8)
                mk.op('dve', lambda e: e.tensor_tensor(out=gate[:].rearrange("t (h k) -> t h k", h=8), in0=b3,
                                                       in1=b3[:, :, 0:1].to_broadcast([128, 8, 16]), op=ALU.subtract),
                      reads=[best], writes=[gate])
                mk.op('act', lambda e: e.activation(out=gate[:], in_=gate[:], func=AF.Exp), reads=[gate], writes=[gate])
                mk.op('dve', lambda e: e.tensor_reduce(out=st2A[:, 8:16], in_=gate[:].rearrange("t (h k) -> t h k", h=8),
                                                       axis=AX.X, op=ALU.add), reads=[gate], writes=[st2A])
                mk.op('dve', lambda e: e.reciprocal(out=st2A[:, 8:16], in_=st2A[:, 8:16]), reads=[st2A], writes=[st2A])
                mk.op('dve', lambda e: e.tensor_tensor(out=gate[:].rearrange("t (h k) -> t h k", h=8),
                                                       in0=gate[:].rearrange("t (h k) -> t h k", h=8),
                                                       in1=st2A[:, 8:16].unsqueeze(2).to_broadcast([128, 8, 16]), op=ALU.mult),
                      reads=[gate, st2A], writes=[gate])

            def stageB(ti, nxt_q):
                par = ti % 2
                x = xt2[par]
                pt = pt2[par]
                xn2 = xn2_[par]
                idxi = idxi_[par]
                gate = gate_[par]
                mk._waits('pool', [comb_t], [])
                for s_ in range(128):
                    gb = gbuf[s_ % NGB]
                    dt_ = dot_r[s_ % NSM]
                    wv = wv_r[s_ % NSM]
                    dg = dg_r[s_ % 8]
                    mk.gather(gb, gb[:], comb[:, :], idxi, idxi[:, s_:s_ + 1])
                    mk.op('dve', lambda e, gb=gb, dt_=dt_: e.scalar_tensor_tensor(
                        out=junk[:], in0=gb[:, 0:D], scalar=1.0, in1=xn2[:], op0=ALU.mult, op1=ALU.mult,
                        accum_out=dt_[:, 0:1]), reads=[gb, xn2], writes=[junk, dt_], skip_self=True)
                    mk.op('act', lambda e, dt_=dt_, wv=wv: e.activation(out=wv[:, 0:1], in_=dt_[:, 0:1], func=GELU),
                          reads=[dt_], writes=[wv])
                    mk.op('act', lambda e, wv=wv, s_=s_: e.mul(out=wv[:, 1:2], in_=wv[:, 0:1], mul=gate[:, s_:s_ + 1]),
                          reads=[wv, gate], writes=[wv])
                    mk.op('act', lambda e, wv=wv, dg=dg: e.activation(out=dg[:], in_=identf[:], func=AF.Copy, scale=wv[:, 1:2]),
                          reads=[wv, identf], writes=[dg])

                    def fv(e, s_=s_, gb=gb, dg=dg):
                        last = None
                        for n in range(2):
                            last = e.matmul(B[5 + n][:], lhsT=dg[:], rhs=gb[:, D + n * 512:D + (n + 1) * 512],
                                            start=(s_ == 0), stop=(s_ == 127))
                        return last
                    mk.op('pe', fv, reads=[gb, dg], writes=[B[5], B[6]], skip_self=True)
                    mk.play(nxt_q, 1.0, lag=5)
                if len(nxt_q) and ti == 1:
                    print('replay leftover at drain:', len(nxt_q), flush=True)
                mk.play(nxt_q, 10 ** 9)
                for n in range(2):
                    sl = slice(n * 512, (n + 1) * 512)
                    mk.op('dve', lambda e, n=n, sl=sl: e.tensor_tensor(out=x[:, sl], in0=B[5 + n][:], in1=x[:, sl], op=ALU.add),
                          reads=[B[5 + n], x], writes=[x])
                mk.rec = []
                mk.rate = 1.0
                rms(x, g_ple, None, xn3, st2, pleg)
                tr8b(xn3, xn3T)
                x3T = xn3T[:].rearrange("p (c t) -> p c t", c=8)

                def fpg(e):
                    last = None
                    for n in range(2):
                        for c in range(8):
                            last = e.matmul(B[1 + n][:], lhsT=x3T[:, c, :], rhs=wpg_v[:, c, n * 512:(n + 1) * 512],
                                            start=(c == 0), stop=(c == 7))
                    return last
                mk.op('pe', fpg, reads=[xn3T, warena2], writes=[B[1], B[2]], skip_self=True)
                for n in range(2):
                    mk.op('act', lambda e, n=n: e.activation(out=pleg[:, n * 512:(n + 1) * 512], in_=B[1 + n][:], func=AF.Sigmoid),
                          reads=[B[1 + n]], writes=[pleg])
                mk.op('act', lambda e: e.copy(out=pb16[:], in_=pt[:]), reads=[pt], writes=[pb16])
                tr8b(pb16, ppT, n=2)
                pT3 = ppT[:].rearrange("p (c t) -> p c t", c=2)

                def fpl(e):
                    last = None
                    for n in range(2):
                        for c in range(2):
                            last = e.matmul(B[3 + n][:], lhsT=pT3[:, c, :], rhs=wpl_v[:, c, n * 512:(n + 1) * 512],
                                            start=(c == 0), stop=(c == 1))
                    return last
                mk.op('pe', fpl, reads=[ppT, warena2], writes=[B[3], B[4]], skip_self=True)
                for n in range(2):
                    sl = slice(n * 512, (n + 1) * 512)
                    mk.op('dve', lambda e, n=n, sl=sl: e.tensor_tensor(out=pleg[:, sl], in0=B[3 + n][:], in1=pleg[:, sl], op=ALU.mult),
                          reads=[B[3 + n], pleg], writes=[pleg])
                mk.op('dve', lambda e: e.tensor_tensor(out=x[:], in0=x[:], in1=pleg[:], op=ALU.add), reads=[x, pleg], writes=[x])
                mk.dma('sp', y[ti * 128:(ti + 1) * 128, :], x[:], reads=[x], semt=x)
                tail = mk.rec
                mk.rec = None
                return tail

            if tiles2:
                mk.rec = []
                stageA(tiles2[0])
                q0 = mk.rec
                mk.rec = None
                mk.play(q0, 10 ** 9)
            pending_tail = []
            for tpos, ti in enumerate(tiles2):
                nxt_q = pending_tail
                if tpos + 1 < len(tiles2):
                    mk.rec = []
                    stageA(tiles2[tpos + 1])
                    nxt_q = nxt_q + mk.rec
                    mk.rec = None
                if tpos == 1:
                    print('replay queue: ops', len(nxt_q), 'slot budget needed', sum(1.0 / r[3] for r in nxt_q), flush=True)
                pending_tail = stageB(ti, nxt_q)
            mk.play(pending_tail, 10 ** 9)
            out_tiles_all = out_tiles + xt2
            mk._waits('sp', [], xt2)
            mk.barrier()
        print("instructions emitted:", mk.n_ins, flush=True)
    return nc


_NC_CACHE = {}


def _get_nc():
    if "nc" not in _NC_CACHE:
        _NC_CACHE["nc"] = build()
    return _NC_CACHE["nc"]


def kernel(x_prompt, x_sample, cache_k, cache_v, p_prompt, p_sample,
           attn_norm_g, w_in, q_norm_g, k_norm_g, attn_sinks,
           sgu_norm_g, sgu_norm_b, sgu_w, sgu_b,
           w_branch_a, w_branch_b, w_out,
           ffn_norm_g, peer_w_q, peer_sub_keys, peer_u, peer_v,
           ple_norm_g, w_ple, w_ple_gate):
    f = lambda a: np.ascontiguousarray(np.asarray(a, dtype=np.float32))
    x_prompt, x_sample, cache_k, cache_v = f(x_prompt), f(x_sample), f(cache_k), f(cache_v)
    p_prompt, p_sample = f(p_prompt), f(p_sample)
    nc = _get_nc()
    consts = _consts()
    rep = lambda v, n=128: np.ascontiguousarray(np.broadcast_to(f(v).reshape(1, -1), (n, f(v).size)))
    sw = f(sgu_w)[0]
    sb = f(sgu_b)[0]
    small = {
        "g_attn": rep(attn_norm_g[0]), "g_ffn": rep(ffn_norm_g[0]), "g_ple": rep(ple_norm_g[0]),
        "gq": rep(q_norm_g[0]), "gk": rep(k_norm_g[0]), "sinks": rep(attn_sinks[0]),
        "ln_g": rep(sgu_norm_g[0]), "ln_b": rep(sgu_norm_b[0]),
        "bcol_p": np.ascontiguousarray(sb.T),
        "bcol_s": np.ascontiguousarray(np.tile(sb[:, :8].T, (16, 1))),
        "wsT_p": np.ascontiguousarray(sw.transpose(2, 0, 1)).reshape(128, 512),
        "wsT_s": np.ascontiguousarray(np.tile(sw[:, :8, :8].transpose(2, 0, 1), (16, 1, 16))).reshape(128, 512),
    }
    shared = {
        "w_in": f(w_in)[0], "w_a": f(w_branch_a)[0], "w_b": f(w_branch_b)[0], "w_out": f(w_out)[0],
        "w_q": f(peer_w_q)[0],
        "keysT": np.ascontiguousarray(f(peer_sub_keys)[0].reshape(16, 128, 128).transpose(2, 0, 1)).reshape(128, 2048),
        "peer_u": f(peer_u)[0], "peer_v": f(peer_v)[0], "w_ple": f(w_ple)[0], "w_pg": f(w_ple_gate)[0],
    }
    for k, v in consts.items():
        shared["c_" + k] = v
    for k, v in small.items():
        shared["s_" + k] = v
    in_maps = []
    for c in range(NCORES):
        m = dict(shared)
        xs = x_sample[c * 16:(c + 1) * 16].reshape(128, D)
        m["xin"] = np.ascontiguousarray(np.concatenate([x_prompt[c], xs], axis=0))
        ps = p_sample[0, c * 16:(c + 1) * 16].reshape(128, 256)
        m["pin"] = np.ascontiguousarray(np.concatenate([p_prompt[0, c], ps], axis=0))
        m["ck"] = np.ascontiguousarray(cache_k[0, c * 16:(c + 1) * 16].reshape(16, 128, 128).transpose(1, 0, 2))
        m["cv"] = np.ascontiguousarray(cache_v[0, c * 16:(c + 1) * 16].reshape(16, 128, 128).transpose(1, 0, 2))
        in_maps.append(m)
    res = run_bass_kernel_spmd(nc, in_maps, core_ids=list(range(NCORES)))
    R = res.results
    y_p = np.stack([R[c]["y"][:SEQ] for c in range(NCORES)])
    y_s = np.concatenate([R[c]["y"][SEQ:].reshape(16, 8, D) for c in range(NCORES)], axis=0)
    nkp = np.stack([R[c]["o_kp"].reshape(128, 2, 64) for c in range(NCORES)])[None]
    nvp = np.stack([R[c]["o_vp"].reshape(128, 2, 64) for c in range(NCORES)])[None]
    nks = np.concatenate([R[c]["o_ks"].reshape(16, 8, 2, 64) for c in range(NCORES)], axis=0)[None]
    nvs = np.concatenate([R[c]["o_vs"].reshape(16, 8, 2, 64) for c in range(NCORES)], axis=0)[None]
    nsp = np.stack([R[c]["o_sp"] for c in range(NCORES)])[None]
    nss = np.concatenate([R[c]["o_ss"].reshape(16, 8, 512) for c in range(NCORES)], axis=0)[None]
    return (y_p.astype(np.float32), y_s.astype(np.float32), nkp.astype(np.float32), nvp.astype(np.float32),
            nks.astype(np.float32), nvs.astype(np.float32), nsp.astype(np.float32), nss.astype(np.float32))
```

```python
import numpy as np
from contextlib import ExitStack
import concourse.bass as bass
import concourse.mybir as mybir
from concourse.bass_utils import run_bass_kernel_spmd

F32 = mybir.dt.float32
BF16 = mybir.dt.bfloat16
I32 = mybir.dt.int32
U32 = mybir.dt.uint32
ALU = mybir.AluOpType
AF = mybir.ActivationFunctionType
AX = mybir.AxisListType

NCORES = 8
D = 1024
SEQ = 4096
NPT = SEQ // 128
NT = NPT + 1
NTOK = NT * 128
IN_W = 3840
EPS = 1e-6
NEG = -30000.0
GELU = AF.Gelu_apprx_tanh


class St:
    __slots__ = ("w", "r", "dsem")

    def __init__(self):
        self.w = None
        self.r = []
        self.dsem = None


class T:
    def __init__(self, name, h=None, st=None):
        self.name = name
        self.h = h
        self.st = st if st is not None else St()

    def __getitem__(self, k):
        return self.h[k]


class MK:
    def __init__(self, nc, es):
        self.nc = nc
        self.es_sem = es
        self.es = es
        self.engs = {'pe': nc.tensor, 'act': nc.scalar, 'dve': nc.vector, 'pool': nc.gpsimd, 'sp': nc.sync}
        self.sems = {}
        self.cnt = {}
        self.waited = {k: {} for k in self.engs}
        for k in ('pe', 'act', 'dve', 'pool'):
            self.sems[k] = es.enter_context(nc.semaphore("c_" + k))
            self.cnt[k] = 0
        self.nd = 0
        self.n_ins = 0
        self.rec = None
        self.rate = 4.0
        self.slot = 0
        self.inter = None
        self._in_inter = False
        self._inter_cnt = 0
        self.prev_eng = 'dve'
        self.prev_slot = -100

    def sb(self, name, shape, dtype):
        h = self.es.enter_context(self.nc.sbuf_tensor(name, list(shape), dtype))
        return T(name, h)

    def ps(self, name, shape, dtype=F32):
        h = self.es.enter_context(self.nc.psum_tensor(name, list(shape), dtype))
        return T(name, h)

    def _dsem(self, t):
        st = t.st
        if st.dsem is None:
            key = "d%d" % self.nd
            self.nd += 1
            self.sems[key] = self.es_sem.enter_context(self.nc.semaphore(key))
            self.cnt[key] = 0
            st.dsem = key
        return st.dsem

    def _waits(self, eng, reads, writes, skip_self=False):
        need = {}

        def add(tok):
            if tok is None:
                return
            k, v = tok
            if need.get(k, 0) < v:
                need[k] = v
        for t in reads:
            add(t.st.w)
        for t in writes:
            add(t.st.w)
            for tok in t.st.r:
                add(tok)
        e = self.engs[eng]
        for k, v in need.items():
            if skip_self and k == eng:
                continue
            if self.waited[eng].get(k, 0) >= v:
                continue
            e.wait_ge(self.sems[k], v)
            self.n_ins += 1
            self.waited[eng][k] = v

    def _commit(self, tok, reads, writes):
        for t in reads:
            r = t.st.r
            r.append(tok)
            if len(r) > 48:
                best = {}
                for k, v in r:
                    if best.get(k, 0) < v:
                        best[k] = v
                t.st.r = list(best.items())
        for t in writes:
            t.st.w = tok
            t.st.r = []

    def op(self, eng, fn, reads=(), writes=(), skip_self=False):
        if self.rec is not None:
            self.rec.append(('op', (eng, fn), dict(reads=list(reads), writes=list(writes), skip_self=skip_self), self.rate))
            return None
        self._waits(eng, reads, writes, skip_self)
        ins = fn(self.engs[eng])
        self.cnt[eng] += 1
        ins.then_inc(self.sems[eng], 1)
        self.n_ins += 1
        self._commit((eng, self.cnt[eng]), reads, writes)
        if self.inter is not None and not self._in_inter:
            q, every = self.inter
            self._inter_cnt += 1
            if q and self._inter_cnt % every == 0:
                self._in_inter = True
                kind, a, kw, _rate = q.pop(0)
                getattr(self, kind)(*a, **kw)
                self._in_inter = False
        return ins

    def dma(self, q, out, in_, reads=(), writes=(), semt=None, **kw):
        if self.rec is not None:
            self.rec.append(('dma', (q, out, in_), dict(reads=list(reads), writes=list(writes), semt=semt, **kw), self.rate))
            return None
        self._waits(q, reads, writes)
        key = self._dsem(semt)
        ins = self.engs[q].dma_start(out=out, in_=in_, **kw)
        self.cnt[key] += 16
        ins.then_inc(self.sems[key], 16)
        self.n_ins += 1
        self._commit((key, self.cnt[key]), reads, writes)
        return ins

    def gather(self, out_t, out_ap, table_ap, idx_t, idx_ap):
        self._waits('pool', [idx_t], [out_t])
        key = self._dsem(out_t)
        ins = self.nc.gpsimd.indirect_dma_start(
            out=out_ap, out_offset=None, in_=table_ap,
            in_offset=bass.IndirectOffsetOnAxis(ap=idx_ap, axis=0))
        self.cnt[key] += 16
        ins.then_inc(self.sems[key], 16)
        self.n_ins += 1
        self._commit((key, self.cnt[key]), [idx_t], [out_t])
        return ins

    def settle(self, tiles):
        for t in tiles:
            if t.st.w is not None and t.st.w[0] in self.cnt and t.st.w[0].startswith("d"):
                t.st.w = (t.st.w[0], self.cnt[t.st.w[0]])

    def play(self, q, budget, lag=0):
        self.slot += 1
        while q and budget > 1e-9:
            kind, a, kw, rate = q[0]
            if budget + 1e-9 < 1.0 / rate and budget < 0.999:
                break
            eng = a[0]
            if lag and eng == 'dve' and self.prev_eng != 'dve' and self.slot - self.prev_slot < lag:
                break
            q.pop(0)
            getattr(self, kind)(*a, **kw)
            budget -= 1.0 / rate
            self.prev_eng = eng
            self.prev_slot = self.slot

    def barrier(self):
        for eng in self.engs:
            e = self.engs[eng]
            for k, v in self.cnt.items():
                if v == 0 or self.waited[eng].get(k, 0) >= v:
                    continue
                e.wait_ge(self.sems[k], v)
                self.n_ins += 1
                self.waited[eng][k] = v


def _consts():
    slopes = np.exp2(-np.arange(1, 9, dtype=np.float64)).astype(np.float32)
    s = np.arange(128)[:, None]
    t = np.arange(128)[None, :]
    c = {}
    c["ident"] = np.eye(128, dtype=np.float32)
    bp = np.full((128, 2, 8, 128), NEG, np.float32)
    d0 = (128 + t - s).astype(np.float32)
    d1 = (t - s).astype(np.float32)
    for h in range(8):
        bp[:, 0, h, :] = np.where(s > t, -slopes[h] * d0, NEG)
        bp[:, 1, h, :] = np.where(s <= t, -slopes[h] * d1, NEG)
    c["bias_p"] = bp.reshape(128, 2048)
    bs_, ts_ = s // 8, s % 8
    bq_, tq_ = t // 8, t % 8
    bsn = np.full((128, 8, 128), NEG, np.float32)
    ok = (bs_ == bq_) & (ts_ <= tq_)
    for h in range(8):
        bsn[:, h, :] = np.where(ok, -slopes[h] * (tq_ - ts_).astype(np.float32), NEG)
    c["bias_sn"] = bsn.reshape(128, 1024)
    r = np.arange(128)[:, None]
    tq = np.arange(8)[None, :]
    bsp = np.full((128, 8, 8), NEG, np.float32)
    for h in range(8):
        bsp[:, h, :] = np.where(r > tq, -slopes[h] * (tq + 128 - r).astype(np.float32), NEG)
    c["bias_sp"] = bsp.reshape(128, 64)
    c["maskT_p"] = (s <= t).astype(np.float32)
    c["maskT_s"] = ok.astype(np.float32)
    c["iota16"] = np.broadcast_to(np.arange(16, dtype=np.float32)[None, :], (128, 16)).copy()
    return c


CONST_SHAPES = {"ident": (128, 128), "bias_p": (128, 2048), "bias_sn": (128, 1024), "bias_sp": (128, 64),
                "maskT_p": (128, 128), "maskT_s": (128, 128), "iota16": (128, 16)}
SMALL_SHAPES = {"g_attn": (128, 1024), "g_ffn": (128, 1024), "g_ple": (128, 1024), "gq": (128, 64), "gk": (128, 64),
                "sinks": (128, 8), "ln_g": (128, 512), "ln_b": (128, 512),
                "bcol_p": (128, 4), "bcol_s": (128, 4), "wsT_p": (128, 512), "wsT_s": (128, 512)}


CFG = {}


class _StopTile(Exception):
    pass


def chk(n):
    if CFG.get('stage', 99) < n:
        raise _StopTile()


def build():
    nc = bass.Bass("TRN2", target_bir_lowering=False)
    cfg = CFG
    tiles = cfg.get('tiles', list(range(NT)))

    def din(name, shape, dt=F32):
        return nc.dram_tensor(name, list(shape), dt, kind="ExternalInput").ap()

    def dout(name, shape, dt=F32):
        return nc.dram_tensor(name, list(shape), dt, kind="ExternalOutput").ap()

    xin = din("xin", [NTOK, D])
    pin = din("pin", [NTOK, 256])
    ck = din("ck", [128, 16, 128])
    cv = din("cv", [128, 16, 128])
    w_in = din("w_in", [D, IN_W])
    w_a = din("w_a", [512, D])
    w_b = din("w_b", [512, D])
    w_out = din("w_out", [D, D])
    w_q = din("w_q", [D, 2048])
    keysT = din("keysT", [128, 2048])
    peer_u = din("peer_u", [16384, D])
    peer_v = din("peer_v", [16384, D])
    w_ple = din("w_ple", [256, D])
    w_pg = din("w_pg", [D, D])
    cin = {k: din("c_" + k, v) for k, v in CONST_SHAPES.items()}
    sin = {k: din("s_" + k, v) for k, v in SMALL_SHAPES.items()}

    y = dout("y", [NTOK, D])
    o_kp = dout("o_kp", [128, 128])
    o_vp = dout("o_vp", [128, 128])
    o_ks = dout("o_ks", [128, 128])
    o_vs = dout("o_vs", [128, 128])
    o_sp = dout("o_sp", [128, 512])
    o_ss = dout("o_ss", [128, 512])
    x1s = nc.dram_tensor("x1s", [NTOK, D], F32, kind="Internal").ap()
    comb = nc.dram_tensor("comb", [16384, 2 * D], BF16, kind="Internal").ap()
    dbg_x1 = dout("dbg_x1", [NTOK, D]) if cfg.get('dbg') else None
    dbg_am = dout("dbg_am", [NTOK, D]) if cfg.get('dbg') else None

    with ExitStack() as es0:
        mk = MK(nc, es0)
        x1_tok = [T("x1s%d" % i) for i in range(NT)]
        out_tiles = []

        ident = mk.sb("ident", [128, 128], BF16)
        cgrp = T("cgrp")
        mk.dma('pool', ident[:], cin["ident"][:, :], writes=[ident], semt=ident)
        pbanks = [mk.ps("pb%d" % i, [128, 512], F32) for i in range(8)]

        def pe_T(bank, col0, src_t, src_ap, n=128, extra_reads=()):
            pv = bank[:].bitcast(BF16)
            return lambda e: e.transpose(out=pv[0:n, col0:col0 + 128], in_=src_ap, identity=ident[:])

        epst = mk.sb("epst", [128, 1], F32)
        mk.op('pool', lambda e: e.memset(epst[:], EPS), writes=[epst])

        def rsqrt(tile, dst, src, scale):
            mk.op('act', lambda e: e.activation(out=dst, in_=src, func=AF.Sqrt, bias=epst[:, 0:1], scale=scale),
                  reads=[tile, epst], writes=[tile])
            mk.op('dve', lambda e: e.reciprocal(out=dst, in_=dst), reads=[tile], writes=[tile])

        with ExitStack() as es1:
            mk.es = es1
            warena = mk.sb("warena1", [128, 8 * IN_W + 4 * D + 4 * D + 8 * D], BF16)
            o = 0
            win_v = warena[:, o:o + 8 * IN_W].rearrange("p (c n) -> p c n", c=8); o += 8 * IN_W
            wa_v = warena[:, o:o + 4 * D].rearrange("p (c n) -> p c n", c=4); o += 4 * D
            wb_v = warena[:, o:o + 4 * D].rearrange("p (c n) -> p c n", c=4); o += 4 * D
            wo_v = warena[:, o:o + 8 * D].rearrange("p (c n) -> p c n", c=8); o += 8 * D
            ZG = [(0, 512), (512, 256), (768, 512), (1280, 512), (1792, 512), (2304, 512), (2816, 512), (3328, 512)]
            wg = [T("wg%d" % i) for i in range(8)]
            w_in_r = w_in.rearrange("(c p) n -> p c n", p=128)
            for gi, (c0, n) in enumerate(ZG):
                for c in range(8):
                    mk.dma('pool', win_v[:, c, c0:c0 + n], w_in_r[:, c, c0:c0 + n], semt=wg[gi])
                wg[gi].st.w = (wg[gi].st.dsem, mk.cnt[wg[gi].st.dsem])
            w_a_r = w_a.rearrange("(c p) n -> p c n", p=128)
            w_b_r = w_b.rearrange("(c p) n -> p c n", p=128)
            w_o_r = w_out.rearrange("(c p) n -> p c n", p=128)
            wab = T("wab")
            wot = T("wot")
            for c in range(4):
                mk.dma('pool', wa_v[:, c, :], w_a_r[:, c, :], semt=wab)
                mk.dma('pool', wb_v[:, c, :], w_b_r[:, c, :], semt=wab)
            wab.st.w = (wab.st.dsem, mk.cnt[wab.st.dsem])
            for c in range(8):
                mk.dma('pool', wo_v[:, c, :], w_o_r[:, c, :], semt=wot)
            wot.st.w = (wot.st.dsem, mk.cnt[wot.st.dsem])
            comb_t = T("comb")
            for c in range(16):
                rs = slice(c * 1024, (c + 1) * 1024)
                mk.dma('pool', comb[rs, 0:D], peer_u[rs, :], semt=comb_t)
                mk.dma('pool', comb[rs, D:2 * D], peer_v[rs, :], semt=comb_t)
            comb_t.st.w = (comb_t.st.dsem, mk.cnt[comb_t.st.dsem])

            def cload(name, src, shape, dt=F32, q='sp'):
                t = mk.sb(name, shape, dt)
                mk.dma(q, t[:], src[:, :], writes=[t], semt=cgrp)
                return t
            g_attn = cload("g_attn", sin["g_attn"], [128, 1024])
            gq = cload("gq", sin["gq"], [128, 64])
            gk = cload("gk", sin["gk"], [128, 64])
            sinks = cload("sinks", sin["sinks"], [128, 8])
            ln_g = cload("ln_g", sin["ln_g"], [128, 512])
            ln_b = cload("ln_b", sin["ln_b"], [128, 512])
            bcol_p = cload("bcol_p", sin["bcol_p"], [128, 4])
            bcol_s = cload("bcol_s", sin["bcol_s"], [128, 4])
            bias_p = cload("bias_p", cin["bias_p"], [128, 2048])
            bias_sn = cload("bias_sn", cin["bias_sn"], [128, 1024])
            bias_sp = cload("bias_sp", cin["bias_sp"], [128, 64])
            maskT_p = cload("maskT_p", cin["maskT_p"], [128, 128])
            maskT_s = cload("maskT_s", cin["maskT_s"], [128, 128])
            t1 = mk.sb("t1", [128, D], F32)
            ws32_p = T("ws32_p", t1[:, 0:512], st=t1.st)
            ws32_s = T("ws32_s", t1[:, 512:1024], st=t1.st)
            mk.dma('sp', ws32_p[:], sin["wsT_p"][:, :], writes=[ws32_p], semt=cgrp)
            mk.dma('sp', ws32_s[:], sin["wsT_s"][:, :], writes=[ws32_s], semt=cgrp)
            consts1 = [ident, g_attn, gq, gk, sinks, ln_g, ln_b, bcol_p, bcol_s, bias_p, bias_sn, bias_sp,
                       maskT_p, maskT_s, ws32_p, ws32_s]
            mk.settle(consts1)

            sinkexp = mk.sb("sinkexp", [128, 8], F32)
            mk.op('act', lambda e: e.activation(out=sinkexp[:], in_=sinks[:], func=AF.Exp), reads=[sinks], writes=[sinkexp])
            gq8 = mk.sb("gq8", [128, 64], F32)
            mk.op('dve', lambda e: e.tensor_scalar(out=gq8[:], in0=gq[:], scalar1=0.125, scalar2=None, op0=ALU.mult),
                  reads=[gq], writes=[gq8])
            wsT_p = mk.sb("wsT_p", [128, 512], BF16)
            wsT_s = mk.sb("wsT_s", [128, 512], BF16)
            mk.op('dve', lambda e: e.tensor_tensor(out=wsT_p[:].rearrange("s (g t) -> s g t", g=4),
                                                   in0=ws32_p[:].rearrange("s (g t) -> s g t", g=4),
                                                   in1=maskT_p[:].unsqueeze(1).to_broadcast([128, 4, 128]), op=ALU.mult),
                  reads=[ws32_p, maskT_p], writes=[wsT_p])
            mk.op('dve', lambda e: e.tensor_tensor(out=wsT_s[:].rearrange("s (g t) -> s g t", g=4),
                                                   in0=ws32_s[:].rearrange("s (g t) -> s g t", g=4),
                                                   in1=maskT_s[:].unsqueeze(1).to_broadcast([128, 4, 128]), op=ALU.mult),
                  reads=[ws32_s, maskT_s], writes=[wsT_s])

            xt = [mk.sb("xt%d" % i, [128, D], F32) for i in range(3)]
            x1t = [mk.sb("x1t%d" % i, [128, D], F32) for i in range(2)]
            xn = mk.sb("xn", [128, D], BF16)
            xnT = mk.sb("xnT", [128, D], BF16)
            st1 = mk.sb("st1", [128, 16], F32)
            q32 = mk.sb("q32", [128, 512], F32)
            qn = mk.sb("qn", [128, 512], BF16)
            qT = mk.sb("qT", [128, 512], BF16)
            k32 = mk.sb("k32", [128, 128], F32)
            kn32 = mk.sb("kn32", [128, 128], F32)
            knb = mk.sb("knb", [128, 128], BF16)
            kT = [mk.sb("kT%d" % i, [128, 128], BF16) for i in range(2)]
            v32 = mk.sb("v32", [128, 128], F32)
            vaug = [mk.sb("vaug%d" % i, [128, 2 * 65], BF16) for i in range(2)]
            sc32 = mk.sb("sc32", [128, 1024], F32)
            pT = mk.sb("pT", [128, 2048], BF16)
            att_ = [mk.sb("att%d" % i, [128, 512], BF16) for i in range(2)]
            u32 = mk.sb("u32", [128, 512], F32)
            gv32 = mk.sb("gv32", [128, 512], F32)
            vn32 = mk.sb("vn32", [128, 512], F32)
            vnb = mk.sb("vnb", [128, 512], BF16)
            mb_ = [mk.sb("mb%d" % i, [128, 512], BF16) for i in range(2)]
            amT = mk.sb("amT", [128, 1024], BF16)
            siga_ = [mk.sb("siga%d" % i, [128, D], BF16) for i in range(2)]
            sigb_ = [mk.sb("sigb%d" % i, [128, D], BF16) for i in range(2)]
            hb = mk.sb("hb", [128, D], BF16)
            hT = mk.sb("hT", [128, D], BF16)
            stage = mk.sb("stage", [128, 2048], F32)
            qsq = T("qsq", stage[:, 1024:1536], st=stage.st)
            ksq = T("ksq", stage[:, 1536:1664], st=stage.st)
            ckb = T("ckb", pT[:], st=pT.st)
            kTp = mk.sb("kTp", [128, 2048], BF16)
            vpaug = mk.sb("vpaug", [128, 16 * 130], BF16)
            qTs = mk.sb("qTs", [128, 512], BF16)
            Eb = [mk.sb("Eb%d" % i, [128, 1024], BF16) for i in range(2)]
            print("phase1 sbuf remaining", nc.sbuf_bytes_remaining, flush=True)

            for i in range(2):
                mk.op('dve', lambda e, i=i: e.memset(vaug[i][:], 1.0), writes=[vaug[i]])
            mk.op('dve', lambda e: e.memset(vpaug[:], 1.0), writes=[vpaug])

            B = pbanks

            def load_x(ti):
                mk.dma('sp', xt[ti % 3][:], xin[ti * 128:(ti + 1) * 128, :], writes=[xt[ti % 3]], semt=xt[ti % 3])

            load_x(tiles[0])

            def tr8(e, src, bank):
                last = None
                for c in range(8):
                    last = pe_T(bank, c * 128, src, src[:, c * 128:(c + 1) * 128])(e)
                return last

            def emit_F1(tj):
                x = xt[tj % 3]
                mk.op('act', lambda e: e.activation(out=xn[:], in_=x[:], func=AF.Square, accum_out=st1[:, 0:1]),
                      reads=[x], writes=[xn, st1])
                rsqrt(st1, st1[:, 2:3], st1[:, 0:1], 1.0 / D)
                mk.op('dve', lambda e: e.scalar_tensor_tensor(out=xn[:], in0=x[:], scalar=st1[:, 2:3], in1=g_attn[:],
                                                              op0=ALU.mult, op1=ALU.mult), reads=[x, st1, g_attn], writes=[xn])
                mk.op('pe', lambda e: tr8(e, src=xn, bank=B[0]), reads=[xn, ident], writes=[B[0]], skip_self=True)
                mk.op('act', lambda e: e.copy(out=xnT[:], in_=B[0][:].bitcast(BF16)), reads=[B[0]], writes=[xnT])


            def tile_body(tpos, ti, prevE):
                try:
                    sample = (ti == NPT)
                    x = xt[ti % 3]
                    att = att_[ti % 2]
                    mb = mb_[ti % 2]
                    siga = siga_[ti % 2]
                    sigb = sigb_[ti % 2]
                    if tpos + 1 < len(tiles):
                        load_x(tiles[tpos + 1])
                    cur, prv = ti % 2, (ti + 1) % 2
                    if tpos == 0:
                        emit_F1(ti)
                    chk(1)
                    xnT3 = xnT[:].rearrange("p (c t) -> p c t", c=8)

                    def zgroup(bank, c0, n):
                        def f(e):
                            last = None
                            for c in range(8):
                                last = e.matmul(bank[:, 0:n], lhsT=xnT3[:, c, :], rhs=win_v[:, c, c0:c0 + n],
                                                start=(c == 0), stop=(c == 7))
                            return last
                        mk.op('pe', f, reads=[xnT, wg[[z[0] for z in ZG].index(c0)]], writes=[bank], skip_self=True)

                    zgroup(B[1], 0, 512)
                    mk.op('act', lambda e: e.copy(out=q32[:], in_=B[1][:]), reads=[B[1]], writes=[q32])
                    mk.op('act', lambda e: e.activation(out=qsq[:], in_=B[1][:], func=AF.Square), reads=[B[1]], writes=[qsq])
                    zgroup(B[2], 512, 256)
                    mk.op('act', lambda e: e.copy(out=k32[:], in_=B[2][:, 0:128]), reads=[B[2]], writes=[k32])
                    mk.op('act', lambda e: e.activation(out=ksq[:], in_=B[2][:, 0:128], func=AF.Square), reads=[B[2]], writes=[ksq])
                    mk.op('act', lambda e: e.copy(out=v32[:], in_=B[2][:, 128:256]), reads=[B[2]], writes=[v32])
                    zgroup(B[3], 768, 512)
                    mk.op('act', lambda e: e.activation(out=u32[:], in_=B[3][:], func=GELU), reads=[B[3]], writes=[u32])
                    zgroup(B[1], 1280, 512)
                    mk.op('act', lambda e: e.activation(out=gv32[:], in_=B[1][:], func=GELU, accum_out=st1[:, 4:5]),
                          reads=[B[1]], writes=[gv32, st1])
                    zgroup(B[2], 1792, 512)
                    mk.op('act', lambda e: e.activation(out=siga[:, 0:512], in_=B[2][:], func=AF.Sigmoid), reads=[B[2]], writes=[siga])
                    zgroup(B[3], 2304, 512)
                    mk.op('act', lambda e: e.activation(out=siga[:, 512:1024], in_=B[3][:], func=AF.Sigmoid), reads=[B[3]], writes=[siga])
                    zgroup(B[1], 2816, 512)
                    mk.op('act', lambda e: e.activation(out=sigb[:, 0:512], in_=B[1][:], func=AF.Sigmoid), reads=[B[1]], writes=[sigb])
                    zgroup(B[2], 3328, 512)
                    mk.op('act', lambda e: e.activation(out=sigb[:, 512:1024], in_=B[2][:], func=AF.Sigmoid), reads=[B[2]], writes=[sigb])

                    chk(2)
                    mk.inter = (prevE, 2)
                    mk.op('dve', lambda e: e.tensor_reduce(out=st1[:, 8:16], in_=qsq[:].rearrange("t (h d) -> t h d", h=8),
                                                           axis=AX.X, op=ALU.add), reads=[qsq], writes=[st1])
                    rsqrt(st1, st1[:, 8:16], st1[:, 8:16], 1.0 / 64)
                    mk.op('dve', lambda e: e.tensor_tensor(out=q32[:].rearrange("t (h d) -> t h d", h=8),
                                                           in0=q32[:].rearrange("t (h d) -> t h d", h=8),
                                                           in1=st1[:, 8:16].unsqueeze(2).to_broadcast([128, 8, 64]), op=ALU.mult),
                          reads=[q32, st1], writes=[q32])
                    mk.op('dve', lambda e: e.tensor_tensor(out=qn[:].rearrange("t (p w d) -> t w p d", p=4, w=2),
                                                           in0=q32[:].rearrange("t (w p d) -> t w p d", w=2, p=4),
                                                           in1=gq8[:].unsqueeze(1).unsqueeze(1).to_broadcast([128, 2, 4, 64]),
                                                           op=ALU.mult), reads=[q32, gq8], writes=[qn])
                    mk.op('dve', lambda e: e.tensor_reduce(out=st1[:, 5:7], in_=ksq[:].rearrange("t (h d) -> t h d", h=2),
                                                           axis=AX.X, op=ALU.add), reads=[ksq], writes=[st1])
                    rsqrt(st1, st1[:, 5:7], st1[:, 5:7], 1.0 / 64)
                    mk.op('dve', lambda e: e.tensor_tensor(out=k32[:].rearrange("t (h d) -> t h d", h=2),
                                                           in0=k32[:].rearrange("t (h d) -> t h d", h=2),
                                                           in1=st1[:, 5:7].unsqueeze(2).to_broadcast([128, 2, 64]), op=ALU.mult),
                          reads=[k32, st1], writes=[k32])
                    mk.op('dve', lambda e: e.tensor_tensor(out=kn32[:].rearrange("t (h d) -> t h d", h=2),
                                                           in0=k32[:].rearrange("t (h d) -> t h d", h=2),
                                                           in1=gk[:].unsqueeze(1).to_broadcast([128, 2, 64]), op=ALU.mult),
                          reads=[k32, gk], writes=[kn32])
                    mk.op('act', lambda e: e.copy(out=knb[:], in_=kn32[:]), reads=[kn32], writes=[knb])
                    mk.op('act', lambda e: e.copy(out=vaug[cur][:].rearrange("t (g e) -> t g e", g=2)[:, :, 0:64],
                                                  in_=v32[:].rearrange("t (g d) -> t g d", g=2)), reads=[v32], writes=[vaug[cur]])
                    chk(5)
                    mk.op('act', lambda e: e.activation(out=sc32[:, 0:512], in_=gv32[:], func=AF.Square, accum_out=st1[:, 3:4]),
                          reads=[gv32], writes=[sc32, st1])
                    mk.op('dve', lambda e: e.tensor_scalar(out=st1[:, 4:5], in0=st1[:, 4:5], scalar1=1.0 / 512, scalar2=None,
                                                           op0=ALU.mult), reads=[st1], writes=[st1])
                    mk.op('dve', lambda e: e.tensor_tensor(out=st1[:, 7:8], in0=st1[:, 4:5], in1=st1[:, 4:5], op=ALU.mult),
                          reads=[st1], writes=[st1])
                    mk.op('dve', lambda e: e.scalar_tensor_tensor(out=st1[:, 3:4], in0=st1[:, 3:4], scalar=1.0 / 512, in1=st1[:, 7:8],
                                                                  op0=ALU.mult, op1=ALU.subtract), reads=[st1], writes=[st1])
                    rsqrt(st1, st1[:, 3:4], st1[:, 3:4], 1.0)
                    mk.op('dve', lambda e: e.tensor_scalar(out=vn32[:], in0=gv32[:], scalar1=st1[:, 4:5], scalar2=st1[:, 3:4],
                                                           op0=ALU.subtract, op1=ALU.mult), reads=[gv32, st1], writes=[vn32])
                    mk.op('dve', lambda e: e.tensor_tensor(out=vn32[:], in0=vn32[:], in1=ln_g[:], op=ALU.mult),
                          reads=[vn32, ln_g], writes=[vn32])
                    mk.op('dve', lambda e: e.tensor_tensor(out=vn32[:], in0=vn32[:], in1=ln_b[:], op=ALU.add),
                          reads=[vn32, ln_b], writes=[vn32])
                    mk.op('act', lambda e: e.copy(out=vnb[:], in_=vn32[:]), reads=[vn32], writes=[vnb])
                    wsT = wsT_s if sample else wsT_p
                    bcol = bcol_s if sample else bcol_p

                    def fsgu(e, wsT=wsT):
                        last = None
                        for g in range(4):
                            last = e.matmul(B[3][:, g * 128:(g + 1) * 128], lhsT=wsT[:, g * 128:(g + 1) * 128],
                                            rhs=vnb[:, g * 128:(g + 1) * 128], start=True, stop=True)
                        return last
                    mk.op('pe', fsgu, reads=[wsT, vnb], writes=[B[3]], skip_self=True)
                    for g in range(4):
                        mk.op('dve', lambda e, g=g, bcol=bcol: e.scalar_tensor_tensor(
                            out=mb[:, g * 128:(g + 1) * 128], in0=B[3][:, g * 128:(g + 1) * 128], scalar=bcol[:, g:g + 1],
                            in1=u32[:, g * 128:(g + 1) * 128], op0=ALU.add, op1=ALU.mult), reads=[B[3], bcol, u32], writes=[mb])

                    chk(3)
                    def trq(e):
                        last = None
                        for p in range(4):
                            last = pe_T(B[0], p * 128, qn, qn[:, p * 128:(p + 1) * 128])(e)
                        last = pe_T(B[0], 512, knb, knb[:])(e)
                        return last
                    mk.op('pe', trq, reads=[qn, knb, ident], writes=[B[0]], skip_self=True)
                    b0v = B[0][:].bitcast(BF16)
                    mk.op('act', lambda e: e.copy(out=qT[:], in_=b0v[:, 0:512]), reads=[B[0]], writes=[qT])
                    mk.op('act', lambda e: e.copy(out=kT[cur][:], in_=b0v[:, 512:640]), reads=[B[0]], writes=[kT[cur]])
                    qT3 = qT[:].rearrange("p (q t) -> p q t", q=4)

                    chk(4)
                    pT4 = pT[:].rearrange("s (b h t) -> s b h t", b=2, h=8)
                    blks = [1] if (ti == 0 or sample) else [0, 1]
                    for blk in blks:
                        kt = kT[cur] if blk == 1 else kT[prv]
                        banks = (B[4], B[5])

                        def fsc(e, kt=kt, banks=banks):
                            last = None
                            for h in range(8):
                                w, p = h // 4, h % 4
                                bank = banks[h // 4]
                                last = e.matmul(bank[:, (h % 4) * 128:(h % 4 + 1) * 128], lhsT=kt[w * 64:(w + 1) * 64, :],
                                                rhs=qT3[w * 64:(w + 1) * 64, p, :], start=True, stop=True)
                            return last
                        mk.op('pe', fsc, reads=[kt, qT], writes=list(banks), skip_self=True)
                        bsrc = bias_sn if sample else bias_p
                        boff = 0 if sample else blk * 1024
                        for hb_ in range(2):
                            mk.op('dve', lambda e, hb_=hb_, banks=banks, bsrc=bsrc, boff=boff: e.tensor_tensor(
                                out=sc32[:, hb_ * 512:(hb_ + 1) * 512], in0=banks[hb_][:],
                                in1=bsrc[:, boff + hb_ * 512: boff + (hb_ + 1) * 512], op=ALU.add),
                                reads=[banks[hb_], bsrc], writes=[sc32])
                        mk.op('act', lambda e, blk=blk: e.activation(out=pT[:, blk * 1024:(blk + 1) * 1024], in_=sc32[:], func=AF.Exp),
                              reads=[sc32], writes=[pT])

                    def pv_out(h):
                        return B[1 + h // 4][:, (h % 4) * 128:(h % 4) * 128 + 65]

                    def fpv(e):
                        last = None
                        for h in range(8):
                            w = h // 4
                            for bi, blk in enumerate(blks):
                                va = vaug[cur] if blk == 1 else vaug[prv]
                                st_ = (bi == 0) if not sample else (h % 4 == 0)
                                last = e.matmul(pv_out(h), lhsT=pT4[:, blk, h, :], rhs=va[:, w * 65:(w + 1) * 65],
                                                start=st_, stop=(bi == len(blks) - 1 and not sample), skip_group_check=sample)
                        return last
                    mk.op('pe', fpv, reads=[pT, vaug[0], vaug[1]], writes=[B[1], B[2]], skip_self=True)

                    if sample:
                        chk(4.1)
                        mk.inter = None
                        mk.play(prevE, 10 ** 9)
                        mk.dma('sp', stage[:].rearrange("s (b c) -> s b c", b=16), ck[:, :, :], writes=[stage], semt=stage)
                        mk.op('act', lambda e: e.copy(out=ckb[:], in_=stage[:]), reads=[stage], writes=[ckb])
                        for half in range(2):
                            bank = B[4 + half * 2], B[5 + half * 2]

                            def ftr(e, half=half, bank=bank):
                                last = None
                                for bb in range(8):
                                    b_ = half * 8 + bb
                                    last = pe_T(bank[bb // 4], (bb % 4) * 128, ckb, ckb[:, b_ * 128:(b_ + 1) * 128])(e)
                                return last
                            mk.op('pe', ftr, reads=[ckb, ident], writes=list(bank), skip_self=True)
                            for q_ in range(2):
                                mk.op('act', lambda e, half=half, q_=q_, bank=bank: e.copy(
                                    out=kTp[:, (half * 8 + q_ * 4) * 128:(half * 8 + q_ * 4 + 4) * 128],
                                    in_=bank[q_][:].bitcast(BF16)[:, 0:512]), reads=[bank[q_]], writes=[kTp])
                        chk(4.2)
                        if not cfg.get('skip_vdma'):
                            mk.dma('sp', stage[:].rearrange("s (b c) -> s b c", b=16), cv[:, :, :], writes=[stage], semt=stage)
                        if cfg.get('skip_vcopy'):
                            raise _StopTile()
                        for g_ in range(2):
                            mk.op('dve', lambda e, g_=g_: e.tensor_copy(
                                out=vpaug[:].rearrange("s (b g e) -> s b g e", b=16, g=2)[:, :, g_, 0:64],
                                in_=stage[:].rearrange("s (b g d) -> s b g d", b=16, g=2)[:, :, g_, :]),
                                reads=[stage], writes=[vpaug])
                        chk(4.3)
                        kTp3 = kTp[:].rearrange("p (b s) -> p b s", b=16)
                        mk.op('act', lambda e: e.copy(out=qTs[:].rearrange("p (b q t) -> p b q t", b=16, q=4),
                                                      in_=qT[:].rearrange("p (q b t) -> p b q t", q=4, b=16)),
                              reads=[qT], writes=[qTs])
                        chk(4.31)

                        def fps(e):
                            last = None
                            for w in range(2):
                                bank = B[4 + w]
                                for b_ in range(16):
                                    c0 = b_ * 32
                                    last = e.matmul(bank[:, c0:c0 + 32], lhsT=kTp3[w * 64:(w + 1) * 64, b_, :],
                                                    rhs=qTs[w * 64:(w + 1) * 64, b_ * 32:(b_ + 1) * 32], start=True, stop=True)
                            return last
                        mk.op('pe', fps, reads=[kTp, qTs], writes=[B[4], B[5]], skip_self=True)
                        chk(4.32)
                        for half in range(2):
                            mk.op('dve', lambda e, half=half: e.tensor_tensor(
                                out=sc32[:, half * 512:(half + 1) * 512].rearrange("s (b c) -> s b c", b=16),
                                in0=B[4 + half][:].rearrange("s (b c) -> s b c", b=16),
                                in1=bias_sp[:, half * 32:(half + 1) * 32].unsqueeze(1).to_broadcast([128, 16, 32]), op=ALU.add),
                                reads=[B[4 + half], bias_sp], writes=[sc32])
                        chk(4.4)
                        sc5 = sc32[:].rearrange("s (w b p t) -> s b w p t", w=2, b=16, p=4)
                        vp4 = vpaug[:].rearrange("s (b g e) -> s b g e", b=16, g=2)
                        for b_ in range(16):
                            E = Eb[b_ % 2]
                            E3 = E[:].rearrange("s (h t) -> s h t", h=8)
                            mk.op('pool', lambda e, E=E: e.memset(E[:], 0.0), writes=[E])
                            mk.op('act', lambda e, E=E, b_=b_: e.activation(
                                out=E[:].rearrange("s (w p t) -> s w p t", w=2, p=4)[:, :, :, b_ * 8:(b_ + 1) * 8],
                                in_=sc5[:, b_, :, :, :], func=AF.Exp), reads=[sc32], writes=[E])

                            def fpp(e, E3=E3, b_=b_):
                                last = None
                                for h in range(8):
                                    last = e.matmul(pv_out(h), lhsT=E3[:, h, :], rhs=vp4[:, b_, h // 4, :],
                                                    start=False, stop=(b_ == 15), skip_group_check=True)
                                return last
                            mk.op('pe', fpp, reads=[E, vpaug], writes=[B[1], B[2]], skip_self=True)

                    chk(4.5)
                    for half in range(2):
                        bank = B[1 + half]
                        b3 = bank[:].rearrange("t (h e) -> t h e", h=4)
                        mk.op('dve', lambda e, b3=b3, half=half: e.tensor_tensor(
                            out=st1[:, 8 + half * 4: 12 + half * 4].unsqueeze(2), in0=b3[:, :, 64:65],
                            in1=sinkexp[:, half * 4:(half + 1) * 4].unsqueeze(2), op=ALU.add),
                            reads=[bank, sinkexp], writes=[st1])
                    mk.op('dve', lambda e: e.reciprocal(out=st1[:, 8:16], in_=st1[:, 8:16]), reads=[st1], writes=[st1])
                    for half in range(2):
                        bank = B[1 + half]
                        b3 = bank[:].rearrange("t (h e) -> t h e", h=4)
                        mk.op('dve', lambda e, b3=b3, half=half: e.tensor_tensor(
                            out=att[:, half * 256:(half + 1) * 256].rearrange("t (h d) -> t h d", h=4), in0=b3[:, :, 0:64],
                            in1=st1[:, 8 + half * 4: 12 + half * 4].unsqueeze(2).to_broadcast([128, 4, 64]), op=ALU.mult),
                            reads=[bank, st1], writes=[att])

                    if tpos + 1 < len(tiles):
                        emit_F1(tiles[tpos + 1])
                    chk(6)
                    if ti == NPT - 1 or sample:
                        ok_, ov_, os_ = (o_ks, o_vs, o_ss) if sample else (o_kp, o_vp, o_sp)
                        mk.dma('sp', ok_[:, :], kn32[:], reads=[kn32], semt=kn32)
                        mk.dma('sp', ov_[:, :], v32[:], reads=[v32], semt=v32)
                        mk.dma('sp', os_[:, :], vn32[:], reads=[vn32], semt=vn32)
                        out_tiles.extend([kn32, v32, vn32])

                    chk(7)
                    mk.inter = None
                    mk.play(prevE, 10 ** 9)
                    if dbg_am is not None:
                        mk.op('dve', lambda e: e.tensor_copy(out=t1[:, 0:512], in_=att[:]), reads=[att], writes=[t1])
                        mk.op('dve', lambda e: e.tensor_copy(out=t1[:, 512:1024], in_=mb[:]), reads=[mb], writes=[t1])
                        mk.dma('sp', dbg_am[ti * 128:(ti + 1) * 128, :], t1[:], reads=[t1], semt=t1)
                    mk.rec = []
                    mk.rate = 1.0
                    def tram(e):
                        last = None
                        for c in range(4):
                            last = pe_T(B[0], c * 128, att, att[:, c * 128:(c + 1) * 128])(e)
                        for c in range(4):
                            last = pe_T(B[0], 512 + c * 128, mb, mb[:, c * 128:(c + 1) * 128])(e)
                        return last
                    mk.op('pe', tram, reads=[att, mb, ident], writes=[B[0]], skip_self=True)
                    mk.op('act', lambda e: e.copy(out=amT[:], in_=B[0][:].bitcast(BF16)), reads=[B[0]], writes=[amT])
                    amT3 = amT[:].rearrange("p (c t) -> p c t", c=8)

                    def fmerge_ab(e, w_v, off):
                        last = None
                        for n in range(2):
                            for c in range(4):
                                last = e.matmul(B[6 + n][:], lhsT=amT3[:, off + c, :], rhs=w_v[:, c, n * 512:(n + 1) * 512],
                                                start=(c == 0), stop=(c == 3))
                        return last
                    mk.op('pe', lambda e: fmerge_ab(e, wa_v, 0), reads=[amT, wab], writes=[B[6], B[7]], skip_self=True)
                    for n in range(2):
                        sl = slice(n * 512, (n + 1) * 512)
                        mk.op('dve', lambda e, n=n, sl=sl: e.tensor_tensor(out=t1[:, sl], in0=B[6 + n][:], in1=siga[:, sl], op=ALU.mult),
                              reads=[B[6 + n], siga], writes=[t1])
                    mk.op('pe', lambda e: fmerge_ab(e, wb_v, 4), reads=[amT, wab], writes=[B[6], B[7]], skip_self=True)
                    for n in range(2):
                        sl = slice(n * 512, (n + 1) * 512)
                        mk.op('dve', lambda e, n=n, sl=sl: e.tensor_tensor(out=stage[:, sl], in0=B[6 + n][:], in1=sigb[:, sl], op=ALU.mult),
                              reads=[B[6 + n], sigb], writes=[stage])
                    mk.op('dve', lambda e: e.tensor_tensor(out=hb[:], in0=stage[:, 0:1024], in1=t1[:], op=ALU.add), reads=[stage, t1], writes=[hb])
                    mk.op('pe', lambda e: tr8(e, src=hb, bank=B[0]), reads=[hb, ident], writes=[B[0]], skip_self=True)
                    mk.op('act', lambda e: e.copy(out=hT[:], in_=B[0][:].bitcast(BF16)), reads=[B[0]], writes=[hT])
                    hT3 = hT[:].rearrange("p (c t) -> p c t", c=8)

                    def fwo(e):
                        last = None
                        for n in range(2):
                            for c in range(8):
                                last = e.matmul(B[6 + n][:], lhsT=hT3[:, c, :], rhs=wo_v[:, c, n * 512:(n + 1) * 512],
                                                start=(c == 0), stop=(c == 7))
                        return last
                    mk.op('pe', fwo, reads=[hT, wot], writes=[B[6], B[7]], skip_self=True)
                    x1 = x1t[ti % 2]
                    for n in range(2):
                        sl = slice(n * 512, (n + 1) * 512)
                        mk.op('dve', lambda e, n=n, sl=sl, x1=x1: e.tensor_tensor(out=x1[:, sl], in0=B[6 + n][:], in1=x[:, sl], op=ALU.add),
                              reads=[B[6 + n], x], writes=[x1])
                    mk.dma('sp', x1s[ti * 128:(ti + 1) * 128, :], x1[:], reads=[x1], writes=[x1_tok[ti]], semt=x1)
                    if dbg_x1 is not None:
                        mk.dma('sp', dbg_x1[ti * 128:(ti + 1) * 128, :], x1[:], reads=[x1], semt=x1)
                    pend = mk.rec
                    mk.rec = None
                    return pend
                except _StopTile:
                    mk.rec = None
                    mk.inter = None
                    return []

            prevE = []
            for tpos, ti in enumerate(tiles):
                prevE = tile_body(tpos, ti, prevE)
            mk.play(prevE, 10 ** 9)

            mk.barrier()
        with ExitStack() as es2:
            mk.es = es2
            warena2 = mk.sb("warena2", [128, 8 * 2048 + 2048 + 8 * D + 2 * D], BF16)
            o = 0
            wq_v = warena2[:, o:o + 8 * 2048].rearrange("p (c n) -> p c n", c=8); o += 8 * 2048
            keys_v = warena2[:, o:o + 2048].rearrange("p (g n) -> p g n", g=16); o += 2048
            wpg_v = warena2[:, o:o + 8 * D].rearrange("p (c n) -> p c n", c=8); o += 8 * D
            wpl_v = warena2[:, o:o + 2 * D].rearrange("p (c n) -> p c n", c=2); o += 2 * D
            wsem2 = T("wsem2")
            w_q_r = w_q.rearrange("(c p) n -> p c n", p=128)
            w_pg_r = w_pg.rearrange("(c p) n -> p c n", p=128)
            w_pl_r = w_ple.rearrange("(c p) n -> p c n", p=128)
            for c in range(8):
                mk.dma('pool', wq_v[:, c, :], w_q_r[:, c, :], semt=wsem2)
                mk.dma('pool', wpg_v[:, c, :], w_pg_r[:, c, :], semt=wsem2)
            for c in range(2):
                mk.dma('pool', wpl_v[:, c, :], w_pl_r[:, c, :], semt=wsem2)
            mk.dma('pool', keys_v, keysT.rearrange("p (g n) -> p g n", g=16), semt=wsem2)
            warena2.st.w = (wsem2.st.dsem, mk.cnt[wsem2.st.dsem])
            cgrp2 = T("cgrp2")

            def cload2(name, src, shape, dt=F32, q='sp'):
                t = mk.sb(name, shape, dt)
                mk.dma(q, t[:], src[:, :], writes=[t], semt=cgrp2)
                return t
            g_ffn = cload2("g_ffn", sin["g_ffn"], [128, 1024])
            g_ple = cload2("g_ple", sin["g_ple"], [128, 1024])
            iota16 = cload2("iota16", cin["iota16"], [128, 16])
            identf = cload2("identf", cin["ident"], [128, 128])
            mk.settle([g_ffn, g_ple, iota16, identf])

            xt2 = [mk.sb("x2t%d" % i, [128, D], F32) for i in range(2)]
            pt2 = [mk.sb("p2t%d" % i, [128, 256], F32) for i in range(2)]
            xn2_ = [mk.sb("xn2_%d" % i, [128, D], F32) for i in range(2)]
            xn2b = mk.sb("xn2b", [128, D], BF16)
            xn2T = mk.sb("xn2T", [128, D], BF16)
            st2 = mk.sb("st2", [128, 16], F32)
            st2A = mk.sb("st2A", [128, 16], F32)
            qpT = mk.sb("qpT", [128, 2048], BF16)
            scp = mk.sb("scp", [128, 2048], F32)
            work = mk.sb("work", [128, 2048], F32)
            stop_ = mk.sb("stop", [128, 256], F32)
            itopu = mk.sb("itopu", [128, 256], U32)
            itopf = mk.sb("itopf", [128, 256], F32)
            cand = mk.sb("cand", [128, 2048], F32)
            best = mk.sb("best", [128, 128], F32)
            posu = mk.sb("posu", [128, 128], U32)
            posi = mk.sb("posi", [128, 256], I32)
            posf = mk.sb("posf", [128, 256], F32)
            oh = T("oh", work[:], st=work.st)
            oh2 = T("oh2", cand[:], st=cand.st)
            sel = mk.sb("sel", [128, 256], F32)
            junkA = T("junkA", oh[:, 0:1024], st=oh.st)
            idxf = mk.sb("idxf", [128, 128], F32)
            idxi_ = [mk.sb("idxi%d" % i, [128, 128], I32) for i in range(2)]
            gate_ = [mk.sb("gate%d" % i, [128, 128], F32) for i in range(2)]
            dots = mk.sb("dots", [128, 128], F32)
            actv = mk.sb("actv", [128, 128], F32)
            wgt = mk.sb("wgt", [128, 128], F32)
            NGB = 16
            gbuf = [mk.sb("gbuf%d" % i, [128, 2 * D], BF16) for i in range(NGB)]
            junk = mk.sb("junk", [128, D], F32)
            pleg = mk.sb("pleg", [128, D], F32)
            NSM = 16
            dot_r = [mk.sb("dot_r%d" % i, [128, 1], F32) for i in range(NSM)]
            wv_r = [mk.sb("wv_r%d" % i, [128, 2], F32) for i in range(NSM)]
            dg_r = [mk.sb("dg_r%d" % i, [128, 128], BF16) for i in range(8)]
            xn3 = mk.sb("xn3", [128, D], BF16)
            xn3T = mk.sb("xn3T", [128, D], BF16)
            pb16 = mk.sb("pb16", [128, 256], BF16)
            ppT = mk.sb("ppT", [128, 256], BF16)
            print("phase2 sbuf remaining", nc.sbuf_bytes_remaining, flush=True)
            B = pbanks

            def load2(ti):
                mk.dma('sp', xt2[ti % 2][:], x1s[ti * 128:(ti + 1) * 128, :], reads=[x1_tok[ti]], writes=[xt2[ti % 2]],
                       semt=xt2[ti % 2])
                mk.dma('sp', pt2[ti % 2][:], pin[ti * 128:(ti + 1) * 128, :], writes=[pt2[ti % 2]], semt=pt2[ti % 2])

            def rms(xsrc, gt, out32, outb, st2, junk):
                mk.op('act', lambda e: e.activation(out=junk[:], in_=xsrc[:], func=AF.Square, accum_out=st2[:, 0:1]),
                      reads=[xsrc], writes=[junk, st2])
                rsqrt(st2, st2[:, 2:3], st2[:, 0:1], 1.0 / D)
                if out32 is not None:
                    mk.op('dve', lambda e: e.scalar_tensor_tensor(out=out32[:], in0=xsrc[:], scalar=st2[:, 2:3], in1=gt[:],
                                                                  op0=ALU.mult, op1=ALU.mult), reads=[xsrc, st2, gt], writes=[out32])
                    mk.op('act', lambda e: e.copy(out=outb[:], in_=out32[:]), reads=[out32], writes=[outb])
                else:
                    mk.op('dve', lambda e: e.scalar_tensor_tensor(out=outb[:], in0=xsrc[:], scalar=st2[:, 2:3], in1=gt[:],
                                                                  op0=ALU.mult, op1=ALU.mult), reads=[xsrc, st2, gt], writes=[outb])

            def tr8b(src, dst, n=8):
                def f(e):
                    last = None
                    for c in range(n):
                        last = pe_T(B[0], c * 128, src, src[:, c * 128:(c + 1) * 128])(e)
                    return last
                mk.op('pe', f, reads=[src, ident], writes=[B[0]], skip_self=True)
                mk.op('act', lambda e: e.copy(out=dst[:, 0:n * 128], in_=B[0][:].bitcast(BF16)[:, 0:n * 128]),
                      reads=[B[0]], writes=[dst])

            tiles2 = tiles if cfg.get('phase2', True) else []

            def stageA(ti):
                par = ti % 2
                x = xt2[par]
                xn2 = xn2_[par]
                idxi = idxi_[par]
                gate = gate_[par]
                mk.rate = 1.0
                load2(ti)
                rms(x, g_ffn, xn2, xn2b, st2A, junkA)
                tr8b(xn2b, xn2T)
                xT3 = xn2T[:].rearrange("p (c t) -> p c t", c=8)
                for bq in range(4):
                    bank = B[1 + bq]

                    def fq(e, bq=bq, bank=bank):
                        last = None
                        for gi in range(4):
                            g_ = bq * 4 + gi
                            for c in range(8):
                                last = e.matmul(bank[:, gi * 128:(gi + 1) * 128], lhsT=wq_v[:, c, g_ * 128:(g_ + 1) * 128],
                                                rhs=xT3[:, c, :], start=(c == 0), stop=(c == 7))
                        return last
                    mk.op('pe', fq, reads=[xn2T, warena2], writes=[bank], skip_self=True)
                    mk.op('act', lambda e, bq=bq, bank=bank: e.copy(out=qpT[:, bq * 512:(bq + 1) * 512], in_=bank[:]),
                          reads=[bank], writes=[qpT])
                for bq in range(4):
                    bank = B[1 + bq]

                    def fs(e, bq=bq, bank=bank):
                        last = None
                        for gi in range(4):
                            g_ = bq * 4 + gi
                            last = e.matmul(bank[:, gi * 128:(gi + 1) * 128], lhsT=qpT[:, g_ * 128:(g_ + 1) * 128],
                                            rhs=keys_v[:, g_, :], start=True, stop=True)
                        return last
                    mk.op('pe', fs, reads=[qpT, warena2], writes=[bank], skip_self=True)
                    mk.op('act', lambda e, bq=bq, bank=bank: e.copy(out=scp[:, bq * 512:(bq + 1) * 512], in_=bank[:]),
                          reads=[bank], writes=[scp])
                mk.rate = 4.0
                def sg_(g_):
                    return scp[:, g_ * 128:(g_ + 1) * 128]

                def wk_(g_):
                    return work[:, g_ * 128:(g_ + 1) * 128]
                for g_ in range(16):
                    mk.op('dve', lambda e, g_=g_: e.max(out=stop_[:, g_ * 16:g_ * 16 + 8], in_=sg_(g_)),
                          reads=[scp], writes=[stop_], skip_self=True)
                for g_ in range(16):
                    mk.op('dve', lambda e, g_=g_: e.match_replace(out=wk_(g_), in_to_replace=stop_[:, g_ * 16:g_ * 16 + 8],
                                                                  in_values=sg_(g_), imm_value=-1e30),
                          reads=[scp, stop_], writes=[work], skip_self=True)
                for g_ in range(16):
                    mk.op('dve', lambda e, g_=g_: e.max(out=stop_[:, g_ * 16 + 8:g_ * 16 + 16], in_=wk_(g_)),
                          reads=[work], writes=[stop_], skip_self=True)
                for g_ in range(16):
                    mk.op('dve', lambda e, g_=g_: e.max_index(out=itopu[:, g_ * 16:g_ * 16 + 8], in_max=stop_[:, g_ * 16:g_ * 16 + 8],
                                                              in_values=sg_(g_)), reads=[scp, stop_], writes=[itopu], skip_self=True)
                for g_ in range(16):
                    mk.op('dve', lambda e, g_=g_: e.max_index(out=itopu[:, g_ * 16 + 8:g_ * 16 + 16],
                                                              in_max=stop_[:, g_ * 16 + 8:g_ * 16 + 16], in_values=sg_(g_)),
                          reads=[scp, stop_], writes=[itopu], skip_self=True)
                mk.op('dve', lambda e: e.tensor_copy(out=itopf[:], in_=itopu[:]), reads=[itopu], writes=[itopf])
                st4 = stop_[:].rearrange("t (h c k) -> t h c k", h=8, c=2)
                mk.op('dve', lambda e: e.tensor_tensor(out=cand[:].rearrange("t (h i j) -> t h i j", h=8, i=16),
                                                       in0=st4[:, :, 0, :].unsqueeze(3).to_broadcast([128, 8, 16, 16]),
                                                       in1=st4[:, :, 1, :].unsqueeze(2).to_broadcast([128, 8, 16, 16]), op=ALU.add),
                      reads=[stop_], writes=[cand])
                def cg_(h):
                    return cand[:, h * 256:(h + 1) * 256]

                def wk2_(h):
                    return work[:, h * 256:(h + 1) * 256]
                for h in range(8):
                    mk.op('dve', lambda e, h=h: e.max(out=best[:, h * 16:h * 16 + 8], in_=cg_(h)),
                          reads=[cand], writes=[best], skip_self=(h > 0))
                for h in range(8):
                    mk.op('dve', lambda e, h=h: e.match_replace(out=wk2_(h), in_to_replace=best[:, h * 16:h * 16 + 8], in_values=cg_(h),
                                                                imm_value=-1e30), reads=[cand, best], writes=[work], skip_self=True)
                for h in range(8):
                    mk.op('dve', lambda e, h=h: e.max(out=best[:, h * 16 + 8:h * 16 + 16], in_=wk2_(h)),
                          reads=[work], writes=[best], skip_self=True)
                for h in range(8):
                    mk.op('dve', lambda e, h=h: e.max_index(out=posu[:, h * 16:h * 16 + 8], in_max=best[:, h * 16:h * 16 + 8],
                                                            in_values=cg_(h)), reads=[cand, best], writes=[posu], skip_self=True)
                for h in range(8):
                    mk.op('dve', lambda e, h=h: e.max_index(out=posu[:, h * 16 + 8:h * 16 + 16], in_max=best[:, h * 16 + 8:h * 16 + 16],
                                                            in_values=cg_(h)), reads=[cand, best], writes=[posu], skip_self=True)
                mk.op('dve', lambda e: e.tensor_single_scalar(out=posi[:, 0:128], in_=posu[:].bitcast(I32), scalar=4,
                                                              op=ALU.logical_shift_right), reads=[posu], writes=[posi])
                mk.op('dve', lambda e: e.tensor_single_scalar(out=posi[:, 128:256], in_=posu[:].bitcast(I32), scalar=15,
                                                              op=ALU.bitwise_and), reads=[posu], writes=[posi])
                mk.op('dve', lambda e: e.tensor_copy(out=posf[:], in_=posi[:]), reads=[posi], writes=[posf])
                it4 = itopf[:].rearrange("t (h c k) -> t h c k", h=8, c=2)
                ohs = (oh, oh2)
                oh4s = [o_[:].rearrange("t (h k i) -> t h k i", h=8, k=16) for o_ in ohs]
                for c in range(2):
                    pf = posf[:, c * 128:(c + 1) * 128].rearrange("t (h k) -> t h k", h=8)
                    mk.op('dve', lambda e, pf=pf, c=c: e.tensor_tensor(
                        out=oh4s[c], in0=iota16[:].unsqueeze(1).unsqueeze(1).to_broadcast([128, 8, 16, 16]),
                        in1=pf.unsqueeze(3).to_broadcast([128, 8, 16, 16]), op=ALU.is_equal), reads=[iota16, posf], writes=[ohs[c]])
                for c in range(2):
                    mk.op('dve', lambda e, c=c: e.tensor_tensor(
                        out=oh4s[c], in0=oh4s[c], in1=it4[:, :, c, :].unsqueeze(2).to_broadcast([128, 8, 16, 16]), op=ALU.mult),
                        reads=[ohs[c], itopf], writes=[ohs[c]], skip_self=True)
                for c in range(2):
                    mk.op('dve', lambda e, c=c: e.tensor_reduce(out=sel[:, c * 128:(c + 1) * 128],
                                                                in_=ohs[c][:].rearrange("t (s i) -> t s i", i=16), axis=AX.X, op=ALU.add),
                          reads=[ohs[c]], writes=[sel], skip_self=True)
                mk.op('dve', lambda e: e.scalar_tensor_tensor(out=idxf[:], in0=sel[:, 0:128], scalar=128.0, in1=sel[:, 128:256],
                                                              op0=ALU.mult, op1=ALU.add), reads=[sel], writes=[idxf])
                mk.op('dve', lambda e: e.tensor_copy(out=idxi[:], in_=idxf[:]), reads=[idxf], writes=[idxi])
                b3 = best[:].rearrange("t (h k) -> t h k", h=8)
                mk.op('dve', lambda e: e.tensor_tensor(out=gate[:].rearrange("t (h k) -> t h k", h=8), in0=b3,
                                                       in1=b3[:, :, 0:1].to_broadcast([128, 8, 16]), op=ALU.subtract),
                      reads=[best], writes=[gate])
                mk.op('act', lambda e: e.activation(out=gate[:], in_=gate[:], func=AF.Exp), reads=[gate], writes=[gate])
                mk.op('dve', lambda e: e.tensor_reduce(out=st2A[:, 8:16], in_=gate[:].rearrange("t (h k) -> t h k", h=8),
                                                       axis=AX.X, op=ALU.add), reads=[gate], writes=[st2A])
                mk.op('dve', lambda e: e.reciprocal(out=st2A[:, 8:16], in_=st2A[:, 8:16]), reads=[st2A], writes=[st2A])
                mk.op('dve', lambda e: e.tensor_tensor(out=gate[:].rearrange("t (h k) -> t h k", h=8),
                                                       in0=gate[:].rearrange("t (h k) -> t h k", h=8),
                                                       in1=st2A[:, 8:16].unsqueeze(2).to_broadcast([128, 8, 16]), op=ALU.mult),
                      reads=[gate, st2A], writes=[gate])

            def stageB(ti, nxt_q):
                par = ti % 2
                x = xt2[par]
                pt = pt2[par]
                xn2 = xn2_[par]
                idxi = idxi_[par]
                gate = gate_[par]
                mk._waits('pool', [comb_t], [])
                for s_ in range(128):
                    gb = gbuf[s_ % NGB]
                    dt_ = dot_r[s_ % NSM]
                    wv = wv_r[s_ % NSM]
                    dg = dg_r[s_ % 8]
                    mk.gather(gb, gb[:], comb[:, :], idxi, idxi[:, s_:s_ + 1])
                    mk.op('dve', lambda e, gb=gb, dt_=dt_: e.scalar_tensor_tensor(
                        out=junk[:], in0=gb[:, 0:D], scalar=1.0, in1=xn2[:], op0=ALU.mult, op1=ALU.mult,
                        accum_out=dt_[:, 0:1]), reads=[gb, xn2], writes=[junk, dt_], skip_self=True)
                    mk.op('act', lambda e, dt_=dt_, wv=wv: e.activation(out=wv[:, 0:1], in_=dt_[:, 0:1], func=GELU),
                          reads=[dt_], writes=[wv])
                    mk.op('act', lambda e, wv=wv, s_=s_: e.mul(out=wv[:, 1:2], in_=wv[:, 0:1], mul=gate[:, s_:s_ + 1]),
                          reads=[wv, gate], writes=[wv])
                    mk.op('act', lambda e, wv=wv, dg=dg: e.activation(out=dg[:], in_=identf[:], func=AF.Copy, scale=wv[:, 1:2]),
                          reads=[wv, identf], writes=[dg])

                    def fv(e, s_=s_, gb=gb, dg=dg):
                        last = None
                        for n in range(2):
                            last = e.matmul(B[5 + n][:], lhsT=dg[:], rhs=gb[:, D + n * 512:D + (n + 1) * 512],
                                            start=(s_ == 0), stop=(s_ == 127))
                        return last
                    mk.op('pe', fv, reads=[gb, dg], writes=[B[5], B[6]], skip_self=True)
                    mk.play(nxt_q, 1.0, lag=5)
                if len(nxt_q) and ti == 1:
                    print('replay leftover at drain:', len(nxt_q), flush=True)
                mk.play(nxt_q, 10 ** 9)
                for n in range(2):
                    sl = slice(n * 512, (n + 1) * 512)
                    mk.op('dve', lambda e, n=n, sl=sl: e.tensor_tensor(out=x[:, sl], in0=B[5 + n][:], in1=x[:, sl], op=ALU.add),
                          reads=[B[5 + n], x], writes=[x])
                mk.rec = []
                mk.rate = 1.0
                rms(x, g_ple, None, xn3, st2, pleg)
                tr8b(xn3, xn3T)
                x3T = xn3T[:].rearrange("p (c t) -> p c t", c=8)

                def fpg(e):
                    last = None
                    for n in range(2):
                        for c in range(8):
                            last = e.matmul(B[1 + n][:], lhsT=x3T[:, c, :], rhs=wpg_v[:, c, n * 512:(n + 1) * 512],
                                            start=(c == 0), stop=(c == 7))
                    return last
                mk.op('pe', fpg, reads=[xn3T, warena2], writes=[B[1], B[2]], skip_self=True)
                for n in range(2):
                    mk.op('act', lambda e, n=n: e.activation(out=pleg[:, n * 512:(n + 1) * 512], in_=B[1 + n][:], func=AF.Sigmoid),
                          reads=[B[1 + n]], writes=[pleg])
                mk.op('act', lambda e: e.copy(out=pb16[:], in_=pt[:]), reads=[pt], writes=[pb16])
                tr8b(pb16, ppT, n=2)
                pT3 = ppT[:].rearrange("p (c t) -> p c t", c=2)

                def fpl(e):
                    last = None
                    for n in range(2):
                        for c in range(2):
                            last = e.matmul(B[3 + n][:], lhsT=pT3[:, c, :], rhs=wpl_v[:, c, n * 512:(n + 1) * 512],
                                            start=(c == 0), stop=(c == 1))
                    return last
                mk.op('pe', fpl, reads=[ppT, warena2], writes=[B[3], B[4]], skip_self=True)
                for n in range(2):
                    sl = slice(n * 512, (n + 1) * 512)
                    mk.op('dve', lambda e, n=n, sl=sl: e.tensor_tensor(out=pleg[:, sl], in0=B[3 + n][:], in1=pleg[:, sl], op=ALU.mult),
                          reads=[B[3 + n], pleg], writes=[pleg])
                mk.op('dve', lambda e: e.tensor_tensor(out=x[:], in0=x[:], in1=pleg[:], op=ALU.add), reads=[x, pleg], writes=[x])
                mk.dma('sp', y[ti * 128:(ti + 1) * 128, :], x[:], reads=[x], semt=x)
                tail = mk.rec
                mk.rec = None
                return tail

            if tiles2:
                mk.rec = []
                stageA(tiles2[0])
                q0 = mk.rec
                mk.rec = None
                mk.play(q0, 10 ** 9)
            pending_tail = []
            for tpos, ti in enumerate(tiles2):
                nxt_q = pending_tail
                if tpos + 1 < len(tiles2):
                    mk.rec = []
                    stageA(tiles2[tpos + 1])
                    nxt_q = nxt_q + mk.rec
                    mk.rec = None
                if tpos == 1:
                    print('replay queue: ops', len(nxt_q), 'slot budget needed', sum(1.0 / r[3] for r in nxt_q), flush=True)
                pending_tail = stageB(ti, nxt_q)
            mk.play(pending_tail, 10 ** 9)
            out_tiles_all = out_tiles + xt2
            mk._waits('sp', [], xt2)
            mk.barrier()
        print("instructions emitted:", mk.n_ins, flush=True)
    return nc


_NC_CACHE = {}


def _get_nc():
    if "nc" not in _NC_CACHE:
        _NC_CACHE["nc"] = build()
    return _NC_CACHE["nc"]


def kernel(x_prompt, x_sample, cache_k, cache_v, p_prompt, p_sample,
           attn_norm_g, w_in, q_norm_g, k_norm_g, attn_sinks,
           sgu_norm_g, sgu_norm_b, sgu_w, sgu_b,
           w_branch_a, w_branch_b, w_out,
           ffn_norm_g, peer_w_q, peer_sub_keys, peer_u, peer_v,
           ple_norm_g, w_ple, w_ple_gate):
    f = lambda a: np.ascontiguousarray(np.asarray(a, dtype=np.float32))
    x_prompt, x_sample, cache_k, cache_v = f(x_prompt), f(x_sample), f(cache_k), f(cache_v)
    p_prompt, p_sample = f(p_prompt), f(p_sample)
    nc = _get_nc()
    consts = _consts()
    rep = lambda v, n=128: np.ascontiguousarray(np.broadcast_to(f(v).reshape(1, -1), (n, f(v).size)))
    sw = f(sgu_w)[0]
    sb = f(sgu_b)[0]
    small = {
        "g_attn": rep(attn_norm_g[0]), "g_ffn": rep(ffn_norm_g[0]), "g_ple": rep(ple_norm_g[0]),
        "gq": rep(q_norm_g[0]), "gk": rep(k_norm_g[0]), "sinks": rep(attn_sinks[0]),
        "ln_g": rep(sgu_norm_g[0]), "ln_b": rep(sgu_norm_b[0]),
        "bcol_p": np.ascontiguousarray(sb.T),
        "bcol_s": np.ascontiguousarray(np.tile(sb[:, :8].T, (16, 1))),
        "wsT_p": np.ascontiguousarray(sw.transpose(2, 0, 1)).reshape(128, 512),
        "wsT_s": np.ascontiguousarray(np.tile(sw[:, :8, :8].transpose(2, 0, 1), (16, 1, 16))).reshape(128, 512),
    }
    shared = {
        "w_in": f(w_in)[0], "w_a": f(w_branch_a)[0], "w_b": f(w_branch_b)[0], "w_out": f(w_out)[0],
        "w_q": f(peer_w_q)[0],
        "keysT": np.ascontiguousarray(f(peer_sub_keys)[0].reshape(16, 128, 128).transpose(2, 0, 1)).reshape(128, 2048),
        "peer_u": f(peer_u)[0], "peer_v": f(peer_v)[0], "w_ple": f(w_ple)[0], "w_pg": f(w_ple_gate)[0],
    }
    for k, v in consts.items():
        shared["c_" + k] = v
    for k, v in small.items():
        shared["s_" + k] = v
    in_maps = []
    for c in range(NCORES):
        m = dict(shared)
        xs = x_sample[c * 16:(c + 1) * 16].reshape(128, D)
        m["xin"] = np.ascontiguousarray(np.concatenate([x_prompt[c], xs], axis=0))
        ps = p_sample[0, c * 16:(c + 1) * 16].reshape(128, 256)
        m["pin"] = np.ascontiguousarray(np.concatenate([p_prompt[0, c], ps], axis=0))
        m["ck"] = np.ascontiguousarray(cache_k[0, c * 16:(c + 1) * 16].reshape(16, 128, 128).transpose(1, 0, 2))
        m["cv"] = np.ascontiguousarray(cache_v[0, c * 16:(c + 1) * 16].reshape(16, 128, 128).transpose(1, 0, 2))
        in_maps.append(m)
    res = run_bass_kernel_spmd(nc, in_maps, core_ids=list(range(NCORES)))
    R = res.results
    y_p = np.stack([R[c]["y"][:SEQ] for c in range(NCORES)])
    y_s = np.concatenate([R[c]["y"][SEQ:].reshape(16, 8, D) for c in range(NCORES)], axis=0)
    nkp = np.stack([R[c]["o_kp"].reshape(128, 2, 64) for c in range(NCORES)])[None]
    nvp = np.stack([R[c]["o_vp"].reshape(128, 2, 64) for c in range(NCORES)])[None]
    nks = np.concatenate([R[c]["o_ks"].reshape(16, 8, 2, 64) for c in range(NCORES)], axis=0)[None]
    nvs = np.concatenate([R[c]["o_vs"].reshape(16, 8, 2, 64) for c in range(NCORES)], axis=0)[None]
    nsp = np.stack([R[c]["o_sp"] for c in range(NCORES)])[None]
    nss = np.concatenate([R[c]["o_ss"].reshape(16, 8, 512) for c in range(NCORES)], axis=0)[None]
    return (y_p.astype(np.float32), y_s.astype(np.float32), nkp.astype(np.float32), nvp.astype(np.float32),
            nks.astype(np.float32), nvs.astype(np.float32), nsp.astype(np.float32), nss.astype(np.float32))
```

```python
import numpy as np
from contextlib import ExitStack
import concourse.bass as bass
import concourse.mybir as mybir
from concourse.bass_utils import run_bass_kernel_spmd

F32 = mybir.dt.float32
BF16 = mybir.dt.bfloat16
I32 = mybir.dt.int32
U32 = mybir.dt.uint32
ALU = mybir.AluOpType
AF = mybir.ActivationFunctionType
AX = mybir.AxisListType

NCORES = 8
D = 1024
SEQ = 4096
NPT = SEQ // 128
NT = NPT + 1
NTOK = NT * 128
IN_W = 3840
EPS = 1e-6
NEG = -30000.0
GELU = AF.Gelu_apprx_tanh


class St:
    __slots__ = ("w", "r", "dsem")

    def __init__(self):
        self.w = None
        self.r = []
        self.dsem = None


class T:
    def __init__(self, name, h=None, st=None):
        self.name = name
        self.h = h
        self.st = st if st is not None else St()

    def __getitem__(self, k):
        return self.h[k]


class MK:
    def __init__(self, nc, es):
        self.nc = nc
        self.es_sem = es
        self.es = es
        self.engs = {'pe': nc.tensor, 'act': nc.scalar, 'dve': nc.vector, 'pool': nc.gpsimd, 'sp': nc.sync}
        self.sems = {}
        self.cnt = {}
        self.waited = {k: {} for k in self.engs}
        for k in ('pe', 'act', 'dve', 'pool'):
            self.sems[k] = es.enter_context(nc.semaphore("c_" + k))
            self.cnt[k] = 0
        self.nd = 0
        self.n_ins = 0
        self.rec = None
        self.rate = 4.0
        self.slot = 0
        self.inter = None
        self._in_inter = False
        self._inter_cnt = 0
        self.prev_eng = 'dve'
        self.prev_slot = -100

    def sb(self, name, shape, dtype):
        h = self.es.enter_context(self.nc.sbuf_tensor(name, list(shape), dtype))
        return T(name, h)

    def ps(self, name, shape, dtype=F32):
        h = self.es.enter_context(self.nc.psum_tensor(name, list(shape), dtype))
        return T(name, h)

    def _dsem(self, t):
        st = t.st
        if st.dsem is None:
            key = "d%d" % self.nd
            self.nd += 1
            self.sems[key] = self.es_sem.enter_context(self.nc.semaphore(key))
            self.cnt[key] = 0
            st.dsem = key
        return st.dsem

    def _waits(self, eng, reads, writes, skip_self=False):
        need = {}

        def add(tok):
            if tok is None:
                return
            k, v = tok
            if need.get(k, 0) < v:
                need[k] = v
        for t in reads:
            add(t.st.w)
        for t in writes:
            add(t.st.w)
            for tok in t.st.r:
                add(tok)
        e = self.engs[eng]
        for k, v in need.items():
            if skip_self and k == eng:
                continue
            if self.waited[eng].get(k, 0) >= v:
                continue
            e.wait_ge(self.sems[k], v)
            self.n_ins += 1
            self.waited[eng][k] = v

    def _commit(self, tok, reads, writes):
        for t in reads:
            r = t.st.r
            r.append(tok)
            if len(r) > 48:
                best = {}
                for k, v in r:
                    if best.get(k, 0) < v:
                        best[k] = v
                t.st.r = list(best.items())
        for t in writes:
            t.st.w = tok
            t.st.r = []

    def op(self, eng, fn, reads=(), writes=(), skip_self=False):
        if self.rec is not None:
            self.rec.append(('op', (eng, fn), dict(reads=list(reads), writes=list(writes), skip_self=skip_self), self.rate))
            return None
        self._waits(eng, reads, writes, skip_self)
        ins = fn(self.engs[eng])
        self.cnt[eng] += 1
        ins.then_inc(self.sems[eng], 1)
        self.n_ins += 1
        self._commit((eng, self.cnt[eng]), reads, writes)
        if self.inter is not None and not self._in_inter:
            q, every = self.inter
            self._inter_cnt += 1
            if q and self._inter_cnt % every == 0:
                self._in_inter = True
                kind, a, kw, _rate = q.pop(0)
                getattr(self, kind)(*a, **kw)
                self._in_inter = False
        return ins

    def dma(self, q, out, in_, reads=(), writes=(), semt=None, **kw):
        if self.rec is not None:
            self.rec.append(('dma', (q, out, in_), dict(reads=list(reads), writes=list(writes), semt=semt, **kw), self.rate))
            return None
        self._waits(q, reads, writes)
        key = self._dsem(semt)
        ins = self.engs[q].dma_start(out=out, in_=in_, **kw)
        self.cnt[key] += 16
        ins.then_inc(self.sems[key], 16)
        self.n_ins += 1
        self._commit((key, self.cnt[key]), reads, writes)
        return ins

    def gather(self, out_t, out_ap, table_ap, idx_t, idx_ap):
        self._waits('pool', [idx_t], [out_t])
        key = self._dsem(out_t)
        ins = self.nc.gpsimd.indirect_dma_start(
            out=out_ap, out_offset=None, in_=table_ap,
            in_offset=bass.IndirectOffsetOnAxis(ap=idx_ap, axis=0))
        self.cnt[key] += 16
        ins.then_inc(self.sems[key], 16)
        self.n_ins += 1
        self._commit((key, self.cnt[key]), [idx_t], [out_t])
        return ins

    def settle(self, tiles):
        for t in tiles:
            if t.st.w is not None and t.st.w[0] in self.cnt and t.st.w[0].startswith("d"):
                t.st.w = (t.st.w[0], self.cnt[t.st.w[0]])

    def play(self, q, budget, lag=0):
        self.slot += 1
        while q and budget > 1e-9:
            kind, a, kw, rate = q[0]
            if budget + 1e-9 < 1.0 / rate and budget < 0.999:
                break
            eng = a[0]
            if lag and eng == 'dve' and self.prev_eng != 'dve' and self.slot - self.prev_slot < lag:
                break
            q.pop(0)
            getattr(self, kind)(*a, **kw)
            budget -= 1.0 / rate
            self.prev_eng = eng
            self.prev_slot = self.slot

    def barrier(self):
        for eng in self.engs:
            e = self.engs[eng]
            for k, v in self.cnt.items():
                if v == 0 or self.waited[eng].get(k, 0) >= v:
                    continue
                e.wait_ge(self.sems[k], v)
                self.n_ins += 1
                self.waited[eng][k] = v


def _consts():
    slopes = np.exp2(-np.arange(1, 9, dtype=np.float64)).astype(np.float32)
    s = np.arange(128)[:, None]
    t = np.arange(128)[None, :]
    c = {}
    c["ident"] = np.eye(128, dtype=np.float32)
    bp = np.full((128, 2, 8, 128), NEG, np.float32)
    d0 = (128 + t - s).astype(np.float32)
    d1 = (t - s).astype(np.float32)
    for h in range(8):
        bp[:, 0, h, :] = np.where(s > t, -slopes[h] * d0, NEG)
        bp[:, 1, h, :] = np.where(s <= t, -slopes[h] * d1, NEG)
    c["bias_p"] = bp.reshape(128, 2048)
    bs_, ts_ = s // 8, s % 8
    bq_, tq_ = t // 8, t % 8
    bsn = np.full((128, 8, 128), NEG, np.float32)
    ok = (bs_ == bq_) & (ts_ <= tq_)
    for h in range(8):
        bsn[:, h, :] = np.where(ok, -slopes[h] * (tq_ - ts_).astype(np.float32), NEG)
    c["bias_sn"] = bsn.reshape(128, 1024)
    r = np.arange(128)[:, None]
    tq = np.arange(8)[None, :]
    bsp = np.full((128, 8, 8), NEG, np.float32)
    for h in range(8):
        bsp[:, h, :] = np.where(r > tq, -slopes[h] * (tq + 128 - r).astype(np.float32), NEG)
    c["bias_sp"] = bsp.reshape(128, 64)
    c["maskT_p"] = (s <= t).astype(np.float32)
    c["maskT_s"] = ok.astype(np.float32)
    c["iota16"] = np.broadcast_to(np.arange(16, dtype=np.float32)[None, :], (128, 16)).copy()
    return c


CONST_SHAPES = {"ident": (128, 128), "bias_p": (128, 2048), "bias_sn": (128, 1024), "bias_sp": (128, 64),
                "maskT_p": (128, 128), "maskT_s": (128, 128), "iota16": (128, 16)}
SMALL_SHAPES = {"g_attn": (128, 1024), "g_ffn": (128, 1024), "g_ple": (128, 1024), "gq": (128, 64), "gk": (128, 64),
                "sinks": (128, 8), "ln_g": (128, 512), "ln_b": (128, 512),
                "bcol_p": (128, 4), "bcol_s": (128, 4), "wsT_p": (128, 512), "wsT_s": (128, 512)}


CFG = {}


class _StopTile(Exception):
    pass


def chk(n):
    if CFG.get('stage', 99) < n:
        raise _StopTile()


def build():
    nc = bass.Bass("TRN2", target_bir_lowering=False)
    cfg = CFG
    tiles = cfg.get('tiles', list(range(NT)))

    def din(name, shape, dt=F32):
        return nc.dram_tensor(name, list(shape), dt, kind="ExternalInput").ap()

    def dout(name, shape, dt=F32):
        return nc.dram_tensor(name, list(shape), dt, kind="ExternalOutput").ap()

    xin = din("xin", [NTOK, D])
    pin = din("pin", [NTOK, 256])
    ck = din("ck", [128, 16, 128])
    cv = din("cv", [128, 16, 128])
    w_in = din("w_in", [D, IN_W])
    w_a = din("w_a", [512, D])
    w_b = din("w_b", [512, D])
    w_out = din("w_out", [D, D])
    w_q = din("w_q", [D, 2048])
    keysT = din("keysT", [128, 2048])
    peer_u = din("peer_u", [16384, D])
    peer_v = din("peer_v", [16384, D])
    w_ple = din("w_ple", [256, D])
    w_pg = din("w_pg", [D, D])
    cin = {k: din("c_" + k, v) for k, v in CONST_SHAPES.items()}
    sin = {k: din("s_" + k, v) for k, v in SMALL_SHAPES.items()}

    y = dout("y", [NTOK, D])
    o_kp = dout("o_kp", [128, 128])
    o_vp = dout("o_vp", [128, 128])
    o_ks = dout("o_ks", [128, 128])
    o_vs = dout("o_vs", [128, 128])
    o_sp = dout("o_sp", [128, 512])
    o_ss = dout("o_ss", [128, 512])
    x1s = nc.dram_tensor("x1s", [NTOK, D], F32, kind="Internal").ap()
    comb = nc.dram_tensor("comb", [16384, 2 * D], BF16, kind="Internal").ap()
    dbg_x1 = dout("dbg_x1", [NTOK, D]) if cfg.get('dbg') else None
    dbg_am = dout("dbg_am", [NTOK, D]) if cfg.get('dbg') else None

    with ExitStack() as es0:
        mk = MK(nc, es0)
        x1_tok = [T("x1s%d" % i) for i in range(NT)]
        out_tiles = []

        ident = mk.sb("ident", [128, 128], BF16)
        cgrp = T("cgrp")
        mk.dma('pool', ident[:], cin["ident"][:, :], writes=[ident], semt=ident)
        pbanks = [mk.ps("pb%d" % i, [128, 512], F32) for i in range(8)]

        def pe_T(bank, col0, src_t, src_ap, n=128, extra_reads=()):
            pv = bank[:].bitcast(BF16)
            return lambda e: e.transpose(out=pv[0:n, col0:col0 + 128], in_=src_ap, identity=ident[:])

        epst = mk.sb("epst", [128, 1], F32)
        mk.op('pool', lambda e: e.memset(epst[:], EPS), writes=[epst])

        def rsqrt(tile, dst, src, scale):
            mk.op('act', lambda e: e.activation(out=dst, in_=src, func=AF.Sqrt, bias=epst[:, 0:1], scale=scale),
                  reads=[tile, epst], writes=[tile])
            mk.op('dve', lambda e: e.reciprocal(out=dst, in_=dst), reads=[tile], writes=[tile])

        with ExitStack() as es1:
            mk.es = es1
            warena = mk.sb("warena1", [128, 8 * IN_W + 4 * D + 4 * D + 8 * D], BF16)
            o = 0
            win_v = warena[:, o:o + 8 * IN_W].rearrange("p (c n) -> p c n", c=8); o += 8 * IN_W
            wa_v = warena[:, o:o + 4 * D].rearrange("p (c n) -> p c n", c=4); o += 4 * D
            wb_v = warena[:, o:o + 4 * D].rearrange("p (c n) -> p c n", c=4); o += 4 * D
            wo_v = warena[:, o:o + 8 * D].rearrange("p (c n) -> p c n", c=8); o += 8 * D
            ZG = [(0, 512), (512, 256), (768, 512), (1280, 512), (1792, 512), (2304, 512), (2816, 512), (3328, 512)]
            wg = [T("wg%d" % i) for i in range(8)]
            w_in_r = w_in.rearrange("(c p) n -> p c n", p=128)
            for gi, (c0, n) in enumerate(ZG):
                for c in range(8):
                    mk.dma('pool', win_v[:, c, c0:c0 + n], w_in_r[:, c, c0:c0 + n], semt=wg[gi])
                wg[gi].st.w = (wg[gi].st.dsem, mk.cnt[wg[gi].st.dsem])
            w_a_r = w_a.rearrange("(c p) n -> p c n", p=128)
            w_b_r = w_b.rearrange("(c p) n -> p c n", p=128)
            w_o_r = w_out.rearrange("(c p) n -> p c n", p=128)
            wab = T("wab")
            wot = T("wot")
            for c in range(4):
                mk.dma('pool', wa_v[:, c, :], w_a_r[:, c, :], semt=wab)
                mk.dma('pool', wb_v[:, c, :], w_b_r[:, c, :], semt=wab)
            wab.st.w = (wab.st.dsem, mk.cnt[wab.st.dsem])
            for c in range(8):
                mk.dma('pool', wo_v[:, c, :], w_o_r[:, c, :], semt=wot)
            wot.st.w = (wot.st.dsem, mk.cnt[wot.st.dsem])
            comb_t = T("comb")
            for c in range(16):
                rs = slice(c * 1024, (c + 1) * 1024)
                mk.dma('pool', comb[rs, 0:D], peer_u[rs, :], semt=comb_t)
                mk.dma('pool', comb[rs, D:2 * D], peer_v[rs, :], semt=comb_t)
            comb_t.st.w = (comb_t.st.dsem, mk.cnt[comb_t.st.dsem])

            def cload(name, src, shape, dt=F32, q='sp'):
                t = mk.sb(name, shape, dt)
                mk.dma(q, t[:], src[:, :], writes=[t], semt=cgrp)
                return t
            g_attn = cload("g_attn", sin["g_attn"], [128, 1024])
            gq = cload("gq", sin["gq"], [128, 64])
            gk = cload("gk", sin["gk"], [128, 64])
            sinks = cload("sinks", sin["sinks"], [128, 8])
            ln_g = cload("ln_g", sin["ln_g"], [128, 512])
            ln_b = cload("ln_b", sin["ln_b"], [128, 512])
            bcol_p = cload("bcol_p", sin["bcol_p"], [128, 4])
            bcol_s = cload("bcol_s", sin["bcol_s"], [128, 4])
            bias_p = cload("bias_p", cin["bias_p"], [128, 2048])
            bias_sn = cload("bias_sn", cin["bias_sn"], [128, 1024])
            bias_sp = cload("bias_sp", cin["bias_sp"], [128, 64])
            maskT_p = cload("maskT_p", cin["maskT_p"], [128, 128])
            maskT_s = cload("maskT_s", cin["maskT_s"], [128, 128])
            t1 = mk.sb("t1", [128, D], F32)
            ws32_p = T("ws32_p", t1[:, 0:512], st=t1.st)
            ws32_s = T("ws32_s", t1[:, 512:1024], st=t1.st)
            mk.dma('sp', ws32_p[:], sin["wsT_p"][:, :], writes=[ws32_p], semt=cgrp)
            mk.dma('sp', ws32_s[:], sin["wsT_s"][:, :], writes=[ws32_s], semt=cgrp)
            consts1 = [ident, g_attn, gq, gk, sinks, ln_g, ln_b, bcol_p, bcol_s, bias_p, bias_sn, bias_sp,
                       maskT_p, maskT_s, ws32_p, ws32_s]
            mk.settle(consts1)

            sinkexp = mk.sb("sinkexp", [128, 8], F32)
            mk.op('act', lambda e: e.activation(out=sinkexp[:], in_=sinks[:], func=AF.Exp), reads=[sinks], writes=[sinkexp])
            gq8 = mk.sb("gq8", [128, 64], F32)
            mk.op('dve', lambda e: e.tensor_scalar(out=gq8[:], in0=gq[:], scalar1=0.125, scalar2=None, op0=ALU.mult),
                  reads=[gq], writes=[gq8])
            wsT_p = mk.sb("wsT_p", [128, 512], BF16)
            wsT_s = mk.sb("wsT_s", [128, 512], BF16)
            mk.op('dve', lambda e: e.tensor_tensor(out=wsT_p[:].rearrange("s (g t) -> s g t", g=4),
                                                   in0=ws32_p[:].rearrange("s (g t) -> s g t", g=4),
                                                   in1=maskT_p[:].unsqueeze(1).to_broadcast([128, 4, 128]), op=ALU.mult),
                  reads=[ws32_p, maskT_p], writes=[wsT_p])
            mk.op('dve', lambda e: e.tensor_tensor(out=wsT_s[:].rearrange("s (g t) -> s g t", g=4),
                                                   in0=ws32_s[:].rearrange("s (g t) -> s g t", g=4),
                                                   in1=maskT_s[:].unsqueeze(1).to_broadcast([128, 4, 128]), op=ALU.mult),
                  reads=[ws32_s, maskT_s], writes=[wsT_s])

            xt = [mk.sb("xt%d" % i, [128, D], F32) for i in range(3)]
            x1t = [mk.sb("x1t%d" % i, [128, D], F32) for i in range(2)]
            xn = mk.sb("xn", [128, D], BF16)
            xnT = mk.sb("xnT", [128, D], BF16)
            st1 = mk.sb("st1", [128, 16], F32)
            q32 = mk.sb("q32", [128, 512], F32)
            qn = mk.sb("qn", [128, 512], BF16)
            qT = mk.sb("qT", [128, 512], BF16)
            k32 = mk.sb("k32", [128, 128], F32)
            kn32 = mk.sb("kn32", [128, 128], F32)
            knb = mk.sb("knb", [128, 128], BF16)
            kT = [mk.sb("kT%d" % i, [128, 128], BF16) for i in range(2)]
            v32 = mk.sb("v32", [128, 128], F32)
            vaug = [mk.sb("vaug%d" % i, [128, 2 * 65], BF16) for i in range(2)]
            sc32 = mk.sb("sc32", [128, 1024], F32)
            pT = mk.sb("pT", [128, 2048], BF16)
            att_ = [mk.sb("att%d" % i, [128, 512], BF16) for i in range(2)]
            u32 = mk.sb("u32", [128, 512], F32)
            gv32 = mk.sb("gv32", [128, 512], F32)
            vn32 = mk.sb("vn32", [128, 512], F32)
            vnb = mk.sb("vnb", [128, 512], BF16)
            mb_ = [mk.sb("mb%d" % i, [128, 512], BF16) for i in range(2)]
            amT = mk.sb("amT", [128, 1024], BF16)
            siga_ = [mk.sb("siga%d" % i, [128, D], BF16) for i in range(2)]
            sigb_ = [mk.sb("sigb%d" % i, [128, D], BF16) for i in range(2)]
            hb = mk.sb("hb", [128, D], BF16)
            hT = mk.sb("hT", [128, D], BF16)
            stage = mk.sb("stage", [128, 2048], F32)
            qsq = T("qsq", stage[:, 1024:1536], st=stage.st)
            ksq = T("ksq", stage[:, 1536:1664], st=stage.st)
            ckb = T("ckb", pT[:], st=pT.st)
            kTp = mk.sb("kTp", [128, 2048], BF16)
            vpaug = mk.sb("vpaug", [128, 16 * 130], BF16)
            qTs = mk.sb("qTs", [128, 512], BF16)
            Eb = [mk.sb("Eb%d" % i, [128, 1024], BF16) for i in range(2)]
            print("phase1 sbuf remaining", nc.sbuf_bytes_remaining, flush=True)

            for i in range(2):
                mk.op('dve', lambda e, i=i: e.memset(vaug[i][:], 1.0), writes=[vaug[i]])
            mk.op('dve', lambda e: e.memset(vpaug[:], 1.0), writes=[vpaug])

            B = pbanks

            def load_x(ti):
                mk.dma('sp', xt[ti % 3][:], xin[ti * 128:(ti + 1) * 128, :], writes=[xt[ti % 3]], semt=xt[ti % 3])

            load_x(tiles[0])

            def tr8(e, src, bank):
                last = None
                for c in range(8):
                    last = pe_T(bank, c * 128, src, src[:, c * 128:(c + 1) * 128])(e)
                return last

            def emit_F1(tj):
                x = xt[tj % 3]
                mk.op('act', lambda e: e.activation(out=xn[:], in_=x[:], func=AF.Square, accum_out=st1[:, 0:1]),
                      reads=[x], writes=[xn, st1])
                rsqrt(st1, st1[:, 2:3], st1[:, 0:1], 1.0 / D)
                mk.op('dve', lambda e: e.scalar_tensor_tensor(out=xn[:], in0=x[:], scalar=st1[:, 2:3], in1=g_attn[:],
                                                              op0=ALU.mult, op1=ALU.mult), reads=[x, st1, g_attn], writes=[xn])
                mk.op('pe', lambda e: tr8(e, src=xn, bank=B[0]), reads=[xn, ident], writes=[B[0]], skip_self=True)
                mk.op('act', lambda e: e.copy(out=xnT[:], in_=B[0][:].bitcast(BF16)), reads=[B[0]], writes=[xnT])


            def tile_body(tpos, ti, prevE):
                try:
                    sample = (ti == NPT)
                    x = xt[ti % 3]
                    att = att_[ti % 2]
                    mb = mb_[ti % 2]
                    siga = siga_[ti % 2]
                    sigb = sigb_[ti % 2]
                    if tpos + 1 < len(tiles):
                        load_x(tiles[tpos + 1])
                    cur, prv = ti % 2, (ti + 1) % 2
                    if tpos == 0:
                        emit_F1(ti)
                    chk(1)
                    xnT3 = xnT[:].rearrange("p (c t) -> p c t", c=8)

                    def zgroup(bank, c0, n):
                        def f(e):
                            last = None
                            for c in range(8):
                                last = e.matmul(bank[:, 0:n], lhsT=xnT3[:, c, :], rhs=win_v[:, c, c0:c0 + n],
                                                start=(c == 0), stop=(c == 7))
                            return last
                        mk.op('pe', f, reads=[xnT, wg[[z[0] for z in ZG].index(c0)]], writes=[bank], skip_self=True)

                    zgroup(B[1], 0, 512)
                    mk.op('act', lambda e: e.copy(out=q32[:], in_=B[1][:]), reads=[B[1]], writes=[q32])
                    mk.op('act', lambda e: e.activation(out=qsq[:], in_=B[1][:], func=AF.Square), reads=[B[1]], writes=[qsq])
                    zgroup(B[2], 512, 256)
                    mk.op('act', lambda e: e.copy(out=k32[:], in_=B[2][:, 0:128]), reads=[B[2]], writes=[k32])
                    mk.op('act', lambda e: e.activation(out=ksq[:], in_=B[2][:, 0:128], func=AF.Square), reads=[B[2]], writes=[ksq])
                    mk.op('act', lambda e: e.copy(out=v32[:], in_=B[2][:, 128:256]), reads=[B[2]], writes=[v32])
                    zgroup(B[3], 768, 512)
                    mk.op('act', lambda e: e.activation(out=u32[:], in_=B[3][:], func=GELU), reads=[B[3]], writes=[u32])
                    zgroup(B[1], 1280, 512)
                    mk.op('act', lambda e: e.activation(out=gv32[:], in_=B[1][:], func=GELU, accum_out=st1[:, 4:5]),
                          reads=[B[1]], writes=[gv32, st1])
                    zgroup(B[2], 1792, 512)
                    mk.op('act', lambda e: e.activation(out=siga[:, 0:512], in_=B[2][:], func=AF.Sigmoid), reads=[B[2]], writes=[siga])
                    zgroup(B[3], 2304, 512)
                    mk.op('act', lambda e: e.activation(out=siga[:, 512:1024], in_=B[3][:], func=AF.Sigmoid), reads=[B[3]], writes=[siga])
                    zgroup(B[1], 2816, 512)
                    mk.op('act', lambda e: e.activation(out=sigb[:, 0:512], in_=B[1][:], func=AF.Sigmoid), reads=[B[1]], writes=[sigb])
                    zgroup(B[2], 3328, 512)
                    mk.op('act', lambda e: e.activation(out=sigb[:, 512:1024], in_=B[2][:], func=AF.Sigmoid), reads=[B[2]], writes=[sigb])

                    chk(2)
                    mk.inter = (prevE, 2)
                    mk.op('dve', lambda e: e.tensor_reduce(out=st1[:, 8:16], in_=qsq[:].rearrange("t (h d) -> t h d", h=8),
                                                           axis=AX.X, op=ALU.add), reads=[qsq], writes=[st1])
                    rsqrt(st1, st1[:, 8:16], st1[:, 8:16], 1.0 / 64)
                    mk.op('dve', lambda e: e.tensor_tensor(out=q32[:].rearrange("t (h d) -> t h d", h=8),
                                                           in0=q32[:].rearrange("t (h d) -> t h d", h=8),
                                                           in1=st1[:, 8:16].unsqueeze(2).to_broadcast([128, 8, 64]), op=ALU.mult),
                          reads=[q32, st1], writes=[q32])
                    mk.op('dve', lambda e: e.tensor_tensor(out=qn[:].rearrange("t (p w d) -> t w p d", p=4, w=2),
                                                           in0=q32[:].rearrange("t (w p d) -> t w p d", w=2, p=4),
                                                           in1=gq8[:].unsqueeze(1).unsqueeze(1).to_broadcast([128, 2, 4, 64]),
                                                           op=ALU.mult), reads=[q32, gq8], writes=[qn])
                    mk.op('dve', lambda e: e.tensor_reduce(out=st1[:, 5:7], in_=ksq[:].rearrange("t (h d) -> t h d", h=2),
                                                           axis=AX.X, op=ALU.add), reads=[ksq], writes=[st1])
                    rsqrt(st1, st1[:, 5:7], st1[:, 5:7], 1.0 / 64)
                    mk.op('dve', lambda e: e.tensor_tensor(out=k32[:].rearrange("t (h d) -> t h d", h=2),
                                                           in0=k32[:].rearrange("t (h d) -> t h d", h=2),
                                                           in1=st1[:, 5:7].unsqueeze(2).to_broadcast([128, 2, 64]), op=ALU.mult),
                          reads=[k32, st1], writes=[k32])
                    mk.op('dve', lambda e: e.tensor_tensor(out=kn32[:].rearrange("t (h d) -> t h d", h=2),
                                                           in0=k32[:].rearrange("t (h d) -> t h d", h=2),
                                                           in1=gk[:].unsqueeze(1).to_broadcast([128, 2, 64]), op=ALU.mult),
                          reads=[k32, gk], writes=[kn32])
                    mk.op('act', lambda e: e.copy(out=knb[:], in_=kn32[:]), reads=[kn32], writes=[knb])
                    mk.op('act', lambda e: e.copy(out=vaug[cur][:].rearrange("t (g e) -> t g e", g=2)[:, :, 0:64],
                                                  in_=v32[:].rearrange("t (g d) -> t g d", g=2)), reads=[v32], writes=[vaug[cur]])
                    chk(5)
                    mk.op('act', lambda e: e.activation(out=sc32[:, 0:512], in_=gv32[:], func=AF.Square, accum_out=st1[:, 3:4]),
                          reads=[gv32], writes=[sc32, st1])
                    mk.op('dve', lambda e: e.tensor_scalar(out=st1[:, 4:5], in0=st1[:, 4:5], scalar1=1.0 / 512, scalar2=None,
                                                           op0=ALU.mult), reads=[st1], writes=[st1])
                    mk.op('dve', lambda e: e.tensor_tensor(out=st1[:, 7:8], in0=st1[:, 4:5], in1=st1[:, 4:5], op=ALU.mult),
                          reads=[st1], writes=[st1])
                    mk.op('dve', lambda e: e.scalar_tensor_tensor(out=st1[:, 3:4], in0=st1[:, 3:4], scalar=1.0 / 512, in1=st1[:, 7:8],
                                                                  op0=ALU.mult, op1=ALU.subtract), reads=[st1], writes=[st1])
                    rsqrt(st1, st1[:, 3:4], st1[:, 3:4], 1.0)
                    mk.op('dve', lambda e: e.tensor_scalar(out=vn32[:], in0=gv32[:], scalar1=st1[:, 4:5], scalar2=st1[:, 3:4],
                                                           op0=ALU.subtract, op1=ALU.mult), reads=[gv32, st1], writes=[vn32])
                    mk.op('dve', lambda e: e.tensor_tensor(out=vn32[:], in0=vn32[:], in1=ln_g[:], op=ALU.mult),
                          reads=[vn32, ln_g], writes=[vn32])
                    mk.op('dve', lambda e: e.tensor_tensor(out=vn32[:], in0=vn32[:], in1=ln_b[:], op=ALU.add),
                          reads=[vn32, ln_b], writes=[vn32])
                    mk.op('act', lambda e: e.copy(out=vnb[:], in_=vn32[:]), reads=[vn32], writes=[vnb])
                    wsT = wsT_s if sample else wsT_p
                    bcol = bcol_s if sample else bcol_p

                    def fsgu(e, wsT=wsT):
                        last = None
                        for g in range(4):
                            last = e.matmul(B[3][:, g * 128:(g + 1) * 128], lhsT=wsT[:, g * 128:(g + 1) * 128],
                                            rhs=vnb[:, g * 128:(g + 1) * 128], start=True, stop=True)
                        return last
                    mk.op('pe', fsgu, reads=[wsT, vnb], writes=[B[3]], skip_self=True)
                    for g in range(4):
                        mk.op('dve', lambda e, g=g, bcol=bcol: e.scalar_tensor_tensor(
                            out=mb[:, g * 128:(g + 1) * 128], in0=B[3][:, g * 128:(g + 1) * 128], scalar=bcol[:, g:g + 1],
                            in1=u32[:, g * 128:(g + 1) * 128], op0=ALU.add, op1=ALU.mult), reads=[B[3], bcol, u32], writes=[mb])

                    chk(3)
                    def trq(e):
                        last = None
                        for p in range(4):
                            last = pe_T(B[0], p * 128, qn, qn[:, p * 128:(p + 1) * 128])(e)
                        last = pe_T(B[0], 512, knb, knb[:])(e)
                        return last
                    mk.op('pe', trq, reads=[qn, knb, ident], writes=[B[0]], skip_self=True)
                    b0v = B[0][:].bitcast(BF16)
                    mk.op('act', lambda e: e.copy(out=qT[:], in_=b0v[:, 0:512]), reads=[B[0]], writes=[qT])
                    mk.op('act', lambda e: e.copy(out=kT[cur][:], in_=b0v[:, 512:640]), reads=[B[0]], writes=[kT[cur]])
                    qT3 = qT[:].rearrange("p (q t) -> p q t", q=4)

                    chk(4)
                    pT4 = pT[:].rearrange("s (b h t) -> s b h t", b=2, h=8)
                    blks = [1] if (ti == 0 or sample) else [0, 1]
                    for blk in blks:
                        kt = kT[cur] if blk == 1 else kT[prv]
                        banks = (B[4], B[5])

                        def fsc(e, kt=kt, banks=banks):
                            last = None
                            for h in range(8):
                                w, p = h // 4, h % 4
                                bank = banks[h // 4]
                                last = e.matmul(bank[:, (h % 4) * 128:(h % 4 + 1) * 128], lhsT=kt[w * 64:(w + 1) * 64, :],
                                                rhs=qT3[w * 64:(w + 1) * 64, p, :], start=True, stop=True)
                            return last
                        mk.op('pe', fsc, reads=[kt, qT], writes=list(banks), skip_self=True)
                        bsrc = bias_sn if sample else bias_p
                        boff = 0 if sample else blk * 1024
                        for hb_ in range(2):
                            mk.op('dve', lambda e, hb_=hb_, banks=banks, bsrc=bsrc, boff=boff: e.tensor_tensor(
                                out=sc32[:, hb_ * 512:(hb_ + 1) * 512], in0=banks[hb_][:],
                                in1=bsrc[:, boff + hb_ * 512: boff + (hb_ + 1) * 512], op=ALU.add),
                                reads=[banks[hb_], bsrc], writes=[sc32])
                        mk.op('act', lambda e, blk=blk: e.activation(out=pT[:, blk * 1024:(blk + 1) * 1024], in_=sc32[:], func=AF.Exp),
                              reads=[sc32], writes=[pT])

                    def pv_out(h):
                        return B[1 + h // 4][:, (h % 4) * 128:(h % 4) * 128 + 65]

                    def fpv(e):
                        last = None
                        for h in range(8):
                            w = h // 4
                            for bi, blk in enumerate(blks):
                                va = vaug[cur] if blk == 1 else vaug[prv]
                                st_ = (bi == 0) if not sample else (h % 4 == 0)
                                last = e.matmul(pv_out(h), lhsT=pT4[:, blk, h, :], rhs=va[:, w * 65:(w + 1) * 65],
                                                start=st_, stop=(bi == len(blks) - 1 and not sample), skip_group_check=sample)
                        return last
                    mk.op('pe', fpv, reads=[pT, vaug[0], vaug[1]], writes=[B[1], B[2]], skip_self=True)

                    if sample:
                        chk(4.1)
                        mk.inter = None
                        mk.play(prevE, 10 ** 9)
                        mk.dma('sp', stage[:].rearrange("s (b c) -> s b c", b=16), ck[:, :, :], writes=[stage], semt=stage)
                        mk.op('act', lambda e: e.copy(out=ckb[:], in_=stage[:]), reads=[stage], writes=[ckb])
                        for half in range(2):
                            bank = B[4 + half * 2], B[5 + half * 2]

                            def ftr(e, half=half, bank=bank):
                                last = None
                                for bb in range(8):
                                    b_ = half * 8 + bb
                                    last = pe_T(bank[bb // 4], (bb % 4) * 128, ckb, ckb[:, b_ * 128:(b_ + 1) * 128])(e)
                                return last
                            mk.op('pe', ftr, reads=[ckb, ident], writes=list(bank), skip_self=True)
                            for q_ in range(2):
                                mk.op('act', lambda e, half=half, q_=q_, bank=bank: e.copy(
                                    out=kTp[:, (half * 8 + q_ * 4) * 128:(half * 8 + q_ * 4 + 4) * 128],
                                    in_=bank[q_][:].bitcast(BF16)[:, 0:512]), reads=[bank[q_]], writes=[kTp])
                        chk(4.2)
                        if not cfg.get('skip_vdma'):
                            mk.dma('sp', stage[:].rearrange("s (b c) -> s b c", b=16), cv[:, :, :], writes=[stage], semt=stage)
                        if cfg.get('skip_vcopy'):
                            raise _StopTile()
                        for g_ in range(2):
                            mk.op('dve', lambda e, g_=g_: e.tensor_copy(
                                out=vpaug[:].rearrange("s (b g e) -> s b g e", b=16, g=2)[:, :, g_, 0:64],
                                in_=stage[:].rearrange("s (b g d) -> s b g d", b=16, g=2)[:, :, g_, :]),
                                reads=[stage], writes=[vpaug])
                        chk(4.3)
                        kTp3 = kTp[:].rearrange("p (b s) -> p b s", b=16)
                        mk.op('act', lambda e: e.copy(out=qTs[:].rearrange("p (b q t) -> p b q t", b=16, q=4),
                                                      in_=qT[:].rearrange("p (q b t) -> p b q t", q=4, b=16)),
                              reads=[qT], writes=[qTs])
                        chk(4.31)

                        def fps(e):
                            last = None
                            for w in range(2):
                                bank = B[4 + w]
                                for b_ in range(16):
                                    c0 = b_ * 32
                                    last = e.matmul(bank[:, c0:c0 + 32], lhsT=kTp3[w * 64:(w + 1) * 64, b_, :],
                                                    rhs=qTs[w * 64:(w + 1) * 64, b_ * 32:(b_ + 1) * 32], start=True, stop=True)
                            return last
                        mk.op('pe', fps, reads=[kTp, qTs], writes=[B[4], B[5]], skip_self=True)
                        chk(4.32)
                        for half in range(2):
                            mk.op('dve', lambda e, half=half: e.tensor_tensor(
                                out=sc32[:, half * 512:(half + 1) * 512].rearrange("s (b c) -> s b c", b=16),
                                in0=B[4 + half][:].rearrange("s (b c) -> s b c", b=16),
                                in1=bias_sp[:, half * 32:(half + 1) * 32].unsqueeze(1).to_broadcast([128, 16, 32]), op=ALU.add),
                                reads=[B[4 + half], bias_sp], writes=[sc32])
                        chk(4.4)
                        sc5 = sc32[:].rearrange("s (w b p t) -> s b w p t", w=2, b=16, p=4)
                        vp4 = vpaug[:].rearrange("s (b g e) -> s b g e", b=16, g=2)
                        for b_ in range(16):
                            E = Eb[b_ % 2]
                            E3 = E[:].rearrange("s (h t) -> s h t", h=8)
                            mk.op('pool', lambda e, E=E: e.memset(E[:], 0.0), writes=[E])
                            mk.op('act', lambda e, E=E, b_=b_: e.activation(
                                out=E[:].rearrange("s (w p t) -> s w p t", w=2, p=4)[:, :, :, b_ * 8:(b_ + 1) * 8],
                                in_=sc5[:, b_, :, :, :], func=AF.Exp), reads=[sc32], writes=[E])

                            def fpp(e, E3=E3, b_=b_):
                                last = None
                                for h in range(8):
                                    last = e.matmul(pv_out(h), lhsT=E3[:, h, :], rhs=vp4[:, b_, h // 4, :],
                                                    start=False, stop=(b_ == 15), skip_group_check=True)
                                return last
                            mk.op('pe', fpp, reads=[E, vpaug], writes=[B[1], B[2]], skip_self=True)

                    chk(4.5)
                    for half in range(2):
                        bank = B[1 + half]
                        b3 = bank[:].rearrange("t (h e) -> t h e", h=4)
                        mk.op('dve', lambda e, b3=b3, half=half: e.tensor_tensor(
                            out=st1[:, 8 + half * 4: 12 + half * 4].unsqueeze(2), in0=b3[:, :, 64:65],
                            in1=sinkexp[:, half * 4:(half + 1) * 4].unsqueeze(2), op=ALU.add),
                            reads=[bank, sinkexp], writes=[st1])
                    mk.op('dve', lambda e: e.reciprocal(out=st1[:, 8:16], in_=st1[:, 8:16]), reads=[st1], writes=[st1])
                    for half in range(2):
                        bank = B[1 + half]
                        b3 = bank[:].rearrange("t (h e) -> t h e", h=4)
                        mk.op('dve', lambda e, b3=b3, half=half: e.tensor_tensor(
                            out=att[:, half * 256:(half + 1) * 256].rearrange("t (h d) -> t h d", h=4), in0=b3[:, :, 0:64],
                            in1=st1[:, 8 + half * 4: 12 + half * 4].unsqueeze(2).to_broadcast([128, 4, 64]), op=ALU.mult),
                            reads=[bank, st1], writes=[att])

                    if tpos + 1 < len(tiles):
                        emit_F1(tiles[tpos + 1])
                    chk(6)
                    if ti == NPT - 1 or sample:
                        ok_, ov_, os_ = (o_ks, o_vs, o_ss) if sample else (o_kp, o_vp, o_sp)
                        mk.dma('sp', ok_[:, :], kn32[:], reads=[kn32], semt=kn32)
                        mk.dma('sp', ov_[:, :], v32[:], reads=[v32], semt=v32)
                        mk.dma('sp', os_[:, :], vn32[:], reads=[vn32], semt=vn32)
                        out_tiles.extend([kn32, v32, vn32])

                    chk(7)
                    mk.inter = None
                    mk.play(prevE, 10 ** 9)
                    if dbg_am is not None:
                        mk.op('dve', lambda e: e.tensor_copy(out=t1[:, 0:512], in_=att[:]), reads=[att], writes=[t1])
                        mk.op('dve', lambda e: e.tensor_copy(out=t1[:, 512:1024], in_=mb[:]), reads=[mb], writes=[t1])
                        mk.dma('sp', dbg_am[ti * 128:(ti + 1) * 128, :], t1[:], reads=[t1], semt=t1)
                    mk.rec = []
                    mk.rate = 1.0
                    def tram(e):
                        last = None
                        for c in range(4):
                            last = pe_T(B[0], c * 128, att, att[:, c * 128:(c + 1) * 128])(e)
                        for c in range(4):
                            last = pe_T(B[0], 512 + c * 128, mb, mb[:, c * 128:(c + 1) * 128])(e)
                        return last
                    mk.op('pe', tram, reads=[att, mb, ident], writes=[B[0]], skip_self=True)
                    mk.op('act', lambda e: e.copy(out=amT[:], in_=B[0][:].bitcast(BF16)), reads=[B[0]], writes=[amT])
                    amT3 = amT[:].rearrange("p (c t) -> p c t", c=8)

                    def fmerge_ab(e, w_v, off):
                        last = None
                        for n in range(2):
                            for c in range(4):
                                last = e.matmul(B[6 + n][:], lhsT=amT3[:, off + c, :], rhs=w_v[:, c, n * 512:(n + 1) * 512],
                                                start=(c == 0), stop=(c == 3))
                        return last
                    mk.op('pe', lambda e: fmerge_ab(e, wa_v, 0), reads=[amT, wab], writes=[B[6], B[7]], skip_self=True)
                    for n in range(2):
                        sl = slice(n * 512, (n + 1) * 512)
                        mk.op('dve', lambda e, n=n, sl=sl: e.tensor_tensor(out=t1[:, sl], in0=B[6 + n][:], in1=siga[:, sl], op=ALU.mult),
                              reads=[B[6 + n], siga], writes=[t1])
                    mk.op('pe', lambda e: fmerge_ab(e, wb_v, 4), reads=[amT, wab], writes=[B[6], B[7]], skip_self=True)
                    for n in range(2):
                        sl = slice(n * 512, (n + 1) * 512)
                        mk.op('dve', lambda e, n=n, sl=sl: e.tensor_tensor(out=stage[:, sl], in0=B[6 + n][:], in1=sigb[:, sl], op=ALU.mult),
                              reads=[B[6 + n], sigb], writes=[stage])
                    mk.op('dve', lambda e: e.tensor_tensor(out=hb[:], in0=stage[:, 0:1024], in1=t1[:], op=ALU.add), reads=[stage, t1], writes=[hb])
                    mk.op('pe', lambda e: tr8(e, src=hb, bank=B[0]), reads=[hb, ident], writes=[B[0]], skip_self=True)
                    mk.op('act', lambda e: e.copy(out=hT[:], in_=B[0][:].bitcast(BF16)), reads=[B[0]], writes=[hT])
                    hT3 = hT[:].rearrange("p (c t) -> p c t", c=8)

                    def fwo(e):
                        last = None
                        for n in range(2):
                            for c in range(8):
                                last = e.matmul(B[6 + n][:], lhsT=hT3[:, c, :], rhs=wo_v[:, c, n * 512:(n + 1) * 512],
                                                start=(c == 0), stop=(c == 7))
                        return last
                    mk.op('pe', fwo, reads=[hT, wot], writes=[B[6], B[7]], skip_self=True)
                    x1 = x1t[ti % 2]
                    for n in range(2):
                        sl = slice(n * 512, (n + 1) * 512)
                        mk.op('dve', lambda e, n=n, sl=sl, x1=x1: e.tensor_tensor(out=x1[:, sl], in0=B[6 + n][:], in1=x[:, sl], op=ALU.add),
                              reads=[B[6 + n], x], writes=[x1])
                    mk.dma('sp', x1s[ti * 128:(ti + 1) * 128, :], x1[:], reads=[x1], writes=[x1_tok[ti]], semt=x1)
                    if dbg_x1 is not None:
                        mk.dma('sp', dbg_x1[ti * 128:(ti + 1) * 128, :], x1[:], reads=[x1], semt=x1)
                    pend = mk.rec
                    mk.rec = None
                    return pend
                except _StopTile:
                    mk.rec = None
                    mk.inter = None
                    return []

            prevE = []
            for tpos, ti in enumerate(tiles):
                prevE = tile_body(tpos, ti, prevE)
            mk.play(prevE, 10 ** 9)

            mk.barrier()
        with ExitStack() as es2:
            mk.es = es2
            warena2 = mk.sb("warena2", [128, 8 * 2048 + 2048 + 8 * D + 2 * D], BF16)
            o = 0
            wq_v = warena2[:, o:o + 8 * 2048].rearrange("p (c n) -> p c n", c=8); o += 8 * 2048
            keys_v = warena2[:, o:o + 2048].rearrange("p (g n) -> p g n", g=16); o += 2048
            wpg_v = warena2[:, o:o + 8 * D].rearrange("p (c n) -> p c n", c=8); o += 8 * D
            wpl_v = warena2[:, o:o + 2 * D].rearrange("p (c n) -> p c n", c=2); o += 2 * D
            wsem2 = T("wsem2")
            w_q_r = w_q.rearrange("(c p) n -> p c n", p=128)
            w_pg_r = w_pg.rearrange("(c p) n -> p c n", p=128)
            w_pl_r = w_ple.rearrange("(c p) n -> p c n", p=128)
            for c in range(8):
                mk.dma('pool', wq_v[:, c, :], w_q_r[:, c, :], semt=wsem2)
                mk.dma('pool', wpg_v[:, c, :], w_pg_r[:, c, :], semt=wsem2)
            for c in range(2):
                mk.dma('pool', wpl_v[:, c, :], w_pl_r[:, c, :], semt=wsem2)
            mk.dma('pool', keys_v, keysT.rearrange("p (g n) -> p g n", g=16), semt=wsem2)
            warena2.st.w = (wsem2.st.dsem, mk.cnt[wsem2.st.dsem])
            cgrp2 = T("cgrp2")

            def cload2(name, src, shape, dt=F32, q='sp'):
                t = mk.sb(name, shape, dt)
                mk.dma(q, t[:], src[:, :], writes=[t], semt=cgrp2)
                return t
            g_ffn = cload2("g_ffn", sin["g_ffn"], [128, 1024])
            g_ple = cload2("g_ple", sin["g_ple"], [128, 1024])
            iota16 = cload2("iota16", cin["iota16"], [128, 16])
            identf = cload2("identf", cin["ident"], [128, 128])
            mk.settle([g_ffn, g_ple, iota16, identf])

            xt2 = [mk.sb("x2t%d" % i, [128, D], F32) for i in range(2)]
            pt2 = [mk.sb("p2t%d" % i, [128, 256], F32) for i in range(2)]
            xn2_ = [mk.sb("xn2_%d" % i, [128, D], F32) for i in range(2)]
            xn2b = mk.sb("xn2b", [128, D], BF16)
            xn2T = mk.sb("xn2T", [128, D], BF16)
            st2 = mk.sb("st2", [128, 16], F32)
            st2A = mk.sb("st2A", [128, 16], F32)
            scp = mk.sb("scp", [128, 2048], F32)
            work = mk.sb("work", [128, 2048], F32)
            qpT = T("qpT", work[:].bitcast(BF16)[:, 0:2048], st=work.st)
            stop_ = mk.sb("stop", [128, 256], F32)
            itopu = mk.sb("itopu", [128, 256], U32)
            itopf = mk.sb("itopf", [128, 256], F32)
            cand = T("cand", scp[:], st=scp.st)
            best = mk.sb("best", [128, 128], F32)
            posu = mk.sb("posu", [128, 128], U32)
            posi = mk.sb("posi", [128, 256], I32)
            posf = mk.sb("posf", [128, 256], F32)
            oh = T("oh", work[:], st=work.st)
            oh2 = T("oh2", cand[:], st=cand.st)
            sel = mk.sb("sel", [128, 256], F32)
            junkA = T("junkA", oh[:, 0:1024], st=oh.st)
            idxf = mk.sb("idxf", [128, 128], F32)
            idxi_ = [mk.sb("idxi%d" % i, [128, 128], I32) for i in range(2)]
            gate_ = [mk.sb("gate%d" % i, [128, 128], F32) for i in range(2)]
            dots = mk.sb("dots", [128, 128], F32)
            actv = mk.sb("actv", [128, 128], F32)
            wgt = mk.sb("wgt", [128, 128], F32)
            NGB = 19
            gbuf = [mk.sb("gbuf%d" % i, [128, 2 * D], BF16) for i in range(NGB)]
            junk = mk.sb("junk", [128, D], F32)
            pleg = mk.sb("pleg", [128, D], F32)
            NSM = 19
            dot_r = [mk.sb("dot_r%d" % i, [128, 1], F32) for i in range(NSM)]
            wv_r = [mk.sb("wv_r%d" % i, [128, 2], F32) for i in range(NSM)]
            dg_r = [mk.sb("dg_r%d" % i, [128, 128], BF16) for i in range(8)]
            xn3 = mk.sb("xn3", [128, D], BF16)
            xn3T = mk.sb("xn3T", [128, D], BF16)
            pb16 = mk.sb("pb16", [128, 256], BF16)
            ppT = mk.sb("ppT", [128, 256], BF16)
            print("phase2 sbuf remaining", nc.sbuf_bytes_remaining, flush=True)
            B = pbanks

            def load2(ti):
                mk.dma('sp', xt2[ti % 2][:], x1s[ti * 128:(ti + 1) * 128, :], reads=[x1_tok[ti]], writes=[xt2[ti % 2]],
                       semt=xt2[ti % 2])
                mk.dma('sp', pt2[ti % 2][:], pin[ti * 128:(ti + 1) * 128, :], writes=[pt2[ti % 2]], semt=pt2[ti % 2])

            def rms(xsrc, gt, out32, outb, st2, junk):
                mk.op('act', lambda e: e.activation(out=junk[:], in_=xsrc[:], func=AF.Square, accum_out=st2[:, 0:1]),
                      reads=[xsrc], writes=[junk, st2])
                rsqrt(st2, st2[:, 2:3], st2[:, 0:1], 1.0 / D)
                if out32 is not None:
                    mk.op('dve', lambda e: e.scalar_tensor_tensor(out=out32[:], in0=xsrc[:], scalar=st2[:, 2:3], in1=gt[:],
                                                                  op0=ALU.mult, op1=ALU.mult), reads=[xsrc, st2, gt], writes=[out32])
                    mk.op('act', lambda e: e.copy(out=outb[:], in_=out32[:]), reads=[out32], writes=[outb])
                else:
                    mk.op('dve', lambda e: e.scalar_tensor_tensor(out=outb[:], in0=xsrc[:], scalar=st2[:, 2:3], in1=gt[:],
                                                                  op0=ALU.mult, op1=ALU.mult), reads=[xsrc, st2, gt], writes=[outb])

            def tr8b(src, dst, n=8):
                def f(e):
                    last = None
                    for c in range(n):
                        last = pe_T(B[0], c * 128, src, src[:, c * 128:(c + 1) * 128])(e)
                    return last
                mk.op('pe', f, reads=[src, ident], writes=[B[0]], skip_self=True)
                mk.op('act', lambda e: e.copy(out=dst[:, 0:n * 128], in_=B[0][:].bitcast(BF16)[:, 0:n * 128]),
                      reads=[B[0]], writes=[dst])

            tiles2 = tiles if cfg.get('phase2', True) else []

            def stageA(ti):
                par = ti % 2
                x = xt2[par]
                xn2 = xn2_[par]
                idxi = idxi_[par]
                gate = gate_[par]
                mk.rate = 1.0
                load2(ti)
                rms(x, g_ffn, xn2, xn2b, st2A, junkA)
                tr8b(xn2b, xn2T)
                xT3 = xn2T[:].rearrange("p (c t) -> p c t", c=8)
                for bq in range(4):
                    bank = B[1 + bq]

                    def fq(e, bq=bq, bank=bank):
                        last = None
                        for gi in range(4):
                            g_ = bq * 4 + gi
                            for c in range(8):
                                last = e.matmul(bank[:, gi * 128:(gi + 1) * 128], lhsT=wq_v[:, c, g_ * 128:(g_ + 1) * 128],
                                                rhs=xT3[:, c, :], start=(c == 0), stop=(c == 7))
                        return last
                    mk.op('pe', fq, reads=[xn2T, warena2], writes=[bank], skip_self=True)
                    mk.op('act', lambda e, bq=bq, bank=bank: e.copy(out=qpT[:, bq * 512:(bq + 1) * 512], in_=bank[:]),
                          reads=[bank], writes=[qpT])
                for bq in range(4):
                    bank = B[1 + bq]

                    def fs(e, bq=bq, bank=bank):
                        last = None
                        for gi in range(4):
                            g_ = bq * 4 + gi
                            last = e.matmul(bank[:, gi * 128:(gi + 1) * 128], lhsT=qpT[:, g_ * 128:(g_ + 1) * 128],
                                            rhs=keys_v[:, g_, :], start=True, stop=True)
                        return last
                    mk.op('pe', fs, reads=[qpT, warena2], writes=[bank], skip_self=True)
                    mk.op('act', lambda e, bq=bq, bank=bank: e.copy(out=scp[:, bq * 512:(bq + 1) * 512], in_=bank[:]),
                          reads=[bank], writes=[scp])
                mk.rate = 4.0
                def sg_(g_):
                    return scp[:, g_ * 128:(g_ + 1) * 128]

                def wk_(g_):
                    return work[:, g_ * 128:(g_ + 1) * 128]
                for g_ in range(16):
                    mk.op('dve', lambda e, g_=g_: e.max(out=stop_[:, g_ * 16:g_ * 16 + 8], in_=sg_(g_)),
                          reads=[scp], writes=[stop_], skip_self=True)
                for g_ in range(16):
                    mk.op('dve', lambda e, g_=g_: e.match_replace(out=wk_(g_), in_to_replace=stop_[:, g_ * 16:g_ * 16 + 8],
                                                                  in_values=sg_(g_), imm_value=-1e30),
                          reads=[scp, stop_], writes=[work], skip_self=True)
                for g_ in range(16):
                    mk.op('dve', lambda e, g_=g_: e.max(out=stop_[:, g_ * 16 + 8:g_ * 16 + 16], in_=wk_(g_)),
                          reads=[work], writes=[stop_], skip_self=True)
                for g_ in range(16):
                    mk.op('dve', lambda e, g_=g_: e.max_index(out=itopu[:, g_ * 16:g_ * 16 + 8], in_max=stop_[:, g_ * 16:g_ * 16 + 8],
                                                              in_values=sg_(g_)), reads=[scp, stop_], writes=[itopu], skip_self=True)
                for g_ in range(16):
                    mk.op('dve', lambda e, g_=g_: e.max_index(out=itopu[:, g_ * 16 + 8:g_ * 16 + 16],
                                                              in_max=stop_[:, g_ * 16 + 8:g_ * 16 + 16], in_values=sg_(g_)),
                          reads=[scp, stop_], writes=[itopu], skip_self=True)
                mk.op('dve', lambda e: e.tensor_copy(out=itopf[:], in_=itopu[:]), reads=[itopu], writes=[itopf])
                st4 = stop_[:].rearrange("t (h c k) -> t h c k", h=8, c=2)
                mk.op('dve', lambda e: e.tensor_tensor(out=cand[:].rearrange("t (h i j) -> t h i j", h=8, i=16),
                                                       in0=st4[:, :, 0, :].unsqueeze(3).to_broadcast([128, 8, 16, 16]),
                                                       in1=st4[:, :, 1, :].unsqueeze(2).to_broadcast([128, 8, 16, 16]), op=ALU.add),
                      reads=[stop_], writes=[cand])
                def cg_(h):
                    return cand[:, h * 256:(h + 1) * 256]

                def wk2_(h):
                    return work[:, h * 256:(h + 1) * 256]
                for h in range(8):
                    mk.op('dve', lambda e, h=h: e.max(out=best[:, h * 16:h * 16 + 8], in_=cg_(h)),
                          reads=[cand], writes=[best], skip_self=(h > 0))
                for h in range(8):
                    mk.op('dve', lambda e, h=h: e.match_replace(out=wk2_(h), in_to_replace=best[:, h * 16:h * 16 + 8], in_values=cg_(h),
                                                                imm_value=-1e30), reads=[cand, best], writes=[work], skip_self=True)
                for h in range(8):
                    mk.op('dve', lambda e, h=h: e.max(out=best[:, h * 16 + 8:h * 16 + 16], in_=wk2_(h)),
                          reads=[work], writes=[best], skip_self=True)
                for h in range(8):
                    mk.op('dve', lambda e, h=h: e.max_index(out=posu[:, h * 16:h * 16 + 8], in_max=best[:, h * 16:h * 16 + 8],
                                                            in_values=cg_(h)), reads=[cand, best], writes=[posu], skip_self=True)
                for h in range(8):
                    mk.op('dve', lambda e, h=h: e.max_index(out=posu[:, h * 16 + 8:h * 16 + 16], in_max=best[:, h * 16 + 8:h * 16 + 16],
                                                            in_values=cg_(h)), reads=[cand, best], writes=[posu], skip_self=True)
                mk.op('dve', lambda e: e.tensor_single_scalar(out=posi[:, 0:128], in_=posu[:].bitcast(I32), scalar=4,
                                                              op=ALU.logical_shift_right), reads=[posu], writes=[posi])
                mk.op('dve', lambda e: e.tensor_single_scalar(out=posi[:, 128:256], in_=posu[:].bitcast(I32), scalar=15,
                                                              op=ALU.bitwise_and), reads=[posu], writes=[posi])
                mk.op('dve', lambda e: e.tensor_copy(out=posf[:], in_=posi[:]), reads=[posi], writes=[posf])
                it4 = itopf[:].rearrange("t (h c k) -> t h c k", h=8, c=2)
                ohs = (oh, oh2)
                oh4s = [o_[:].rearrange("t (h k i) -> t h k i", h=8, k=16) for o_ in ohs]
                for c in range(2):
                    pf = posf[:, c * 128:(c + 1) * 128].rearrange("t (h k) -> t h k", h=8)
                    mk.op('dve', lambda e, pf=pf, c=c: e.tensor_tensor(
                        out=oh4s[c], in0=iota16[:].unsqueeze(1).unsqueeze(1).to_broadcast([128, 8, 16, 16]),
                        in1=pf.unsqueeze(3).to_broadcast([128, 8, 16, 16]), op=ALU.is_equal), reads=[iota16, posf], writes=[ohs[c]])
                for c in range(2):
                    mk.op('dve', lambda e, c=c: e.tensor_tensor(
                        out=oh4s[c], in0=oh4s[c], in1=it4[:, :, c, :].unsqueeze(2).to_broadcast([128, 8, 16, 16]), op=ALU.mult),
                        reads=[ohs[c], itopf], writes=[ohs[c]], skip_self=True)
                for c in range(2):
                    mk.op('dve', lambda e, c=c: e.tensor_reduce(out=sel[:, c * 128:(c + 1) * 128],
                                                                in_=ohs[c][:].rearrange("t (s i) -> t s i", i=16), axis=AX.X, op=ALU.add),
                          reads=[ohs[c]], writes=[sel], skip_self=True)
                mk.op('dve', lambda e: e.scalar_tensor_tensor(out=idxf[:], in0=sel[:, 0:128], scalar=128.0, in1=sel[:, 128:256],
                                                              op0=ALU.mult, op1=ALU.add), reads=[sel], writes=[idxf])
                mk.op('dve', lambda e: e.tensor_copy(out=idxi[:], in_=idxf[:]), reads=[idxf], writes=[idxi])
                b3 = best[:].rearrange("t (h k) -> t h k", h=8)
                mk.op('dve', lambda e: e.tensor_tensor(out=gate[:].rearrange("t (h k) -> t h k", h=8), in0=b3,
                                                       in1=b3[:, :, 0:1].to_broadcast([128, 8, 16]), op=ALU.subtract),
                      reads=[best], writes=[gate])
                mk.op('act', lambda e: e.activation(out=gate[:], in_=gate[:], func=AF.Exp), reads=[gate], writes=[gate])
                mk.op('dve', lambda e: e.tensor_reduce(out=st2A[:, 8:16], in_=gate[:].rearrange("t (h k) -> t h k", h=8),
                                                       axis=AX.X, op=ALU.add), reads=[gate], writes=[st2A])
                mk.op('dve', lambda e: e.reciprocal(out=st2A[:, 8:16], in_=st2A[:, 8:16]), reads=[st2A], writes=[st2A])
                mk.op('dve', lambda e: e.tensor_tensor(out=gate[:].rearrange("t (h k) -> t h k", h=8),
                                                       in0=gate[:].rearrange("t (h k) -> t h k", h=8),
                                                       in1=st2A[:, 8:16].unsqueeze(2).to_broadcast([128, 8, 16]), op=ALU.mult),
                      reads=[gate, st2A], writes=[gate])

            def stageB(ti, nxt_q):
                par = ti % 2
                x = xt2[par]
                pt = pt2[par]
                xn2 = xn2_[par]
                idxi = idxi_[par]
                gate = gate_[par]
                mk._waits('pool', [comb_t], [])
                for s_ in range(128):
                    gb = gbuf[s_ % NGB]
                    dt_ = dot_r[s_ % NSM]
                    wv = wv_r[s_ % NSM]
                    dg = dg_r[s_ % 8]
                    mk.gather(gb, gb[:], comb[:, :], idxi, idxi[:, s_:s_ + 1])
                    mk.op('dve', lambda e, gb=gb, dt_=dt_: e.scalar_tensor_tensor(
                        out=junk[:], in0=gb[:, 0:D], scalar=1.0, in1=xn2[:], op0=ALU.mult, op1=ALU.mult,
                        accum_out=dt_[:, 0:1]), reads=[gb, xn2], writes=[junk, dt_], skip_self=True)
                    mk.op('act', lambda e, dt_=dt_, wv=wv: e.activation(out=wv[:, 0:1], in_=dt_[:, 0:1], func=GELU),
                          reads=[dt_], writes=[wv])
                    mk.op('act', lambda e, wv=wv, s_=s_: e.mul(out=wv[:, 1:2], in_=wv[:, 0:1], mul=gate[:, s_:s_ + 1]),
                          reads=[wv, gate], writes=[wv])
                    mk.op('act', lambda e, wv=wv, dg=dg: e.activation(out=dg[:], in_=identf[:], func=AF.Copy, scale=wv[:, 1:2]),
                          reads=[wv, identf], writes=[dg])

                    def fv(e, s_=s_, gb=gb, dg=dg):
                        last = None
                        for n in range(2):
                            last = e.matmul(B[5 + n][:], lhsT=dg[:], rhs=gb[:, D + n * 512:D + (n + 1) * 512],
                                            start=(s_ == 0), stop=(s_ == 127))
                        return last
                    mk.op('pe', fv, reads=[gb, dg], writes=[B[5], B[6]], skip_self=True)
                    mk.play(nxt_q, 1.0, lag=5)
                if len(nxt_q) and ti == 1:
                    print('replay leftover at drain:', len(nxt_q), flush=True)
                mk.play(nxt_q, 10 ** 9)
                for n in range(2):
                    sl = slice(n * 512, (n + 1) * 512)
                    mk.op('dve', lambda e, n=n, sl=sl: e.tensor_tensor(out=x[:, sl], in0=B[5 + n][:], in1=x[:, sl], op=ALU.add),
                          reads=[B[5 + n], x], writes=[x])
                mk.rec = []
                mk.rate = 1.0
                rms(x, g_ple, None, xn3, st2, pleg)
                tr8b(xn3, xn3T)
                x3T = xn3T[:].rearrange("p (c t) -> p c t", c=8)

                def fpg(e):
                    last = None
                    for n in range(2):
                        for c in range(8):
                            last = e.matmul(B[1 + n][:], lhsT=x3T[:, c, :], rhs=wpg_v[:, c, n * 512:(n + 1) * 512],
                                            start=(c == 0), stop=(c == 7))
                    return last
                mk.op('pe', fpg, reads=[xn3T, warena2], writes=[B[1], B[2]], skip_self=True)
                for n in range(2):
                    mk.op('act', lambda e, n=n: e.activation(out=pleg[:, n * 512:(n + 1) * 512], in_=B[1 + n][:], func=AF.Sigmoid),
                          reads=[B[1 + n]], writes=[pleg])
                mk.op('act', lambda e: e.copy(out=pb16[:], in_=pt[:]), reads=[pt], writes=[pb16])
                tr8b(pb16, ppT, n=2)
                pT3 = ppT[:].rearrange("p (c t) -> p c t", c=2)

                def fpl(e):
                    last = None
                    for n in range(2):
                        for c in range(2):
                            last = e.matmul(B[3 + n][:], lhsT=pT3[:, c, :], rhs=wpl_v[:, c, n * 512:(n + 1) * 512],
                                            start=(c == 0), stop=(c == 1))
                    return last
                mk.op('pe', fpl, reads=[ppT, warena2], writes=[B[3], B[4]], skip_self=True)
                for n in range(2):
                    sl = slice(n * 512, (n + 1) * 512)
                    mk.op('dve', lambda e, n=n, sl=sl: e.tensor_tensor(out=pleg[:, sl], in0=B[3 + n][:], in1=pleg[:, sl], op=ALU.mult),
                          reads=[B[3 + n], pleg], writes=[pleg])
                mk.op('dve', lambda e: e.tensor_tensor(out=x[:], in0=x[:], in1=pleg[:], op=ALU.add), reads=[x, pleg], writes=[x])
                mk.dma('sp', y[ti * 128:(ti + 1) * 128, :], x[:], reads=[x], semt=x)
                tail = mk.rec
                mk.rec = None
                return tail

            if tiles2:
                mk.rec = []
                stageA(tiles2[0])
                q0 = mk.rec
                mk.rec = None
                mk.play(q0, 10 ** 9)
            pending_tail = []
            for tpos, ti in enumerate(tiles2):
                nxt_q = pending_tail
                if tpos + 1 < len(tiles2):
                    mk.rec = []
                    stageA(tiles2[tpos + 1])
                    nxt_q = nxt_q + mk.rec
                    mk.rec = None
                if tpos == 1:
                    print('replay queue: ops', len(nxt_q), 'slot budget needed', sum(1.0 / r[3] for r in nxt_q), flush=True)
                pending_tail = stageB(ti, nxt_q)
            mk.play(pending_tail, 10 ** 9)
            out_tiles_all = out_tiles + xt2
            mk._waits('sp', [], xt2)
            mk.barrier()
        print("instructions emitted:", mk.n_ins, flush=True)
    return nc


_NC_CACHE = {}


def _get_nc():
    if "nc" not in _NC_CACHE:
        _NC_CACHE["nc"] = build()
    return _NC_CACHE["nc"]


def kernel(x_prompt, x_sample, cache_k, cache_v, p_prompt, p_sample,
           attn_norm_g, w_in, q_norm_g, k_norm_g, attn_sinks,
           sgu_norm_g, sgu_norm_b, sgu_w, sgu_b,
           w_branch_a, w_branch_b, w_out,
           ffn_norm_g, peer_w_q, peer_sub_keys, peer_u, peer_v,
           ple_norm_g, w_ple, w_ple_gate):
    f = lambda a: np.ascontiguousarray(np.asarray(a, dtype=np.float32))
    x_prompt, x_sample, cache_k, cache_v = f(x_prompt), f(x_sample), f(cache_k), f(cache_v)
    p_prompt, p_sample = f(p_prompt), f(p_sample)
    nc = _get_nc()
    consts = _consts()
    rep = lambda v, n=128: np.ascontiguousarray(np.broadcast_to(f(v).reshape(1, -1), (n, f(v).size)))
    sw = f(sgu_w)[0]
    sb = f(sgu_b)[0]
    small = {
        "g_attn": rep(attn_norm_g[0]), "g_ffn": rep(ffn_norm_g[0]), "g_ple": rep(ple_norm_g[0]),
        "gq": rep(q_norm_g[0]), "gk": rep(k_norm_g[0]), "sinks": rep(attn_sinks[0]),
        "ln_g": rep(sgu_norm_g[0]), "ln_b": rep(sgu_norm_b[0]),
        "bcol_p": np.ascontiguousarray(sb.T),
        "bcol_s": np.ascontiguousarray(np.tile(sb[:, :8].T, (16, 1))),
        "wsT_p": np.ascontiguousarray(sw.transpose(2, 0, 1)).reshape(128, 512),
        "wsT_s": np.ascontiguousarray(np.tile(sw[:, :8, :8].transpose(2, 0, 1), (16, 1, 16))).reshape(128, 512),
    }
    shared = {
        "w_in": f(w_in)[0], "w_a": f(w_branch_a)[0], "w_b": f(w_branch_b)[0], "w_out": f(w_out)[0],
        "w_q": f(peer_w_q)[0],
        "keysT": np.ascontiguousarray(f(peer_sub_keys)[0].reshape(16, 128, 128).transpose(2, 0, 1)).reshape(128, 2048),
        "peer_u": f(peer_u)[0], "peer_v": f(peer_v)[0], "w_ple": f(w_ple)[0], "w_pg": f(w_ple_gate)[0],
    }
    for k, v in consts.items():
        shared["c_" + k] = v
    for k, v in small.items():
        shared["s_" + k] = v
    in_maps = []
    for c in range(NCORES):
        m = dict(shared)
        xs = x_sample[c * 16:(c + 1) * 16].reshape(128, D)
        m["xin"] = np.ascontiguousarray(np.concatenate([x_prompt[c], xs], axis=0))
        ps = p_sample[0, c * 16:(c + 1) * 16].reshape(128, 256)
        m["pin"] = np.ascontiguousarray(np.concatenate([p_prompt[0, c], ps], axis=0))
        m["ck"] = np.ascontiguousarray(cache_k[0, c * 16:(c + 1) * 16].reshape(16, 128, 128).transpose(1, 0, 2))
        m["cv"] = np.ascontiguousarray(cache_v[0, c * 16:(c + 1) * 16].reshape(16, 128, 128).transpose(1, 0, 2))
        in_maps.append(m)
    res = run_bass_kernel_spmd(nc, in_maps, core_ids=list(range(NCORES)))
    R = res.results
    y_p = np.stack([R[c]["y"][:SEQ] for c in range(NCORES)])
    y_s = np.concatenate([R[c]["y"][SEQ:].reshape(16, 8, D) for c in range(NCORES)], axis=0)
    nkp = np.stack([R[c]["o_kp"].reshape(128, 2, 64) for c in range(NCORES)])[None]
    nvp = np.stack([R[c]["o_vp"].reshape(128, 2, 64) for c in range(NCORES)])[None]
    nks = np.concatenate([R[c]["o_ks"].reshape(16, 8, 2, 64) for c in range(NCORES)], axis=0)[None]
    nvs = np.concatenate([R[c]["o_vs"].reshape(16, 8, 2, 64) for c in range(NCORES)], axis=0)[None]
    nsp = np.stack([R[c]["o_sp"] for c in range(NCORES)])[None]
    nss = np.concatenate([R[c]["o_ss"].reshape(16, 8, 512) for c in range(NCORES)], axis=0)[None]
    return (y_p.astype(np.float32), y_s.astype(np.float32), nkp.astype(np.float32), nvp.astype(np.float32),
            nks.astype(np.float32), nvs.astype(np.float32), nsp.astype(np.float32), nss.astype(np.float32))
```

```python
import numpy as np
from contextlib import ExitStack
import concourse.bass as bass
import concourse.mybir as mybir
from concourse.bass_utils import run_bass_kernel_spmd

F32 = mybir.dt.float32
BF16 = mybir.dt.bfloat16
I32 = mybir.dt.int32
U32 = mybir.dt.uint32
ALU = mybir.AluOpType
AF = mybir.ActivationFunctionType
AX = mybir.AxisListType

NCORES = 8
D = 1024
SEQ = 4096
NPT = SEQ // 128
NT = NPT + 1
NTOK = NT * 128
IN_W = 3840
EPS = 1e-6
NEG = -30000.0
GELU = AF.Gelu_apprx_tanh


class St:
    __slots__ = ("w", "r", "dsem")

    def __init__(self):
        self.w = None
        self.r = []
        self.dsem = None


class T:
    def __init__(self, name, h=None, st=None):
        self.name = name
        self.h = h
        self.st = st if st is not None else St()

    def __getitem__(self, k):
        return self.h[k]


class MK:
    def __init__(self, nc, es):
        self.nc = nc
        self.es_sem = es
        self.es = es
        self.engs = {'pe': nc.tensor, 'act': nc.scalar, 'dve': nc.vector, 'pool': nc.gpsimd, 'sp': nc.sync}
        self.sems = {}
        self.cnt = {}
        self.waited = {k: {} for k in self.engs}
        for k in ('pe', 'act', 'dve', 'pool'):
            self.sems[k] = es.enter_context(nc.semaphore("c_" + k))
            self.cnt[k] = 0
        self.nd = 0
        self.n_ins = 0
        self.rec = None
        self.rate = 4.0
        self.slot = 0
        self.inter = None
        self._in_inter = False
        self._inter_cnt = 0
        self.prev_eng = 'dve'
        self.prev_slot = -100

    def sb(self, name, shape, dtype):
        h = self.es.enter_context(self.nc.sbuf_tensor(name, list(shape), dtype))
        return T(name, h)

    def ps(self, name, shape, dtype=F32):
        h = self.es.enter_context(self.nc.psum_tensor(name, list(shape), dtype))
        return T(name, h)

    def _dsem(self, t):
        st = t.st
        if st.dsem is None:
            key = "d%d" % self.nd
            self.nd += 1
            self.sems[key] = self.es_sem.enter_context(self.nc.semaphore(key))
            self.cnt[key] = 0
            st.dsem = key
        return st.dsem

    def _waits(self, eng, reads, writes, skip_self=False):
        need = {}

        def add(tok):
            if tok is None:
                return
            k, v = tok
            if need.get(k, 0) < v:
                need[k] = v
        for t in reads:
            add(t.st.w)
        for t in writes:
            add(t.st.w)
            for tok in t.st.r:
                add(tok)
        e = self.engs[eng]
        for k, v in need.items():
            if skip_self and k == eng:
                continue
            if self.waited[eng].get(k, 0) >= v:
                continue
            e.wait_ge(self.sems[k], v)
            self.n_ins += 1
            self.waited[eng][k] = v

    def _commit(self, tok, reads, writes):
        for t in reads:
            r = t.st.r
            r.append(tok)
            if len(r) > 48:
                best = {}
                for k, v in r:
                    if best.get(k, 0) < v:
                        best[k] = v
                t.st.r = list(best.items())
        for t in writes:
            t.st.w = tok
            t.st.r = []

    def op(self, eng, fn, reads=(), writes=(), skip_self=False):
        if self.rec is not None:
            self.rec.append(('op', (eng, fn), dict(reads=list(reads), writes=list(writes), skip_self=skip_self), self.rate))
            return None
        self._waits(eng, reads, writes, skip_self)
        ins = fn(self.engs[eng])
        self.cnt[eng] += 1
        ins.then_inc(self.sems[eng], 1)
        self.n_ins += 1
        self._commit((eng, self.cnt[eng]), reads, writes)
        if self.inter is not None and not self._in_inter:
            q, every = self.inter
            self._inter_cnt += 1
            if q and self._inter_cnt % every == 0:
                self._in_inter = True
                kind, a, kw, _rate = q.pop(0)
                getattr(self, kind)(*a, **kw)
                self._in_inter = False
        return ins

    def dma(self, q, out, in_, reads=(), writes=(), semt=None, **kw):
        if self.rec is not None:
            self.rec.append(('dma', (q, out, in_), dict(reads=list(reads), writes=list(writes), semt=semt, **kw), self.rate))
            return None
        self._waits(q, reads, writes)
        key = self._dsem(semt)
        ins = self.engs[q].dma_start(out=out, in_=in_, **kw)
        self.cnt[key] += 16
        ins.then_inc(self.sems[key], 16)
        self.n_ins += 1
        self._commit((key, self.cnt[key]), reads, writes)
        return ins

    def gather(self, out_t, out_ap, table_ap, idx_t, idx_ap):
        self._waits('pool', [idx_t], [out_t])
        key = self._dsem(out_t)
        ins = self.nc.gpsimd.indirect_dma_start(
            out=out_ap, out_offset=None, in_=table_ap,
            in_offset=bass.IndirectOffsetOnAxis(ap=idx_ap, axis=0))
        self.cnt[key] += 16
        ins.then_inc(self.sems[key], 16)
        self.n_ins += 1
        self._commit((key, self.cnt[key]), [idx_t], [out_t])
        return ins

    def settle(self, tiles):
        for t in tiles:
            if t.st.w is not None and t.st.w[0] in self.cnt and t.st.w[0].startswith("d"):
                t.st.w = (t.st.w[0], self.cnt[t.st.w[0]])

    def play(self, q, budget, lag=0):
        self.slot += 1
        while q and budget > 1e-9:
            kind, a, kw, rate = q[0]
            if budget + 1e-9 < 1.0 / rate and budget < 0.999:
                break
            eng = a[0]
            if lag and eng == 'dve' and self.prev_eng != 'dve' and self.slot - self.prev_slot < lag:
                break
            q.pop(0)
            getattr(self, kind)(*a, **kw)
            budget -= 1.0 / rate
            self.prev_eng = eng
            self.prev_slot = self.slot

    def barrier(self):
        for eng in self.engs:
            e = self.engs[eng]
            for k, v in self.cnt.items():
                if v == 0 or self.waited[eng].get(k, 0) >= v:
                    continue
                e.wait_ge(self.sems[k], v)
                self.n_ins += 1
                self.waited[eng][k] = v


def _consts():
    slopes = np.exp2(-np.arange(1, 9, dtype=np.float64)).astype(np.float32)
    s = np.arange(128)[:, None]
    t = np.arange(128)[None, :]
    c = {}
    c["ident"] = np.eye(128, dtype=np.float32)
    bp = np.full((128, 2, 8, 128), NEG, np.float32)
    d0 = (128 + t - s).astype(np.float32)
    d1 = (t - s).astype(np.float32)
    for h in range(8):
        bp[:, 0, h, :] = np.where(s > t, -slopes[h] * d0, NEG)
        bp[:, 1, h, :] = np.where(s <= t, -slopes[h] * d1, NEG)
    c["bias_p"] = bp.reshape(128, 2048)
    bs_, ts_ = s // 8, s % 8
    bq_, tq_ = t // 8, t % 8
    bsn = np.full((128, 8, 128), NEG, np.float32)
    ok = (bs_ == bq_) & (ts_ <= tq_)
    for h in range(8):
        bsn[:, h, :] = np.where(ok, -slopes[h] * (tq_ - ts_).astype(np.float32), NEG)
    c["bias_sn"] = bsn.reshape(128, 1024)
    r = np.arange(128)[:, None]
    tq = np.arange(8)[None, :]
    bsp = np.full((128, 8, 8), NEG, np.float32)
    for h in range(8):
        bsp[:, h, :] = np.where(r > tq, -slopes[h] * (tq + 128 - r).astype(np.float32), NEG)
    c["bias_sp"] = bsp.reshape(128, 64)
    c["maskT_p"] = (s <= t).astype(np.float32)
    c["maskT_s"] = ok.astype(np.float32)
    c["iota16"] = np.broadcast_to(np.arange(16, dtype=np.float32)[None, :], (128, 16)).copy()
    return c


CONST_SHAPES = {"ident": (128, 128), "bias_p": (128, 2048), "bias_sn": (128, 1024), "bias_sp": (128, 64),
                "maskT_p": (128, 128), "maskT_s": (128, 128), "iota16": (128, 16)}
SMALL_SHAPES = {"g_attn": (128, 1024), "g_ffn": (128, 1024), "g_ple": (128, 1024), "gq": (128, 64), "gk": (128, 64),
                "sinks": (128, 8), "ln_g": (128, 512), "ln_b": (128, 512),
                "bcol_p": (128, 4), "bcol_s": (128, 4), "wsT_p": (128, 512), "wsT_s": (128, 512)}


CFG = {}


class _StopTile(Exception):
    pass


def chk(n):
    if CFG.get('stage', 99) < n:
        raise _StopTile()


def build():
    nc = bass.Bass("TRN2", target_bir_lowering=False)
    cfg = CFG
    tiles = cfg.get('tiles', list(range(NT)))

    def din(name, shape, dt=F32):
        return nc.dram_tensor(name, list(shape), dt, kind="ExternalInput").ap()

    def dout(name, shape, dt=F32):
        return nc.dram_tensor(name, list(shape), dt, kind="ExternalOutput").ap()

    xin = din("xin", [NTOK, D])
    pin = din("pin", [NTOK, 256])
    ck = din("ck", [128, 16, 128])
    cv = din("cv", [128, 16, 128])
    w_in = din("w_in", [D, IN_W])
    w_a = din("w_a", [512, D])
    w_b = din("w_b", [512, D])
    w_out = din("w_out", [D, D])
    w_q = din("w_q", [D, 2048])
    keysT = din("keysT", [128, 2048])
    peer_u = din("peer_u", [16384, D])
    peer_v = din("peer_v", [16384, D])
    w_ple = din("w_ple", [256, D])
    w_pg = din("w_pg", [D, D])
    cin = {k: din("c_" + k, v) for k, v in CONST_SHAPES.items()}
    sin = {k: din("s_" + k, v) for k, v in SMALL_SHAPES.items()}

    y = dout("y", [NTOK, D])
    o_kp = dout("o_kp", [128, 128])
    o_vp = dout("o_vp", [128, 128])
    o_ks = dout("o_ks", [128, 128])
    o_vs = dout("o_vs", [128, 128])
    o_sp = dout("o_sp", [128, 512])
    o_ss = dout("o_ss", [128, 512])
    x1s = nc.dram_tensor("x1s", [NTOK, D], F32, kind="Internal").ap()
    comb = nc.dram_tensor("comb", [16384, 2 * D], BF16, kind="Internal").ap()
    dbg_x1 = dout("dbg_x1", [NTOK, D]) if cfg.get('dbg') else None
    dbg_am = dout("dbg_am", [NTOK, D]) if cfg.get('dbg') else None

    with ExitStack() as es0:
        mk = MK(nc, es0)
        x1_tok = [T("x1s%d" % i) for i in range(NT)]
        out_tiles = []

        ident = mk.sb("ident", [128, 128], BF16)
        cgrp = T("cgrp")
        mk.dma('pool', ident[:], cin["ident"][:, :], writes=[ident], semt=ident)
        pbanks = [mk.ps("pb%d" % i, [128, 512], F32) for i in range(8)]

        def pe_T(bank, col0, src_t, src_ap, n=128, extra_reads=()):
            pv = bank[:].bitcast(BF16)
            return lambda e: e.transpose(out=pv[0:n, col0:col0 + 128], in_=src_ap, identity=ident[:])

        epst = mk.sb("epst", [128, 1], F32)
        mk.op('pool', lambda e: e.memset(epst[:], EPS), writes=[epst])

        def rsqrt(tile, dst, src, scale):
            mk.op('act', lambda e: e.activation(out=dst, in_=src, func=AF.Sqrt, bias=epst[:, 0:1], scale=scale),
                  reads=[tile, epst], writes=[tile])
            mk.op('dve', lambda e: e.reciprocal(out=dst, in_=dst), reads=[tile], writes=[tile])

        with ExitStack() as es1:
            mk.es = es1
            warena = mk.sb("warena1", [128, 8 * IN_W + 4 * D + 4 * D + 8 * D], BF16)
            o = 0
            win_v = warena[:, o:o + 8 * IN_W].rearrange("p (c n) -> p c n", c=8); o += 8 * IN_W
            wa_v = warena[:, o:o + 4 * D].rearrange("p (c n) -> p c n", c=4); o += 4 * D
            wb_v = warena[:, o:o + 4 * D].rearrange("p (c n) -> p c n", c=4); o += 4 * D
            wo_v = warena[:, o:o + 8 * D].rearrange("p (c n) -> p c n", c=8); o += 8 * D
            ZG = [(0, 512), (512, 256), (768, 512), (1280, 512), (1792, 512), (2304, 512), (2816, 512), (3328, 512)]
            wg = [T("wg%d" % i) for i in range(8)]
            w_in_r = w_in.rearrange("(c p) n -> p c n", p=128)
            for gi, (c0, n) in enumerate(ZG):
                for c in range(8):
                    mk.dma('pool', win_v[:, c, c0:c0 + n], w_in_r[:, c, c0:c0 + n], semt=wg[gi])
                wg[gi].st.w = (wg[gi].st.dsem, mk.cnt[wg[gi].st.dsem])
            w_a_r = w_a.rearrange("(c p) n -> p c n", p=128)
            w_b_r = w_b.rearrange("(c p) n -> p c n", p=128)
            w_o_r = w_out.rearrange("(c p) n -> p c n", p=128)
            wab = T("wab")
            wot = T("wot")
            for c in range(4):
                mk.dma('pool', wa_v[:, c, :], w_a_r[:, c, :], semt=wab)
                mk.dma('pool', wb_v[:, c, :], w_b_r[:, c, :], semt=wab)
            wab.st.w = (wab.st.dsem, mk.cnt[wab.st.dsem])
            for c in range(8):
                mk.dma('pool', wo_v[:, c, :], w_o_r[:, c, :], semt=wot)
            wot.st.w = (wot.st.dsem, mk.cnt[wot.st.dsem])
            comb_t = T("comb")
            for c in range(16):
                rs = slice(c * 1024, (c + 1) * 1024)
                mk.dma('pool', comb[rs, 0:D], peer_u[rs, :], semt=comb_t)
                mk.dma('pool', comb[rs, D:2 * D], peer_v[rs, :], semt=comb_t)
            comb_t.st.w = (comb_t.st.dsem, mk.cnt[comb_t.st.dsem])

            def cload(name, src, shape, dt=F32, q='sp'):
                t = mk.sb(name, shape, dt)
                mk.dma(q, t[:], src[:, :], writes=[t], semt=cgrp)
                return t
            g_attn = cload("g_attn", sin["g_attn"], [128, 1024])
            gq = cload("gq", sin["gq"], [128, 64])
            gk = cload("gk", sin["gk"], [128, 64])
            sinks = cload("sinks", sin["sinks"], [128, 8])
            ln_g = cload("ln_g", sin["ln_g"], [128, 512])
            ln_b = cload("ln_b", sin["ln_b"], [128, 512])
            bcol_p = cload("bcol_p", sin["bcol_p"], [128, 4])
            bcol_s = cload("bcol_s", sin["bcol_s"], [128, 4])
            bias_p = cload("bias_p", cin["bias_p"], [128, 2048])
            bias_sn = cload("bias_sn", cin["bias_sn"], [128, 1024])
            bias_sp = cload("bias_sp", cin["bias_sp"], [128, 64])
            maskT_p = cload("maskT_p", cin["maskT_p"], [128, 128])
            maskT_s = cload("maskT_s", cin["maskT_s"], [128, 128])
            t1 = mk.sb("t1", [128, D], F32)
            ws32_p = T("ws32_p", t1[:, 0:512], st=t1.st)
            ws32_s = T("ws32_s", t1[:, 512:1024], st=t1.st)
            mk.dma('sp', ws32_p[:], sin["wsT_p"][:, :], writes=[ws32_p], semt=cgrp)
            mk.dma('sp', ws32_s[:], sin["wsT_s"][:, :], writes=[ws32_s], semt=cgrp)
            consts1 = [ident, g_attn, gq, gk, sinks, ln_g, ln_b, bcol_p, bcol_s, bias_p, bias_sn, bias_sp,
                       maskT_p, maskT_s, ws32_p, ws32_s]
            mk.settle(consts1)

            sinkexp = mk.sb("sinkexp", [128, 8], F32)
            mk.op('act', lambda e: e.activation(out=sinkexp[:], in_=sinks[:], func=AF.Exp), reads=[sinks], writes=[sinkexp])
            gq8 = mk.sb("gq8", [128, 64], F32)
            mk.op('dve', lambda e: e.tensor_scalar(out=gq8[:], in0=gq[:], scalar1=0.125, scalar2=None, op0=ALU.mult),
                  reads=[gq], writes=[gq8])
            wsT_p = mk.sb("wsT_p", [128, 512], BF16)
            wsT_s = mk.sb("wsT_s", [128, 512], BF16)
            mk.op('dve', lambda e: e.tensor_tensor(out=wsT_p[:].rearrange("s (g t) -> s g t", g=4),
                                                   in0=ws32_p[:].rearrange("s (g t) -> s g t", g=4),
                                                   in1=maskT_p[:].unsqueeze(1).to_broadcast([128, 4, 128]), op=ALU.mult),
                  reads=[ws32_p, maskT_p], writes=[wsT_p])
            mk.op('dve', lambda e: e.tensor_tensor(out=wsT_s[:].rearrange("s (g t) -> s g t", g=4),
                                                   in0=ws32_s[:].rearrange("s (g t) -> s g t", g=4),
                                                   in1=maskT_s[:].unsqueeze(1).to_broadcast([128, 4, 128]), op=ALU.mult),
                  reads=[ws32_s, maskT_s], writes=[wsT_s])

            xt = [mk.sb("xt%d" % i, [128, D], F32) for i in range(3)]
            x1t = [mk.sb("x1t%d" % i, [128, D], F32) for i in range(2)]
            xn = mk.sb("xn", [128, D], BF16)
            xnT = mk.sb("xnT", [128, D], BF16)
            st1 = mk.sb("st1", [128, 16], F32)
            q32 = mk.sb("q32", [128, 512], F32)
            qn = mk.sb("qn", [128, 512], BF16)
            qT = mk.sb("qT", [128, 512], BF16)
            k32 = mk.sb("k32", [128, 128], F32)
            kn32 = mk.sb("kn32", [128, 128], F32)
            knb = mk.sb("knb", [128, 128], BF16)
            kT = [mk.sb("kT%d" % i, [128, 128], BF16) for i in range(2)]
            v32 = mk.sb("v32", [128, 128], F32)
            vaug = [mk.sb("vaug%d" % i, [128, 2 * 65], BF16) for i in range(2)]
            sc32 = mk.sb("sc32", [128, 1024], F32)
            pT = mk.sb("pT", [128, 2048], BF16)
            att_ = [mk.sb("att%d" % i, [128, 512], BF16) for i in range(2)]
            u32 = mk.sb("u32", [128, 512], F32)
            gv32 = mk.sb("gv32", [128, 512], F32)
            vn32 = mk.sb("vn32", [128, 512], F32)
            vnb = mk.sb("vnb", [128, 512], BF16)
            mb_ = [mk.sb("mb%d" % i, [128, 512], BF16) for i in range(2)]
            amT = mk.sb("amT", [128, 1024], BF16)
            siga_ = [mk.sb("siga%d" % i, [128, D], BF16) for i in range(2)]
            sigb_ = [mk.sb("sigb%d" % i, [128, D], BF16) for i in range(2)]
            hb = mk.sb("hb", [128, D], BF16)
            hT = mk.sb("hT", [128, D], BF16)
            stage = mk.sb("stage", [128, 2048], F32)
            qsq = T("qsq", stage[:, 1024:1536], st=stage.st)
            ksq = T("ksq", stage[:, 1536:1664], st=stage.st)
            ckb = T("ckb", pT[:], st=pT.st)
            kTp = mk.sb("kTp", [128, 2048], BF16)
            vpaug = mk.sb("vpaug", [128, 16 * 130], BF16)
            qTs = mk.sb("qTs", [128, 512], BF16)
            Eb = [mk.sb("Eb%d" % i, [128, 1024], BF16) for i in range(2)]
            print("phase1 sbuf remaining", nc.sbuf_bytes_remaining, flush=True)

            for i in range(2):
                mk.op('dve', lambda e, i=i: e.memset(vaug[i][:], 1.0), writes=[vaug[i]])
            mk.op('dve', lambda e: e.memset(vpaug[:], 1.0), writes=[vpaug])

            B = pbanks

            def load_x(ti):
                mk.dma('sp', xt[ti % 3][:], xin[ti * 128:(ti + 1) * 128, :], writes=[xt[ti % 3]], semt=xt[ti % 3])

            load_x(tiles[0])

            def tr8(e, src, bank):
                last = None
                for c in range(8):
                    last = pe_T(bank, c * 128, src, src[:, c * 128:(c + 1) * 128])(e)
                return last

            def emit_F1(tj):
                x = xt[tj % 3]
                mk.op('act', lambda e: e.activation(out=xn[:], in_=x[:], func=AF.Square, accum_out=st1[:, 0:1]),
                      reads=[x], writes=[xn, st1])
                rsqrt(st1, st1[:, 2:3], st1[:, 0:1], 1.0 / D)
                mk.op('dve', lambda e: e.scalar_tensor_tensor(out=xn[:], in0=x[:], scalar=st1[:, 2:3], in1=g_attn[:],
                                                              op0=ALU.mult, op1=ALU.mult), reads=[x, st1, g_attn], writes=[xn])
                mk.op('pe', lambda e: tr8(e, src=xn, bank=B[0]), reads=[xn, ident], writes=[B[0]], skip_self=True)
                mk.op('act', lambda e: e.copy(out=xnT[:], in_=B[0][:].bitcast(BF16)), reads=[B[0]], writes=[xnT])


            def tile_body(tpos, ti, prevE):
                try:
                    sample = (ti == NPT)
                    x = xt[ti % 3]
                    att = att_[ti % 2]
                    mb = mb_[ti % 2]
                    siga = siga_[ti % 2]
                    sigb = sigb_[ti % 2]
                    if tpos + 1 < len(tiles):
                        load_x(tiles[tpos + 1])
                    cur, prv = ti % 2, (ti + 1) % 2
                    if tpos == 0:
                        emit_F1(ti)
                    chk(1)
                    xnT3 = xnT[:].rearrange("p (c t) -> p c t", c=8)

                    def zgroup(bank, c0, n):
                        def f(e):
                            last = None
                            for c in range(8):
                                last = e.matmul(bank[:, 0:n], lhsT=xnT3[:, c, :], rhs=win_v[:, c, c0:c0 + n],
                                                start=(c == 0), stop=(c == 7))
                            return last
                        mk.op('pe', f, reads=[xnT, wg[[z[0] for z in ZG].index(c0)]], writes=[bank], skip_self=True)

                    zgroup(B[1], 0, 512)
                    mk.op('act', lambda e: e.copy(out=q32[:], in_=B[1][:]), reads=[B[1]], writes=[q32])
                    mk.op('act', lambda e: e.activation(out=qsq[:], in_=B[1][:], func=AF.Square), reads=[B[1]], writes=[qsq])
                    zgroup(B[2], 512, 256)
                    mk.op('act', lambda e: e.copy(out=k32[:], in_=B[2][:, 0:128]), reads=[B[2]], writes=[k32])
                    mk.op('act', lambda e: e.activation(out=ksq[:], in_=B[2][:, 0:128], func=AF.Square), reads=[B[2]], writes=[ksq])
                    mk.op('act', lambda e: e.copy(out=v32[:], in_=B[2][:, 128:256]), reads=[B[2]], writes=[v32])
                    zgroup(B[3], 768, 512)
                    mk.op('act', lambda e: e.activation(out=u32[:], in_=B[3][:], func=GELU), reads=[B[3]], writes=[u32])
                    zgroup(B[1], 1280, 512)
                    mk.op('act', lambda e: e.activation(out=gv32[:], in_=B[1][:], func=GELU, accum_out=st1[:, 4:5]),
                          reads=[B[1]], writes=[gv32, st1])
                    zgroup(B[2], 1792, 512)
                    mk.op('act', lambda e: e.activation(out=siga[:, 0:512], in_=B[2][:], func=AF.Sigmoid), reads=[B[2]], writes=[siga])
                    zgroup(B[3], 2304, 512)
                    mk.op('act', lambda e: e.activation(out=siga[:, 512:1024], in_=B[3][:], func=AF.Sigmoid), reads=[B[3]], writes=[siga])
                    zgroup(B[1], 2816, 512)
                    mk.op('act', lambda e: e.activation(out=sigb[:, 0:512], in_=B[1][:], func=AF.Sigmoid), reads=[B[1]], writes=[sigb])
                    zgroup(B[2], 3328, 512)
                    mk.op('act', lambda e: e.activation(out=sigb[:, 512:1024], in_=B[2][:], func=AF.Sigmoid), reads=[B[2]], writes=[sigb])

                    chk(2)
                    mk.inter = (prevE, 2)
                    mk.op('dve', lambda e: e.tensor_reduce(out=st1[:, 8:16], in_=qsq[:].rearrange("t (h d) -> t h d", h=8),
                                                           axis=AX.X, op=ALU.add), reads=[qsq], writes=[st1])
                    rsqrt(st1, st1[:, 8:16], st1[:, 8:16], 1.0 / 64)
                    mk.op('dve', lambda e: e.tensor_tensor(out=q32[:].rearrange("t (h d) -> t h d", h=8),
                                                           in0=q32[:].rearrange("t (h d) -> t h d", h=8),
                                                           in1=st1[:, 8:16].unsqueeze(2).to_broadcast([128, 8, 64]), op=ALU.mult),
                          reads=[q32, st1], writes=[q32])
                    mk.op('dve', lambda e: e.tensor_tensor(out=qn[:].rearrange("t (p w d) -> t w p d", p=4, w=2),
                                                           in0=q32[:].rearrange("t (w p d) -> t w p d", w=2, p=4),
                                                           in1=gq8[:].unsqueeze(1).unsqueeze(1).to_broadcast([128, 2, 4, 64]),
                                                           op=ALU.mult), reads=[q32, gq8], writes=[qn])
                    mk.op('dve', lambda e: e.tensor_reduce(out=st1[:, 5:7], in_=ksq[:].rearrange("t (h d) -> t h d", h=2),
                                                           axis=AX.X, op=ALU.add), reads=[ksq], writes=[st1])
                    rsqrt(st1, st1[:, 5:7], st1[:, 5:7], 1.0 / 64)
                    mk.op('dve', lambda e: e.tensor_tensor(out=k32[:].rearrange("t (h d) -> t h d", h=2),
                                                           in0=k32[:].rearrange("t (h d) -> t h d", h=2),
                                                           in1=st1[:, 5:7].unsqueeze(2).to_broadcast([128, 2, 64]), op=ALU.mult),
                          reads=[k32, st1], writes=[k32])
                    mk.op('dve', lambda e: e.tensor_tensor(out=kn32[:].rearrange("t (h d) -> t h d", h=2),
                                                           in0=k32[:].rearrange("t (h d) -> t h d", h=2),
                                                           in1=gk[:].unsqueeze(1).to_broadcast([128, 2, 64]), op=ALU.mult),
                          reads=[k32, gk], writes=[kn32])
                    mk.op('act', lambda e: e.copy(out=knb[:], in_=kn32[:]), reads=[kn32], writes=[knb])
                    mk.op('act', lambda e: e.copy(out=vaug[cur][:].rearrange("t (g e) -> t g e", g=2)[:, :, 0:64],
                                                  in_=v32[:].rearrange("t (g d) -> t g d", g=2)), reads=[v32], writes=[vaug[cur]])
                    chk(5)
                    mk.op('act', lambda e: e.activation(out=sc32[:, 0:512], in_=gv32[:], func=AF.Square, accum_out=st1[:, 3:4]),
                          reads=[gv32], writes=[sc32, st1])
                    mk.op('dve', lambda e: e.tensor_scalar(out=st1[:, 4:5], in0=st1[:, 4:5], scalar1=1.0 / 512, scalar2=None,
                                                           op0=ALU.mult), reads=[st1], writes=[st1])
                    mk.op('dve', lambda e: e.tensor_tensor(out=st1[:, 7:8], in0=st1[:, 4:5], in1=st1[:, 4:5], op=ALU.mult),
                          reads=[st1], writes=[st1])
                    mk.op('dve', lambda e: e.scalar_tensor_tensor(out=st1[:, 3:4], in0=st1[:, 3:4], scalar=1.0 / 512, in1=st1[:, 7:8],
                                                                  op0=ALU.mult, op1=ALU.subtract), reads=[st1], writes=[st1])
                    rsqrt(st1, st1[:, 3:4], st1[:, 3:4], 1.0)
                    mk.op('dve', lambda e: e.tensor_scalar(out=vn32[:], in0=gv32[:], scalar1=st1[:, 4:5], scalar2=st1[:, 3:4],
                                                           op0=ALU.subtract, op1=ALU.mult), reads=[gv32, st1], writes=[vn32])
                    mk.op('dve', lambda e: e.tensor_tensor(out=vn32[:], in0=vn32[:], in1=ln_g[:], op=ALU.mult),
                          reads=[vn32, ln_g], writes=[vn32])
                    mk.op('dve', lambda e: e.tensor_tensor(out=vn32[:], in0=vn32[:], in1=ln_b[:], op=ALU.add),
                          reads=[vn32, ln_b], writes=[vn32])
                    mk.op('act', lambda e: e.copy(out=vnb[:], in_=vn32[:]), reads=[vn32], writes=[vnb])
                    wsT = wsT_s if sample else wsT_p
                    bcol = bcol_s if sample else bcol_p

                    def fsgu(e, wsT=wsT):
                        last = None
                        for g in range(4):
                            last = e.matmul(B[3][:, g * 128:(g + 1) * 128], lhsT=wsT[:, g * 128:(g + 1) * 128],
                                            rhs=vnb[:, g * 128:(g + 1) * 128], start=True, stop=True)
                        return last
                    mk.op('pe', fsgu, reads=[wsT, vnb], writes=[B[3]], skip_self=True)
                    for g in range(4):
                        mk.op('dve', lambda e, g=g, bcol=bcol: e.scalar_tensor_tensor(
                            out=mb[:, g * 128:(g + 1) * 128], in0=B[3][:, g * 128:(g + 1) * 128], scalar=bcol[:, g:g + 1],
                            in1=u32[:, g * 128:(g + 1) * 128], op0=ALU.add, op1=ALU.mult), reads=[B[3], bcol, u32], writes=[mb])

                    chk(3)
                    def trq(e):
                        last = None
                        for p in range(4):
                            last = pe_T(B[0], p * 128, qn, qn[:, p * 128:(p + 1) * 128])(e)
                        last = pe_T(B[0], 512, knb, knb[:])(e)
                        return last
                    mk.op('pe', trq, reads=[qn, knb, ident], writes=[B[0]], skip_self=True)
                    b0v = B[0][:].bitcast(BF16)
                    mk.op('act', lambda e: e.copy(out=qT[:], in_=b0v[:, 0:512]), reads=[B[0]], writes=[qT])
                    mk.op('act', lambda e: e.copy(out=kT[cur][:], in_=b0v[:, 512:640]), reads=[B[0]], writes=[kT[cur]])
                    qT3 = qT[:].rearrange("p (q t) -> p q t", q=4)

                    chk(4)
                    pT4 = pT[:].rearrange("s (b h t) -> s b h t", b=2, h=8)
                    blks = [1] if (ti == 0 or sample) else [0, 1]
                    for blk in blks:
                        kt = kT[cur] if blk == 1 else kT[prv]
                        banks = (B[4], B[5])

                        def fsc(e, kt=kt, banks=banks):
                            last = None
                            for h in range(8):
                                w, p = h // 4, h % 4
                                bank = banks[h // 4]
                                last = e.matmul(bank[:, (h % 4) * 128:(h % 4 + 1) * 128], lhsT=kt[w * 64:(w + 1) * 64, :],
                                                rhs=qT3[w * 64:(w + 1) * 64, p, :], start=True, stop=True)
                            return last
                        mk.op('pe', fsc, reads=[kt, qT], writes=list(banks), skip_self=True)
                        bsrc = bias_sn if sample else bias_p
                        boff = 0 if sample else blk * 1024
                        for hb_ in range(2):
                            mk.op('dve', lambda e, hb_=hb_, banks=banks, bsrc=bsrc, boff=boff: e.tensor_tensor(
                                out=sc32[:, hb_ * 512:(hb_ + 1) * 512], in0=banks[hb_][:],
                                in1=bsrc[:, boff + hb_ * 512: boff + (hb_ + 1) * 512], op=ALU.add),
                                reads=[banks[hb_], bsrc], writes=[sc32])
                        mk.op('act', lambda e, blk=blk: e.activation(out=pT[:, blk * 1024:(blk + 1) * 1024], in_=sc32[:], func=AF.Exp),
                              reads=[sc32], writes=[pT])

                    def pv_out(h):
                        return B[1 + h // 4][:, (h % 4) * 128:(h % 4) * 128 + 65]

                    def fpv(e):
                        last = None
                        for h in range(8):
                            w = h // 4
                            for bi, blk in enumerate(blks):
                                va = vaug[cur] if blk == 1 else vaug[prv]
                                st_ = (bi == 0) if not sample else (h % 4 == 0)
                                last = e.matmul(pv_out(h), lhsT=pT4[:, blk, h, :], rhs=va[:, w * 65:(w + 1) * 65],
                                                start=st_, stop=(bi == len(blks) - 1 and not sample), skip_group_check=sample)
                        return last
                    mk.op('pe', fpv, reads=[pT, vaug[0], vaug[1]], writes=[B[1], B[2]], skip_self=True)

                    if sample:
                        chk(4.1)
                        mk.inter = None
                        mk.play(prevE, 10 ** 9)
                        mk.dma('sp', stage[:].rearrange("s (b c) -> s b c", b=16), ck[:, :, :], writes=[stage], semt=stage)
                        mk.op('act', lambda e: e.copy(out=ckb[:], in_=stage[:]), reads=[stage], writes=[ckb])
                        for half in range(2):
                            bank = B[4 + half * 2], B[5 + half * 2]

                            def ftr(e, half=half, bank=bank):
                                last = None
                                for bb in range(8):
                                    b_ = half * 8 + bb
                                    last = pe_T(bank[bb // 4], (bb % 4) * 128, ckb, ckb[:, b_ * 128:(b_ + 1) * 128])(e)
                                return last
                            mk.op('pe', ftr, reads=[ckb, ident], writes=list(bank), skip_self=True)
                            for q_ in range(2):
                                mk.op('act', lambda e, half=half, q_=q_, bank=bank: e.copy(
                                    out=kTp[:, (half * 8 + q_ * 4) * 128:(half * 8 + q_ * 4 + 4) * 128],
                                    in_=bank[q_][:].bitcast(BF16)[:, 0:512]), reads=[bank[q_]], writes=[kTp])
                        chk(4.2)
                        if not cfg.get('skip_vdma'):
                            mk.dma('sp', stage[:].rearrange("s (b c) -> s b c", b=16), cv[:, :, :], writes=[stage], semt=stage)
                        if cfg.get('skip_vcopy'):
                            raise _StopTile()
                        for g_ in range(2):
                            mk.op('dve', lambda e, g_=g_: e.tensor_copy(
                                out=vpaug[:].rearrange("s (b g e) -> s b g e", b=16, g=2)[:, :, g_, 0:64],
                                in_=stage[:].rearrange("s (b g d) -> s b g d", b=16, g=2)[:, :, g_, :]),
                                reads=[stage], writes=[vpaug])
                        chk(4.3)
                        kTp3 = kTp[:].rearrange("p (b s) -> p b s", b=16)
                        mk.op('act', lambda e: e.copy(out=qTs[:].rearrange("p (b q t) -> p b q t", b=16, q=4),
                                                      in_=qT[:].rearrange("p (q b t) -> p b q t", q=4, b=16)),
                              reads=[qT], writes=[qTs])
                        chk(4.31)

                        def fps(e):
                            last = None
                            for w in range(2):
                                bank = B[4 + w]
                                for b_ in range(16):
                                    c0 = b_ * 32
                                    last = e.matmul(bank[:, c0:c0 + 32], lhsT=kTp3[w * 64:(w + 1) * 64, b_, :],
                                                    rhs=qTs[w * 64:(w + 1) * 64, b_ * 32:(b_ + 1) * 32], start=True, stop=True)
                            return last
                        mk.op('pe', fps, reads=[kTp, qTs], writes=[B[4], B[5]], skip_self=True)
                        chk(4.32)
                        for half in range(2):
                            mk.op('dve', lambda e, half=half: e.tensor_tensor(
                                out=sc32[:, half * 512:(half + 1) * 512].rearrange("s (b c) -> s b c", b=16),
                                in0=B[4 + half][:].rearrange("s (b c) -> s b c", b=16),
                                in1=bias_sp[:, half * 32:(half + 1) * 32].unsqueeze(1).to_broadcast([128, 16, 32]), op=ALU.add),
                                reads=[B[4 + half], bias_sp], writes=[sc32])
                        chk(4.4)
                        sc5 = sc32[:].rearrange("s (w b p t) -> s b w p t", w=2, b=16, p=4)
                        vp4 = vpaug[:].rearrange("s (b g e) -> s b g e", b=16, g=2)
                        for b_ in range(16):
                            E = Eb[b_ % 2]
                            E3 = E[:].rearrange("s (h t) -> s h t", h=8)
                            mk.op('pool', lambda e, E=E: e.memset(E[:], 0.0), writes=[E])
                            mk.op('act', lambda e, E=E, b_=b_: e.activation(
                                out=E[:].rearrange("s (w p t) -> s w p t", w=2, p=4)[:, :, :, b_ * 8:(b_ + 1) * 8],
                                in_=sc5[:, b_, :, :, :], func=AF.Exp), reads=[sc32], writes=[E])

                            def fpp(e, E3=E3, b_=b_):
                                last = None
                                for h in range(8):
                                    last = e.matmul(pv_out(h), lhsT=E3[:, h, :], rhs=vp4[:, b_, h // 4, :],
                                                    start=False, stop=(b_ == 15), skip_group_check=True)
                                return last
                            mk.op('pe', fpp, reads=[E, vpaug], writes=[B[1], B[2]], skip_self=True)

                    chk(4.5)
                    for half in range(2):
                        bank = B[1 + half]
                        b3 = bank[:].rearrange("t (h e) -> t h e", h=4)
                        mk.op('dve', lambda e, b3=b3, half=half: e.tensor_tensor(
                            out=st1[:, 8 + half * 4: 12 + half * 4].unsqueeze(2), in0=b3[:, :, 64:65],
                            in1=sinkexp[:, half * 4:(half + 1) * 4].unsqueeze(2), op=ALU.add),
                            reads=[bank, sinkexp], writes=[st1])
                    mk.op('dve', lambda e: e.reciprocal(out=st1[:, 8:16], in_=st1[:, 8:16]), reads=[st1], writes=[st1])
                    for half in range(2):
                        bank = B[1 + half]
                        b3 = bank[:].rearrange("t (h e) -> t h e", h=4)
                        mk.op('dve', lambda e, b3=b3, half=half: e.tensor_tensor(
                            out=att[:, half * 256:(half + 1) * 256].rearrange("t (h d) -> t h d", h=4), in0=b3[:, :, 0:64],
                            in1=st1[:, 8 + half * 4: 12 + half * 4].unsqueeze(2).to_broadcast([128, 4, 64]), op=ALU.mult),
                            reads=[bank, st1], writes=[att])

                    if tpos + 1 < len(tiles):
                        emit_F1(tiles[tpos + 1])
                    chk(6)
                    if ti == NPT - 1 or sample:
                        ok_, ov_, os_ = (o_ks, o_vs, o_ss) if sample else (o_kp, o_vp, o_sp)
                        mk.dma('sp', ok_[:, :], kn32[:], reads=[kn32], semt=kn32)
                        mk.dma('sp', ov_[:, :], v32[:], reads=[v32], semt=v32)
                        mk.dma('sp', os_[:, :], vn32[:], reads=[vn32], semt=vn32)
                        out_tiles.extend([kn32, v32, vn32])

                    chk(7)
                    mk.inter = None
                    mk.play(prevE, 10 ** 9)
                    if dbg_am is not None:
                        mk.op('dve', lambda e: e.tensor_copy(out=t1[:, 0:512], in_=att[:]), reads=[att], writes=[t1])
                        mk.op('dve', lambda e: e.tensor_copy(out=t1[:, 512:1024], in_=mb[:]), reads=[mb], writes=[t1])
                        mk.dma('sp', dbg_am[ti * 128:(ti + 1) * 128, :], t1[:], reads=[t1], semt=t1)
                    mk.rec = []
                    mk.rate = 1.0
                    def tram(e):
                        last = None
                        for c in range(4):
                            last = pe_T(B[0], c * 128, att, att[:, c * 128:(c + 1) * 128])(e)
                        for c in range(4):
                            last = pe_T(B[0], 512 + c * 128, mb, mb[:, c * 128:(c + 1) * 128])(e)
                        return last
                    mk.op('pe', tram, reads=[att, mb, ident], writes=[B[0]], skip_self=True)
                    mk.op('act', lambda e: e.copy(out=amT[:], in_=B[0][:].bitcast(BF16)), reads=[B[0]], writes=[amT])
                    amT3 = amT[:].rearrange("p (c t) -> p c t", c=8)

                    def fmerge_ab(e, w_v, off):
                        last = None
                        for n in range(2):
                            for c in range(4):
                                last = e.matmul(B[6 + n][:], lhsT=amT3[:, off + c, :], rhs=w_v[:, c, n * 512:(n + 1) * 512],
                                                start=(c == 0), stop=(c == 3))
                        return last
                    mk.op('pe', lambda e: fmerge_ab(e, wa_v, 0), reads=[amT, wab], writes=[B[6], B[7]], skip_self=True)
                    for n in range(2):
                        sl = slice(n * 512, (n + 1) * 512)
                        mk.op('dve', lambda e, n=n, sl=sl: e.tensor_tensor(out=t1[:, sl], in0=B[6 + n][:], in1=siga[:, sl], op=ALU.mult),
                              reads=[B[6 + n], siga], writes=[t1])
                    mk.op('pe', lambda e: fmerge_ab(e, wb_v, 4), reads=[amT, wab], writes=[B[6], B[7]], skip_self=True)
                    for n in range(2):
                        sl = slice(n * 512, (n + 1) * 512)
                        mk.op('dve', lambda e, n=n, sl=sl: e.tensor_tensor(out=stage[:, sl], in0=B[6 + n][:], in1=sigb[:, sl], op=ALU.mult),
                              reads=[B[6 + n], sigb], writes=[stage])
                    mk.op('dve', lambda e: e.tensor_tensor(out=hb[:], in0=stage[:, 0:1024], in1=t1[:], op=ALU.add), reads=[stage, t1], writes=[hb])
                    mk.op('pe', lambda e: tr8(e, src=hb, bank=B[0]), reads=[hb, ident], writes=[B[0]], skip_self=True)
                    mk.op('act', lambda e: e.copy(out=hT[:], in_=B[0][:].bitcast(BF16)), reads=[B[0]], writes=[hT])
                    hT3 = hT[:].rearrange("p (c t) -> p c t", c=8)

                    def fwo(e):
                        last = None
                        for n in range(2):
                            for c in range(8):
                                last = e.matmul(B[6 + n][:], lhsT=hT3[:, c, :], rhs=wo_v[:, c, n * 512:(n + 1) * 512],
                                                start=(c == 0), stop=(c == 7))
                        return last
                    mk.op('pe', fwo, reads=[hT, wot], writes=[B[6], B[7]], skip_self=True)
                    x1 = x1t[ti % 2]
                    for n in range(2):
                        sl = slice(n * 512, (n + 1) * 512)
                        mk.op('dve', lambda e, n=n, sl=sl, x1=x1: e.tensor_tensor(out=x1[:, sl], in0=B[6 + n][:], in1=x[:, sl], op=ALU.add),
                              reads=[B[6 + n], x], writes=[x1])
                    mk.dma('sp', x1s[ti * 128:(ti + 1) * 128, :], x1[:], reads=[x1], writes=[x1_tok[ti]], semt=x1)
                    if dbg_x1 is not None:
                        mk.dma('sp', dbg_x1[ti * 128:(ti + 1) * 128, :], x1[:], reads=[x1], semt=x1)
                    pend = mk.rec
                    mk.rec = None
                    return pend
                except _StopTile:
                    mk.rec = None
                    mk.inter = None
                    return []

            prevE = []
            for tpos, ti in enumerate(tiles):
                prevE = tile_body(tpos, ti, prevE)
            mk.play(prevE, 10 ** 9)

            mk.barrier()
        with ExitStack() as es2:
            mk.es = es2
            warena2 = mk.sb("warena2", [128, 8 * 2048 + 2048 + 8 * D + 2 * D], BF16)
            o = 0
            wq_v = warena2[:, o:o + 8 * 2048].rearrange("p (c n) -> p c n", c=8); o += 8 * 2048
            keys_v = warena2[:, o:o + 2048].rearrange("p (g n) -> p g n", g=16); o += 2048
            wpg_v = warena2[:, o:o + 8 * D].rearrange("p (c n) -> p c n", c=8); o += 8 * D
            wpl_v = warena2[:, o:o + 2 * D].rearrange("p (c n) -> p c n", c=2); o += 2 * D
            wsem2 = T("wsem2")
            w_q_r = w_q.rearrange("(c p) n -> p c n", p=128)
            w_pg_r = w_pg.rearrange("(c p) n -> p c n", p=128)
            w_pl_r = w_ple.rearrange("(c p) n -> p c n", p=128)
            for c in range(8):
                mk.dma('pool', wq_v[:, c, :], w_q_r[:, c, :], semt=wsem2)
                mk.dma('pool', wpg_v[:, c, :], w_pg_r[:, c, :], semt=wsem2)
            for c in range(2):
                mk.dma('pool', wpl_v[:, c, :], w_pl_r[:, c, :], semt=wsem2)
            mk.dma('pool', keys_v, keysT.rearrange("p (g n) -> p g n", g=16), semt=wsem2)
            warena2.st.w = (wsem2.st.dsem, mk.cnt[wsem2.st.dsem])
            cgrp2 = T("cgrp2")

            def cload2(name, src, shape, dt=F32, q='sp'):
                t = mk.sb(name, shape, dt)
                mk.dma(q, t[:], src[:, :], writes=[t], semt=cgrp2)
                return t
            g_ffn = cload2("g_ffn", sin["g_ffn"], [128, 1024])
            g_ple = cload2("g_ple", sin["g_ple"], [128, 1024])
            iota16 = cload2("iota16", cin["iota16"], [128, 16])
            identf = cload2("identf", cin["ident"], [128, 128])
            mk.settle([g_ffn, g_ple, iota16, identf])

            xt2 = [mk.sb("x2t%d" % i, [128, D], F32) for i in range(2)]
            pt2 = [mk.sb("p2t%d" % i, [128, 256], F32) for i in range(2)]
            xn2_ = [mk.sb("xn2_%d" % i, [128, D], F32) for i in range(2)]
            xn2b = mk.sb("xn2b", [128, D], BF16)
            xn2T = mk.sb("xn2T", [128, D], BF16)
            st2 = mk.sb("st2", [128, 16], F32)
            st2A = mk.sb("st2A", [128, 16], F32)
            scp = mk.sb("scp", [128, 2048], F32)
            work = mk.sb("work", [128, 2048], F32)
            qpT = T("qpT", work[:].bitcast(BF16)[:, 0:2048], st=work.st)
            stop_ = mk.sb("stop", [128, 256], F32)
            itopu = mk.sb("itopu", [128, 256], U32)
            itopf = mk.sb("itopf", [128, 256], F32)
            cand = T("cand", scp[:], st=scp.st)
            best = mk.sb("best", [128, 128], F32)
            posu = mk.sb("posu", [128, 128], U32)
            posi = mk.sb("posi", [128, 256], I32)
            posf = mk.sb("posf", [128, 256], F32)
            oh = T("oh", work[:], st=work.st)
            oh2 = T("oh2", cand[:], st=cand.st)
            sel = mk.sb("sel", [128, 256], F32)
            junkA = T("junkA", oh[:, 0:1024], st=oh.st)
            idxf = mk.sb("idxf", [128, 128], F32)
            idxi_ = [mk.sb("idxi%d" % i, [128, 128], I32) for i in range(2)]
            gate_ = [mk.sb("gate%d" % i, [128, 128], F32) for i in range(2)]
            dots = mk.sb("dots", [128, 128], F32)
            actv = mk.sb("actv", [128, 128], F32)
            wgt = mk.sb("wgt", [128, 128], F32)
            NGB = 19
            gbuf = [mk.sb("gbuf%d" % i, [128, 2 * D], BF16) for i in range(NGB)]
            junk = mk.sb("junk", [128, D], F32)
            pleg = mk.sb("pleg", [128, D], F32)
            NSM = 19
            dot_r = [mk.sb("dot_r%d" % i, [128, 1], F32) for i in range(NSM)]
            wv_r = [mk.sb("wv_r%d" % i, [128, 2], F32) for i in range(NSM)]
            dg_r = [mk.sb("dg_r%d" % i, [128, 128], BF16) for i in range(8)]
            xn3 = mk.sb("xn3", [128, D], BF16)
            xn3T = mk.sb("xn3T", [128, D], BF16)
            pb16 = mk.sb("pb16", [128, 256], BF16)
            ppT = mk.sb("ppT", [128, 256], BF16)
            print("phase2 sbuf remaining", nc.sbuf_bytes_remaining, flush=True)
            B = pbanks

            def load2(ti):
                mk.dma('sp', xt2[ti % 2][:], x1s[ti * 128:(ti + 1) * 128, :], reads=[x1_tok[ti]], writes=[xt2[ti % 2]],
                       semt=xt2[ti % 2])
                mk.dma('sp', pt2[ti % 2][:], pin[ti * 128:(ti + 1) * 128, :], writes=[pt2[ti % 2]], semt=pt2[ti % 2])

            def rms(xsrc, gt, out32, outb, st2, junk):
                mk.op('act', lambda e: e.activation(out=junk[:], in_=xsrc[:], func=AF.Square, accum_out=st2[:, 0:1]),
                      reads=[xsrc], writes=[junk, st2])
                rsqrt(st2, st2[:, 2:3], st2[:, 0:1], 1.0 / D)
                if out32 is not None:
                    mk.op('dve', lambda e: e.scalar_tensor_tensor(out=out32[:], in0=xsrc[:], scalar=st2[:, 2:3], in1=gt[:],
                                                                  op0=ALU.mult, op1=ALU.mult), reads=[xsrc, st2, gt], writes=[out32])
                    mk.op('act', lambda e: e.copy(out=outb[:], in_=out32[:]), reads=[out32], writes=[outb])
                else:
                    mk.op('dve', lambda e: e.scalar_tensor_tensor(out=outb[:], in0=xsrc[:], scalar=st2[:, 2:3], in1=gt[:],
                                                                  op0=ALU.mult, op1=ALU.mult), reads=[xsrc, st2, gt], writes=[outb])

            def tr8b(src, dst, n=8):
                def f(e):
                    last = None
                    for c in range(n):
                        last = pe_T(B[0], c * 128, src, src[:, c * 128:(c + 1) * 128])(e)
                    return last
                mk.op('pe', f, reads=[src, ident], writes=[B[0]], skip_self=True)
                mk.op('act', lambda e: e.copy(out=dst[:, 0:n * 128], in_=B[0][:].bitcast(BF16)[:, 0:n * 128]),
                      reads=[B[0]], writes=[dst])

            tiles2 = tiles if cfg.get('phase2', True) else []

            def stageA(ti):
                par = ti % 2
                x = xt2[par]
                xn2 = xn2_[par]
                idxi = idxi_[par]
                gate = gate_[par]
                mk.rate = 2.0
                load2(ti)
                rms(x, g_ffn, xn2, xn2b, st2A, junkA)
                tr8b(xn2b, xn2T)
                xT3 = xn2T[:].rearrange("p (c t) -> p c t", c=8)
                for bq in range(4):
                    bank = B[1 + bq]

                    def fq(e, bq=bq, bank=bank):
                        last = None
                        for gi in range(4):
                            g_ = bq * 4 + gi
                            for c in range(8):
                                last = e.matmul(bank[:, gi * 128:(gi + 1) * 128], lhsT=wq_v[:, c, g_ * 128:(g_ + 1) * 128],
                                                rhs=xT3[:, c, :], start=(c == 0), stop=(c == 7))
                        return last
                    mk.op('pe', fq, reads=[xn2T, warena2], writes=[bank], skip_self=True)
                    mk.op('act', lambda e, bq=bq, bank=bank: e.copy(out=qpT[:, bq * 512:(bq + 1) * 512], in_=bank[:]),
                          reads=[bank], writes=[qpT])
                for bq in range(4):
                    bank = B[1 + bq]

                    def fs(e, bq=bq, bank=bank):
                        last = None
                        for gi in range(4):
                            g_ = bq * 4 + gi
                            last = e.matmul(bank[:, gi * 128:(gi + 1) * 128], lhsT=qpT[:, g_ * 128:(g_ + 1) * 128],
                                            rhs=keys_v[:, g_, :], start=True, stop=True)
                        return last
                    mk.op('pe', fs, reads=[qpT, warena2], writes=[bank], skip_self=True)
                    mk.op('act', lambda e, bq=bq, bank=bank: e.copy(out=scp[:, bq * 512:(bq + 1) * 512], in_=bank[:]),
                          reads=[bank], writes=[scp])
                mk.rate = 4.0
                def sg_(g_):
                    return scp[:, g_ * 128:(g_ + 1) * 128]

                def wk_(g_):
                    return work[:, g_ * 128:(g_ + 1) * 128]
                for g_ in range(16):
                    mk.op('dve', lambda e, g_=g_: e.max(out=stop_[:, g_ * 16:g_ * 16 + 8], in_=sg_(g_)),
                          reads=[scp], writes=[stop_], skip_self=True)
                for g_ in range(16):
                    mk.op('dve', lambda e, g_=g_: e.match_replace(out=wk_(g_), in_to_replace=stop_[:, g_ * 16:g_ * 16 + 8],
                                                                  in_values=sg_(g_), imm_value=-1e30),
                          reads=[scp, stop_], writes=[work], skip_self=True)
                for g_ in range(16):
                    mk.op('dve', lambda e, g_=g_: e.max(out=stop_[:, g_ * 16 + 8:g_ * 16 + 16], in_=wk_(g_)),
                          reads=[work], writes=[stop_], skip_self=True)
                for g_ in range(16):
                    mk.op('dve', lambda e, g_=g_: e.max_index(out=itopu[:, g_ * 16:g_ * 16 + 8], in_max=stop_[:, g_ * 16:g_ * 16 + 8],
                                                              in_values=sg_(g_)), reads=[scp, stop_], writes=[itopu], skip_self=True)
                for g_ in range(16):
                    mk.op('dve', lambda e, g_=g_: e.max_index(out=itopu[:, g_ * 16 + 8:g_ * 16 + 16],
                                                              in_max=stop_[:, g_ * 16 + 8:g_ * 16 + 16], in_values=sg_(g_)),
                          reads=[scp, stop_], writes=[itopu], skip_self=True)
                mk.op('dve', lambda e: e.tensor_copy(out=itopf[:], in_=itopu[:]), reads=[itopu], writes=[itopf])
                st4 = stop_[:].rearrange("t (h c k) -> t h c k", h=8, c=2)
                mk.op('dve', lambda e: e.tensor_tensor(out=cand[:].rearrange("t (h i j) -> t h i j", h=8, i=16),
                                                       in0=st4[:, :, 0, :].unsqueeze(3).to_broadcast([128, 8, 16, 16]),
                                                       in1=st4[:, :, 1, :].unsqueeze(2).to_broadcast([128, 8, 16, 16]), op=ALU.add),
                      reads=[stop_], writes=[cand])
                def cg_(h):
                    return cand[:, h * 256:(h + 1) * 256]

                def wk2_(h):
                    return work[:, h * 256:(h + 1) * 256]
                for h in range(8):
                    mk.op('dve', lambda e, h=h: e.max(out=best[:, h * 16:h * 16 + 8], in_=cg_(h)),
                          reads=[cand], writes=[best], skip_self=(h > 0))
                for h in range(8):
                    mk.op('dve', lambda e, h=h: e.match_replace(out=wk2_(h), in_to_replace=best[:, h * 16:h * 16 + 8], in_values=cg_(h),
                                                                imm_value=-1e30), reads=[cand, best], writes=[work], skip_self=True)
                for h in range(8):
                    mk.op('dve', lambda e, h=h: e.max(out=best[:, h * 16 + 8:h * 16 + 16], in_=wk2_(h)),
                          reads=[work], writes=[best], skip_self=True)
                for h in range(8):
                    mk.op('dve', lambda e, h=h: e.max_index(out=posu[:, h * 16:h * 16 + 8], in_max=best[:, h * 16:h * 16 + 8],
                                                            in_values=cg_(h)), reads=[cand, best], writes=[posu], skip_self=True)
                for h in range(8):
                    mk.op('dve', lambda e, h=h: e.max_index(out=posu[:, h * 16 + 8:h * 16 + 16], in_max=best[:, h * 16 + 8:h * 16 + 16],
                                                            in_values=cg_(h)), reads=[cand, best], writes=[posu], skip_self=True)
                mk.op('dve', lambda e: e.tensor_single_scalar(out=posi[:, 0:128], in_=posu[:].bitcast(I32), scalar=4,
                                                              op=ALU.logical_shift_right), reads=[posu], writes=[posi])
                mk.op('dve', lambda e: e.tensor_single_scalar(out=posi[:, 128:256], in_=posu[:].bitcast(I32), scalar=15,
                                                              op=ALU.bitwise_and), reads=[posu], writes=[posi])
                mk.op('dve', lambda e: e.tensor_copy(out=posf[:], in_=posi[:]), reads=[posi], writes=[posf])
                it4 = itopf[:].rearrange("t (h c k) -> t h c k", h=8, c=2)
                ohs = (oh, oh2)
                oh4s = [o_[:].rearrange("t (h k i) -> t h k i", h=8, k=16) for o_ in ohs]
                for c in range(2):
                    pf = posf[:, c * 128:(c + 1) * 128].rearrange("t (h k) -> t h k", h=8)
                    mk.op('dve', lambda e, pf=pf, c=c: e.tensor_tensor(
                        out=oh4s[c], in0=iota16[:].unsqueeze(1).unsqueeze(1).to_broadcast([128, 8, 16, 16]),
                        in1=pf.unsqueeze(3).to_broadcast([128, 8, 16, 16]), op=ALU.is_equal), reads=[iota16, posf], writes=[ohs[c]])
                for c in range(2):
                    mk.op('dve', lambda e, c=c: e.tensor_tensor(
                        out=oh4s[c], in0=oh4s[c], in1=it4[:, :, c, :].unsqueeze(2).to_broadcast([128, 8, 16, 16]), op=ALU.mult),
                        reads=[ohs[c], itopf], writes=[ohs[c]], skip_self=True)
                for c in range(2):
                    mk.op('dve', lambda e, c=c: e.tensor_reduce(out=sel[:, c * 128:(c + 1) * 128],
                                                                in_=ohs[c][:].rearrange("t (s i) -> t s i", i=16), axis=AX.X, op=ALU.add),
                          reads=[ohs[c]], writes=[sel], skip_self=True)
                mk.op('dve', lambda e: e.scalar_tensor_tensor(out=idxf[:], in0=sel[:, 0:128], scalar=128.0, in1=sel[:, 128:256],
                                                              op0=ALU.mult, op1=ALU.add), reads=[sel], writes=[idxf])
                mk.op('dve', lambda e: e.tensor_copy(out=idxi[:], in_=idxf[:]), reads=[idxf], writes=[idxi])
                b3 = best[:].rearrange("t (h k) -> t h k", h=8)
                mk.op('dve', lambda e: e.tensor_tensor(out=gate[:].rearrange("t (h k) -> t h k", h=8), in0=b3,
                                                       in1=b3[:, :, 0:1].to_broadcast([128, 8, 16]), op=ALU.subtract),
                      reads=[best], writes=[gate])
                mk.op('act', lambda e: e.activation(out=gate[:], in_=gate[:], func=AF.Exp), reads=[gate], writes=[gate])
                mk.op('dve', lambda e: e.tensor_reduce(out=st2A[:, 8:16], in_=gate[:].rearrange("t (h k) -> t h k", h=8),
                                                       axis=AX.X, op=ALU.add), reads=[gate], writes=[st2A])
                mk.op('dve', lambda e: e.reciprocal(out=st2A[:, 8:16], in_=st2A[:, 8:16]), reads=[st2A], writes=[st2A])
                mk.op('dve', lambda e: e.tensor_tensor(out=gate[:].rearrange("t (h k) -> t h k", h=8),
                                                       in0=gate[:].rearrange("t (h k) -> t h k", h=8),
                                                       in1=st2A[:, 8:16].unsqueeze(2).to_broadcast([128, 8, 16]), op=ALU.mult),
                      reads=[gate, st2A], writes=[gate])

            def stageB(ti, nxt_q):
                par = ti % 2
                x = xt2[par]
                pt = pt2[par]
                xn2 = xn2_[par]
                idxi = idxi_[par]
                gate = gate_[par]
                mk._waits('pool', [comb_t], [])
                for s_ in range(128):
                    gb = gbuf[s_ % NGB]
                    dt_ = dot_r[s_ % NSM]
                    wv = wv_r[s_ % NSM]
                    dg = dg_r[s_ % 8]
                    mk.gather(gb, gb[:], comb[:, :], idxi, idxi[:, s_:s_ + 1])
                    mk.op('dve', lambda e, gb=gb, dt_=dt_: e.scalar_tensor_tensor(
                        out=junk[:], in0=gb[:, 0:D], scalar=1.0, in1=xn2[:], op0=ALU.mult, op1=ALU.mult,
                        accum_out=dt_[:, 0:1]), reads=[gb, xn2], writes=[junk, dt_], skip_self=True)
                    mk.op('act', lambda e, dt_=dt_, wv=wv: e.activation(out=wv[:, 0:1], in_=dt_[:, 0:1], func=GELU),
                          reads=[dt_], writes=[wv])
                    mk.op('act', lambda e, wv=wv, s_=s_: e.mul(out=wv[:, 1:2], in_=wv[:, 0:1], mul=gate[:, s_:s_ + 1]),
                          reads=[wv, gate], writes=[wv])
                    mk.op('act', lambda e, wv=wv, dg=dg: e.activation(out=dg[:], in_=identf[:], func=AF.Copy, scale=wv[:, 1:2]),
                          reads=[wv, identf], writes=[dg])

                    def fv(e, s_=s_, gb=gb, dg=dg):
                        last = None
                        for n in range(2):
                            last = e.matmul(B[5 + n][:], lhsT=dg[:], rhs=gb[:, D + n * 512:D + (n + 1) * 512],
                                            start=(s_ == 0), stop=(s_ == 127))
                        return last
                    mk.op('pe', fv, reads=[gb, dg], writes=[B[5], B[6]], skip_self=True)
                    mk.play(nxt_q, 1.0, lag=8)
                if len(nxt_q) and ti == 1:
                    print('replay leftover at drain:', len(nxt_q), flush=True)
                mk.play(nxt_q, 10 ** 9)
                for n in range(2):
                    sl = slice(n * 512, (n + 1) * 512)
                    mk.op('dve', lambda e, n=n, sl=sl: e.tensor_tensor(out=x[:, sl], in0=B[5 + n][:], in1=x[:, sl], op=ALU.add),
                          reads=[B[5 + n], x], writes=[x])
                mk.rec = []
                mk.rate = 1.0
                rms(x, g_ple, None, xn3, st2, pleg)
                tr8b(xn3, xn3T)
                x3T = xn3T[:].rearrange("p (c t) -> p c t", c=8)

                def fpg(e):
                    last = None
                    for n in range(2):
                        for c in range(8):
                            last = e.matmul(B[1 + n][:], lhsT=x3T[:, c, :], rhs=wpg_v[:, c, n * 512:(n + 1) * 512],
                                            start=(c == 0), stop=(c == 7))
                    return last
                mk.op('pe', fpg, reads=[xn3T, warena2], writes=[B[1], B[2]], skip_self=True)
                for n in range(2):
                    mk.op('act', lambda e, n=n: e.activation(out=pleg[:, n * 512:(n + 1) * 512], in_=B[1 + n][:], func=AF.Sigmoid),
                          reads=[B[1 + n]], writes=[pleg])
                mk.op('act', lambda e: e.copy(out=pb16[:], in_=pt[:]), reads=[pt], writes=[pb16])
                tr8b(pb16, ppT, n=2)
                pT3 = ppT[:].rearrange("p (c t) -> p c t", c=2)

                def fpl(e):
                    last = None
                    for n in range(2):
                        for c in range(2):
                            last = e.matmul(B[3 + n][:], lhsT=pT3[:, c, :], rhs=wpl_v[:, c, n * 512:(n + 1) * 512],
                                            start=(c == 0), stop=(c == 1))
                    return last
                mk.op('pe', fpl, reads=[ppT, warena2], writes=[B[3], B[4]], skip_self=True)
                for n in range(2):
                    sl = slice(n * 512, (n + 1) * 512)
                    mk.op('dve', lambda e, n=n, sl=sl: e.tensor_tensor(out=pleg[:, sl], in0=B[3 + n][:], in1=pleg[:, sl], op=ALU.mult),
                          reads=[B[3 + n], pleg], writes=[pleg])
                mk.op('dve', lambda e: e.tensor_tensor(out=x[:], in0=x[:], in1=pleg[:], op=ALU.add), reads=[x, pleg], writes=[x])
                mk.dma('sp', y[ti * 128:(ti + 1) * 128, :], x[:], reads=[x], semt=x)
                tail = mk.rec
                mk.rec = None
                return tail

            if tiles2:
                mk.rec = []
                stageA(tiles2[0])
                q0 = mk.rec
                mk.rec = None
                mk.play(q0, 10 ** 9)
            pending_tail = []
            for tpos, ti in enumerate(tiles2):
                nxt_q = pending_tail
                if tpos + 1 < len(tiles2):
                    mk.rec = []
                    stageA(tiles2[tpos + 1])
                    nxt_q = nxt_q + mk.rec
                    mk.rec = None
                if tpos == 1:
                    print('replay queue: ops', len(nxt_q), 'slot budget needed', sum(1.0 / r[3] for r in nxt_q), flush=True)
                pending_tail = stageB(ti, nxt_q)
            mk.play(pending_tail, 10 ** 9)
            out_tiles_all = out_tiles + xt2
            mk._waits('sp', [], xt2)
            mk.barrier()
        print("instructions emitted:", mk.n_ins, flush=True)
    return nc


_NC_CACHE = {}


def _get_nc():
    if "nc" not in _NC_CACHE:
        _NC_CACHE["nc"] = build()
    return _NC_CACHE["nc"]


def kernel(x_prompt, x_sample, cache_k, cache_v, p_prompt, p_sample,
           attn_norm_g, w_in, q_norm_g, k_norm_g, attn_sinks,
           sgu_norm_g, sgu_norm_b, sgu_w, sgu_b,
           w_branch_a, w_branch_b, w_out,
           ffn_norm_g, peer_w_q, peer_sub_keys, peer_u, peer_v,
           ple_norm_g, w_ple, w_ple_gate):
    f = lambda a: np.ascontiguousarray(np.asarray(a, dtype=np.float32))
    x_prompt, x_sample, cache_k, cache_v = f(x_prompt), f(x_sample), f(cache_k), f(cache_v)
    p_prompt, p_sample = f(p_prompt), f(p_sample)
    nc = _get_nc()
    consts = _consts()
    rep = lambda v, n=128: np.ascontiguousarray(np.broadcast_to(f(v).reshape(1, -1), (n, f(v).size)))
    sw = f(sgu_w)[0]
    sb = f(sgu_b)[0]
    small = {
        "g_attn": rep(attn_norm_g[0]), "g_ffn": rep(ffn_norm_g[0]), "g_ple": rep(ple_norm_g[0]),
        "gq": rep(q_norm_g[0]), "gk": rep(k_norm_g[0]), "sinks": rep(attn_sinks[0]),
        "ln_g": rep(sgu_norm_g[0]), "ln_b": rep(sgu_norm_b[0]),
        "bcol_p": np.ascontiguousarray(sb.T),
        "bcol_s": np.ascontiguousarray(np.tile(sb[:, :8].T, (16, 1))),
        "wsT_p": np.ascontiguousarray(sw.transpose(2, 0, 1)).reshape(128, 512),
        "wsT_s": np.ascontiguousarray(np.tile(sw[:, :8, :8].transpose(2, 0, 1), (16, 1, 16))).reshape(128, 512),
    }
    shared = {
        "w_in": f(w_in)[0], "w_a": f(w_branch_a)[0], "w_b": f(w_branch_b)[0], "w_out": f(w_out)[0],
        "w_q": f(peer_w_q)[0],
        "keysT": np.ascontiguousarray(f(peer_sub_keys)[0].reshape(16, 128, 128).transpose(2, 0, 1)).reshape(128, 2048),
        "peer_u": f(peer_u)[0], "peer_v": f(peer_v)[0], "w_ple": f(w_ple)[0], "w_pg": f(w_ple_gate)[0],
    }
    for k, v in consts.items():
        shared["c_" + k] = v
    for k, v in small.items():
        shared["s_" + k] = v
    in_maps = []
    for c in range(NCORES):
        m = dict(shared)
        xs = x_sample[c * 16:(c + 1) * 16].reshape(128, D)
        m["xin"] = np.ascontiguousarray(np.concatenate([x_prompt[c], xs], axis=0))
        ps = p_sample[0, c * 16:(c + 1) * 16].reshape(128, 256)
        m["pin"] = np.ascontiguousarray(np.concatenate([p_prompt[0, c], ps], axis=0))
        m["ck"] = np.ascontiguousarray(cache_k[0, c * 16:(c + 1) * 16].reshape(16, 128, 128).transpose(1, 0, 2))
        m["cv"] = np.ascontiguousarray(cache_v[0, c * 16:(c + 1) * 16].reshape(16, 128, 128).transpose(1, 0, 2))
        in_maps.append(m)
    res = run_bass_kernel_spmd(nc, in_maps, core_ids=list(range(NCORES)))
    R = res.results
    y_p = np.stack([R[c]["y"][:SEQ] for c in range(NCORES)])
    y_s = np.concatenate([R[c]["y"][SEQ:].reshape(16, 8, D) for c in range(NCORES)], axis=0)
    nkp = np.stack([R[c]["o_kp"].reshape(128, 2, 64) for c in range(NCORES)])[None]
    nvp = np.stack([R[c]["o_vp"].reshape(128, 2, 64) for c in range(NCORES)])[None]
    nks = np.concatenate([R[c]["o_ks"].reshape(16, 8, 2, 64) for c in range(NCORES)], axis=0)[None]
    nvs = np.concatenate([R[c]["o_vs"].reshape(16, 8, 2, 64) for c in range(NCORES)], axis=0)[None]
    nsp = np.stack([R[c]["o_sp"] for c in range(NCORES)])[None]
    nss = np.concatenate([R[c]["o_ss"].reshape(16, 8, 512) for c in range(NCORES)], axis=0)[None]
    return (y_p.astype(np.float32), y_s.astype(np.float32), nkp.astype(np.float32), nvp.astype(np.float32),
            nks.astype(np.float32), nvs.astype(np.float32), nsp.astype(np.float32), nss.astype(np.float32))
```

```python
import numpy as np
from contextlib import ExitStack
import concourse.bass as bass
import concourse.mybir as mybir
from concourse.bass_utils import run_bass_kernel_spmd

F32 = mybir.dt.float32
BF16 = mybir.dt.bfloat16
I32 = mybir.dt.int32
U32 = mybir.dt.uint32
ALU = mybir.AluOpType
AF = mybir.ActivationFunctionType
AX = mybir.AxisListType

NCORES = 8
D = 1024
SEQ = 4096
NPT = SEQ // 128
NT = NPT + 1
NTOK = NT * 128
IN_W = 3840
EPS = 1e-6
NEG = -30000.0
GELU = AF.Gelu_apprx_tanh


class St:
    __slots__ = ("w", "r", "dsem")

    def __init__(self):
        self.w = None
        self.r = []
        self.dsem = None


class T:
    def __init__(self, name, h=None, st=None):
        self.name = name
        self.h = h
        self.st = st if st is not None else St()

    def __getitem__(self, k):
        return self.h[k]


class MK:
    def __init__(self, nc, es):
        self.nc = nc
        self.es_sem = es
        self.es = es
        self.engs = {'pe': nc.tensor, 'act': nc.scalar, 'dve': nc.vector, 'pool': nc.gpsimd, 'sp': nc.sync}
        self.sems = {}
        self.cnt = {}
        self.waited = {k: {} for k in self.engs}
        for k in ('pe', 'act', 'dve', 'pool'):
            self.sems[k] = es.enter_context(nc.semaphore("c_" + k))
            self.cnt[k] = 0
        self.nd = 0
        self.n_ins = 0
        self.rec = None
        self.rate = 4.0
        self.slot = 0
        self.inter = None
        self._in_inter = False
        self._inter_cnt = 0
        self.prev_eng = 'dve'
        self.prev_slot = -100

    def sb(self, name, shape, dtype):
        h = self.es.enter_context(self.nc.sbuf_tensor(name, list(shape), dtype))
        return T(name, h)

    def ps(self, name, shape, dtype=F32):
        h = self.es.enter_context(self.nc.psum_tensor(name, list(shape), dtype))
        return T(name, h)

    def _dsem(self, t):
        st = t.st
        if st.dsem is None:
            key = "d%d" % self.nd
            self.nd += 1
            self.sems[key] = self.es_sem.enter_context(self.nc.semaphore(key))
            self.cnt[key] = 0
            st.dsem = key
        return st.dsem

    def _waits(self, eng, reads, writes, skip_self=False):
        need = {}

        def add(tok):
            if tok is None:
                return
            k, v = tok
            if need.get(k, 0) < v:
                need[k] = v
        for t in reads:
            add(t.st.w)
        for t in writes:
            add(t.st.w)
            for tok in t.st.r:
                add(tok)
        e = self.engs[eng]
        for k, v in need.items():
            if skip_self and k == eng:
                continue
            if self.waited[eng].get(k, 0) >= v:
                continue
            e.wait_ge(self.sems[k], v)
            self.n_ins += 1
            self.waited[eng][k] = v

    def _commit(self, tok, reads, writes):
        for t in reads:
            r = t.st.r
            r.append(tok)
            if len(r) > 48:
                best = {}
                for k, v in r:
                    if best.get(k, 0) < v:
                        best[k] = v
                t.st.r = list(best.items())
        for t in writes:
            t.st.w = tok
            t.st.r = []

    def op(self, eng, fn, reads=(), writes=(), skip_self=False):
        if self.rec is not None:
            self.rec.append(('op', (eng, fn), dict(reads=list(reads), writes=list(writes), skip_self=skip_self), self.rate))
            return None
        self._waits(eng, reads, writes, skip_self)
        ins = fn(self.engs[eng])
        self.cnt[eng] += 1
        ins.then_inc(self.sems[eng], 1)
        self.n_ins += 1
        self._commit((eng, self.cnt[eng]), reads, writes)
        if self.inter is not None and not self._in_inter:
            q, every = self.inter
            self._inter_cnt += 1
            if q and self._inter_cnt % every == 0:
                self._in_inter = True
                kind, a, kw, _rate = q.pop(0)
                getattr(self, kind)(*a, **kw)
                self._in_inter = False
        return ins

    def dma(self, q, out, in_, reads=(), writes=(), semt=None, **kw):
        if self.rec is not None:
            self.rec.append(('dma', (q, out, in_), dict(reads=list(reads), writes=list(writes), semt=semt, **kw), self.rate))
            return None
        self._waits(q, reads, writes)
        key = self._dsem(semt)
        ins = self.engs[q].dma_start(out=out, in_=in_, **kw)
        self.cnt[key] += 16
        ins.then_inc(self.sems[key], 16)
        self.n_ins += 1
        self._commit((key, self.cnt[key]), reads, writes)
        return ins

    def gather(self, out_t, out_ap, table_ap, idx_t, idx_ap):
        self._waits('pool', [idx_t], [out_t])
        key = self._dsem(out_t)
        ins = self.nc.gpsimd.indirect_dma_start(
            out=out_ap, out_offset=None, in_=table_ap,
            in_offset=bass.IndirectOffsetOnAxis(ap=idx_ap, axis=0))
        self.cnt[key] += 16
        ins.then_inc(self.sems[key], 16)
        self.n_ins += 1
        self._commit((key, self.cnt[key]), [idx_t], [out_t])
        return ins

    def settle(self, tiles):
        for t in tiles:
            if t.st.w is not None and t.st.w[0] in self.cnt and t.st.w[0].startswith("d"):
                t.st.w = (t.st.w[0], self.cnt[t.st.w[0]])

    def play(self, q, budget, lag=0):
        self.slot += 1
        while q and budget > 1e-9:
            kind, a, kw, rate = q[0]
            if budget + 1e-9 < 1.0 / rate and budget < 0.999:
                break
            eng = a[0]
            if lag and eng == 'dve' and self.prev_eng != 'dve' and self.slot - self.prev_slot < lag:
                break
            q.pop(0)
            getattr(self, kind)(*a, **kw)
            budget -= 1.0 / rate
            self.prev_eng = eng
            self.prev_slot = self.slot

    def barrier(self):
        for eng in self.engs:
            e = self.engs[eng]
            for k, v in self.cnt.items():
                if v == 0 or self.waited[eng].get(k, 0) >= v:
                    continue
                e.wait_ge(self.sems[k], v)
                self.n_ins += 1
                self.waited[eng][k] = v


def _consts():
    slopes = np.exp2(-np.arange(1, 9, dtype=np.float64)).astype(np.float32)
    s = np.arange(128)[:, None]
    t = np.arange(128)[None, :]
    c = {}
    c["ident"] = np.eye(128, dtype=np.float32)
    bp = np.full((128, 2, 8, 128), NEG, np.float32)
    d0 = (128 + t - s).astype(np.float32)
    d1 = (t - s).astype(np.float32)
    for h in range(8):
        bp[:, 0, h, :] = np.where(s > t, -slopes[h] * d0, NEG)
        bp[:, 1, h, :] = np.where(s <= t, -slopes[h] * d1, NEG)
    c["bias_p"] = bp.reshape(128, 2048)
    bs_, ts_ = s // 8, s % 8
    bq_, tq_ = t // 8, t % 8
    bsn = np.full((128, 8, 128), NEG, np.float32)
    ok = (bs_ == bq_) & (ts_ <= tq_)
    for h in range(8):
        bsn[:, h, :] = np.where(ok, -slopes[h] * (tq_ - ts_).astype(np.float32), NEG)
    c["bias_sn"] = bsn.reshape(128, 1024)
    r = np.arange(128)[:, None]
    tq = np.arange(8)[None, :]
    bsp = np.full((128, 8, 8), NEG, np.float32)
    for h in range(8):
        bsp[:, h, :] = np.where(r > tq, -slopes[h] * (tq + 128 - r).astype(np.float32), NEG)
    c["bias_sp"] = bsp.reshape(128, 64)
    c["maskT_p"] = (s <= t).astype(np.float32)
    c["maskT_s"] = ok.astype(np.float32)
    c["iota16"] = np.broadcast_to(np.arange(16, dtype=np.float32)[None, :], (128, 16)).copy()
    return c


CONST_SHAPES = {"ident": (128, 128), "bias_p": (128, 2048), "bias_sn": (128, 1024), "bias_sp": (128, 64),
                "maskT_p": (128, 128), "maskT_s": (128, 128), "iota16": (128, 16)}
SMALL_SHAPES = {"g_attn": (128, 1024), "g_ffn": (128, 1024), "g_ple": (128, 1024), "gq": (128, 64), "gk": (128, 64),
                "sinks": (128, 8), "ln_g": (128, 512), "ln_b": (128, 512),
                "bcol_p": (128, 4), "bcol_s": (128, 4), "wsT_p": (128, 512), "wsT_s": (128, 512)}


CFG = {}


class _StopTile(Exception):
    pass


def chk(n):
    if CFG.get('stage', 99) < n:
        raise _StopTile()


def build():
    nc = bass.Bass("TRN2", target_bir_lowering=False)
    cfg = CFG
    tiles = cfg.get('tiles', list(range(NT)))

    def din(name, shape, dt=F32):
        return nc.dram_tensor(name, list(shape), dt, kind="ExternalInput").ap()

    def dout(name, shape, dt=F32):
        return nc.dram_tensor(name, list(shape), dt, kind="ExternalOutput").ap()

    xin = din("xin", [NTOK, D])
    pin = din("pin", [NTOK, 256])
    ck = din("ck", [128, 16, 128])
    cv = din("cv", [128, 16, 128])
    w_in = din("w_in", [D, IN_W])
    w_a = din("w_a", [512, D])
    w_b = din("w_b", [512, D])
    w_out = din("w_out", [D, D])
    w_q = din("w_q", [D, 2048])
    keysT = din("keysT", [128, 2048])
    peer_u = din("peer_u", [16384, D])
    peer_v = din("peer_v", [16384, D])
    w_ple = din("w_ple", [256, D])
    w_pg = din("w_pg", [D, D])
    cin = {k: din("c_" + k, v) for k, v in CONST_SHAPES.items()}
    sin = {k: din("s_" + k, v) for k, v in SMALL_SHAPES.items()}

    y = dout("y", [NTOK, D])
    o_kp = dout("o_kp", [128, 128])
    o_vp = dout("o_vp", [128, 128])
    o_ks = dout("o_ks", [128, 128])
    o_vs = dout("o_vs", [128, 128])
    o_sp = dout("o_sp", [128, 512])
    o_ss = dout("o_ss", [128, 512])
    x1s = nc.dram_tensor("x1s", [NTOK, D], F32, kind="Internal").ap()
    comb = nc.dram_tensor("comb", [16384, 2 * D], BF16, kind="Internal").ap()
    dbg_x1 = dout("dbg_x1", [NTOK, D]) if cfg.get('dbg') else None
    dbg_am = dout("dbg_am", [NTOK, D]) if cfg.get('dbg') else None

    with ExitStack() as es0:
        mk = MK(nc, es0)
        x1_tok = [T("x1s%d" % i) for i in range(NT)]
        out_tiles = []

        ident = mk.sb("ident", [128, 128], BF16)
        cgrp = T("cgrp")
        mk.dma('pool', ident[:], cin["ident"][:, :], writes=[ident], semt=ident)
        pbanks = [mk.ps("pb%d" % i, [128, 512], F32) for i in range(8)]

        def pe_T(bank, col0, src_t, src_ap, n=128, extra_reads=()):
            pv = bank[:].bitcast(BF16)
            return lambda e: e.transpose(out=pv[0:n, col0:col0 + 128], in_=src_ap, identity=ident[:])

        epst = mk.sb("epst", [128, 1], F32)
        mk.op('pool', lambda e: e.memset(epst[:], EPS), writes=[epst])

        def rsqrt(tile, dst, src, scale):
            mk.op('act', lambda e: e.activation(out=dst, in_=src, func=AF.Sqrt, bias=epst[:, 0:1], scale=scale),
                  reads=[tile, epst], writes=[tile])
            mk.op('dve', lambda e: e.reciprocal(out=dst, in_=dst), reads=[tile], writes=[tile])

        with ExitStack() as es1:
            mk.es = es1
            warena = mk.sb("warena1", [128, 8 * IN_W + 4 * D + 4 * D + 8 * D], BF16)
            o = 0
            win_v = warena[:, o:o + 8 * IN_W].rearrange("p (c n) -> p c n", c=8); o += 8 * IN_W
            wa_v = warena[:, o:o + 4 * D].rearrange("p (c n) -> p c n", c=4); o += 4 * D
            wb_v = warena[:, o:o + 4 * D].rearrange("p (c n) -> p c n", c=4); o += 4 * D
            wo_v = warena[:, o:o + 8 * D].rearrange("p (c n) -> p c n", c=8); o += 8 * D
            ZG = [(0, 512), (512, 256), (768, 512), (1280, 512), (1792, 512), (2304, 512), (2816, 512), (3328, 512)]
            wg = [T("wg%d" % i) for i in range(8)]
            w_in_r = w_in.rearrange("(c p) n -> p c n", p=128)
            for gi, (c0, n) in enumerate(ZG):
                for c in range(8):
                    mk.dma('pool', win_v[:, c, c0:c0 + n], w_in_r[:, c, c0:c0 + n], semt=wg[gi])
                wg[gi].st.w = (wg[gi].st.dsem, mk.cnt[wg[gi].st.dsem])
            w_a_r = w_a.rearrange("(c p) n -> p c n", p=128)
            w_b_r = w_b.rearrange("(c p) n -> p c n", p=128)
            w_o_r = w_out.rearrange("(c p) n -> p c n", p=128)
            wab = T("wab")
            wot = T("wot")
            for c in range(4):
                mk.dma('pool', wa_v[:, c, :], w_a_r[:, c, :], semt=wab)
                mk.dma('pool', wb_v[:, c, :], w_b_r[:, c, :], semt=wab)
            wab.st.w = (wab.st.dsem, mk.cnt[wab.st.dsem])
            for c in range(8):
                mk.dma('pool', wo_v[:, c, :], w_o_r[:, c, :], semt=wot)
            wot.st.w = (wot.st.dsem, mk.cnt[wot.st.dsem])
            comb_t = T("comb")
            for c in range(16):
                rs = slice(c * 1024, (c + 1) * 1024)
                mk.dma('pool', comb[rs, 0:D], peer_u[rs, :], semt=comb_t)
                mk.dma('pool', comb[rs, D:2 * D], peer_v[rs, :], semt=comb_t)
            comb_t.st.w = (comb_t.st.dsem, mk.cnt[comb_t.st.dsem])

            def cload(name, src, shape, dt=F32, q='sp'):
                t = mk.sb(name, shape, dt)
                mk.dma(q, t[:], src[:, :], writes=[t], semt=cgrp)
                return t
            g_attn = cload("g_attn", sin["g_attn"], [128, 1024])
            gq = cload("gq", sin["gq"], [128, 64])
            gk = cload("gk", sin["gk"], [128, 64])
            sinks = cload("sinks", sin["sinks"], [128, 8])
            ln_g = cload("ln_g", sin["ln_g"], [128, 512])
            ln_b = cload("ln_b", sin["ln_b"], [128, 512])
            bcol_p = cload("bcol_p", sin["bcol_p"], [128, 4])
            bcol_s = cload("bcol_s", sin["bcol_s"], [128, 4])
            bias_p = cload("bias_p", cin["bias_p"], [128, 2048])
            bias_sn = cload("bias_sn", cin["bias_sn"], [128, 1024])
            bias_sp = cload("bias_sp", cin["bias_sp"], [128, 64])
            maskT_p = cload("maskT_p", cin["maskT_p"], [128, 128])
            maskT_s = cload("maskT_s", cin["maskT_s"], [128, 128])
            t1 = mk.sb("t1", [128, D], F32)
            ws32_p = T("ws32_p", t1[:, 0:512], st=t1.st)
            ws32_s = T("ws32_s", t1[:, 512:1024], st=t1.st)
            mk.dma('sp', ws32_p[:], sin["wsT_p"][:, :], writes=[ws32_p], semt=cgrp)
            mk.dma('sp', ws32_s[:], sin["wsT_s"][:, :], writes=[ws32_s], semt=cgrp)
            consts1 = [ident, g_attn, gq, gk, sinks, ln_g, ln_b, bcol_p, bcol_s, bias_p, bias_sn, bias_sp,
                       maskT_p, maskT_s, ws32_p, ws32_s]
            mk.settle(consts1)

            sinkexp = mk.sb("sinkexp", [128, 8], F32)
            mk.op('act', lambda e: e.activation(out=sinkexp[:], in_=sinks[:], func=AF.Exp), reads=[sinks], writes=[sinkexp])
            gq8 = mk.sb("gq8", [128, 64], F32)
            mk.op('dve', lambda e: e.tensor_scalar(out=gq8[:], in0=gq[:], scalar1=0.125, scalar2=None, op0=ALU.mult),
                  reads=[gq], writes=[gq8])
            wsT_p = mk.sb("wsT_p", [128, 512], BF16)
            wsT_s = mk.sb("wsT_s", [128, 512], BF16)
            mk.op('dve', lambda e: e.tensor_tensor(out=wsT_p[:].rearrange("s (g t) -> s g t", g=4),
                                                   in0=ws32_p[:].rearrange("s (g t) -> s g t", g=4),
                                                   in1=maskT_p[:].unsqueeze(1).to_broadcast([128, 4, 128]), op=ALU.mult),
                  reads=[ws32_p, maskT_p], writes=[wsT_p])
            mk.op('dve', lambda e: e.tensor_tensor(out=wsT_s[:].rearrange("s (g t) -> s g t", g=4),
                                                   in0=ws32_s[:].rearrange("s (g t) -> s g t", g=4),
                                                   in1=maskT_s[:].unsqueeze(1).to_broadcast([128, 4, 128]), op=ALU.mult),
                  reads=[ws32_s, maskT_s], writes=[wsT_s])

            xt = [mk.sb("xt%d" % i, [128, D], F32) for i in range(3)]
            x1t = [mk.sb("x1t%d" % i, [128, D], F32) for i in range(2)]
            xn = mk.sb("xn", [128, D], BF16)
            xnT = mk.sb("xnT", [128, D], BF16)
            st1 = mk.sb("st1", [128, 16], F32)
            q32 = mk.sb("q32", [128, 512], F32)
            qn = mk.sb("qn", [128, 512], BF16)
            qT = mk.sb("qT", [128, 512], BF16)
            k32 = mk.sb("k32", [128, 128], F32)
            kn32 = mk.sb("kn32", [128, 128], F32)
            knb = mk.sb("knb", [128, 128], BF16)
            kT = [mk.sb("kT%d" % i, [128, 128], BF16) for i in range(2)]
            v32 = mk.sb("v32", [128, 128], F32)
            vaug = [mk.sb("vaug%d" % i, [128, 2 * 65], BF16) for i in range(2)]
            sc32 = mk.sb("sc32", [128, 1024], F32)
            pT = mk.sb("pT", [128, 2048], BF16)
            att_ = [mk.sb("att%d" % i, [128, 512], BF16) for i in range(2)]
            u32 = mk.sb("u32", [128, 512], F32)
            gv32 = mk.sb("gv32", [128, 512], F32)
            vn32 = mk.sb("vn32", [128, 512], F32)
            vnb = mk.sb("vnb", [128, 512], BF16)
            mb_ = [mk.sb("mb%d" % i, [128, 512], BF16) for i in range(2)]
            amT = mk.sb("amT", [128, 1024], BF16)
            siga_ = [mk.sb("siga%d" % i, [128, D], BF16) for i in range(2)]
            sigb_ = [mk.sb("sigb%d" % i, [128, D], BF16) for i in range(2)]
            hb = mk.sb("hb", [128, D], BF16)
            hT = mk.sb("hT", [128, D], BF16)
            stage = mk.sb("stage", [128, 2048], F32)
            qsq = T("qsq", stage[:, 1024:1536], st=stage.st)
            ksq = T("ksq", stage[:, 1536:1664], st=stage.st)
            ckb = T("ckb", pT[:], st=pT.st)
            kTp = mk.sb("kTp", [128, 2048], BF16)
            vpaug = mk.sb("vpaug", [128, 16 * 130], BF16)
            qTs = mk.sb("qTs", [128, 512], BF16)
            Eb = [mk.sb("Eb%d" % i, [128, 1024], BF16) for i in range(2)]
            print("phase1 sbuf remaining", nc.sbuf_bytes_remaining, flush=True)

            for i in range(2):
                mk.op('dve', lambda e, i=i: e.memset(vaug[i][:], 1.0), writes=[vaug[i]])
            mk.op('dve', lambda e: e.memset(vpaug[:], 1.0), writes=[vpaug])

            B = pbanks

            def load_x(ti):
                mk.dma('sp', xt[ti % 3][:], xin[ti * 128:(ti + 1) * 128, :], writes=[xt[ti % 3]], semt=xt[ti % 3])

            load_x(tiles[0])

            def tr8(e, src, bank):
                last = None
                for c in range(8):
                    last = pe_T(bank, c * 128, src, src[:, c * 128:(c + 1) * 128])(e)
                return last

            def emit_F1(tj):
                x = xt[tj % 3]
                mk.op('act', lambda e: e.activation(out=xn[:], in_=x[:], func=AF.Square, accum_out=st1[:, 0:1]),
                      reads=[x], writes=[xn, st1])
                rsqrt(st1, st1[:, 2:3], st1[:, 0:1], 1.0 / D)
                mk.op('dve', lambda e: e.scalar_tensor_tensor(out=xn[:], in0=x[:], scalar=st1[:, 2:3], in1=g_attn[:],
                                                              op0=ALU.mult, op1=ALU.mult), reads=[x, st1, g_attn], writes=[xn])
                mk.op('pe', lambda e: tr8(e, src=xn, bank=B[0]), reads=[xn, ident], writes=[B[0]], skip_self=True)
                mk.op('act', lambda e: e.copy(out=xnT[:], in_=B[0][:].bitcast(BF16)), reads=[B[0]], writes=[xnT])


            def tile_body(tpos, ti, prevE):
                try:
                    sample = (ti == NPT)
                    x = xt[ti % 3]
                    att = att_[ti % 2]
                    mb = mb_[ti % 2]
                    siga = siga_[ti % 2]
                    sigb = sigb_[ti % 2]
                    if tpos + 1 < len(tiles):
                        load_x(tiles[tpos + 1])
                    cur, prv = ti % 2, (ti + 1) % 2
                    if tpos == 0:
                        emit_F1(ti)
                    chk(1)
                    xnT3 = xnT[:].rearrange("p (c t) -> p c t", c=8)

                    def zgroup(bank, c0, n):
                        def f(e):
                            last = None
                            for c in range(8):
                                last = e.matmul(bank[:, 0:n], lhsT=xnT3[:, c, :], rhs=win_v[:, c, c0:c0 + n],
                                                start=(c == 0), stop=(c == 7))
                            return last
                        mk.op('pe', f, reads=[xnT, wg[[z[0] for z in ZG].index(c0)]], writes=[bank], skip_self=True)

                    zgroup(B[1], 0, 512)
                    mk.op('act', lambda e: e.copy(out=q32[:], in_=B[1][:]), reads=[B[1]], writes=[q32])
                    mk.op('act', lambda e: e.activation(out=qsq[:], in_=B[1][:], func=AF.Square), reads=[B[1]], writes=[qsq])
                    zgroup(B[2], 512, 256)
                    mk.op('act', lambda e: e.copy(out=k32[:], in_=B[2][:, 0:128]), reads=[B[2]], writes=[k32])
                    mk.op('act', lambda e: e.activation(out=ksq[:], in_=B[2][:, 0:128], func=AF.Square), reads=[B[2]], writes=[ksq])
                    mk.op('act', lambda e: e.copy(out=v32[:], in_=B[2][:, 128:256]), reads=[B[2]], writes=[v32])
                    zgroup(B[3], 768, 512)
                    mk.op('act', lambda e: e.activation(out=u32[:], in_=B[3][:], func=GELU), reads=[B[3]], writes=[u32])
                    zgroup(B[1], 1280, 512)
                    mk.op('act', lambda e: e.activation(out=gv32[:], in_=B[1][:], func=GELU, accum_out=st1[:, 4:5]),
                          reads=[B[1]], writes=[gv32, st1])
                    zgroup(B[2], 1792, 512)
                    mk.op('act', lambda e: e.activation(out=siga[:, 0:512], in_=B[2][:], func=AF.Sigmoid), reads=[B[2]], writes=[siga])
                    zgroup(B[3], 2304, 512)
                    mk.op('act', lambda e: e.activation(out=siga[:, 512:1024], in_=B[3][:], func=AF.Sigmoid), reads=[B[3]], writes=[siga])
                    zgroup(B[1], 2816, 512)
                    mk.op('act', lambda e: e.activation(out=sigb[:, 0:512], in_=B[1][:], func=AF.Sigmoid), reads=[B[1]], writes=[sigb])
                    zgroup(B[2], 3328, 512)
                    mk.op('act', lambda e: e.activation(out=sigb[:, 512:1024], in_=B[2][:], func=AF.Sigmoid), reads=[B[2]], writes=[sigb])

                    chk(2)
                    mk.inter = (prevE, 2)
                    mk.op('dve', lambda e: e.tensor_reduce(out=st1[:, 8:16], in_=qsq[:].rearrange("t (h d) -> t h d", h=8),
                                                           axis=AX.X, op=ALU.add), reads=[qsq], writes=[st1])
                    rsqrt(st1, st1[:, 8:16], st1[:, 8:16], 1.0 / 64)
                    mk.op('dve', lambda e: e.tensor_tensor(out=q32[:].rearrange("t (h d) -> t h d", h=8),
                                                           in0=q32[:].rearrange("t (h d) -> t h d", h=8),
                                                           in1=st1[:, 8:16].unsqueeze(2).to_broadcast([128, 8, 64]), op=ALU.mult),
                          reads=[q32, st1], writes=[q32])
                    mk.op('dve', lambda e: e.tensor_tensor(out=qn[:].rearrange("t (p w d) -> t w p d", p=4, w=2),
                                                           in0=q32[:].rearrange("t (w p d) -> t w p d", w=2, p=4),
                                                           in1=gq8[:].unsqueeze(1).unsqueeze(1).to_broadcast([128, 2, 4, 64]),
                                                           op=ALU.mult), reads=[q32, gq8], writes=[qn])
                    mk.op('dve', lambda e: e.tensor_reduce(out=st1[:, 5:7], in_=ksq[:].rearrange("t (h d) -> t h d", h=2),
                                                           axis=AX.X, op=ALU.add), reads=[ksq], writes=[st1])
                    rsqrt(st1, st1[:, 5:7], st1[:, 5:7], 1.0 / 64)
                    mk.op('dve', lambda e: e.tensor_tensor(out=k32[:].rearrange("t (h d) -> t h d", h=2),
                                                           in0=k32[:].rearrange("t (h d) -> t h d", h=2),
                                                           in1=st1[:, 5:7].unsqueeze(2).to_broadcast([128, 2, 64]), op=ALU.mult),
                          reads=[k32, st1], writes=[k32])
                    mk.op('dve', lambda e: e.tensor_tensor(out=kn32[:].rearrange("t (h d) -> t h d", h=2),
                                                           in0=k32[:].rearrange("t (h d) -> t h d", h=2),
                                                           in1=gk[:].unsqueeze(1).to_broadcast([128, 2, 64]), op=ALU.mult),
                          reads=[k32, gk], writes=[kn32])
                    mk.op('act', lambda e: e.copy(out=knb[:], in_=kn32[:]), reads=[kn32], writes=[knb])
                    mk.op('act', lambda e: e.copy(out=vaug[cur][:].rearrange("t (g e) -> t g e", g=2)[:, :, 0:64],
                                                  in_=v32[:].rearrange("t (g d) -> t g d", g=2)), reads=[v32], writes=[vaug[cur]])
                    chk(5)
                    mk.op('act', lambda e: e.activation(out=sc32[:, 0:512], in_=gv32[:], func=AF.Square, accum_out=st1[:, 3:4]),
                          reads=[gv32], writes=[sc32, st1])
                    mk.op('dve', lambda e: e.tensor_scalar(out=st1[:, 4:5], in0=st1[:, 4:5], scalar1=1.0 / 512, scalar2=None,
                                                           op0=ALU.mult), reads=[st1], writes=[st1])
                    mk.op('dve', lambda e: e.tensor_tensor(out=st1[:, 7:8], in0=st1[:, 4:5], in1=st1[:, 4:5], op=ALU.mult),
                          reads=[st1], writes=[st1])
                    mk.op('dve', lambda e: e.scalar_tensor_tensor(out=st1[:, 3:4], in0=st1[:, 3:4], scalar=1.0 / 512, in1=st1[:, 7:8],
                                                                  op0=ALU.mult, op1=ALU.subtract), reads=[st1], writes=[st1])
                    rsqrt(st1, st1[:, 3:4], st1[:, 3:4], 1.0)
                    mk.op('dve', lambda e: e.tensor_scalar(out=vn32[:], in0=gv32[:], scalar1=st1[:, 4:5], scalar2=st1[:, 3:4],
                                                           op0=ALU.subtract, op1=ALU.mult), reads=[gv32, st1], writes=[vn32])
                    mk.op('dve', lambda e: e.tensor_tensor(out=vn32[:], in0=vn32[:], in1=ln_g[:], op=ALU.mult),
                          reads=[vn32, ln_g], writes=[vn32])
                    mk.op('dve', lambda e: e.tensor_tensor(out=vn32[:], in0=vn32[:], in1=ln_b[:], op=ALU.add),
                          reads=[vn32, ln_b], writes=[vn32])
                    mk.op('act', lambda e: e.copy(out=vnb[:], in_=vn32[:]), reads=[vn32], writes=[vnb])
                    wsT = wsT_s if sample else wsT_p
                    bcol = bcol_s if sample else bcol_p

                    def fsgu(e, wsT=wsT):
                        last = None
                        for g in range(4):
                            last = e.matmul(B[3][:, g * 128:(g + 1) * 128], lhsT=wsT[:, g * 128:(g + 1) * 128],
                                            rhs=vnb[:, g * 128:(g + 1) * 128], start=True, stop=True)
                        return last
                    mk.op('pe', fsgu, reads=[wsT, vnb], writes=[B[3]], skip_self=True)
                    for g in range(4):
                        mk.op('dve', lambda e, g=g, bcol=bcol: e.scalar_tensor_tensor(
                            out=mb[:, g * 128:(g + 1) * 128], in0=B[3][:, g * 128:(g + 1) * 128], scalar=bcol[:, g:g + 1],
                            in1=u32[:, g * 128:(g + 1) * 128], op0=ALU.add, op1=ALU.mult), reads=[B[3], bcol, u32], writes=[mb])

                    chk(3)
                    def trq(e):
                        last = None
                        for p in range(4):
                            last = pe_T(B[0], p * 128, qn, qn[:, p * 128:(p + 1) * 128])(e)
                        last = pe_T(B[0], 512, knb, knb[:])(e)
                        return last
                    mk.op('pe', trq, reads=[qn, knb, ident], writes=[B[0]], skip_self=True)
                    b0v = B[0][:].bitcast(BF16)
                    mk.op('act', lambda e: e.copy(out=qT[:], in_=b0v[:, 0:512]), reads=[B[0]], writes=[qT])
                    mk.op('act', lambda e: e.copy(out=kT[cur][:], in_=b0v[:, 512:640]), reads=[B[0]], writes=[kT[cur]])
                    qT3 = qT[:].rearrange("p (q t) -> p q t", q=4)

                    chk(4)
                    pT4 = pT[:].rearrange("s (b h t) -> s b h t", b=2, h=8)
                    blks = [1] if (ti == 0 or sample) else [0, 1]
                    for blk in blks:
                        kt = kT[cur] if blk == 1 else kT[prv]
                        banks = (B[4], B[5])

                        def fsc(e, kt=kt, banks=banks):
                            last = None
                            for h in range(8):
                                w, p = h // 4, h % 4
                                bank = banks[h // 4]
                                last = e.matmul(bank[:, (h % 4) * 128:(h % 4 + 1) * 128], lhsT=kt[w * 64:(w + 1) * 64, :],
                                                rhs=qT3[w * 64:(w + 1) * 64, p, :], start=True, stop=True)
                            return last
                        mk.op('pe', fsc, reads=[kt, qT], writes=list(banks), skip_self=True)
                        bsrc = bias_sn if sample else bias_p
                        boff = 0 if sample else blk * 1024
                        for hb_ in range(2):
                            mk.op('dve', lambda e, hb_=hb_, banks=banks, bsrc=bsrc, boff=boff: e.tensor_tensor(
                                out=sc32[:, hb_ * 512:(hb_ + 1) * 512], in0=banks[hb_][:],
                                in1=bsrc[:, boff + hb_ * 512: boff + (hb_ + 1) * 512], op=ALU.add),
                                reads=[banks[hb_], bsrc], writes=[sc32])
                        mk.op('act', lambda e, blk=blk: e.activation(out=pT[:, blk * 1024:(blk + 1) * 1024], in_=sc32[:], func=AF.Exp),
                              reads=[sc32], writes=[pT])

                    def pv_out(h):
                        return B[1 + h // 4][:, (h % 4) * 128:(h % 4) * 128 + 65]

                    def fpv(e):
                        last = None
                        for h in range(8):
                            w = h // 4
                            for bi, blk in enumerate(blks):
                                va = vaug[cur] if blk == 1 else vaug[prv]
                                st_ = (bi == 0) if not sample else (h % 4 == 0)
                                last = e.matmul(pv_out(h), lhsT=pT4[:, blk, h, :], rhs=va[:, w * 65:(w + 1) * 65],
                                                start=st_, stop=(bi == len(blks) - 1 and not sample), skip_group_check=sample)
                        return last
                    mk.op('pe', fpv, reads=[pT, vaug[0], vaug[1]], writes=[B[1], B[2]], skip_self=True)

                    if sample:
                        chk(4.1)
                        mk.inter = None
                        mk.play(prevE, 10 ** 9)
                        mk.dma('sp', stage[:].rearrange("s (b c) -> s b c", b=16), ck[:, :, :], writes=[stage], semt=stage)
                        mk.op('act', lambda e: e.copy(out=ckb[:], in_=stage[:]), reads=[stage], writes=[ckb])
                        for half in range(2):
                            bank = B[4 + half * 2], B[5 + half * 2]

                            def ftr(e, half=half, bank=bank):
                                last = None
                                for bb in range(8):
                                    b_ = half * 8 + bb
                                    last = pe_T(bank[bb // 4], (bb % 4) * 128, ckb, ckb[:, b_ * 128:(b_ + 1) * 128])(e)
                                return last
                            mk.op('pe', ftr, reads=[ckb, ident], writes=list(bank), skip_self=True)
                            for q_ in range(2):
                                mk.op('act', lambda e, half=half, q_=q_, bank=bank: e.copy(
                                    out=kTp[:, (half * 8 + q_ * 4) * 128:(half * 8 + q_ * 4 + 4) * 128],
                                    in_=bank[q_][:].bitcast(BF16)[:, 0:512]), reads=[bank[q_]], writes=[kTp])
                        chk(4.2)
                        if not cfg.get('skip_vdma'):
                            mk.dma('sp', stage[:].rearrange("s (b c) -> s b c", b=16), cv[:, :, :], writes=[stage], semt=stage)
                        if cfg.get('skip_vcopy'):
                            raise _StopTile()
                        for g_ in range(2):
                            mk.op('dve', lambda e, g_=g_: e.tensor_copy(
                                out=vpaug[:].rearrange("s (b g e) -> s b g e", b=16, g=2)[:, :, g_, 0:64],
                                in_=stage[:].rearrange("s (b g d) -> s b g d", b=16, g=2)[:, :, g_, :]),
                                reads=[stage], writes=[vpaug])
                        chk(4.3)
                        kTp3 = kTp[:].rearrange("p (b s) -> p b s", b=16)
                        mk.op('act', lambda e: e.copy(out=qTs[:].rearrange("p (b q t) -> p b q t", b=16, q=4),
                                                      in_=qT[:].rearrange("p (q b t) -> p b q t", q=4, b=16)),
                              reads=[qT], writes=[qTs])
                        chk(4.31)

                        def fps(e):
                            last = None
                            for w in range(2):
                                bank = B[4 + w]
                                for b_ in range(16):
                                    c0 = b_ * 32
                                    last = e.matmul(bank[:, c0:c0 + 32], lhsT=kTp3[w * 64:(w + 1) * 64, b_, :],
                                                    rhs=qTs[w * 64:(w + 1) * 64, b_ * 32:(b_ + 1) * 32], start=True, stop=True)
                            return last
                        mk.op('pe', fps, reads=[kTp, qTs], writes=[B[4], B[5]], skip_self=True)
                        chk(4.32)
                        for half in range(2):
                            mk.op('dve', lambda e, half=half: e.tensor_tensor(
                                out=sc32[:, half * 512:(half + 1) * 512].rearrange("s (b c) -> s b c", b=16),
                                in0=B[4 + half][:].rearrange("s (b c) -> s b c", b=16),
                                in1=bias_sp[:, half * 32:(half + 1) * 32].unsqueeze(1).to_broadcast([128, 16, 32]), op=ALU.add),
                                reads=[B[4 + half], bias_sp], writes=[sc32])
                        chk(4.4)
                        sc5 = sc32[:].rearrange("s (w b p t) -> s b w p t", w=2, b=16, p=4)
                        vp4 = vpaug[:].rearrange("s (b g e) -> s b g e", b=16, g=2)
                        for b_ in range(16):
                            E = Eb[b_ % 2]
                            E3 = E[:].rearrange("s (h t) -> s h t", h=8)
                            mk.op('pool', lambda e, E=E: e.memset(E[:], 0.0), writes=[E])
                            mk.op('act', lambda e, E=E, b_=b_: e.activation(
                                out=E[:].rearrange("s (w p t) -> s w p t", w=2, p=4)[:, :, :, b_ * 8:(b_ + 1) * 8],
                                in_=sc5[:, b_, :, :, :], func=AF.Exp), reads=[sc32], writes=[E])

                            def fpp(e, E3=E3, b_=b_):
                                last = None
                                for h in range(8):
                                    last = e.matmul(pv_out(h), lhsT=E3[:, h, :], rhs=vp4[:, b_, h // 4, :],
                                                    start=False, stop=(b_ == 15), skip_group_check=True)
                                return last
                            mk.op('pe', fpp, reads=[E, vpaug], writes=[B[1], B[2]], skip_self=True)

                    chk(4.5)
                    for half in range(2):
                        bank = B[1 + half]
                        b3 = bank[:].rearrange("t (h e) -> t h e", h=4)
                        mk.op('dve', lambda e, b3=b3, half=half: e.tensor_tensor(
                            out=st1[:, 8 + half * 4: 12 + half * 4].unsqueeze(2), in0=b3[:, :, 64:65],
                            in1=sinkexp[:, half * 4:(half + 1) * 4].unsqueeze(2), op=ALU.add),
                            reads=[bank, sinkexp], writes=[st1])
                    mk.op('dve', lambda e: e.reciprocal(out=st1[:, 8:16], in_=st1[:, 8:16]), reads=[st1], writes=[st1])
                    for half in range(2):
                        bank = B[1 + half]
                        b3 = bank[:].rearrange("t (h e) -> t h e", h=4)
                        mk.op('dve', lambda e, b3=b3, half=half: e.tensor_tensor(
                            out=att[:, half * 256:(half + 1) * 256].rearrange("t (h d) -> t h d", h=4), in0=b3[:, :, 0:64],
                            in1=st1[:, 8 + half * 4: 12 + half * 4].unsqueeze(2).to_broadcast([128, 4, 64]), op=ALU.mult),
                            reads=[bank, st1], writes=[att])

                    if tpos + 1 < len(tiles):
                        emit_F1(tiles[tpos + 1])
                    chk(6)
                    if ti == NPT - 1 or sample:
                        ok_, ov_, os_ = (o_ks, o_vs, o_ss) if sample else (o_kp, o_vp, o_sp)
                        mk.dma('sp', ok_[:, :], kn32[:], reads=[kn32], semt=kn32)
                        mk.dma('sp', ov_[:, :], v32[:], reads=[v32], semt=v32)
                        mk.dma('sp', os_[:, :], vn32[:], reads=[vn32], semt=vn32)
                        out_tiles.extend([kn32, v32, vn32])

                    chk(7)
                    mk.inter = None
                    mk.play(prevE, 10 ** 9)
                    if dbg_am is not None:
                        mk.op('dve', lambda e: e.tensor_copy(out=t1[:, 0:512], in_=att[:]), reads=[att], writes=[t1])
                        mk.op('dve', lambda e: e.tensor_copy(out=t1[:, 512:1024], in_=mb[:]), reads=[mb], writes=[t1])
                        mk.dma('sp', dbg_am[ti * 128:(ti + 1) * 128, :], t1[:], reads=[t1], semt=t1)
                    mk.rec = []
                    mk.rate = 1.0
                    def tram(e):
                        last = None
                        for c in range(4):
                            last = pe_T(B[0], c * 128, att, att[:, c * 128:(c + 1) * 128])(e)
                        for c in range(4):
                            last = pe_T(B[0], 512 + c * 128, mb, mb[:, c * 128:(c + 1) * 128])(e)
                        return last
                    mk.op('pe', tram, reads=[att, mb, ident], writes=[B[0]], skip_self=True)
                    mk.op('act', lambda e: e.copy(out=amT[:], in_=B[0][:].bitcast(BF16)), reads=[B[0]], writes=[amT])
                    amT3 = amT[:].rearrange("p (c t) -> p c t", c=8)

                    def fmerge_ab(e, w_v, off):
                        last = None
                        for n in range(2):
                            for c in range(4):
                                last = e.matmul(B[6 + n][:], lhsT=amT3[:, off + c, :], rhs=w_v[:, c, n * 512:(n + 1) * 512],
                                                start=(c == 0), stop=(c == 3))
                        return last
                    mk.op('pe', lambda e: fmerge_ab(e, wa_v, 0), reads=[amT, wab], writes=[B[6], B[7]], skip_self=True)
                    for n in range(2):
                        sl = slice(n * 512, (n + 1) * 512)
                        mk.op('dve', lambda e, n=n, sl=sl: e.tensor_tensor(out=t1[:, sl], in0=B[6 + n][:], in1=siga[:, sl], op=ALU.mult),
                              reads=[B[6 + n], siga], writes=[t1])
                    mk.op('pe', lambda e: fmerge_ab(e, wb_v, 4), reads=[amT, wab], writes=[B[6], B[7]], skip_self=True)
                    for n in range(2):
                        sl = slice(n * 512, (n + 1) * 512)
                        mk.op('dve', lambda e, n=n, sl=sl: e.tensor_tensor(out=stage[:, sl], in0=B[6 + n][:], in1=sigb[:, sl], op=ALU.mult),
                              reads=[B[6 + n], sigb], writes=[stage])
                    mk.op('dve', lambda e: e.tensor_tensor(out=hb[:], in0=stage[:, 0:1024], in1=t1[:], op=ALU.add), reads=[stage, t1], writes=[hb])
                    mk.op('pe', lambda e: tr8(e, src=hb, bank=B[0]), reads=[hb, ident], writes=[B[0]], skip_self=True)
                    mk.op('act', lambda e: e.copy(out=hT[:], in_=B[0][:].bitcast(BF16)), reads=[B[0]], writes=[hT])
                    hT3 = hT[:].rearrange("p (c t) -> p c t", c=8)

                    def fwo(e):
                        last = None
                        for n in range(2):
                            for c in range(8):
                                last = e.matmul(B[6 + n][:], lhsT=hT3[:, c, :], rhs=wo_v[:, c, n * 512:(n + 1) * 512],
                                                start=(c == 0), stop=(c == 7))
                        return last
                    mk.op('pe', fwo, reads=[hT, wot], writes=[B[6], B[7]], skip_self=True)
                    x1 = x1t[ti % 2]
                    for n in range(2):
                        sl = slice(n * 512, (n + 1) * 512)
                        mk.op('dve', lambda e, n=n, sl=sl, x1=x1: e.tensor_tensor(out=x1[:, sl], in0=B[6 + n][:], in1=x[:, sl], op=ALU.add),
                              reads=[B[6 + n], x], writes=[x1])
                    mk.dma('sp', x1s[ti * 128:(ti + 1) * 128, :], x1[:], reads=[x1], writes=[x1_tok[ti]], semt=x1)
                    if dbg_x1 is not None:
                        mk.dma('sp', dbg_x1[ti * 128:(ti + 1) * 128, :], x1[:], reads=[x1], semt=x1)
                    pend = mk.rec
                    mk.rec = None
                    return pend
                except _StopTile:
                    mk.rec = None
                    mk.inter = None
                    return []

            prevE = []
            for tpos, ti in enumerate(tiles):
                prevE = tile_body(tpos, ti, prevE)
            mk.play(prevE, 10 ** 9)

            mk.barrier()
        with ExitStack() as es2:
            mk.es = es2
            warena2 = mk.sb("warena2", [128, 8 * 2048 + 2048 + 8 * D + 2 * D], BF16)
            o = 0
            wq_v = warena2[:, o:o + 8 * 2048].rearrange("p (c n) -> p c n", c=8); o += 8 * 2048
            keys_v = warena2[:, o:o + 2048].rearrange("p (g n) -> p g n", g=16); o += 2048
            wpg_v = warena2[:, o:o + 8 * D].rearrange("p (c n) -> p c n", c=8); o += 8 * D
            wpl_v = warena2[:, o:o + 2 * D].rearrange("p (c n) -> p c n", c=2); o += 2 * D
            wsem2 = T("wsem2")
            w_q_r = w_q.rearrange("(c p) n -> p c n", p=128)
            w_pg_r = w_pg.rearrange("(c p) n -> p c n", p=128)
            w_pl_r = w_ple.rearrange("(c p) n -> p c n", p=128)
            for c in range(8):
                mk.dma('pool', wq_v[:, c, :], w_q_r[:, c, :], semt=wsem2)
                mk.dma('pool', wpg_v[:, c, :], w_pg_r[:, c, :], semt=wsem2)
            for c in range(2):
                mk.dma('pool', wpl_v[:, c, :], w_pl_r[:, c, :], semt=wsem2)
            mk.dma('pool', keys_v, keysT.rearrange("p (g n) -> p g n", g=16), semt=wsem2)
            warena2.st.w = (wsem2.st.dsem, mk.cnt[wsem2.st.dsem])
            cgrp2 = T("cgrp2")

            def cload2(name, src, shape, dt=F32, q='sp'):
                t = mk.sb(name, shape, dt)
                mk.dma(q, t[:], src[:, :], writes=[t], semt=cgrp2)
                return t
            g_ffn = cload2("g_ffn", sin["g_ffn"], [128, 1024])
            g_ple = cload2("g_ple", sin["g_ple"], [128, 1024])
            iota16 = cload2("iota16", cin["iota16"], [128, 16])
            identf = cload2("identf", cin["ident"], [128, 128])
            mk.settle([g_ffn, g_ple, iota16, identf])

            xt2 = [mk.sb("x2t%d" % i, [128, D], F32) for i in range(2)]
            pt2 = [mk.sb("p2t%d" % i, [128, 256], F32) for i in range(2)]
            xn2_ = [mk.sb("xn2_%d" % i, [128, D], F32) for i in range(2)]
            xn2b = mk.sb("xn2b", [128, D], BF16)
            xn2T = mk.sb("xn2T", [128, D], BF16)
            st2 = mk.sb("st2", [128, 16], F32)
            st2A = mk.sb("st2A", [128, 16], F32)
            scp = mk.sb("scp", [128, 2048], F32)
            work = mk.sb("work", [128, 2048], F32)
            qpT = T("qpT", work[:].bitcast(BF16)[:, 0:2048], st=work.st)
            stop_ = mk.sb("stop", [128, 256], F32)
            itopu = mk.sb("itopu", [128, 256], U32)
            itopf = mk.sb("itopf", [128, 256], F32)
            cand = T("cand", scp[:], st=scp.st)
            best = mk.sb("best", [128, 128], F32)
            posu = mk.sb("posu", [128, 128], U32)
            posi = mk.sb("posi", [128, 256], I32)
            posf = mk.sb("posf", [128, 256], F32)
            oh = T("oh", work[:], st=work.st)
            oh2 = T("oh2", cand[:], st=cand.st)
            sel = mk.sb("sel", [128, 256], F32)
            junkA = T("junkA", oh[:, 0:1024], st=oh.st)
            idxf = mk.sb("idxf", [128, 128], F32)
            idxi_ = [mk.sb("idxi%d" % i, [128, 128], I32) for i in range(2)]
            gate_ = [mk.sb("gate%d" % i, [128, 128], F32) for i in range(2)]
            dots = mk.sb("dots", [128, 128], F32)
            actv = mk.sb("actv", [128, 128], F32)
            wgt = mk.sb("wgt", [128, 128], F32)
            NGB = 19
            gbuf = [mk.sb("gbuf%d" % i, [128, 2 * D], BF16) for i in range(NGB)]
            junk = mk.sb("junk", [128, D], F32)
            pleg = mk.sb("pleg", [128, D], F32)
            NSM = 19
            dot_r = [mk.sb("dot_r%d" % i, [128, 1], F32) for i in range(NSM)]
            wv_r = [mk.sb("wv_r%d" % i, [128, 2], F32) for i in range(NSM)]
            dg_r = [mk.sb("dg_r%d" % i, [128, 128], BF16) for i in range(8)]
            xn3 = mk.sb("xn3", [128, D], BF16)
            xn3T = mk.sb("xn3T", [128, D], BF16)
            pb16 = mk.sb("pb16", [128, 256], BF16)
            ppT = mk.sb("ppT", [128, 256], BF16)
            print("phase2 sbuf remaining", nc.sbuf_bytes_remaining, flush=True)
            B = pbanks

            def load2(ti):
                mk.dma('sp', xt2[ti % 2][:], x1s[ti * 128:(ti + 1) * 128, :], reads=[x1_tok[ti]], writes=[xt2[ti % 2]],
                       semt=xt2[ti % 2])
                mk.dma('sp', pt2[ti % 2][:], pin[ti * 128:(ti + 1) * 128, :], writes=[pt2[ti % 2]], semt=pt2[ti % 2])

            def rms(xsrc, gt, out32, outb, st2, junk):
                mk.op('act', lambda e: e.activation(out=junk[:], in_=xsrc[:], func=AF.Square, accum_out=st2[:, 0:1]),
                      reads=[xsrc], writes=[junk, st2])
                rsqrt(st2, st2[:, 2:3], st2[:, 0:1], 1.0 / D)
                if out32 is not None:
                    mk.op('dve', lambda e: e.scalar_tensor_tensor(out=out32[:], in0=xsrc[:], scalar=st2[:, 2:3], in1=gt[:],
                                                                  op0=ALU.mult, op1=ALU.mult), reads=[xsrc, st2, gt], writes=[out32])
                    mk.op('act', lambda e: e.copy(out=outb[:], in_=out32[:]), reads=[out32], writes=[outb])
                else:
                    mk.op('dve', lambda e: e.scalar_tensor_tensor(out=outb[:], in0=xsrc[:], scalar=st2[:, 2:3], in1=gt[:],
                                                                  op0=ALU.mult, op1=ALU.mult), reads=[xsrc, st2, gt], writes=[outb])

            def tr8b(src, dst, n=8):
                def f(e):
                    last = None
                    for c in range(n):
                        last = pe_T(B[0], c * 128, src, src[:, c * 128:(c + 1) * 128])(e)
                    return last
                mk.op('pe', f, reads=[src, ident], writes=[B[0]], skip_self=True)
                mk.op('act', lambda e: e.copy(out=dst[:, 0:n * 128], in_=B[0][:].bitcast(BF16)[:, 0:n * 128]),
                      reads=[B[0]], writes=[dst])

            tiles2 = tiles if cfg.get('phase2', True) else []

            def stageA(ti):
                par = ti % 2
                x = xt2[par]
                xn2 = xn2_[par]
                idxi = idxi_[par]
                gate = gate_[par]
                mk.rate = 2.0
                load2(ti)
                rms(x, g_ffn, xn2, xn2b, st2A, junkA)
                tr8b(xn2b, xn2T)
                xT3 = xn2T[:].rearrange("p (c t) -> p c t", c=8)
                for bq in range(4):
                    bank = B[1 + bq]

                    def fq(e, bq=bq, bank=bank):
                        last = None
                        for gi in range(4):
                            g_ = bq * 4 + gi
                            for c in range(8):
                                last = e.matmul(bank[:, gi * 128:(gi + 1) * 128], lhsT=wq_v[:, c, g_ * 128:(g_ + 1) * 128],
                                                rhs=xT3[:, c, :], start=(c == 0), stop=(c == 7))
                        return last
                    mk.op('pe', fq, reads=[xn2T, warena2], writes=[bank], skip_self=True)
                    mk.op('act', lambda e, bq=bq, bank=bank: e.copy(out=qpT[:, bq * 512:(bq + 1) * 512], in_=bank[:]),
                          reads=[bank], writes=[qpT])
                for bq in range(4):
                    bank = B[1 + bq]

                    def fs(e, bq=bq, bank=bank):
                        last = None
                        for gi in range(4):
                            g_ = bq * 4 + gi
                            last = e.matmul(bank[:, gi * 128:(gi + 1) * 128], lhsT=qpT[:, g_ * 128:(g_ + 1) * 128],
                                            rhs=keys_v[:, g_, :], start=True, stop=True)
                        return last
                    mk.op('pe', fs, reads=[qpT, warena2], writes=[bank], skip_self=True)
                    mk.op('act', lambda e, bq=bq, bank=bank: e.copy(out=scp[:, bq * 512:(bq + 1) * 512], in_=bank[:]),
                          reads=[bank], writes=[scp])
                mk.rate = 4.0
                def sg_(g_):
                    return scp[:, g_ * 128:(g_ + 1) * 128]

                def wk_(g_):
                    return work[:, g_ * 128:(g_ + 1) * 128]
                for g_ in range(16):
                    mk.op('dve', lambda e, g_=g_: e.max(out=stop_[:, g_ * 16:g_ * 16 + 8], in_=sg_(g_)),
                          reads=[scp], writes=[stop_], skip_self=True)
                for g_ in range(16):
                    mk.op('dve', lambda e, g_=g_: e.match_replace(out=wk_(g_), in_to_replace=stop_[:, g_ * 16:g_ * 16 + 8],
                                                                  in_values=sg_(g_), imm_value=-1e30),
                          reads=[scp, stop_], writes=[work], skip_self=True)
                for g_ in range(16):
                    mk.op('dve', lambda e, g_=g_: e.max(out=stop_[:, g_ * 16 + 8:g_ * 16 + 16], in_=wk_(g_)),
                          reads=[work], writes=[stop_], skip_self=True)
                for g_ in range(16):
                    mk.op('dve', lambda e, g_=g_: e.max_index(out=itopu[:, g_ * 16:g_ * 16 + 8], in_max=stop_[:, g_ * 16:g_ * 16 + 8],
                                                              in_values=sg_(g_)), reads=[scp, stop_], writes=[itopu], skip_self=True)
                for g_ in range(16):
                    mk.op('dve', lambda e, g_=g_: e.max_index(out=itopu[:, g_ * 16 + 8:g_ * 16 + 16],
                                                              in_max=stop_[:, g_ * 16 + 8:g_ * 16 + 16], in_values=sg_(g_)),
                          reads=[scp, stop_], writes=[itopu], skip_self=True)
                mk.op('dve', lambda e: e.tensor_copy(out=itopf[:], in_=itopu[:]), reads=[itopu], writes=[itopf])
                st4 = stop_[:].rearrange("t (h c k) -> t h c k", h=8, c=2)
                mk.op('dve', lambda e: e.tensor_tensor(out=cand[:].rearrange("t (h i j) -> t h i j", h=8, i=16),
                                                       in0=st4[:, :, 0, :].unsqueeze(3).to_broadcast([128, 8, 16, 16]),
                                                       in1=st4[:, :, 1, :].unsqueeze(2).to_broadcast([128, 8, 16, 16]), op=ALU.add),
                      reads=[stop_], writes=[cand])
                def cg_(h):
                    return cand[:, h * 256:(h + 1) * 256]

                def wk2_(h):
                    return work[:, h * 256:(h + 1) * 256]
                for h in range(8):
                    mk.op('dve', lambda e, h=h: e.max(out=best[:, h * 16:h * 16 + 8], in_=cg_(h)),
                          reads=[cand], writes=[best], skip_self=(h > 0))
                for h in range(8):
                    mk.op('dve', lambda e, h=h: e.match_replace(out=wk2_(h), in_to_replace=best[:, h * 16:h * 16 + 8], in_values=cg_(h),
                                                                imm_value=-1e30), reads=[cand, best], writes=[work], skip_self=True)
                for h in range(8):
                    mk.op('dve', lambda e, h=h: e.max(out=best[:, h * 16 + 8:h * 16 + 16], in_=wk2_(h)),
                          reads=[work], writes=[best], skip_self=True)
                for h in range(8):
                    mk.op('dve', lambda e, h=h: e.max_index(out=posu[:, h * 16:h * 16 + 8], in_max=best[:, h * 16:h * 16 + 8],
                                                            in_values=cg_(h)), reads=[cand, best], writes=[posu], skip_self=True)
                for h in range(8):
                    mk.op('dve', lambda e, h=h: e.max_index(out=posu[:, h * 16 + 8:h * 16 + 16], in_max=best[:, h * 16 + 8:h * 16 + 16],
                                                            in_values=cg_(h)), reads=[cand, best], writes=[posu], skip_self=True)
                mk.op('dve', lambda e: e.tensor_single_scalar(out=posi[:, 0:128], in_=posu[:].bitcast(I32), scalar=4,
                                                              op=ALU.logical_shift_right), reads=[posu], writes=[posi])
                mk.op('dve', lambda e: e.tensor_single_scalar(out=posi[:, 128:256], in_=posu[:].bitcast(I32), scalar=15,
                                                              op=ALU.bitwise_and), reads=[posu], writes=[posi])
                mk.op('dve', lambda e: e.tensor_copy(out=posf[:], in_=posi[:]), reads=[posi], writes=[posf])
                it4 = itopf[:].rearrange("t (h c k) -> t h c k", h=8, c=2)
                ohs = (oh, oh2)
                oh4s = [o_[:].rearrange("t (h k i) -> t h k i", h=8, k=16) for o_ in ohs]
                for c in range(2):
                    pf = posf[:, c * 128:(c + 1) * 128].rearrange("t (h k) -> t h k", h=8)
                    mk.op('dve', lambda e, pf=pf, c=c: e.tensor_tensor(
                        out=oh4s[c], in0=iota16[:].unsqueeze(1).unsqueeze(1).to_broadcast([128, 8, 16, 16]),
                        in1=pf.unsqueeze(3).to_broadcast([128, 8, 16, 16]), op=ALU.is_equal), reads=[iota16, posf], writes=[ohs[c]])
                for c in range(2):
                    mk.op('dve', lambda e, c=c: e.tensor_tensor(
                        out=oh4s[c], in0=oh4s[c], in1=it4[:, :, c, :].unsqueeze(2).to_broadcast([128, 8, 16, 16]), op=ALU.mult),
                        reads=[ohs[c], itopf], writes=[ohs[c]], skip_self=True)
                for c in range(2):
                    mk.op('dve', lambda e, c=c: e.tensor_reduce(out=sel[:, c * 128:(c + 1) * 128],
                                                                in_=ohs[c][:].rearrange("t (s i) -> t s i", i=16), axis=AX.X, op=ALU.add),
                          reads=[ohs[c]], writes=[sel], skip_self=True)
                mk.op('dve', lambda e: e.scalar_tensor_tensor(out=idxf[:], in0=sel[:, 0:128], scalar=128.0, in1=sel[:, 128:256],
                                                              op0=ALU.mult, op1=ALU.add), reads=[sel], writes=[idxf])
                mk.op('dve', lambda e: e.tensor_copy(out=idxi[:], in_=idxf[:]), reads=[idxf], writes=[idxi])
                b3 = best[:].rearrange("t (h k) -> t h k", h=8)
                mk.op('dve', lambda e: e.tensor_tensor(out=gate[:].rearrange("t (h k) -> t h k", h=8), in0=b3,
                                                       in1=b3[:, :, 0:1].to_broadcast([128, 8, 16]), op=ALU.subtract),
                      reads=[best], writes=[gate])
                mk.op('act', lambda e: e.activation(out=gate[:], in_=gate[:], func=AF.Exp), reads=[gate], writes=[gate])
                mk.op('dve', lambda e: e.tensor_reduce(out=st2A[:, 8:16], in_=gate[:].rearrange("t (h k) -> t h k", h=8),
                                                       axis=AX.X, op=ALU.add), reads=[gate], writes=[st2A])
                mk.op('dve', lambda e: e.reciprocal(out=st2A[:, 8:16], in_=st2A[:, 8:16]), reads=[st2A], writes=[st2A])
                mk.op('dve', lambda e: e.tensor_tensor(out=gate[:].rearrange("t (h k) -> t h k", h=8),
                                                       in0=gate[:].rearrange("t (h k) -> t h k", h=8),
                                                       in1=st2A[:, 8:16].unsqueeze(2).to_broadcast([128, 8, 16]), op=ALU.mult),
                      reads=[gate, st2A], writes=[gate])

            def stageB(ti, nxt_q):
                par = ti % 2
                x = xt2[par]
                pt = pt2[par]
                xn2 = xn2_[par]
                idxi = idxi_[par]
                gate = gate_[par]
                mk._waits('pool', [comb_t], [])
                for s_ in range(128):
                    gb = gbuf[s_ % NGB]
                    dt_ = dot_r[s_ % NSM]
                    wv = wv_r[s_ % NSM]
                    dg = dg_r[s_ % 8]
                    mk.gather(gb, gb[:], comb[:, :], idxi, idxi[:, s_:s_ + 1])
                    mk.op('dve', lambda e, gb=gb, dt_=dt_: e.scalar_tensor_tensor(
                        out=junk[:], in0=gb[:, 0:D], scalar=1.0, in1=xn2[:], op0=ALU.mult, op1=ALU.mult,
                        accum_out=dt_[:, 0:1]), reads=[gb, xn2], writes=[junk, dt_], skip_self=True)
                    mk.op('act', lambda e, dt_=dt_, wv=wv: e.activation(out=wv[:, 0:1], in_=dt_[:, 0:1], func=GELU),
                          reads=[dt_], writes=[wv])
                    mk.op('act', lambda e, wv=wv, s_=s_: e.mul(out=wv[:, 1:2], in_=wv[:, 0:1], mul=gate[:, s_:s_ + 1]),
                          reads=[wv, gate], writes=[wv])
                    mk.op('act', lambda e, wv=wv, dg=dg: e.activation(out=dg[:], in_=identf[:], func=AF.Copy, scale=wv[:, 1:2]),
                          reads=[wv, identf], writes=[dg])

                    def fv(e, s_=s_, gb=gb, dg=dg):
                        last = None
                        for n in range(2):
                            last = e.matmul(B[5 + n][:], lhsT=dg[:], rhs=gb[:, D + n * 512:D + (n + 1) * 512],
                                            start=(s_ == 0), stop=(s_ == 127))
                        return last
                    mk.op('pe', fv, reads=[gb, dg], writes=[B[5], B[6]], skip_self=True)
                    mk.play(nxt_q, 1.0, lag=10)
                if len(nxt_q) and ti == 1:
                    print('replay leftover at drain:', len(nxt_q), flush=True)
                mk.play(nxt_q, 10 ** 9)
                for n in range(2):
                    sl = slice(n * 512, (n + 1) * 512)
                    mk.op('dve', lambda e, n=n, sl=sl: e.tensor_tensor(out=x[:, sl], in0=B[5 + n][:], in1=x[:, sl], op=ALU.add),
                          reads=[B[5 + n], x], writes=[x])
                mk.rec = []
                mk.rate = 2.0
                rms(x, g_ple, None, xn3, st2, pleg)
                tr8b(xn3, xn3T)
                x3T = xn3T[:].rearrange("p (c t) -> p c t", c=8)

                def fpg(e):
                    last = None
                    for n in range(2):
                        for c in range(8):
                            last = e.matmul(B[1 + n][:], lhsT=x3T[:, c, :], rhs=wpg_v[:, c, n * 512:(n + 1) * 512],
                                            start=(c == 0), stop=(c == 7))
                    return last
                mk.op('pe', fpg, reads=[xn3T, warena2], writes=[B[1], B[2]], skip_self=True)
                for n in range(2):
                    mk.op('act', lambda e, n=n: e.activation(out=pleg[:, n * 512:(n + 1) * 512], in_=B[1 + n][:], func=AF.Sigmoid),
                          reads=[B[1 + n]], writes=[pleg])
                mk.op('act', lambda e: e.copy(out=pb16[:], in_=pt[:]), reads=[pt], writes=[pb16])
                tr8b(pb16, ppT, n=2)
                pT3 = ppT[:].rearrange("p (c t) -> p c t", c=2)

                def fpl(e):
                    last = None
                    for n in range(2):
                        for c in range(2):
                            last = e.matmul(B[3 + n][:], lhsT=pT3[:, c, :], rhs=wpl_v[:, c, n * 512:(n + 1) * 512],
                                            start=(c == 0), stop=(c == 1))
                    return last
                mk.op('pe', fpl, reads=[ppT, warena2], writes=[B[3], B[4]], skip_self=True)
                for n in range(2):
                    sl = slice(n * 512, (n + 1) * 512)
                    mk.op('dve', lambda e, n=n, sl=sl: e.tensor_tensor(out=pleg[:, sl], in0=B[3 + n][:], in1=pleg[:, sl], op=ALU.mult),
                          reads=[B[3 + n], pleg], writes=[pleg])
                mk.op('dve', lambda e: e.tensor_tensor(out=x[:], in0=x[:], in1=pleg[:], op=ALU.add), reads=[x, pleg], writes=[x])
                mk.dma('sp', y[ti * 128:(ti + 1) * 128, :], x[:], reads=[x], semt=x)
                tail = mk.rec
                mk.rec = None
                return tail

            if tiles2:
                mk.rec = []
                stageA(tiles2[0])
                q0 = mk.rec
                mk.rec = None
                mk.play(q0, 10 ** 9)
            pending_tail = []
            for tpos, ti in enumerate(tiles2):
                nxt_q = pending_tail
                if tpos + 1 < len(tiles2):
                    mk.rec = []
                    stageA(tiles2[tpos + 1])
                    nxt_q = nxt_q + mk.rec
                    mk.rec = None
                if tpos == 1:
                    print('replay queue: ops', len(nxt_q), 'slot budget needed', sum(1.0 / r[3] for r in nxt_q), flush=True)
                pending_tail = stageB(ti, nxt_q)
            mk.play(pending_tail, 10 ** 9)
            out_tiles_all = out_tiles + xt2
            mk._waits('sp', [], xt2)
            mk.barrier()
        print("instructions emitted:", mk.n_ins, flush=True)
    return nc


_NC_CACHE = {}


def _get_nc():
    if "nc" not in _NC_CACHE:
        _NC_CACHE["nc"] = build()
    return _NC_CACHE["nc"]


def kernel(x_prompt, x_sample, cache_k, cache_v, p_prompt, p_sample,
           attn_norm_g, w_in, q_norm_g, k_norm_g, attn_sinks,
           sgu_norm_g, sgu_norm_b, sgu_w, sgu_b,
           w_branch_a, w_branch_b, w_out,
           ffn_norm_g, peer_w_q, peer_sub_keys, peer_u, peer_v,
           ple_norm_g, w_ple, w_ple_gate):
    f = lambda a: np.ascontiguousarray(np.asarray(a, dtype=np.float32))
    x_prompt, x_sample, cache_k, cache_v = f(x_prompt), f(x_sample), f(cache_k), f(cache_v)
    p_prompt, p_sample = f(p_prompt), f(p_sample)
    nc = _get_nc()
    consts = _consts()
    rep = lambda v, n=128: np.ascontiguousarray(np.broadcast_to(f(v).reshape(1, -1), (n, f(v).size)))
    sw = f(sgu_w)[0]
    sb = f(sgu_b)[0]
    small = {
        "g_attn": rep(attn_norm_g[0]), "g_ffn": rep(ffn_norm_g[0]), "g_ple": rep(ple_norm_g[0]),
        "gq": rep(q_norm_g[0]), "gk": rep(k_norm_g[0]), "sinks": rep(attn_sinks[0]),
        "ln_g": rep(sgu_norm_g[0]), "ln_b": rep(sgu_norm_b[0]),
        "bcol_p": np.ascontiguousarray(sb.T),
        "bcol_s": np.ascontiguousarray(np.tile(sb[:, :8].T, (16, 1))),
        "wsT_p": np.ascontiguousarray(sw.transpose(2, 0, 1)).reshape(128, 512),
        "wsT_s": np.ascontiguousarray(np.tile(sw[:, :8, :8].transpose(2, 0, 1), (16, 1, 16))).reshape(128, 512),
    }
    shared = {
        "w_in": f(w_in)[0], "w_a": f(w_branch_a)[0], "w_b": f(w_branch_b)[0], "w_out": f(w_out)[0],
        "w_q": f(peer_w_q)[0],
        "keysT": np.ascontiguousarray(f(peer_sub_keys)[0].reshape(16, 128, 128).transpose(2, 0, 1)).reshape(128, 2048),
        "peer_u": f(peer_u)[0], "peer_v": f(peer_v)[0], "w_ple": f(w_ple)[0], "w_pg": f(w_ple_gate)[0],
    }
    for k, v in consts.items():
        shared["c_" + k] = v
    for k, v in small.items():
        shared["s_" + k] = v
    in_maps = []
    for c in range(NCORES):
        m = dict(shared)
        xs = x_sample[c * 16:(c + 1) * 16].reshape(128, D)
        m["xin"] = np.ascontiguousarray(np.concatenate([x_prompt[c], xs], axis=0))
        ps = p_sample[0, c * 16:(c + 1) * 16].reshape(128, 256)
        m["pin"] = np.ascontiguousarray(np.concatenate([p_prompt[0, c], ps], axis=0))
        m["ck"] = np.ascontiguousarray(cache_k[0, c * 16:(c + 1) * 16].reshape(16, 128, 128).transpose(1, 0, 2))
        m["cv"] = np.ascontiguousarray(cache_v[0, c * 16:(c + 1) * 16].reshape(16, 128, 128).transpose(1, 0, 2))
        in_maps.append(m)
    res = run_bass_kernel_spmd(nc, in_maps, core_ids=list(range(NCORES)))
    R = res.results
    y_p = np.stack([R[c]["y"][:SEQ] for c in range(NCORES)])
    y_s = np.concatenate([R[c]["y"][SEQ:].reshape(16, 8, D) for c in range(NCORES)], axis=0)
    nkp = np.stack([R[c]["o_kp"].reshape(128, 2, 64) for c in range(NCORES)])[None]
    nvp = np.stack([R[c]["o_vp"].reshape(128, 2, 64) for c in range(NCORES)])[None]
    nks = np.concatenate([R[c]["o_ks"].reshape(16, 8, 2, 64) for c in range(NCORES)], axis=0)[None]
    nvs = np.concatenate([R[c]["o_vs"].reshape(16, 8, 2, 64) for c in range(NCORES)], axis=0)[None]
    nsp = np.stack([R[c]["o_sp"] for c in range(NCORES)])[None]
    nss = np.concatenate([R[c]["o_ss"].reshape(16, 8, 512) for c in range(NCORES)], axis=0)[None]
    return (y_p.astype(np.float32), y_s.astype(np.float32), nkp.astype(np.float32), nvp.astype(np.float32),
            nks.astype(np.float32), nvs.astype(np.float32), nsp.astype(np.float32), nss.astype(np.float32))
```

```python
import numpy as np
from contextlib import ExitStack
import concourse.bass as bass
import concourse.mybir as mybir
from concourse.bass_utils import run_bass_kernel_spmd

F32 = mybir.dt.float32
BF16 = mybir.dt.bfloat16
I32 = mybir.dt.int32
U32 = mybir.dt.uint32
ALU = mybir.AluOpType
AF = mybir.ActivationFunctionType
AX = mybir.AxisListType

NCORES = 8
D = 1024
SEQ = 4096
NPT = SEQ // 128
NT = NPT + 1
NTOK = NT * 128
IN_W = 3840
EPS = 1e-6
NEG = -30000.0
GELU = AF.Gelu_apprx_tanh


class St:
    __slots__ = ("w", "r", "dsem")

    def __init__(self):
        self.w = None
        self.r = []
        self.dsem = None


class T:
    def __init__(self, name, h=None, st=None):
        self.name = name
        self.h = h
        self.st = st if st is not None else St()

    def __getitem__(self, k):
        return self.h[k]


class MK:
    def __init__(self, nc, es):
        self.nc = nc
        self.es_sem = es
        self.es = es
        self.engs = {'pe': nc.tensor, 'act': nc.scalar, 'dve': nc.vector, 'pool': nc.gpsimd, 'sp': nc.sync}
        self.sems = {}
        self.cnt = {}
        self.waited = {k: {} for k in self.engs}
        for k in ('pe', 'act', 'dve', 'pool'):
            self.sems[k] = es.enter_context(nc.semaphore("c_" + k))
            self.cnt[k] = 0
        self.nd = 0
        self.n_ins = 0
        self.rec = None
        self.rate = 4.0
        self.slot = 0
        self.inter = None
        self._in_inter = False
        self._inter_cnt = 0
        self._iprev_eng = 'dve'
        self._iprev_cnt = -100
        self.prev_eng = 'dve'
        self.prev_slot = -100

    def sb(self, name, shape, dtype):
        h = self.es.enter_context(self.nc.sbuf_tensor(name, list(shape), dtype))
        return T(name, h)

    def ps(self, name, shape, dtype=F32):
        h = self.es.enter_context(self.nc.psum_tensor(name, list(shape), dtype))
        return T(name, h)

    def _dsem(self, t):
        st = t.st
        if st.dsem is None:
            key = "d%d" % self.nd
            self.nd += 1
            self.sems[key] = self.es_sem.enter_context(self.nc.semaphore(key))
            self.cnt[key] = 0
            st.dsem = key
        return st.dsem

    def _waits(self, eng, reads, writes, skip_self=False):
        need = {}

        def add(tok):
            if tok is None:
                return
            k, v = tok
            if need.get(k, 0) < v:
                need[k] = v
        for t in reads:
            add(t.st.w)
        for t in writes:
            add(t.st.w)
            for tok in t.st.r:
                add(tok)
        e = self.engs[eng]
        for k, v in need.items():
            if skip_self and k == eng:
                continue
            if self.waited[eng].get(k, 0) >= v:
                continue
            e.wait_ge(self.sems[k], v)
            self.n_ins += 1
            self.waited[eng][k] = v

    def _commit(self, tok, reads, writes):
        for t in reads:
            r = t.st.r
            r.append(tok)
            if len(r) > 48:
                best = {}
                for k, v in r:
                    if best.get(k, 0) < v:
                        best[k] = v
                t.st.r = list(best.items())
        for t in writes:
            t.st.w = tok
            t.st.r = []

    def op(self, eng, fn, reads=(), writes=(), skip_self=False):
        if self.rec is not None:
            self.rec.append(('op', (eng, fn), dict(reads=list(reads), writes=list(writes), skip_self=skip_self), self.rate))
            return None
        self._waits(eng, reads, writes, skip_self)
        ins = fn(self.engs[eng])
        self.cnt[eng] += 1
        ins.then_inc(self.sems[eng], 1)
        self.n_ins += 1
        self._commit((eng, self.cnt[eng]), reads, writes)
        if self.inter is not None and not self._in_inter:
            q, every = self.inter
            self._inter_cnt += 1
            if q and self._inter_cnt % every == 0:
                eng_n = q[0][1][0]
                hold = (eng_n == 'dve' and self._iprev_eng != 'dve' and self._inter_cnt - self._iprev_cnt < 6)
                if not hold:
                    self._in_inter = True
                    kind, a, kw, _rate = q.pop(0)
                    getattr(self, kind)(*a, **kw)
                    self._in_inter = False
                    self._iprev_eng = eng_n
                    self._iprev_cnt = self._inter_cnt
        return ins

    def dma(self, q, out, in_, reads=(), writes=(), semt=None, **kw):
        if self.rec is not None:
            self.rec.append(('dma', (q, out, in_), dict(reads=list(reads), writes=list(writes), semt=semt, **kw), self.rate))
            return None
        self._waits(q, reads, writes)
        key = self._dsem(semt)
        ins = self.engs[q].dma_start(out=out, in_=in_, **kw)
        self.cnt[key] += 16
        ins.then_inc(self.sems[key], 16)
        self.n_ins += 1
        self._commit((key, self.cnt[key]), reads, writes)
        return ins

    def gather(self, out_t, out_ap, table_ap, idx_t, idx_ap):
        self._waits('pool', [idx_t], [out_t])
        key = self._dsem(out_t)
        ins = self.nc.gpsimd.indirect_dma_start(
            out=out_ap, out_offset=None, in_=table_ap,
            in_offset=bass.IndirectOffsetOnAxis(ap=idx_ap, axis=0))
        self.cnt[key] += 16
        ins.then_inc(self.sems[key], 16)
        self.n_ins += 1
        self._commit((key, self.cnt[key]), [idx_t], [out_t])
        return ins

    def settle(self, tiles):
        for t in tiles:
            if t.st.w is not None and t.st.w[0] in self.cnt and t.st.w[0].startswith("d"):
                t.st.w = (t.st.w[0], self.cnt[t.st.w[0]])

    def play(self, q, budget, lag=0):
        self.slot += 1
        while q and budget > 1e-9:
            kind, a, kw, rate = q[0]
            if budget + 1e-9 < 1.0 / rate and budget < 0.999:
                break
            eng = a[0]
            if lag and eng == 'dve' and self.prev_eng != 'dve' and self.slot - self.prev_slot < lag:
                break
            q.pop(0)
            getattr(self, kind)(*a, **kw)
            budget -= 1.0 / rate
            self.prev_eng = eng
            self.prev_slot = self.slot

    def barrier(self):
        for eng in self.engs:
            e = self.engs[eng]
            for k, v in self.cnt.items():
                if v == 0 or self.waited[eng].get(k, 0) >= v:
                    continue
                e.wait_ge(self.sems[k], v)
                self.n_ins += 1
                self.waited[eng][k] = v


def _consts():
    slopes = np.exp2(-np.arange(1, 9, dtype=np.float64)).astype(np.float32)
    s = np.arange(128)[:, None]
    t = np.arange(128)[None, :]
    c = {}
    c["ident"] = np.eye(128, dtype=np.float32)
    bp = np.full((128, 2, 8, 128), NEG, np.float32)
    d0 = (128 + t - s).astype(np.float32)
    d1 = (t - s).astype(np.float32)
    for h in range(8):
        bp[:, 0, h, :] = np.where(s > t, -slopes[h] * d0, NEG)
        bp[:, 1, h, :] = np.where(s <= t, -slopes[h] * d1, NEG)
    c["bias_p"] = bp.reshape(128, 2048)
    bs_, ts_ = s // 8, s % 8
    bq_, tq_ = t // 8, t % 8
    bsn = np.full((128, 8, 128), NEG, np.float32)
    ok = (bs_ == bq_) & (ts_ <= tq_)
    for h in range(8):
        bsn[:, h, :] = np.where(ok, -slopes[h] * (tq_ - ts_).astype(np.float32), NEG)
    c["bias_sn"] = bsn.reshape(128, 1024)
    r = np.arange(128)[:, None]
    tq = np.arange(8)[None, :]
    bsp = np.full((128, 8, 8), NEG, np.float32)
    for h in range(8):
        bsp[:, h, :] = np.where(r > tq, -slopes[h] * (tq + 128 - r).astype(np.float32), NEG)
    c["bias_sp"] = bsp.reshape(128, 64)
    c["maskT_p"] = (s <= t).astype(np.float32)
    c["maskT_s"] = ok.astype(np.float32)
    c["iota16"] = np.broadcast_to(np.arange(16, dtype=np.float32)[None, :], (128, 16)).copy()
    return c


CONST_SHAPES = {"ident": (128, 128), "bias_p": (128, 2048), "bias_sn": (128, 1024), "bias_sp": (128, 64),
                "maskT_p": (128, 128), "maskT_s": (128, 128), "iota16": (128, 16)}
SMALL_SHAPES = {"g_attn": (128, 1024), "g_ffn": (128, 1024), "g_ple": (128, 1024), "gq": (128, 64), "gk": (128, 64),
                "sinks": (128, 8), "ln_g": (128, 512), "ln_b": (128, 512),
                "bcol_p": (128, 4), "bcol_s": (128, 4), "wsT_p": (128, 512), "wsT_s": (128, 512)}


CFG = {}


class _StopTile(Exception):
    pass


def chk(n):
    if CFG.get('stage', 99) < n:
        raise _StopTile()


def build():
    nc = bass.Bass("TRN2", target_bir_lowering=False)
    cfg = CFG
    tiles = cfg.get('tiles', list(range(NT)))

    def din(name, shape, dt=F32):
        return nc.dram_tensor(name, list(shape), dt, kind="ExternalInput").ap()

    def dout(name, shape, dt=F32):
        return nc.dram_tensor(name, list(shape), dt, kind="ExternalOutput").ap()

    xin = din("xin", [NTOK, D])
    pin = din("pin", [NTOK, 256])
    ck = din("ck", [128, 16, 128])
    cv = din("cv", [128, 16, 128])
    w_in = din("w_in", [D, IN_W])
    w_a = din("w_a", [512, D])
    w_b = din("w_b", [512, D])
    w_out = din("w_out", [D, D])
    w_q = din("w_q", [D, 2048])
    keysT = din("keysT", [128, 2048])
    peer_u = din("peer_u", [16384, D])
    peer_v = din("peer_v", [16384, D])
    w_ple = din("w_ple", [256, D])
    w_pg = din("w_pg", [D, D])
    cin = {k: din("c_" + k, v) for k, v in CONST_SHAPES.items()}
    sin = {k: din("s_" + k, v) for k, v in SMALL_SHAPES.items()}

    y = dout("y", [NTOK, D])
    o_kp = dout("o_kp", [128, 128])
    o_vp = dout("o_vp", [128, 128])
    o_ks = dout("o_ks", [128, 128])
    o_vs = dout("o_vs", [128, 128])
    o_sp = dout("o_sp", [128, 512])
    o_ss = dout("o_ss", [128, 512])
    x1s = nc.dram_tensor("x1s", [NTOK, D], F32, kind="Internal").ap()
    comb = nc.dram_tensor("comb", [16384, 2 * D], BF16, kind="Internal").ap()
    dbg_x1 = dout("dbg_x1", [NTOK, D]) if cfg.get('dbg') else None
    dbg_am = dout("dbg_am", [NTOK, D]) if cfg.get('dbg') else None

    with ExitStack() as es0:
        mk = MK(nc, es0)
        x1_tok = [T("x1s%d" % i) for i in range(NT)]
        out_tiles = []

        ident = mk.sb("ident", [128, 128], BF16)
        cgrp = T("cgrp")
        mk.dma('pool', ident[:], cin["ident"][:, :], writes=[ident], semt=ident)
        pbanks = [mk.ps("pb%d" % i, [128, 512], F32) for i in range(8)]

        def pe_T(bank, col0, src_t, src_ap, n=128, extra_reads=()):
            pv = bank[:].bitcast(BF16)
            return lambda e: e.transpose(out=pv[0:n, col0:col0 + 128], in_=src_ap, identity=ident[:])

        epst = mk.sb("epst", [128, 1], F32)
        mk.op('pool', lambda e: e.memset(epst[:], EPS), writes=[epst])

        def rsqrt(tile, dst, src, scale):
            mk.op('act', lambda e: e.activation(out=dst, in_=src, func=AF.Sqrt, bias=epst[:, 0:1], scale=scale),
                  reads=[tile, epst], writes=[tile])
            mk.op('dve', lambda e: e.reciprocal(out=dst, in_=dst), reads=[tile], writes=[tile])

        with ExitStack() as es1:
            mk.es = es1
            warena = mk.sb("warena1", [128, 8 * IN_W + 4 * D + 4 * D + 8 * D], BF16)
            o = 0
            win_v = warena[:, o:o + 8 * IN_W].rearrange("p (c n) -> p c n", c=8); o += 8 * IN_W
            wa_v = warena[:, o:o + 4 * D].rearrange("p (c n) -> p c n", c=4); o += 4 * D
            wb_v = warena[:, o:o + 4 * D].rearrange("p (c n) -> p c n", c=4); o += 4 * D
            wo_v = warena[:, o:o + 8 * D].rearrange("p (c n) -> p c n", c=8); o += 8 * D
            ZG = [(0, 512), (512, 256), (768, 512), (1280, 512), (1792, 512), (2304, 512), (2816, 512), (3328, 512)]
            wg = [T("wg%d" % i) for i in range(8)]
            w_in_r = w_in.rearrange("(c p) n -> p c n", p=128)
            for gi, (c0, n) in enumerate(ZG):
                for c in range(8):
                    mk.dma('pool', win_v[:, c, c0:c0 + n], w_in_r[:, c, c0:c0 + n], semt=wg[gi])
                wg[gi].st.w = (wg[gi].st.dsem, mk.cnt[wg[gi].st.dsem])
            w_a_r = w_a.rearrange("(c p) n -> p c n", p=128)
            w_b_r = w_b.rearrange("(c p) n -> p c n", p=128)
            w_o_r = w_out.rearrange("(c p) n -> p c n", p=128)
            wab = T("wab")
            wot = T("wot")
            for c in range(4):
                mk.dma('pool', wa_v[:, c, :], w_a_r[:, c, :], semt=wab)
                mk.dma('pool', wb_v[:, c, :], w_b_r[:, c, :], semt=wab)
            wab.st.w = (wab.st.dsem, mk.cnt[wab.st.dsem])
            for c in range(8):
                mk.dma('pool', wo_v[:, c, :], w_o_r[:, c, :], semt=wot)
            wot.st.w = (wot.st.dsem, mk.cnt[wot.st.dsem])
            comb_t = T("comb")
            for c in range(16):
                rs = slice(c * 1024, (c + 1) * 1024)
                mk.dma('pool', comb[rs, 0:D], peer_u[rs, :], semt=comb_t)
                mk.dma('pool', comb[rs, D:2 * D], peer_v[rs, :], semt=comb_t)
            comb_t.st.w = (comb_t.st.dsem, mk.cnt[comb_t.st.dsem])

            def cload(name, src, shape, dt=F32, q='sp'):
                t = mk.sb(name, shape, dt)
                mk.dma(q, t[:], src[:, :], writes=[t], semt=cgrp)
                return t
            g_attn = cload("g_attn", sin["g_attn"], [128, 1024])
            gq = cload("gq", sin["gq"], [128, 64])
            gk = cload("gk", sin["gk"], [128, 64])
            sinks = cload("sinks", sin["sinks"], [128, 8])
            ln_g = cload("ln_g", sin["ln_g"], [128, 512])
            ln_b = cload("ln_b", sin["ln_b"], [128, 512])
            bcol_p = cload("bcol_p", sin["bcol_p"], [128, 4])
            bcol_s = cload("bcol_s", sin["bcol_s"], [128, 4])
            bias_p = cload("bias_p", cin["bias_p"], [128, 2048])
            bias_sn = cload("bias_sn", cin["bias_sn"], [128, 1024])
            bias_sp = cload("bias_sp", cin["bias_sp"], [128, 64])
            maskT_p = cload("maskT_p", cin["maskT_p"], [128, 128])
            maskT_s = cload("maskT_s", cin["maskT_s"], [128, 128])
            t1 = mk.sb("t1", [128, D], F32)
            ws32_p = T("ws32_p", t1[:, 0:512], st=t1.st)
            ws32_s = T("ws32_s", t1[:, 512:1024], st=t1.st)
            mk.dma('sp', ws32_p[:], sin["wsT_p"][:, :], writes=[ws32_p], semt=cgrp)
            mk.dma('sp', ws32_s[:], sin["wsT_s"][:, :], writes=[ws32_s], semt=cgrp)
            consts1 = [ident, g_attn, gq, gk, sinks, ln_g, ln_b, bcol_p, bcol_s, bias_p, bias_sn, bias_sp,
                       maskT_p, maskT_s, ws32_p, ws32_s]
            mk.settle(consts1)

            sinkexp = mk.sb("sinkexp", [128, 8], F32)
            mk.op('act', lambda e: e.activation(out=sinkexp[:], in_=sinks[:], func=AF.Exp), reads=[sinks], writes=[sinkexp])
            gq8 = mk.sb("gq8", [128, 64], F32)
            mk.op('dve', lambda e: e.tensor_scalar(out=gq8[:], in0=gq[:], scalar1=0.125, scalar2=None, op0=ALU.mult),
                  reads=[gq], writes=[gq8])
            wsT_p = mk.sb("wsT_p", [128, 512], BF16)
            wsT_s = mk.sb("wsT_s", [128, 512], BF16)
            mk.op('dve', lambda e: e.tensor_tensor(out=wsT_p[:].rearrange("s (g t) -> s g t", g=4),
                                                   in0=ws32_p[:].rearrange("s (g t) -> s g t", g=4),
                                                   in1=maskT_p[:].unsqueeze(1).to_broadcast([128, 4, 128]), op=ALU.mult),
                  reads=[ws32_p, maskT_p], writes=[wsT_p])
            mk.op('dve', lambda e: e.tensor_tensor(out=wsT_s[:].rearrange("s (g t) -> s g t", g=4),
                                                   in0=ws32_s[:].rearrange("s (g t) -> s g t", g=4),
                                                   in1=maskT_s[:].unsqueeze(1).to_broadcast([128, 4, 128]), op=ALU.mult),
                  reads=[ws32_s, maskT_s], writes=[wsT_s])

            xt = [mk.sb("xt%d" % i, [128, D], F32) for i in range(3)]
            x1t = [mk.sb("x1t%d" % i, [128, D], F32) for i in range(2)]
            xn = mk.sb("xn", [128, D], BF16)
            xnT = mk.sb("xnT", [128, D], BF16)
            st1 = mk.sb("st1", [128, 16], F32)
            q32 = mk.sb("q32", [128, 512], F32)
            qn = mk.sb("qn", [128, 512], BF16)
            qT = mk.sb("qT", [128, 512], BF16)
            k32 = mk.sb("k32", [128, 128], F32)
            kn32 = mk.sb("kn32", [128, 128], F32)
            knb = mk.sb("knb", [128, 128], BF16)
            kT = [mk.sb("kT%d" % i, [128, 128], BF16) for i in range(2)]
            v32 = mk.sb("v32", [128, 128], F32)
            vaug = [mk.sb("vaug%d" % i, [128, 2 * 65], BF16) for i in range(2)]
            sc32 = mk.sb("sc32", [128, 1024], F32)
            pT = mk.sb("pT", [128, 2048], BF16)
            att_ = [mk.sb("att%d" % i, [128, 512], BF16) for i in range(2)]
            u32 = mk.sb("u32", [128, 512], F32)
            gv32 = mk.sb("gv32", [128, 512], F32)
            vn32 = mk.sb("vn32", [128, 512], F32)
            vnb = mk.sb("vnb", [128, 512], BF16)
            mb_ = [mk.sb("mb%d" % i, [128, 512], BF16) for i in range(2)]
            amT = mk.sb("amT", [128, 1024], BF16)
            siga_ = [mk.sb("siga%d" % i, [128, D], BF16) for i in range(2)]
            sigb_ = [mk.sb("sigb%d" % i, [128, D], BF16) for i in range(2)]
            hb = mk.sb("hb", [128, D], BF16)
            hT = mk.sb("hT", [128, D], BF16)
            stage = mk.sb("stage", [128, 2048], F32)
            qsq = T("qsq", stage[:, 1024:1536], st=stage.st)
            ksq = T("ksq", stage[:, 1536:1664], st=stage.st)
            ckb = T("ckb", pT[:], st=pT.st)
            kTp = mk.sb("kTp", [128, 2048], BF16)
            vpaug = mk.sb("vpaug", [128, 16 * 130], BF16)
            qTs = mk.sb("qTs", [128, 512], BF16)
            Eb = [mk.sb("Eb%d" % i, [128, 1024], BF16) for i in range(2)]
            print("phase1 sbuf remaining", nc.sbuf_bytes_remaining, flush=True)

            for i in range(2):
                mk.op('dve', lambda e, i=i: e.memset(vaug[i][:], 1.0), writes=[vaug[i]])
            mk.op('dve', lambda e: e.memset(vpaug[:], 1.0), writes=[vpaug])

            B = pbanks

            def load_x(ti):
                mk.dma('sp', xt[ti % 3][:], xin[ti * 128:(ti + 1) * 128, :], writes=[xt[ti % 3]], semt=xt[ti % 3])

            load_x(tiles[0])

            def tr8(e, src, bank):
                last = None
                for c in range(8):
                    last = pe_T(bank, c * 128, src, src[:, c * 128:(c + 1) * 128])(e)
                return last

            def emit_F1(tj):
                x = xt[tj % 3]
                mk.op('act', lambda e: e.activation(out=xn[:], in_=x[:], func=AF.Square, accum_out=st1[:, 0:1]),
                      reads=[x], writes=[xn, st1])
                rsqrt(st1, st1[:, 2:3], st1[:, 0:1], 1.0 / D)
                mk.op('dve', lambda e: e.scalar_tensor_tensor(out=xn[:], in0=x[:], scalar=st1[:, 2:3], in1=g_attn[:],
                                                              op0=ALU.mult, op1=ALU.mult), reads=[x, st1, g_attn], writes=[xn])
                mk.op('pe', lambda e: tr8(e, src=xn, bank=B[0]), reads=[xn, ident], writes=[B[0]], skip_self=True)
                mk.op('act', lambda e: e.copy(out=xnT[:], in_=B[0][:].bitcast(BF16)), reads=[B[0]], writes=[xnT])


            def tile_body(tpos, ti, prevE):
                try:
                    sample = (ti == NPT)
                    x = xt[ti % 3]
                    att = att_[ti % 2]
                    mb = mb_[ti % 2]
                    siga = siga_[ti % 2]
                    sigb = sigb_[ti % 2]
                    if tpos + 1 < len(tiles):
                        load_x(tiles[tpos + 1])
                    cur, prv = ti % 2, (ti + 1) % 2
                    if tpos == 0:
                        emit_F1(ti)
                    chk(1)
                    xnT3 = xnT[:].rearrange("p (c t) -> p c t", c=8)

                    def zgroup(bank, c0, n):
                        def f(e):
                            last = None
                            for c in range(8):
                                last = e.matmul(bank[:, 0:n], lhsT=xnT3[:, c, :], rhs=win_v[:, c, c0:c0 + n],
                                                start=(c == 0), stop=(c == 7))
                            return last
                        mk.op('pe', f, reads=[xnT, wg[[z[0] for z in ZG].index(c0)]], writes=[bank], skip_self=True)

                    zgroup(B[1], 0, 512)
                    mk.op('act', lambda e: e.copy(out=q32[:], in_=B[1][:]), reads=[B[1]], writes=[q32])
                    mk.op('act', lambda e: e.activation(out=qsq[:], in_=B[1][:], func=AF.Square), reads=[B[1]], writes=[qsq])
                    zgroup(B[2], 512, 256)
                    mk.op('act', lambda e: e.copy(out=k32[:], in_=B[2][:, 0:128]), reads=[B[2]], writes=[k32])
                    mk.op('act', lambda e: e.activation(out=ksq[:], in_=B[2][:, 0:128], func=AF.Square), reads=[B[2]], writes=[ksq])
                    mk.op('act', lambda e: e.copy(out=v32[:], in_=B[2][:, 128:256]), reads=[B[2]], writes=[v32])
                    zgroup(B[3], 768, 512)
                    mk.op('act', lambda e: e.activation(out=u32[:], in_=B[3][:], func=GELU), reads=[B[3]], writes=[u32])
                    zgroup(B[1], 1280, 512)
                    mk.op('act', lambda e: e.activation(out=gv32[:], in_=B[1][:], func=GELU, accum_out=st1[:, 4:5]),
                          reads=[B[1]], writes=[gv32, st1])
                    zgroup(B[2], 1792, 512)
                    mk.op('act', lambda e: e.activation(out=siga[:, 0:512], in_=B[2][:], func=AF.Sigmoid), reads=[B[2]], writes=[siga])
                    zgroup(B[3], 2304, 512)
                    mk.op('act', lambda e: e.activation(out=siga[:, 512:1024], in_=B[3][:], func=AF.Sigmoid), reads=[B[3]], writes=[siga])
                    zgroup(B[1], 2816, 512)
                    mk.op('act', lambda e: e.activation(out=sigb[:, 0:512], in_=B[1][:], func=AF.Sigmoid), reads=[B[1]], writes=[sigb])
                    zgroup(B[2], 3328, 512)
                    mk.op('act', lambda e: e.activation(out=sigb[:, 512:1024], in_=B[2][:], func=AF.Sigmoid), reads=[B[2]], writes=[sigb])

                    chk(2)
                    mk.inter = (prevE, 2)
                    mk.op('dve', lambda e: e.tensor_reduce(out=st1[:, 8:16], in_=qsq[:].rearrange("t (h d) -> t h d", h=8),
                                                           axis=AX.X, op=ALU.add), reads=[qsq], writes=[st1])
                    rsqrt(st1, st1[:, 8:16], st1[:, 8:16], 1.0 / 64)
                    mk.op('dve', lambda e: e.tensor_tensor(out=q32[:].rearrange("t (h d) -> t h d", h=8),
                                                           in0=q32[:].rearrange("t (h d) -> t h d", h=8),
                                                           in1=st1[:, 8:16].unsqueeze(2).to_broadcast([128, 8, 64]), op=ALU.mult),
                          reads=[q32, st1], writes=[q32])
                    mk.op('dve', lambda e: e.tensor_tensor(out=qn[:].rearrange("t (p w d) -> t w p d", p=4, w=2),
                                                           in0=q32[:].rearrange("t (w p d) -> t w p d", w=2, p=4),
                                                           in1=gq8[:].unsqueeze(1).unsqueeze(1).to_broadcast([128, 2, 4, 64]),
                                                           op=ALU.mult), reads=[q32, gq8], writes=[qn])
                    mk.op('dve', lambda e: e.tensor_reduce(out=st1[:, 5:7], in_=ksq[:].rearrange("t (h d) -> t h d", h=2),
                                                           axis=AX.X, op=ALU.add), reads=[ksq], writes=[st1])
                    rsqrt(st1, st1[:, 5:7], st1[:, 5:7], 1.0 / 64)
                    mk.op('dve', lambda e: e.tensor_tensor(out=k32[:].rearrange("t (h d) -> t h d", h=2),
                                                           in0=k32[:].rearrange("t (h d) -> t h d", h=2),
                                                           in1=st1[:, 5:7].unsqueeze(2).to_broadcast([128, 2, 64]), op=ALU.mult),
                          reads=[k32, st1], writes=[k32])
                    mk.op('dve', lambda e: e.tensor_tensor(out=kn32[:].rearrange("t (h d) -> t h d", h=2),
                                                           in0=k32[:].rearrange("t (h d) -> t h d", h=2),
                                                           in1=gk[:].unsqueeze(1).to_broadcast([128, 2, 64]), op=ALU.mult),
                          reads=[k32, gk], writes=[kn32])
                    mk.op('act', lambda e: e.copy(out=knb[:], in_=kn32[:]), reads=[kn32], writes=[knb])
                    mk.op('act', lambda e: e.copy(out=vaug[cur][:].rearrange("t (g e) -> t g e", g=2)[:, :, 0:64],
                                                  in_=v32[:].rearrange("t (g d) -> t g d", g=2)), reads=[v32], writes=[vaug[cur]])
                    chk(5)
                    mk.op('act', lambda e: e.activation(out=sc32[:, 0:512], in_=gv32[:], func=AF.Square, accum_out=st1[:, 3:4]),
                          reads=[gv32], writes=[sc32, st1])
                    mk.op('dve', lambda e: e.tensor_scalar(out=st1[:, 4:5], in0=st1[:, 4:5], scalar1=1.0 / 512, scalar2=None,
                                                           op0=ALU.mult), reads=[st1], writes=[st1])
                    mk.op('dve', lambda e: e.tensor_tensor(out=st1[:, 7:8], in0=st1[:, 4:5], in1=st1[:, 4:5], op=ALU.mult),
                          reads=[st1], writes=[st1])
                    mk.op('dve', lambda e: e.scalar_tensor_tensor(out=st1[:, 3:4], in0=st1[:, 3:4], scalar=1.0 / 512, in1=st1[:, 7:8],
                                                                  op0=ALU.mult, op1=ALU.subtract), reads=[st1], writes=[st1])
                    rsqrt(st1, st1[:, 3:4], st1[:, 3:4], 1.0)
                    mk.op('dve', lambda e: e.tensor_scalar(out=vn32[:], in0=gv32[:], scalar1=st1[:, 4:5], scalar2=st1[:, 3:4],
                                                           op0=ALU.subtract, op1=ALU.mult), reads=[gv32, st1], writes=[vn32])
                    mk.op('dve', lambda e: e.tensor_tensor(out=vn32[:], in0=vn32[:], in1=ln_g[:], op=ALU.mult),
                          reads=[vn32, ln_g], writes=[vn32])
                    mk.op('dve', lambda e: e.tensor_tensor(out=vn32[:], in0=vn32[:], in1=ln_b[:], op=ALU.add),
                          reads=[vn32, ln_b], writes=[vn32])
                    mk.op('act', lambda e: e.copy(out=vnb[:], in_=vn32[:]), reads=[vn32], writes=[vnb])
                    wsT = wsT_s if sample else wsT_p
                    bcol = bcol_s if sample else bcol_p

                    def fsgu(e, wsT=wsT):
                        last = None
                        for g in range(4):
                            last = e.matmul(B[3][:, g * 128:(g + 1) * 128], lhsT=wsT[:, g * 128:(g + 1) * 128],
                                            rhs=vnb[:, g * 128:(g + 1) * 128], start=True, stop=True)
                        return last
                    mk.op('pe', fsgu, reads=[wsT, vnb], writes=[B[3]], skip_self=True)
                    for g in range(4):
                        mk.op('dve', lambda e, g=g, bcol=bcol: e.scalar_tensor_tensor(
                            out=mb[:, g * 128:(g + 1) * 128], in0=B[3][:, g * 128:(g + 1) * 128], scalar=bcol[:, g:g + 1],
                            in1=u32[:, g * 128:(g + 1) * 128], op0=ALU.add, op1=ALU.mult), reads=[B[3], bcol, u32], writes=[mb])

                    chk(3)
                    def trq(e):
                        last = None
                        for p in range(4):
                            last = pe_T(B[0], p * 128, qn, qn[:, p * 128:(p + 1) * 128])(e)
                        last = pe_T(B[0], 512, knb, knb[:])(e)
                        return last
                    mk.op('pe', trq, reads=[qn, knb, ident], writes=[B[0]], skip_self=True)
                    b0v = B[0][:].bitcast(BF16)
                    mk.op('act', lambda e: e.copy(out=qT[:], in_=b0v[:, 0:512]), reads=[B[0]], writes=[qT])
                    mk.op('act', lambda e: e.copy(out=kT[cur][:], in_=b0v[:, 512:640]), reads=[B[0]], writes=[kT[cur]])
                    qT3 = qT[:].rearrange("p (q t) -> p q t", q=4)

                    chk(4)
                    pT4 = pT[:].rearrange("s (b h t) -> s b h t", b=2, h=8)
                    blks = [1] if (ti == 0 or sample) else [0, 1]
                    for blk in blks:
                        kt = kT[cur] if blk == 1 else kT[prv]
                        banks = (B[4], B[5])

                        def fsc(e, kt=kt, banks=banks):
                            last = None
                            for h in range(8):
                                w, p = h // 4, h % 4
                                bank = banks[h // 4]
                                last = e.matmul(bank[:, (h % 4) * 128:(h % 4 + 1) * 128], lhsT=kt[w * 64:(w + 1) * 64, :],
                                                rhs=qT3[w * 64:(w + 1) * 64, p, :], start=True, stop=True)
                            return last
                        mk.op('pe', fsc, reads=[kt, qT], writes=list(banks), skip_self=True)
                        bsrc = bias_sn if sample else bias_p
                        boff = 0 if sample else blk * 1024
                        for hb_ in range(2):
                            mk.op('dve', lambda e, hb_=hb_, banks=banks, bsrc=bsrc, boff=boff: e.tensor_tensor(
                                out=sc32[:, hb_ * 512:(hb_ + 1) * 512], in0=banks[hb_][:],
                                in1=bsrc[:, boff + hb_ * 512: boff + (hb_ + 1) * 512], op=ALU.add),
                                reads=[banks[hb_], bsrc], writes=[sc32])
                        mk.op('act', lambda e, blk=blk: e.activation(out=pT[:, blk * 1024:(blk + 1) * 1024], in_=sc32[:], func=AF.Exp),
                              reads=[sc32], writes=[pT])

                    def pv_out(h):
                        return B[1 + h // 4][:, (h % 4) * 128:(h % 4) * 128 + 65]

                    def fpv(e):
                        last = None
                        for h in range(8):
                            w = h // 4
                            for bi, blk in enumerate(blks):
                                va = vaug[cur] if blk == 1 else vaug[prv]
                                st_ = (bi == 0) if not sample else (h % 4 == 0)
                                last = e.matmul(pv_out(h), lhsT=pT4[:, blk, h, :], rhs=va[:, w * 65:(w + 1) * 65],
                                                start=st_, stop=(bi == len(blks) - 1 and not sample), skip_group_check=sample)
                        return last
                    mk.op('pe', fpv, reads=[pT, vaug[0], vaug[1]], writes=[B[1], B[2]], skip_self=True)

                    if sample:
                        chk(4.1)
                        mk.inter = None
                        mk.play(prevE, 10 ** 9)
                        mk.dma('sp', stage[:].rearrange("s (b c) -> s b c", b=16), ck[:, :, :], writes=[stage], semt=stage)
                        mk.op('act', lambda e: e.copy(out=ckb[:], in_=stage[:]), reads=[stage], writes=[ckb])
                        for half in range(2):
                            bank = B[4 + half * 2], B[5 + half * 2]

                            def ftr(e, half=half, bank=bank):
                                last = None
                                for bb in range(8):
                                    b_ = half * 8 + bb
                                    last = pe_T(bank[bb // 4], (bb % 4) * 128, ckb, ckb[:, b_ * 128:(b_ + 1) * 128])(e)
                                return last
                            mk.op('pe', ftr, reads=[ckb, ident], writes=list(bank), skip_self=True)
                            for q_ in range(2):
                                mk.op('act', lambda e, half=half, q_=q_, bank=bank: e.copy(
                                    out=kTp[:, (half * 8 + q_ * 4) * 128:(half * 8 + q_ * 4 + 4) * 128],
                                    in_=bank[q_][:].bitcast(BF16)[:, 0:512]), reads=[bank[q_]], writes=[kTp])
                        chk(4.2)
                        if not cfg.get('skip_vdma'):
                            mk.dma('sp', stage[:].rearrange("s (b c) -> s b c", b=16), cv[:, :, :], writes=[stage], semt=stage)
                        if cfg.get('skip_vcopy'):
                            raise _StopTile()
                        for g_ in range(2):
                            mk.op('dve', lambda e, g_=g_: e.tensor_copy(
                                out=vpaug[:].rearrange("s (b g e) -> s b g e", b=16, g=2)[:, :, g_, 0:64],
                                in_=stage[:].rearrange("s (b g d) -> s b g d", b=16, g=2)[:, :, g_, :]),
                                reads=[stage], writes=[vpaug])
                        chk(4.3)
                        kTp3 = kTp[:].rearrange("p (b s) -> p b s", b=16)
                        mk.op('act', lambda e: e.copy(out=qTs[:].rearrange("p (b q t) -> p b q t", b=16, q=4),
                                                      in_=qT[:].rearrange("p (q b t) -> p b q t", q=4, b=16)),
                              reads=[qT], writes=[qTs])
                        chk(4.31)

                        def fps(e):
                            last = None
                            for w in range(2):
                                bank = B[4 + w]
                                for b_ in range(16):
                                    c0 = b_ * 32
                                    last = e.matmul(bank[:, c0:c0 + 32], lhsT=kTp3[w * 64:(w + 1) * 64, b_, :],
                                                    rhs=qTs[w * 64:(w + 1) * 64, b_ * 32:(b_ + 1) * 32], start=True, stop=True)
                            return last
                        mk.op('pe', fps, reads=[kTp, qTs], writes=[B[4], B[5]], skip_self=True)
                        chk(4.32)
                        for half in range(2):
                            mk.op('dve', lambda e, half=half: e.tensor_tensor(
                                out=sc32[:, half * 512:(half + 1) * 512].rearrange("s (b c) -> s b c", b=16),
                                in0=B[4 + half][:].rearrange("s (b c) -> s b c", b=16),
                                in1=bias_sp[:, half * 32:(half + 1) * 32].unsqueeze(1).to_broadcast([128, 16, 32]), op=ALU.add),
                                reads=[B[4 + half], bias_sp], writes=[sc32])
                        chk(4.4)
                        sc5 = sc32[:].rearrange("s (w b p t) -> s b w p t", w=2, b=16, p=4)
                        vp4 = vpaug[:].rearrange("s (b g e) -> s b g e", b=16, g=2)
                        for b_ in range(16):
                            E = Eb[b_ % 2]
                            E3 = E[:].rearrange("s (h t) -> s h t", h=8)
                            mk.op('pool', lambda e, E=E: e.memset(E[:], 0.0), writes=[E])
                            mk.op('act', lambda e, E=E, b_=b_: e.activation(
                                out=E[:].rearrange("s (w p t) -> s w p t", w=2, p=4)[:, :, :, b_ * 8:(b_ + 1) * 8],
                                in_=sc5[:, b_, :, :, :], func=AF.Exp), reads=[sc32], writes=[E])

                            def fpp(e, E3=E3, b_=b_):
                                last = None
                                for h in range(8):
                                    last = e.matmul(pv_out(h), lhsT=E3[:, h, :], rhs=vp4[:, b_, h // 4, :],
                                                    start=False, stop=(b_ == 15), skip_group_check=True)
                                return last
                            mk.op('pe', fpp, reads=[E, vpaug], writes=[B[1], B[2]], skip_self=True)

                    chk(4.5)
                    for half in range(2):
                        bank = B[1 + half]
                        b3 = bank[:].rearrange("t (h e) -> t h e", h=4)
                        mk.op('dve', lambda e, b3=b3, half=half: e.tensor_tensor(
                            out=st1[:, 8 + half * 4: 12 + half * 4].unsqueeze(2), in0=b3[:, :, 64:65],
                            in1=sinkexp[:, half * 4:(half + 1) * 4].unsqueeze(2), op=ALU.add),
                            reads=[bank, sinkexp], writes=[st1])
                    mk.op('dve', lambda e: e.reciprocal(out=st1[:, 8:16], in_=st1[:, 8:16]), reads=[st1], writes=[st1])
                    for half in range(2):
                        bank = B[1 + half]
                        b3 = bank[:].rearrange("t (h e) -> t h e", h=4)
                        mk.op('dve', lambda e, b3=b3, half=half: e.tensor_tensor(
                            out=att[:, half * 256:(half + 1) * 256].rearrange("t (h d) -> t h d", h=4), in0=b3[:, :, 0:64],
                            in1=st1[:, 8 + half * 4: 12 + half * 4].unsqueeze(2).to_broadcast([128, 4, 64]), op=ALU.mult),
                            reads=[bank, st1], writes=[att])

                    if tpos + 1 < len(tiles):
                        emit_F1(tiles[tpos + 1])
                    chk(6)
                    if ti == NPT - 1 or sample:
                        ok_, ov_, os_ = (o_ks, o_vs, o_ss) if sample else (o_kp, o_vp, o_sp)
                        mk.dma('sp', ok_[:, :], kn32[:], reads=[kn32], semt=kn32)
                        mk.dma('sp', ov_[:, :], v32[:], reads=[v32], semt=v32)
                        mk.dma('sp', os_[:, :], vn32[:], reads=[vn32], semt=vn32)
                        out_tiles.extend([kn32, v32, vn32])

                    chk(7)
                    mk.inter = None
                    mk.play(prevE, 10 ** 9)
                    if dbg_am is not None:
                        mk.op('dve', lambda e: e.tensor_copy(out=t1[:, 0:512], in_=att[:]), reads=[att], writes=[t1])
                        mk.op('dve', lambda e: e.tensor_copy(out=t1[:, 512:1024], in_=mb[:]), reads=[mb], writes=[t1])
                        mk.dma('sp', dbg_am[ti * 128:(ti + 1) * 128, :], t1[:], reads=[t1], semt=t1)
                    mk.rec = []
                    mk.rate = 1.0
                    def tram(e):
                        last = None
                        for c in range(4):
                            last = pe_T(B[7], c * 128, att, att[:, c * 128:(c + 1) * 128])(e)
                        for c in range(4):
                            last = pe_T(B[7], 512 + c * 128, mb, mb[:, c * 128:(c + 1) * 128])(e)
                        return last
                    mk.op('pe', tram, reads=[att, mb, ident], writes=[B[7]], skip_self=True)
                    mk.op('act', lambda e: e.copy(out=amT[:], in_=B[7][:].bitcast(BF16)), reads=[B[7]], writes=[amT])
                    amT3 = amT[:].rearrange("p (c t) -> p c t", c=8)

                    def fmerge_ab(e, w_v, off):
                        last = None
                        for n in range(2):
                            for c in range(4):
                                last = e.matmul(B[6 + n][:], lhsT=amT3[:, off + c, :], rhs=w_v[:, c, n * 512:(n + 1) * 512],
                                                start=(c == 0), stop=(c == 3))
                        return last
                    mk.op('pe', lambda e: fmerge_ab(e, wa_v, 0), reads=[amT, wab], writes=[B[6], B[7]], skip_self=True)
                    for n in range(2):
                        sl = slice(n * 512, (n + 1) * 512)
                        mk.op('dve', lambda e, n=n, sl=sl: e.tensor_tensor(out=t1[:, sl], in0=B[6 + n][:], in1=siga[:, sl], op=ALU.mult),
                              reads=[B[6 + n], siga], writes=[t1])
                    mk.op('pe', lambda e: fmerge_ab(e, wb_v, 4), reads=[amT, wab], writes=[B[6], B[7]], skip_self=True)
                    for n in range(2):
                        sl = slice(n * 512, (n + 1) * 512)
                        mk.op('dve', lambda e, n=n, sl=sl: e.tensor_tensor(out=stage[:, sl], in0=B[6 + n][:], in1=sigb[:, sl], op=ALU.mult),
                              reads=[B[6 + n], sigb], writes=[stage])
                    mk.op('dve', lambda e: e.tensor_tensor(out=hb[:], in0=stage[:, 0:1024], in1=t1[:], op=ALU.add), reads=[stage, t1], writes=[hb])
                    mk.op('pe', lambda e: tr8(e, src=hb, bank=B[7]), reads=[hb, ident], writes=[B[7]], skip_self=True)
                    mk.op('act', lambda e: e.copy(out=hT[:], in_=B[7][:].bitcast(BF16)), reads=[B[7]], writes=[hT])
                    hT3 = hT[:].rearrange("p (c t) -> p c t", c=8)

                    def fwo(e):
                        last = None
                        for n in range(2):
                            for c in range(8):
                                last = e.matmul(B[6 + n][:], lhsT=hT3[:, c, :], rhs=wo_v[:, c, n * 512:(n + 1) * 512],
                                                start=(c == 0), stop=(c == 7))
                        return last
                    mk.op('pe', fwo, reads=[hT, wot], writes=[B[6], B[7]], skip_self=True)
                    x1 = x1t[ti % 2]
                    for n in range(2):
                        sl = slice(n * 512, (n + 1) * 512)
                        mk.op('dve', lambda e, n=n, sl=sl, x1=x1: e.tensor_tensor(out=x1[:, sl], in0=B[6 + n][:], in1=x[:, sl], op=ALU.add),
                              reads=[B[6 + n], x], writes=[x1])
                    mk.dma('sp', x1s[ti * 128:(ti + 1) * 128, :], x1[:], reads=[x1], writes=[x1_tok[ti]], semt=x1)
                    if dbg_x1 is not None:
                        mk.dma('sp', dbg_x1[ti * 128:(ti + 1) * 128, :], x1[:], reads=[x1], semt=x1)
                    pend = mk.rec
                    mk.rec = None
                    return pend
                except _StopTile:
                    mk.rec = None
                    mk.inter = None
                    return []

            prevE = []
            for tpos, ti in enumerate(tiles):
                prevE = tile_body(tpos, ti, prevE)
            mk.play(prevE, 10 ** 9)

            mk.barrier()
        with ExitStack() as es2:
            mk.es = es2
            warena2 = mk.sb("warena2", [128, 8 * 2048 + 2048 + 8 * D + 2 * D], BF16)
            o = 0
            wq_v = warena2[:, o:o + 8 * 2048].rearrange("p (c n) -> p c n", c=8); o += 8 * 2048
            keys_v = warena2[:, o:o + 2048].rearrange("p (g n) -> p g n", g=16); o += 2048
            wpg_v = warena2[:, o:o + 8 * D].rearrange("p (c n) -> p c n", c=8); o += 8 * D
            wpl_v = warena2[:, o:o + 2 * D].rearrange("p (c n) -> p c n", c=2); o += 2 * D
            wsem2 = T("wsem2")
            w_q_r = w_q.rearrange("(c p) n -> p c n", p=128)
            w_pg_r = w_pg.rearrange("(c p) n -> p c n", p=128)
            w_pl_r = w_ple.rearrange("(c p) n -> p c n", p=128)
            for c in range(8):
                mk.dma('pool', wq_v[:, c, :], w_q_r[:, c, :], semt=wsem2)
                mk.dma('pool', wpg_v[:, c, :], w_pg_r[:, c, :], semt=wsem2)
            for c in range(2):
                mk.dma('pool', wpl_v[:, c, :], w_pl_r[:, c, :], semt=wsem2)
            mk.dma('pool', keys_v, keysT.rearrange("p (g n) -> p g n", g=16), semt=wsem2)
            warena2.st.w = (wsem2.st.dsem, mk.cnt[wsem2.st.dsem])
            cgrp2 = T("cgrp2")

            def cload2(name, src, shape, dt=F32, q='sp'):
                t = mk.sb(name, shape, dt)
                mk.dma(q, t[:], src[:, :], writes=[t], semt=cgrp2)
                return t
            g_ffn = cload2("g_ffn", sin["g_ffn"], [128, 1024])
            g_ple = cload2("g_ple", sin["g_ple"], [128, 1024])
            iota16 = cload2("iota16", cin["iota16"], [128, 16])
            identf = cload2("identf", cin["ident"], [128, 128])
            mk.settle([g_ffn, g_ple, iota16, identf])

            xt2 = [mk.sb("x2t%d" % i, [128, D], F32) for i in range(2)]
            pt2 = [mk.sb("p2t%d" % i, [128, 256], F32) for i in range(2)]
            xn2_ = [mk.sb("xn2_%d" % i, [128, D], F32) for i in range(2)]
            xn2b = mk.sb("xn2b", [128, D], BF16)
            xn2T = mk.sb("xn2T", [128, D], BF16)
            st2 = mk.sb("st2", [128, 16], F32)
            st2A = mk.sb("st2A", [128, 16], F32)
            scp = mk.sb("scp", [128, 2048], F32)
            work = mk.sb("work", [128, 2048], F32)
            qpT = T("qpT", work[:].bitcast(BF16)[:, 0:2048], st=work.st)
            stop_ = mk.sb("stop", [128, 256], F32)
            itopu = mk.sb("itopu", [128, 256], U32)
            itopf = mk.sb("itopf", [128, 256], F32)
            cand = T("cand", scp[:], st=scp.st)
            best = mk.sb("best", [128, 128], F32)
            posu = mk.sb("posu", [128, 128], U32)
            posi = mk.sb("posi", [128, 256], I32)
            posf = mk.sb("posf", [128, 256], F32)
            oh = T("oh", work[:], st=work.st)
            oh2 = T("oh2", cand[:], st=cand.st)
            sel = mk.sb("sel", [128, 256], F32)
            junkA = T("junkA", oh[:, 0:1024], st=oh.st)
            idxf = mk.sb("idxf", [128, 128], F32)
            idxi_ = [mk.sb("idxi%d" % i, [128, 128], I32) for i in range(2)]
            gate_ = [mk.sb("gate%d" % i, [128, 128], F32) for i in range(2)]
            dots = mk.sb("dots", [128, 128], F32)
            actv = mk.sb("actv", [128, 128], F32)
            wgt = mk.sb("wgt", [128, 128], F32)
            NGB = 19
            gbuf = [mk.sb("gbuf%d" % i, [128, 2 * D], BF16) for i in range(NGB)]
            junk = mk.sb("junk", [128, D], F32)
            pleg = mk.sb("pleg", [128, D], F32)
            NSM = 19
            dot_r = [mk.sb("dot_r%d" % i, [128, 1], F32) for i in range(NSM)]
            wv_r = [mk.sb("wv_r%d" % i, [128, 2], F32) for i in range(NSM)]
            dg_r = [mk.sb("dg_r%d" % i, [128, 128], BF16) for i in range(8)]
            xn3 = mk.sb("xn3", [128, D], BF16)
            xn3T = mk.sb("xn3T", [128, D], BF16)
            pb16 = mk.sb("pb16", [128, 256], BF16)
            ppT = mk.sb("ppT", [128, 256], BF16)
            print("phase2 sbuf remaining", nc.sbuf_bytes_remaining, flush=True)
            B = pbanks

            def load2(ti):
                mk.dma('sp', xt2[ti % 2][:], x1s[ti * 128:(ti + 1) * 128, :], reads=[x1_tok[ti]], writes=[xt2[ti % 2]],
                       semt=xt2[ti % 2])
                mk.dma('sp', pt2[ti % 2][:], pin[ti * 128:(ti + 1) * 128, :], writes=[pt2[ti % 2]], semt=pt2[ti % 2])

            def rms(xsrc, gt, out32, outb, st2, junk):
                mk.op('act', lambda e: e.activation(out=junk[:], in_=xsrc[:], func=AF.Square, accum_out=st2[:, 0:1]),
                      reads=[xsrc], writes=[junk, st2])
                rsqrt(st2, st2[:, 2:3], st2[:, 0:1], 1.0 / D)
                if out32 is not None:
                    mk.op('dve', lambda e: e.scalar_tensor_tensor(out=out32[:], in0=xsrc[:], scalar=st2[:, 2:3], in1=gt[:],
                                                                  op0=ALU.mult, op1=ALU.mult), reads=[xsrc, st2, gt], writes=[out32])
                    mk.op('act', lambda e: e.copy(out=outb[:], in_=out32[:]), reads=[out32], writes=[outb])
                else:
                    mk.op('dve', lambda e: e.scalar_tensor_tensor(out=outb[:], in0=xsrc[:], scalar=st2[:, 2:3], in1=gt[:],
                                                                  op0=ALU.mult, op1=ALU.mult), reads=[xsrc, st2, gt], writes=[outb])

            def tr8b(src, dst, n=8):
                def f(e):
                    last = None
                    for c in range(n):
                        last = pe_T(B[0], c * 128, src, src[:, c * 128:(c + 1) * 128])(e)
                    return last
                mk.op('pe', f, reads=[src, ident], writes=[B[0]], skip_self=True)
                mk.op('act', lambda e: e.copy(out=dst[:, 0:n * 128], in_=B[0][:].bitcast(BF16)[:, 0:n * 128]),
                      reads=[B[0]], writes=[dst])

            tiles2 = tiles if cfg.get('phase2', True) else []

            def stageA(ti):
                par = ti % 2
                x = xt2[par]
                xn2 = xn2_[par]
                idxi = idxi_[par]
                gate = gate_[par]
                mk.rate = 2.0
                load2(ti)
                rms(x, g_ffn, xn2, xn2b, st2A, junkA)
                tr8b(xn2b, xn2T)
                xT3 = xn2T[:].rearrange("p (c t) -> p c t", c=8)
                for bq in range(4):
                    bank = B[1 + bq]

                    def fq(e, bq=bq, bank=bank):
                        last = None
                        for gi in range(4):
                            g_ = bq * 4 + gi
                            for c in range(8):
                                last = e.matmul(bank[:, gi * 128:(gi + 1) * 128], lhsT=wq_v[:, c, g_ * 128:(g_ + 1) * 128],
                                                rhs=xT3[:, c, :], start=(c == 0), stop=(c == 7))
                        return last
                    mk.op('pe', fq, reads=[xn2T, warena2], writes=[bank], skip_self=True)
                    mk.op('act', lambda e, bq=bq, bank=bank: e.copy(out=qpT[:, bq * 512:(bq + 1) * 512], in_=bank[:]),
                          reads=[bank], writes=[qpT])
                for bq in range(4):
                    bank = B[1 + bq]

                    def fs(e, bq=bq, bank=bank):
                        last = None
                        for gi in range(4):
                            g_ = bq * 4 + gi
                            last = e.matmul(bank[:, gi * 128:(gi + 1) * 128], lhsT=qpT[:, g_ * 128:(g_ + 1) * 128],
                                            rhs=keys_v[:, g_, :], start=True, stop=True)
                        return last
                    mk.op('pe', fs, reads=[qpT, warena2], writes=[bank], skip_self=True)
                    mk.op('act', lambda e, bq=bq, bank=bank: e.copy(out=scp[:, bq * 512:(bq + 1) * 512], in_=bank[:]),
                          reads=[bank], writes=[scp])
                mk.rate = 4.0
                def sg_(g_):
                    return scp[:, g_ * 128:(g_ + 1) * 128]

                def wk_(g_):
                    return work[:, g_ * 128:(g_ + 1) * 128]
                for g_ in range(16):
                    mk.op('dve', lambda e, g_=g_: e.max(out=stop_[:, g_ * 16:g_ * 16 + 8], in_=sg_(g_)),
                          reads=[scp], writes=[stop_], skip_self=True)
                for g_ in range(16):
                    mk.op('dve', lambda e, g_=g_: e.match_replace(out=wk_(g_), in_to_replace=stop_[:, g_ * 16:g_ * 16 + 8],
                                                                  in_values=sg_(g_), imm_value=-1e30),
                          reads=[scp, stop_], writes=[work], skip_self=True)
                for g_ in range(16):
                    mk.op('dve', lambda e, g_=g_: e.max(out=stop_[:, g_ * 16 + 8:g_ * 16 + 16], in_=wk_(g_)),
                          reads=[work], writes=[stop_], skip_self=True)
                for g_ in range(16):
                    mk.op('dve', lambda e, g_=g_: e.max_index(out=itopu[:, g_ * 16:g_ * 16 + 8], in_max=stop_[:, g_ * 16:g_ * 16 + 8],
                                                              in_values=sg_(g_)), reads=[scp, stop_], writes=[itopu], skip_self=True)
                for g_ in range(16):
                    mk.op('dve', lambda e, g_=g_: e.max_index(out=itopu[:, g_ * 16 + 8:g_ * 16 + 16],
                                                              in_max=stop_[:, g_ * 16 + 8:g_ * 16 + 16], in_values=sg_(g_)),
                          reads=[scp, stop_], writes=[itopu], skip_self=True)
                mk.op('dve', lambda e: e.tensor_copy(out=itopf[:], in_=itopu[:]), reads=[itopu], writes=[itopf])
                st4 = stop_[:].rearrange("t (h c k) -> t h c k", h=8, c=2)
                mk.op('dve', lambda e: e.tensor_tensor(out=cand[:].rearrange("t (h i j) -> t h i j", h=8, i=16),
                                                       in0=st4[:, :, 0, :].unsqueeze(3).to_broadcast([128, 8, 16, 16]),
                                                       in1=st4[:, :, 1, :].unsqueeze(2).to_broadcast([128, 8, 16, 16]), op=ALU.add),
                      reads=[stop_], writes=[cand])
                def cg_(h):
                    return cand[:, h * 256:(h + 1) * 256]

                def wk2_(h):
                    return work[:, h * 256:(h + 1) * 256]
                for h in range(8):
                    mk.op('dve', lambda e, h=h: e.max(out=best[:, h * 16:h * 16 + 8], in_=cg_(h)),
                          reads=[cand], writes=[best], skip_self=(h > 0))
                for h in range(8):
                    mk.op('dve', lambda e, h=h: e.match_replace(out=wk2_(h), in_to_replace=best[:, h * 16:h * 16 + 8], in_values=cg_(h),
                                                                imm_value=-1e30), reads=[cand, best], writes=[work], skip_self=True)
                for h in range(8):
                    mk.op('dve', lambda e, h=h: e.max(out=best[:, h * 16 + 8:h * 16 + 16], in_=wk2_(h)),
                          reads=[work], writes=[best], skip_self=True)
                for h in range(8):
                    mk.op('dve', lambda e, h=h: e.max_index(out=posu[:, h * 16:h * 16 + 8], in_max=best[:, h * 16:h * 16 + 8],
                                                            in_values=cg_(h)), reads=[cand, best], writes=[posu], skip_self=True)
                for h in range(8):
                    mk.op('dve', lambda e, h=h: e.max_index(out=posu[:, h * 16 + 8:h * 16 + 16], in_max=best[:, h * 16 + 8:h * 16 + 16],
                                                            in_values=cg_(h)), reads=[cand, best], writes=[posu], skip_self=True)
                mk.op('dve', lambda e: e.tensor_single_scalar(out=posi[:, 0:128], in_=posu[:].bitcast(I32), scalar=4,
                                                              op=ALU.logical_shift_right), reads=[posu], writes=[posi])
                mk.op('dve', lambda e: e.tensor_single_scalar(out=posi[:, 128:256], in_=posu[:].bitcast(I32), scalar=15,
                                                              op=ALU.bitwise_and), reads=[posu], writes=[posi])
                mk.op('dve', lambda e: e.tensor_copy(out=posf[:], in_=posi[:]), reads=[posi], writes=[posf])
                it4 = itopf[:].rearrange("t (h c k) -> t h c k", h=8, c=2)
                ohs = (oh, oh2)
                oh4s = [o_[:].rearrange("t (h k i) -> t h k i", h=8, k=16) for o_ in ohs]
                for c in range(2):
                    pf = posf[:, c * 128:(c + 1) * 128].rearrange("t (h k) -> t h k", h=8)
                    mk.op('dve', lambda e, pf=pf, c=c: e.tensor_tensor(
                        out=oh4s[c], in0=iota16[:].unsqueeze(1).unsqueeze(1).to_broadcast([128, 8, 16, 16]),
                        in1=pf.unsqueeze(3).to_broadcast([128, 8, 16, 16]), op=ALU.is_equal), reads=[iota16, posf], writes=[ohs[c]])
                for c in range(2):
                    mk.op('dve', lambda e, c=c: e.tensor_tensor(
                        out=oh4s[c], in0=oh4s[c], in1=it4[:, :, c, :].unsqueeze(2).to_broadcast([128, 8, 16, 16]), op=ALU.mult),
                        reads=[ohs[c], itopf], writes=[ohs[c]], skip_self=True)
                for c in range(2):
                    mk.op('dve', lambda e, c=c: e.tensor_reduce(out=sel[:, c * 128:(c + 1) * 128],
                                                                in_=ohs[c][:].rearrange("t (s i) -> t s i", i=16), axis=AX.X, op=ALU.add),
                          reads=[ohs[c]], writes=[sel], skip_self=True)
                mk.op('dve', lambda e: e.scalar_tensor_tensor(out=idxf[:], in0=sel[:, 0:128], scalar=128.0, in1=sel[:, 128:256],
                                                              op0=ALU.mult, op1=ALU.add), reads=[sel], writes=[idxf])
                mk.op('dve', lambda e: e.tensor_copy(out=idxi[:], in_=idxf[:]), reads=[idxf], writes=[idxi])
                b3 = best[:].rearrange("t (h k) -> t h k", h=8)
                mk.op('dve', lambda e: e.tensor_tensor(out=gate[:].rearrange("t (h k) -> t h k", h=8), in0=b3,
                                                       in1=b3[:, :, 0:1].to_broadcast([128, 8, 16]), op=ALU.subtract),
                      reads=[best], writes=[gate])
                mk.op('act', lambda e: e.activation(out=gate[:], in_=gate[:], func=AF.Exp), reads=[gate], writes=[gate])
                mk.op('dve', lambda e: e.tensor_reduce(out=st2A[:, 8:16], in_=gate[:].rearrange("t (h k) -> t h k", h=8),
                                                       axis=AX.X, op=ALU.add), reads=[gate], writes=[st2A])
                mk.op('dve', lambda e: e.reciprocal(out=st2A[:, 8:16], in_=st2A[:, 8:16]), reads=[st2A], writes=[st2A])
                mk.op('dve', lambda e: e.tensor_tensor(out=gate[:].rearrange("t (h k) -> t h k", h=8),
                                                       in0=gate[:].rearrange("t (h k) -> t h k", h=8),
                                                       in1=st2A[:, 8:16].unsqueeze(2).to_broadcast([128, 8, 16]), op=ALU.mult),
                      reads=[gate, st2A], writes=[gate])

            def stageB(ti, nxt_q):
                par = ti % 2
                x = xt2[par]
                pt = pt2[par]
                xn2 = xn2_[par]
                idxi = idxi_[par]
                gate = gate_[par]
                mk._waits('pool', [comb_t], [])
                for s_ in range(128):
                    gb = gbuf[s_ % NGB]
                    dt_ = dot_r[s_ % NSM]
                    wv = wv_r[s_ % NSM]
                    dg = dg_r[s_ % 8]
                    mk.gather(gb, gb[:], comb[:, :], idxi, idxi[:, s_:s_ + 1])
                    mk.op('dve', lambda e, gb=gb, dt_=dt_: e.scalar_tensor_tensor(
                        out=junk[:], in0=gb[:, 0:D], scalar=1.0, in1=xn2[:], op0=ALU.mult, op1=ALU.mult,
                        accum_out=dt_[:, 0:1]), reads=[gb, xn2], writes=[junk, dt_], skip_self=True)
                    mk.op('act', lambda e, dt_=dt_, wv=wv: e.activation(out=wv[:, 0:1], in_=dt_[:, 0:1], func=GELU),
                          reads=[dt_], writes=[wv])
                    mk.op('act', lambda e, wv=wv, s_=s_: e.mul(out=wv[:, 1:2], in_=wv[:, 0:1], mul=gate[:, s_:s_ + 1]),
                          reads=[wv, gate], writes=[wv])
                    mk.op('act', lambda e, wv=wv, dg=dg: e.activation(out=dg[:], in_=identf[:], func=AF.Copy, scale=wv[:, 1:2]),
                          reads=[wv, identf], writes=[dg])

                    def fv(e, s_=s_, gb=gb, dg=dg):
                        last = None
                        for n in range(2):
                            last = e.matmul(B[5 + n][:], lhsT=dg[:], rhs=gb[:, D + n * 512:D + (n + 1) * 512],
                                            start=(s_ == 0), stop=(s_ == 127))
                        return last
                    mk.op('pe', fv, reads=[gb, dg], writes=[B[5], B[6]], skip_self=True)
                    mk.play(nxt_q, 1.0, lag=10)
                if len(nxt_q) and ti == 1:
                    print('replay leftover at drain:', len(nxt_q), flush=True)
                mk.play(nxt_q, 10 ** 9)
                for n in range(2):
                    sl = slice(n * 512, (n + 1) * 512)
                    mk.op('dve', lambda e, n=n, sl=sl: e.tensor_tensor(out=x[:, sl], in0=B[5 + n][:], in1=x[:, sl], op=ALU.add),
                          reads=[B[5 + n], x], writes=[x])
                mk.rec = []
                mk.rate = 2.0
                rms(x, g_ple, None, xn3, st2, pleg)
                tr8b(xn3, xn3T)
                x3T = xn3T[:].rearrange("p (c t) -> p c t", c=8)

                def fpg(e):
                    last = None
                    for n in range(2):
                        for c in range(8):
                            last = e.matmul(B[1 + n][:], lhsT=x3T[:, c, :], rhs=wpg_v[:, c, n * 512:(n + 1) * 512],
                                            start=(c == 0), stop=(c == 7))
                    return last
                mk.op('pe', fpg, reads=[xn3T, warena2], writes=[B[1], B[2]], skip_self=True)
                for n in range(2):
                    mk.op('act', lambda e, n=n: e.activation(out=pleg[:, n * 512:(n + 1) * 512], in_=B[1 + n][:], func=AF.Sigmoid),
                          reads=[B[1 + n]], writes=[pleg])
                mk.op('act', lambda e: e.copy(out=pb16[:], in_=pt[:]), reads=[pt], writes=[pb16])
                tr8b(pb16, ppT, n=2)
                pT3 = ppT[:].rearrange("p (c t) -> p c t", c=2)

                def fpl(e):
                    last = None
                    for n in range(2):
                        for c in range(2):
                            last = e.matmul(B[3 + n][:], lhsT=pT3[:, c, :], rhs=wpl_v[:, c, n * 512:(n + 1) * 512],
                                            start=(c == 0), stop=(c == 1))
                    return last
                mk.op('pe', fpl, reads=[ppT, warena2], writes=[B[3], B[4]], skip_self=True)
                for n in range(2):
                    sl = slice(n * 512, (n + 1) * 512)
                    mk.op('dve', lambda e, n=n, sl=sl: e.tensor_tensor(out=pleg[:, sl], in0=B[3 + n][:], in1=pleg[:, sl], op=ALU.mult),
                          reads=[B[3 + n], pleg], writes=[pleg])
                mk.op('dve', lambda e: e.tensor_tensor(out=x[:], in0=x[:], in1=pleg[:], op=ALU.add), reads=[x, pleg], writes=[x])
                mk.dma('sp', y[ti * 128:(ti + 1) * 128, :], x[:], reads=[x], semt=x)
                tail = mk.rec
                mk.rec = None
                return tail

            if tiles2:
                mk.rec = []
                stageA(tiles2[0])
                q0 = mk.rec
                mk.rec = None
                mk.play(q0, 10 ** 9)
            pending_tail = []
            for tpos, ti in enumerate(tiles2):
                nxt_q = pending_tail
                if tpos + 1 < len(tiles2):
                    mk.rec = []
                    stageA(tiles2[tpos + 1])
                    nxt_q = nxt_q + mk.rec
                    mk.rec = None
                if tpos == 1:
                    print('replay queue: ops', len(nxt_q), 'slot budget needed', sum(1.0 / r[3] for r in nxt_q), flush=True)
                pending_tail = stageB(ti, nxt_q)
            mk.play(pending_tail, 10 ** 9)
            out_tiles_all = out_tiles + xt2
            mk._waits('sp', [], xt2)
            mk.barrier()
        print("instructions emitted:", mk.n_ins, flush=True)
    return nc


_NC_CACHE = {}


def _get_nc():
    if "nc" not in _NC_CACHE:
        _NC_CACHE["nc"] = build()
    return _NC_CACHE["nc"]


def kernel(x_prompt, x_sample, cache_k, cache_v, p_prompt, p_sample,
           attn_norm_g, w_in, q_norm_g, k_norm_g, attn_sinks,
           sgu_norm_g, sgu_norm_b, sgu_w, sgu_b,
           w_branch_a, w_branch_b, w_out,
           ffn_norm_g, peer_w_q, peer_sub_keys, peer_u, peer_v,
           ple_norm_g, w_ple, w_ple_gate):
    f = lambda a: np.ascontiguousarray(np.asarray(a, dtype=np.float32))
    x_prompt, x_sample, cache_k, cache_v = f(x_prompt), f(x_sample), f(cache_k), f(cache_v)
    p_prompt, p_sample = f(p_prompt), f(p_sample)
    nc = _get_nc()
    consts = _consts()
    rep = lambda v, n=128: np.ascontiguousarray(np.broadcast_to(f(v).reshape(1, -1), (n, f(v).size)))
    sw = f(sgu_w)[0]
    sb = f(sgu_b)[0]
    small = {
        "g_attn": rep(attn_norm_g[0]), "g_ffn": rep(ffn_norm_g[0]), "g_ple": rep(ple_norm_g[0]),
        "gq": rep(q_norm_g[0]), "gk": rep(k_norm_g[0]), "sinks": rep(attn_sinks[0]),
        "ln_g": rep(sgu_norm_g[0]), "ln_b": rep(sgu_norm_b[0]),
        "bcol_p": np.ascontiguousarray(sb.T),
        "bcol_s": np.ascontiguousarray(np.tile(sb[:, :8].T, (16, 1))),
        "wsT_p": np.ascontiguousarray(sw.transpose(2, 0, 1)).reshape(128, 512),
        "wsT_s": np.ascontiguousarray(np.tile(sw[:, :8, :8].transpose(2, 0, 1), (16, 1, 16))).reshape(128, 512),
    }
    shared = {
        "w_in": f(w_in)[0], "w_a": f(w_branch_a)[0], "w_b": f(w_branch_b)[0], "w_out": f(w_out)[0],
        "w_q": f(peer_w_q)[0],
        "keysT": np.ascontiguousarray(f(peer_sub_keys)[0].reshape(16, 128, 128).transpose(2, 0, 1)).reshape(128, 2048),
        "peer_u": f(peer_u)[0], "peer_v": f(peer_v)[0], "w_ple": f(w_ple)[0], "w_pg": f(w_ple_gate)[0],
    }
    for k, v in consts.items():
        shared["c_" + k] = v
    for k, v in small.items():
        shared["s_" + k] = v
    in_maps = []
    for c in range(NCORES):
        m = dict(shared)
        xs = x_sample[c * 16:(c + 1) * 16].reshape(128, D)
        m["xin"] = np.ascontiguousarray(np.concatenate([x_prompt[c], xs], axis=0))
        ps = p_sample[0, c * 16:(c + 1) * 16].reshape(128, 256)
        m["pin"] = np.ascontiguousarray(np.concatenate([p_prompt[0, c], ps], axis=0))
        m["ck"] = np.ascontiguousarray(cache_k[0, c * 16:(c + 1) * 16].reshape(16, 128, 128).transpose(1, 0, 2))
        m["cv"] = np.ascontiguousarray(cache_v[0, c * 16:(c + 1) * 16].reshape(16, 128, 128).transpose(1, 0, 2))
        in_maps.append(m)
    res = run_bass_kernel_spmd(nc, in_maps, core_ids=list(range(NCORES)))
    R = res.results
    y_p = np.stack([R[c]["y"][:SEQ] for c in range(NCORES)])
    y_s = np.concatenate([R[c]["y"][SEQ:].reshape(16, 8, D) for c in range(NCORES)], axis=0)
    nkp = np.stack([R[c]["o_kp"].reshape(128, 2, 64) for c in range(NCORES)])[None]
    nvp = np.stack([R[c]["o_vp"].reshape(128, 2, 64) for c in range(NCORES)])[None]
    nks = np.concatenate([R[c]["o_ks"].reshape(16, 8, 2, 64) for c in range(NCORES)], axis=0)[None]
    nvs = np.concatenate([R[c]["o_vs"].reshape(16, 8, 2, 64) for c in range(NCORES)], axis=0)[None]
    nsp = np.stack([R[c]["o_sp"] for c in range(NCORES)])[None]
    nss = np.concatenate([R[c]["o_ss"].reshape(16, 8, 512) for c in range(NCORES)], axis=0)[None]
    return (y_p.astype(np.float32), y_s.astype(np.float32), nkp.astype(np.float32), nvp.astype(np.float32),
            nks.astype(np.float32), nvs.astype(np.float32), nsp.astype(np.float32), nss.astype(np.float32))
```
